# Optimizing a Trainium2 kernel written in Bass

```python
import jax
import jax.numpy as jnp
from jax import lax
import numpy as np


D_MODEL = 1024
BATCH = 16
SEQ = 2048
DEPTH = 2

N_EVEN = (DEPTH + 1) // 2
N_ODD = DEPTH // 2
CONV_WIDTH = 4
EPS = 1e-6
SSD_HEADS = 16
SSD_HEAD_DIM = 64
SSD_INNER = SSD_HEADS * SSD_HEAD_DIM
SSD_GROUPS = 2
SSD_STATE = 128
SSD_CHUNK = 128
SSD_CONV_CH = SSD_INNER + 2 * SSD_GROUPS * SSD_STATE
GDN_HEADS = 8
GDN_DK = 128
GDN_DV = 128
GDN_CHUNK = 64
GDN_QK = GDN_HEADS * GDN_DK
GDN_VAL = GDN_HEADS * GDN_DV
GDN_CONV_CH = 2 * GDN_QK + GDN_VAL
A_SPLIT_SIZES = (SSD_INNER, SSD_CONV_CH, SSD_HEADS, GDN_CONV_CH, GDN_VAL, GDN_HEADS, GDN_HEADS)
A_IN_WIDTH = sum(A_SPLIT_SIZES)
A_OUT_WIDTH = SSD_INNER + GDN_VAL
SB_HEADS = 16
SB_HEAD_DIM = D_MODEL // SB_HEADS
SB_BLOCK = 128
D_FF = 4 * D_MODEL

kernel_name = 'hybrid_ssd_gdn_stickbreak_block'


def rmsnorm(x, w):
    xf = x.astype(jnp.float32)
    y = xf * lax.rsqrt(jnp.mean(xf * xf, axis=-1, keepdims=True) + EPS)
    return (y * w.astype(jnp.float32)).astype(x.dtype)


def l2norm(x):
    xf = x.astype(jnp.float32)
    return xf * lax.rsqrt(jnp.sum(xf * xf, axis=-1, keepdims=True) + EPS)


def split_cols(t, sizes):
    offs = np.cumsum(np.array(sizes))[:-1].tolist()
    return jnp.split(t, offs, axis=-1)


def causal_conv(x, w):
    k = w.shape[0]
    return lax.conv_general_dilated(
        x, w[:, None, :].astype(x.dtype), window_strides=(1,), padding=[(k - 1, 0)],
        dimension_numbers=('NWC', 'WIO', 'NWC'), feature_group_count=x.shape[-1])


def ssd_mixer(z, xbc, dt_raw, conv_w, conv_b, dt_bias, a_log, d_skip, norm_w):
    bsz, seqlen, _ = z.shape
    nc = seqlen // SSD_CHUNK
    hpg = SSD_HEADS // SSD_GROUPS
    xbc = jax.nn.silu(causal_conv(xbc, conv_w) + conv_b.astype(xbc.dtype)).astype(jnp.float32)
    xs, bm, cm = split_cols(xbc, (SSD_INNER, SSD_GROUPS * SSD_STATE, SSD_GROUPS * SSD_STATE))
    xs = xs.reshape(bsz, nc, SSD_CHUNK, SSD_GROUPS, hpg, SSD_HEAD_DIM)
    bm = bm.reshape(bsz, nc, SSD_CHUNK, SSD_GROUPS, SSD_STATE)
    cm = cm.reshape(bsz, nc, SSD_CHUNK, SSD_GROUPS, SSD_STATE)
    dt = jax.nn.softplus(dt_raw.astype(jnp.float32) + dt_bias.astype(jnp.float32))
    dt = dt.reshape(bsz, nc, SSD_CHUNK, SSD_GROUPS, hpg)
    a = dt * (-jnp.exp(a_log.astype(jnp.float32))).reshape(SSD_GROUPS, hpg)
    a_cum = jnp.cumsum(a, axis=2)
    xdt = xs * dt[..., None]
    causal = jnp.tril(jnp.ones((SSD_CHUNK, SSD_CHUNK), dtype=bool))
    seg = a_cum[:, :, :, None] - a_cum[:, :, None, :]
    lmat = jnp.exp(jnp.where(causal[:, :, None, None], seg, -jnp.inf))
    cb = jnp.einsum('bclgn,bcsgn->bclsg', cm, bm)
    y_diag = jnp.einsum('bclsg,bclsgh,bcsghp->bclghp', cb, lmat, xdt)
    decay_to_end = jnp.exp(a_cum[:, :, -1:] - a_cum)
    states = jnp.einsum('bclgn,bclgh,bclghp->bcghpn', bm, decay_to_end, xdt)
    chunk_decay = jnp.exp(a_cum[:, :, -1])

    def step(s, inp):
        st, dec = inp
        return s * dec[..., None, None] + st, s

    init = jnp.zeros((bsz, SSD_GROUPS, hpg, SSD_HEAD_DIM, SSD_STATE), jnp.float32)
    _, prev = lax.scan(step, init, (jnp.moveaxis(states, 1, 0), jnp.moveaxis(chunk_decay, 1, 0)))
    prev = jnp.moveaxis(prev, 0, 1)
    y_off = jnp.einsum('bclgn,bcghpn,bclgh->bclghp', cm, prev, jnp.exp(a_cum))
    y = y_diag + y_off + xs * d_skip.astype(jnp.float32).reshape(SSD_GROUPS, hpg)[:, :, None]
    y = y.reshape(bsz, seqlen, SSD_INNER) * jax.nn.silu(z.astype(jnp.float32))
    y = y.reshape(bsz, seqlen, SSD_GROUPS, SSD_INNER // SSD_GROUPS)
    y = y * lax.rsqrt(jnp.mean(y * y, axis=-1, keepdims=True) + EPS)
    return y.reshape(bsz, seqlen, SSD_INNER) * norm_w.astype(jnp.float32)


def gdn_mixer(qkv, zg, b_raw, a_raw, conv_w, a_log, dt_bias, norm_w):
    bsz, seqlen, _ = qkv.shape
    c = GDN_CHUNK
    nc = seqlen // c
    qkv = jax.nn.silu(causal_conv(qkv, conv_w))
    q, k, v = split_cols(qkv, (GDN_QK, GDN_QK, GDN_VAL))
    q = l2norm(q.reshape(bsz, seqlen, GDN_HEADS, GDN_DK)) * (GDN_DK ** -0.5)
    k = l2norm(k.reshape(bsz, seqlen, GDN_HEADS, GDN_DK))
    v = v.reshape(bsz, seqlen, GDN_HEADS, GDN_DV).astype(jnp.float32)
    beta = jax.nn.sigmoid(b_raw.astype(jnp.float32))
    g = -jnp.exp(a_log.astype(jnp.float32)) * jax.nn.softplus(a_raw.astype(jnp.float32) + dt_bias.astype(jnp.float32))

    def chunks(t):
        return t.reshape(bsz, nc, c, GDN_HEADS, -1).transpose(0, 3, 1, 2, 4)

    q, k, v = chunks(q), chunks(k), chunks(v)
    beta = beta.reshape(bsz, nc, c, GDN_HEADS).transpose(0, 3, 1, 2)
    gc = jnp.cumsum(g.reshape(bsz, nc, c, GDN_HEADS).transpose(0, 3, 1, 2), axis=-1)
    causal = jnp.tril(jnp.ones((c, c), dtype=bool))
    strict = jnp.tril(jnp.ones((c, c), dtype=bool), k=-1)
    decay_mat = jnp.exp(jnp.where(causal, gc[..., :, None] - gc[..., None, :], -jnp.inf))
    kk = jnp.einsum('bhnld,bhnsd->bhnls', k, k)
    amat = jnp.where(strict, kk * beta[..., None] * decay_mat, 0.0)
    eye = jnp.eye(c, dtype=jnp.float32)
    tmat = lax.linalg.triangular_solve(eye + amat, jnp.broadcast_to(eye, amat.shape),
                                       left_side=True, lower=True, unit_diagonal=True)
    u_base = jnp.einsum('bhnls,bhnsd->bhnld', tmat, v * beta[..., None])
    w = jnp.einsum('bhnls,bhnsd->bhnld', tmat, k * (beta * jnp.exp(gc))[..., None])
    qk = jnp.where(causal, jnp.einsum('bhnld,bhnsd->bhnls', q, k) * decay_mat, 0.0)
    q_dec = q * jnp.exp(gc)[..., None]
    k_dec = k * jnp.exp(gc[..., -1:] - gc)[..., None]
    chunk_decay = jnp.exp(gc[..., -1])

    def step(s, inp):
        u_b, w_c, q_d, k_d, qk_c, dec = inp
        u = u_b - jnp.einsum('bhld,bhde->bhle', w_c, s)
        o = jnp.einsum('bhld,bhde->bhle', q_d, s) + jnp.einsum('bhls,bhse->bhle', qk_c, u)
        s = s * dec[..., None, None] + jnp.einsum('bhld,bhle->bhde', k_d, u)
        return s, o

    seq_in = tuple(jnp.moveaxis(t, 2, 0) for t in (u_base, w, q_dec, k_dec, qk, chunk_decay))
    s0 = jnp.zeros((bsz, GDN_HEADS, GDN_DK, GDN_DV), jnp.float32)
    _, o = lax.scan(step, s0, seq_in)
    o = o.transpose(1, 0, 3, 2, 4).reshape(bsz, seqlen, GDN_HEADS, GDN_DV)
    o = rmsnorm(o, norm_w) * jax.nn.silu(zg.astype(jnp.float32).reshape(bsz, seqlen, GDN_HEADS, GDN_DV))
    return o.reshape(bsz, seqlen, GDN_VAL)


def stick_breaking_attention(h, w_qkv, q_norm_w, k_norm_w, w_o):
    bsz, seqlen, _ = h.shape
    q, k, v = split_cols(h @ w_qkv, (D_MODEL, D_MODEL, D_MODEL))
    q = rmsnorm(q.reshape(bsz, seqlen, SB_HEADS, SB_HEAD_DIM), q_norm_w).transpose(0, 2, 1, 3)
    k = rmsnorm(k.reshape(bsz, seqlen, SB_HEADS, SB_HEAD_DIM), k_norm_w).transpose(0, 2, 1, 3)
    v = v.reshape(bsz, seqlen, SB_HEADS, SB_HEAD_DIM).transpose(0, 2, 1, 3).astype(jnp.float32)
    scale = SB_HEAD_DIM ** -0.5
    outs = []
    for blk in range(seqlen // SB_BLOCK):
        q0 = blk * SB_BLOCK
        kend = q0 + SB_BLOCK
        logits = jnp.einsum('bhtd,bhsd->bhts', q[:, :, q0:kend], k[:, :, :kend]).astype(jnp.float32) * scale
        t_pos = q0 + jnp.arange(SB_BLOCK)[:, None]
        s_pos = jnp.arange(kend)[None, :]
        valid = s_pos < t_pos
        log_beta = jax.nn.log_sigmoid(logits)
        log_1m = jnp.where(valid, jax.nn.log_sigmoid(-logits), 0.0)
        suffix = lax.cumsum(log_1m, axis=3, reverse=True) - log_1m
        att = jnp.where(valid, jnp.exp(log_beta + suffix), 0.0)
        outs.append(jnp.einsum('bhts,bhsd->bhtd', att, v[:, :, :kend]))
    o = jnp.concatenate(outs, axis=2).transpose(0, 2, 1, 3).reshape(bsz, seqlen, D_MODEL)
    return o.astype(h.dtype) @ w_o


def sqrelu_mlp(h, w1, w2):
    a = jax.nn.relu(h @ w1)
    return (a * a) @ w2


def _normal(k, shape, scale):
    return jax.random.normal(k, shape, jnp.float32) * scale


def _gain(k, shape):
    return 1.0 + 0.02 * jax.random.normal(k, shape, jnp.float32)


def _dt_bias(k, shape):
    dt = jnp.exp(jax.random.uniform(k, shape, jnp.float32) * (jnp.log(0.1) - jnp.log(0.001)) + jnp.log(0.001))
    return dt + jnp.log(-jnp.expm1(-dt))


def _a_log(k, shape):
    return jnp.log(jax.random.uniform(k, shape, jnp.float32, 1.0, 16.0))


def setup_inputs(seed: int = 0) -> dict:
    key = jax.random.key(seed)
    ks = jax.random.split(key, 24)
    return {
        'x': jax.random.normal(ks[0], (BATCH, SEQ, D_MODEL), jnp.float32),
        'a_norm_w': _gain(ks[1], (N_EVEN, D_MODEL)),
        'a_w_in': _normal(ks[2], (N_EVEN, D_MODEL, A_IN_WIDTH), D_MODEL ** -0.5),
        'ssd_conv_w': _normal(ks[3], (N_EVEN, CONV_WIDTH, SSD_CONV_CH), CONV_WIDTH ** -0.5),
        'ssd_conv_b': _normal(ks[4], (N_EVEN, SSD_CONV_CH), 0.02),
        'ssd_dt_bias': _dt_bias(ks[5], (N_EVEN, SSD_HEADS)),
        'ssd_a_log': _a_log(ks[6], (N_EVEN, SSD_HEADS)),
        'ssd_d_skip': _gain(ks[7], (N_EVEN, SSD_HEADS)),
        'ssd_norm_w': _gain(ks[8], (N_EVEN, SSD_INNER)),
        'gdn_conv_w': _normal(ks[9], (N_EVEN, CONV_WIDTH, GDN_CONV_CH), CONV_WIDTH ** -0.5),
        'gdn_a_log': _a_log(ks[10], (N_EVEN, GDN_HEADS)),
        'gdn_dt_bias': _dt_bias(ks[11], (N_EVEN, GDN_HEADS)),
        'gdn_norm_w': _gain(ks[12], (N_EVEN, GDN_DV)),
        'a_w_out': _normal(ks[13], (N_EVEN, A_OUT_WIDTH, D_MODEL), A_OUT_WIDTH ** -0.5),
        'c_norm_w': _gain(ks[14], (N_ODD, D_MODEL)),
        'c_w_qkv': _normal(ks[15], (N_ODD, D_MODEL, 3 * D_MODEL), D_MODEL ** -0.5),
        'c_q_norm_w': _gain(ks[16], (N_ODD, SB_HEAD_DIM)),
        'c_k_norm_w': _gain(ks[17], (N_ODD, SB_HEAD_DIM)),
        'c_w_o': _normal(ks[18], (N_ODD, D_MODEL, D_MODEL), D_MODEL ** -0.5),
        'mlp_norm_w': _gain(ks[19], (DEPTH, D_MODEL)),
        'mlp_w1': _normal(ks[20], (DEPTH, D_MODEL, D_FF), D_MODEL ** -0.5),
        'mlp_w2': _normal(ks[21], (DEPTH, D_FF, D_MODEL), D_FF ** -0.5),
    }


def reference(x, a_norm_w, a_w_in, ssd_conv_w, ssd_conv_b, ssd_dt_bias, ssd_a_log, ssd_d_skip,
              ssd_norm_w, gdn_conv_w, gdn_a_log, gdn_dt_bias, gdn_norm_w, a_w_out,
              c_norm_w, c_w_qkv, c_q_norm_w, c_k_norm_w, c_w_o,
              mlp_norm_w, mlp_w1, mlp_w2):
    for layer in range(DEPTH):
        i = layer // 2
        if layer % 2 == 0:
            h = rmsnorm(x, a_norm_w[i])
            proj = h @ a_w_in[i]
            s_z, s_xbc, s_dt, g_qkv, g_z, g_b, g_a = split_cols(proj, A_SPLIT_SIZES)
            y_ssd = ssd_mixer(s_z, s_xbc, s_dt, ssd_conv_w[i], ssd_conv_b[i], ssd_dt_bias[i],
                              ssd_a_log[i], ssd_d_skip[i], ssd_norm_w[i])
            y_gdn = gdn_mixer(g_qkv, g_z, g_b, g_a, gdn_conv_w[i], gdn_a_log[i], gdn_dt_bias[i], gdn_norm_w[i])
            mix = jnp.concatenate([y_ssd, y_gdn], axis=-1).astype(x.dtype) @ a_w_out[i]
        else:
            h = rmsnorm(x, c_norm_w[i])
            mix = stick_breaking_attention(h, c_w_qkv[i], c_q_norm_w[i], c_k_norm_w[i], c_w_o[i])
        x = x + mix.astype(x.dtype)
        x = x + sqrelu_mlp(rmsnorm(x, mlp_norm_w[layer]), mlp_w1[layer], mlp_w2[layer]).astype(x.dtype)
    return x
```

```python
import os
import numpy as np
from contextlib import ExitStack
import concourse.bass as bass
import concourse.mybir as mybir
from concourse.bass_utils import run_bass_kernel_spmd

F32 = mybir.dt.float32
BF16 = mybir.dt.bfloat16
AF = mybir.ActivationFunctionType
ALU = mybir.AluOpType

L = 2048
D = 1024
EPS = 1e-6
EPOCH = 30000
NDSEM = 8


class Sched:
    CE = ('pe', 'act', 'dve', 'pool')

    def __init__(self, nc, esem, dsem):
        self.nc = nc
        self.esem = esem
        self.dsem = dsem
        self.cnt = {e: 0 for e in self.CE}
        self.ndma = {'sp': 0, 'pool': 0}
        self.seen = {e: {} for e in ('pe', 'act', 'dve', 'pool', 'sp')}
        self.reset()

    def reset(self):
        self.ops = {e: [] for e in ('pe', 'act', 'dve', 'pool', 'sp')}
        self.last_w = {}
        self.readers = {}

    def op(self, eng, fn, reads=(), writes=(), dma=False):
        if eng == 'pool' and not dma and os.environ.get("POOL2DVE"):
            eng = 'dve'
        self.nop_total = getattr(self, 'nop_total', 0) + 1
        cut = os.environ.get("OPCUT")
        if cut and self.nop_total > int(cut):
            return None
        if os.environ.get("OPTRACE"):
            import traceback
            fr = traceback.extract_stack(limit=4)
            print("OP", self.nop_total, eng, "dma" if dma else "", [f"{f.name}:{f.lineno}" for f in fr[:-1]])
        writes = list(writes) + [r for r in reads if r[0] in ('ps', 'psacc') and r not in writes]
        idx = len(self.ops[eng])
        tok = (eng, idx)
        deps = {}
        for r in reads:
            w = self.last_w.get(r)
            if w is not None:
                deps[w] = True
        for r in writes:
            w = self.last_w.get(r)
            if w is not None and w not in deps:
                deps[w] = False
            for t in self.readers.get(r, ()):
                if t not in deps:
                    deps[t] = False
        for r in reads:
            self.readers.setdefault(r, []).append(tok)
        for r in writes:
            self.last_w[r] = tok
            self.readers[r] = []
        self.ops[eng].append(dict(fn=fn, deps=deps, dma=dma))
        return tok

    def flush(self):
        nc = self.nc
        ops = self.ops
        need = set()
        for e, lst in ops.items():
            for i, o in enumerate(lst):
                keep = []
                for (e2, i2), raw in o['deps'].items():
                    o2 = ops[e2][i2]
                    if o2['dma']:
                        keep.append((e2, i2))
                    elif e2 == e:
                        if e == 'pe':
                            continue
                        keep.append((e2, i2))
                    else:
                        keep.append((e2, i2))
                o['keep'] = keep
                for k in keep:
                    if not ops[k[0]][k[1]]['dma']:
                        need.add(k)
        for e in self.CE:
            for i, o in enumerate(ops[e]):
                if o['dma']:
                    continue
                if (e, i) in need:
                    self.cnt[e] += 1
                    c = self.cnt[e]
                    o['sig'] = (self.esem[e][(c - 1) // EPOCH], (c - 1) % EPOCH + 1)
                else:
                    o['sig'] = None
        pending = {'sp': [], 'pool': []}
        for q in ('sp', 'pool'):
            for o in ops[q]:
                if o['dma']:
                    n = self.ndma[q]
                    self.ndma[q] += 1
                    o['sig'] = (self.dsem[q][n % NDSEM], 16 * (n // NDSEM + 1))
                    o['prewait'] = (self.dsem[q][n % NDSEM], 16 * (n // NDSEM)) if n >= NDSEM else None
                    pending[q].append(o['sig'])

        def emit(e, eng):
            seen = self.seen[e]

            def wait(s, v):
                if seen.get(s[0], 0) >= v:
                    return
                seen[s[0]] = v
                eng.wait_ge(s[1], v)

            for o in ops[e]:
                if o.get('prewait') is not None:
                    wait(*o['prewait'])
                for k in o['keep']:
                    sg = ops[k[0]][k[1]]['sig']
                    wait(*sg)
                ins = o['fn'](eng)
                if o['sig'] is not None:
                    ins.then_inc(o['sig'][0][1], 16 if o['dma'] else 1)
            if e in pending:
                for sg in pending[e][-NDSEM:]:
                    wait(*sg)

        with nc.Block() as block:
            if ops['sp']:
                @block.sync
                def _(eng):
                    emit('sp', eng)
            if ops['pe']:
                @block.tensor
                def _(eng):
                    emit('pe', eng)
            if ops['act']:
                @block.scalar
                def _(eng):
                    emit('act', eng)
            if ops['dve']:
                @block.vector
                def _(eng):
                    emit('dve', eng)
            if ops['pool']:
                @block.gpsimd
                def _(eng):
                    emit('pool', eng)
        self.reset()


def XR(c, t0, t1):
    return [('X', c, tt) for tt in range(t0 // 128, (t1 + 127) // 128)]


def HR(c, t0, t1):
    return [('H', c, tt) for tt in range(t0 // 128, (t1 + 127) // 128)]


class Builder:
    def __init__(self, nseq, stages):
        self.nseq = nseq
        self.stages = stages
        nc = bass.Bass("TRN2", target_bir_lowering=False)
        self.nc = nc
        self.es = ExitStack()
        self.dram = {}
        self._uid = 0

    def din(self, name, shape, dtype=F32):
        t = self.nc.dram_tensor(name, list(shape), dtype, kind="ExternalInput").ap()
        self.dram[name] = t
        return t

    def sb(self, name, shape, dtype, es=None):
        es = es or self.es
        return es.enter_context(self.nc.sbuf_tensor(f"{name}_u{self.uid()}", list(shape), dtype))

    def psum(self, name, shape, dtype):
        return self.es.enter_context(self.nc.psum_tensor(name, list(shape), dtype))

    def uid(self):
        self._uid += 1
        return self._uid

    def mm(self, out, pairs, reads, writes):
        pairs = list(pairs)

        def fn(pe):
            n = len(pairs)
            ins = None
            for i, (l, r) in enumerate(pairs):
                ins = pe.matmul(out, l, r, start=(i == 0), stop=(i == n - 1))
            return ins
        self.s.op('pe', fn, reads, writes)

    def mm1(self, out, l, r, start, stop, reads, writes):
        self.s.op('pe', lambda pe: pe.matmul(out, l, r, start=start, stop=stop), reads, writes)

    def asel(self, out, in_, base, cm, n, reads, writes):
        self.s.op('pool', lambda e: e.affine_select(out, in_, [[1, n]], ALU.is_gt, 0.0, base=base,
                                                    channel_multiplier=cm), reads, writes)

    def tr(self, out, in_, ident, reads, writes):
        self.s.op('pe', lambda pe: pe.transpose(out, in_, ident), reads, writes)

    def act(self, out, in_, func, reads, writes, bias=None, scale=None, accum_out=None, eng='act'):
        kw = {}
        if bias is not None:
            kw['bias'] = bias
        if scale is not None:
            kw['scale'] = scale
        if accum_out is not None:
            kw['accum_out'] = accum_out
        self.s.op('act', lambda e: e.activation(out, in_, func, **kw), reads, writes)

    def tt(self, eng, out, a, b, op, reads, writes):
        self.s.op(eng, lambda e: e.tensor_tensor(out, a, b, op), reads, writes)

    def ts(self, eng, out, a, s1, s2, op0, op1, reads, writes):
        if op1 is None:
            self.s.op(eng, lambda e: e.tensor_scalar(out, a, s1, None, op0), reads, writes)
        else:
            self.s.op(eng, lambda e: e.tensor_scalar(out, a, s1, s2, op0, op1), reads, writes)

    def stt(self, out, a, sc, b, op0, op1, reads, writes):
        self.s.op('dve', lambda e: e.scalar_tensor_tensor(out, a, sc, b, op0, op1), reads, writes)

    def copy(self, eng, out, in_, reads, writes):
        if eng == 'act':
            self.s.op('act', lambda e: e.copy(out, in_), reads, writes)
        else:
            self.s.op(eng, lambda e: e.tensor_copy(out, in_), reads, writes)

    def dma(self, q, out, in_, reads, writes):
        self.s.op(q, lambda e: e.dma_start(out=out, in_=in_), reads, writes, dma=True)

    def next_ps(self):
        i = self._psi
        self._psi = (i + 1) % len(self.psr)
        return self.psr[i], ('ps', i)

    def build(self):
        nc = self.nc
        ns = self.nseq
        x = self.din("x", [ns, L, D])
        out = nc.dram_tensor("out", [ns, L, D], F32, kind="ExternalOutput").ap()
        self.x_d, self.out_d = x, out
        d = self.din
        W = {}
        W['ident'] = d("c_ident", [128, 128])
        W['ones'] = d("c_ones", [128, 128])
        W['uincl'] = d("c_uincl", [128, 128])
        W['trile'] = d("c_trile", [128, 128])
        W['gt'] = d("c_gt", [128, 128])
        W['blk'] = d("c_blk", [128, 128])
        W['mask4'] = d("c_mask4", [128, 4, 512])
        W['normw'] = d("normw", [128, 4, 8])
        W['mlp_w1'] = d("mlp_w1", [2, D, 4096])
        W['mlp_w2'] = d("mlp_w2", [2, 4096, D])
        W['c_w_qkv'] = d("c_w_qkv", [D, 3072])
        W['c_w_o'] = d("c_w_o", [D, D])
        W['qkw'] = d("qkw", [128, 2])
        W['a_w_in'] = d("a_w_in", [D, 6688])
        W['a_w_out'] = d("a_w_out", [2048, D])
        W['convw_s'] = d("convw_s", [128, 12, 4])
        W['convb_s'] = d("convb_s", [128, 12])
        W['convw_g'] = d("convw_g", [128, 24, 4])
        W['dsk'] = d("dsk", [128, 16])
        W['snw'] = d("snw", [128, 1024])
        W['gnw'] = d("gnw", [128, 128])
        W['bias_bc'] = d("bias_bc", [128, 32])
        W['alog_bc'] = d("alog_bc", [128, 32])
        self.W = W

        es = self.es
        sems = {}
        si = [0]

        def newsem(name):
            h = es.enter_context(nc.semaphore(name))
            si[0] += 1
            return (si[0], h)
        esem = {e: [newsem(f"s_{e}{i}") for i in range(3)] for e in Sched.CE}
        dsem = {q: [newsem(f"d_{q}{i}") for i in range(NDSEM)] for q in ('sp', 'pool')}
        self.s = Sched(nc, esem, dsem)

        self.X = self.sb("X", [128, 8, L], F32)
        self.H = self.sb("H", [128, 8, L], BF16)
        self.ident_f = self.sb("ident_f", [128, 128], F32)
        self.ones_f = self.sb("ones_f", [128, 128], F32)
        self.uinclneg_f = self.sb("uinclneg_f", [128, 128], F32)
        self.uincl_f = self.sb("uincl_f", [128, 128], F32)
        self.trile_f = self.sb("trile_f", [128, 128], F32)
        self.gt_f = self.sb("gt_f", [128, 128], F32)
        self.gtle_f = self.sb("gtle_f", [128, 256], F32)
        self.ones_b = self.sb("ones_b", [128, 128], BF16)
        self.ident_b = self.sb("ident_b", [128, 128], BF16)
        self.blk_b = self.sb("blk_b", [128, 128], BF16)
        self.normw = self.sb("normw_sb", [128, 4, 8], F32)
        self.qkw = self.sb("qkw_sb", [128, 2], F32)
        self.cc = self.sb("constcols", [128, 8], F32)
        self.psr = [self.psum(f"ps{i}", [128, 512], F32) for i in range(6)]
        self.psacc = [self.psum(f"psacc{i}", [128, 512], F32) for i in range(2)]
        self._psi = 0

        self.stage_consts()
        for sq in range(ns):
            self.stage_load(sq)
            if 'mix0' in self.stages:
                self.stage_mix0()
            if 'mlp0' in self.stages:
                self.stage_mlp(0)
            if 'attn' in self.stages:
                self.stage_attn()
            if 'mlp1' in self.stages:
                self.stage_mlp(1)
            self.stage_store(sq)
        self.es.close()
        return nc

    def stage_consts(self):
        W = self.W
        q = 'sp'
        for name, t in (('ident', self.ident_f), ('ones', self.ones_f), ('uincl', self.uincl_f),
                        ('trile', self.trile_f), ('gt', self.gt_f)):
            self.dma(q, t[:, :], W[name][:, :], [], [(name,)])
        self.dma(q, self.gtle_f[:, 0:128], W['gt'][:, :], [], [('gtle',)])
        self.dma(q, self.gtle_f[:, 128:256], W['trile'][:, :], [], [('gtle',)])
        self.dma(q, self.normw[:, :, :], W['normw'][:, :, :], [], [('normw',)])
        self.dma(q, self.qkw[:, :], W['qkw'][:, :], [], [('qkw',)])
        for i, v in enumerate((EPS, float(np.log(0.125)), 1.0, 0.0, float(-0.5 * np.log(128.0)))):
            self.s.op('pool', (lambda e, i=i, v=v: e.memset(self.cc[:, i:i + 1], v)), [], [('cc', i)])
        with ExitStack() as es:
            tmp = self.sb("ctmp", [128, 128], F32, es)
            self.dma(q, tmp[:, :], W['blk'][:, :], [], [('ctmp',)])
            self.copy('dve', self.blk_b[:, :], tmp[:, :], [('ctmp',)], [('blk_b',)])
            self.copy('dve', self.ones_b[:, :], self.ones_f[:, :], [('ones',)], [('ones_b',)])
            self.copy('dve', self.ident_b[:, :], self.ident_f[:, :], [('ident',)], [('ident_b',)])
            self.ts('dve', self.uinclneg_f[:, :], self.uincl_f[:, :], -1.0, None, ALU.mult, None,
                    [('uincl',)], [('uinclneg',)])
            self.s.flush()

    def stage_load(self, sq):
        X = self.X
        with ExitStack() as es:
            stg = [self.sb(f"ldstg{i}", [128, D], F32, es) for i in range(2)]
            for tt in range(16):
                st = stg[tt % 2]
                sr = ('ldstg', tt % 2)
                self.dma('sp', st[:, :], self.x_d[sq, tt * 128:(tt + 1) * 128, :], [], [sr])
                for half in range(2):
                    ps, pr = self.next_ps()
                    for j in range(4):
                        c = half * 4 + j
                        self.tr(ps[:, j * 128:(j + 1) * 128], st[:, c * 128:(c + 1) * 128], self.ident_f[:, :],
                                [sr, ('ident',)], [pr])
                    dst = X[:, half * 4:half * 4 + 4, tt * 128:(tt + 1) * 128]
                    src = ps[:, :].rearrange("p (c t) -> p c t", c=4)
                    wr = [('X', half * 4 + j, tt) for j in range(4)]
                    self.copy('act' if half == 0 else 'dve', dst, src, [pr], wr)
            self.s.flush()

    def stage_store(self, sq):
        X = self.X
        with ExitStack() as es:
            stg = [self.sb(f"ststg{i}", [128, D], F32, es) for i in range(2)]
            for tt in range(16):
                st = stg[tt % 2]
                sr = ('ststg', tt % 2)
                for half in range(2):
                    ps, pr = self.next_ps()
                    for j in range(4):
                        c = half * 4 + j
                        self.tr(ps[:, j * 128:(j + 1) * 128], X[:, c, tt * 128:(tt + 1) * 128], self.ident_f[:, :],
                                [('X', c, tt), ('ident',)], [pr])
                    self.copy('act' if half == 0 else 'dve', st[:, half * 512:(half + 1) * 512], ps[:, :], [pr], [sr])
                self.dma('sp', self.out_d[sq, tt * 128:(tt + 1) * 128, :], st[:, :], [sr], [('out', sq, tt)])
            self.s.flush()

    def rmsnorm(self, widx):
        with ExitStack() as es:
            self._rmsnorm(widx, es)
            self.s.flush()

    def _rmsnorm(self, widx, es):
        X, H = self.X, self.H
        sq = [self.sb(f"rn_sq{i}", [128, 8, 512], BF16, es) for i in range(2)]
        lnv = [self.sb(f"rn_ln{i}", [128, 512], F32, es) for i in range(2)]
        rstd = [self.sb(f"rn_rs{i}", [128, 512], F32, es) for i in range(2)]
        for tb in range(4):
            b = tb % 2
            t0, t1 = tb * 512, (tb + 1) * 512
            for c in range(8):
                if c % 2 == 0:
                    self.act(sq[b][:, c, :], X[:, c, t0:t1], AF.Square, XR(c, t0, t1), [('rn_sq', b, c)])
                else:
                    self.tt('pool', sq[b][:, c, :], X[:, c, t0:t1], X[:, c, t0:t1], ALU.mult,
                            XR(c, t0, t1), [('rn_sq', b, c)])
            ps, pr = self.next_ps()
            self.mm(ps[:, :], [(self.ones_b[:, :], sq[b][:, c, :]) for c in range(8)],
                    [('rn_sq', b, c) for c in range(8)] + [('ones_b',)], [pr])
            self.act(lnv[b][:, :], ps[:, :], AF.Ln, [pr], [('rn_ln', b)], bias=self.cc[:, 0:1], scale=1.0 / D)
            self.act(rstd[b][:, :], lnv[b][:, :], AF.Exp, [('rn_ln', b)], [('rn_rs', b)], scale=-0.5)
            for c in range(8):
                self.stt(H[:, c, t0:t1], X[:, c, t0:t1], self.normw[:, widx, c:c + 1], rstd[b][:, :],
                         ALU.mult, ALU.mult, XR(c, t0, t1) + [('rn_rs', b), ('normw',)], HR(c, t0, t1))

    def stage_mlp(self, layer):
        X, H = self.X, self.H
        W1 = self.W['mlp_w1']
        W2 = self.W['mlp_w2']
        self.rmsnorm(1 + 2 * layer)
        with ExitStack() as es:
            w1 = [self.sb(f"w1_{i}", [128, 8, 512], BF16, es) for i in range(2)]
            w2 = [self.sb(f"w2_{i}", [128, 4, D], BF16, es) for i in range(2)]
            A = [self.sb(f"mlpA{i}", [128, 4, L], BF16, es) for i in range(2)]
            R = [self.sb(f"mlpR{i}", [128, 512], F32, es) for i in range(3)]
            ri = 0
            for e in range(8):
                b = e % 2
                self.dma('pool', w1[b][:, :, :],
                         W1[layer, :, e * 512:(e + 1) * 512].rearrange("(c p) n -> p c n", p=128),
                         [], [('w1', b)])
                self.dma('pool', w2[b][:, :, :],
                         W2[layer, e * 512:(e + 1) * 512, :].rearrange("(c p) n -> p c n", p=128),
                         [], [('w2', b)])
                for tb in range(4):
                    t0, t1 = tb * 512, (tb + 1) * 512
                    for j in range(4):
                        ps, pr = self.next_ps()
                        self.mm(ps[:, :], [(w1[b][:, kc, j * 128:(j + 1) * 128], H[:, kc, t0:t1]) for kc in range(8)],
                                [('w1', b)] + [r for kc in range(8) for r in HR(kc, t0, t1)], [pr])
                        r = R[ri % 3]
                        rr = ('mlpR', ri % 3)
                        ri += 1
                        self.act(r[:, :], ps[:, :], AF.Relu, [pr], [rr])
                        self.tt('pool', A[b][:, j, t0:t1], r[:, :], r[:, :], ALU.mult, [rr], [('mlpA', b, j, tb)])
                for tb in range(4):
                    t0, t1 = tb * 512, (tb + 1) * 512
                    for dt in range(8):
                        ps, pr = self.next_ps()
                        self.mm(ps[:, :], [(w2[b][:, j, dt * 128:(dt + 1) * 128], A[b][:, j, t0:t1]) for j in range(4)],
                                [('w2', b)] + [('mlpA', b, j, tb) for j in range(4)], [pr])
                        self.tt('dve', X[:, dt, t0:t1], ps[:, :], X[:, dt, t0:t1], ALU.add,
                                [pr] + XR(dt, t0, t1), XR(dt, t0, t1))
            self.s.flush()

    def stage_attn(self):
        X, H = self.X, self.H
        Wqkv = self.W['c_w_qkv']
        Wo = self.W['c_w_o']
        cc = self.cc
        self.rmsnorm(2)
        with ExitStack() as es0:
            qT = self.sb("qT", [128, 8, L], BF16, es0)
            kT = self.sb("kT", [128, 8, L], BF16, es0)
            with ExitStack() as es:
                wq = [self.sb(f"wq{i}", [128, 8, 128], BF16, es) for i in range(3)]
                wv = [self.sb(f"wv{i}", [128, 8, 512], BF16, es) for i in range(2)]
                qraw = [self.sb(f"qraw{i}", [128, 512], F32, es) for i in range(2)]
                sqq = [self.sb(f"sqq{i}", [128, 512], BF16, es) for i in range(2)]
                lnv = [self.sb(f"qln{i}", [128, 512], F32, es) for i in range(2)]
                rstd = [self.sb(f"qrs{i}", [128, 512], F32, es) for i in range(2)]
                for half in range(2):
                    self.dma('pool', wv[half][:, :, :],
                             Wqkv[:, 2048 + half * 512:2048 + (half + 1) * 512].rearrange("(c p) n -> p c n", p=128),
                             [], [('wv', half)])
                wi = 0
                bi = 0
                PA = int(os.environ.get("ATT_PA", "3"))
                for c in range(8 if (PA & 1) else 0):
                    for which in range(2):
                        dst = qT if which == 0 else kT
                        dn = 'qT' if which == 0 else 'kT'
                        col0 = which * 1024 + c * 128
                        w = wq[wi % 3]
                        wr = ('wq', wi % 3)
                        wi += 1
                        self.dma('pool', w[:, :, :], Wqkv[:, col0:col0 + 128].rearrange("(c p) n -> p c n", p=128),
                                 [], [wr])
                        for tb in range(4):
                            t0, t1 = tb * 512, (tb + 1) * 512
                            b = bi % 2
                            bi += 1
                            ps, pr = self.next_ps()
                            self.mm(ps[:, :], [(w[:, kc, :], H[:, kc, t0:t1]) for kc in range(8)],
                                    [wr] + [r for kc in range(8) for r in HR(kc, t0, t1)], [pr])
                            self.act(sqq[b][:, :], ps[:, :], AF.Square, [pr], [('sqq', b)])
                            self.copy('dve', qraw[b][:, :], ps[:, :], [pr], [('qraw', b)])
                            ps2, pr2 = self.next_ps()
                            self.mm(ps2[:, :], [(self.blk_b[:, :], sqq[b][:, :])], [('sqq', b), ('blk_b',)], [pr2])
                            self.act(lnv[b][:, :], ps2[:, :], AF.Ln, [pr2], [('qln', b)], bias=cc[:, 0:1], scale=1.0 / 64)
                            self.act(rstd[b][:, :], lnv[b][:, :], AF.Exp, [('qln', b)], [('qrs', b)],
                                     bias=cc[:, 1:2] if which == 0 else cc[:, 3:4], scale=-0.5)
                            self.stt(dst[:, c, t0:t1], qraw[b][:, :], self.qkw[:, which:which + 1], rstd[b][:, :],
                                     ALU.mult, ALU.mult, [('qraw', b), ('qrs', b), ('qkw',)], [(dn, c, tb)])
                for tt in range(16 if (PA & 2) else 0):
                    t0, t1 = tt * 128, (tt + 1) * 128
                    pss = []
                    for half in range(2):
                        ps, pr = self.next_ps()
                        self.mm(ps[:, :], [(H[:, kc, t0:t1], wv[half][:, kc, :]) for kc in range(8)],
                                [('wv', half)] + [('H', kc, tt) for kc in range(8)], [pr])
                        pss.append((ps, pr))
                    for half in range(2):
                        ps, pr = pss[half]
                        self.copy('act' if half == 0 else 'dve', H[:, half * 4:half * 4 + 4, t0:t1],
                                  ps[:, :].rearrange("p (c t) -> p c t", c=4), [pr],
                                  [('H', half * 4 + j, tt) for j in range(4)])
                self.s.flush()
            with ExitStack() as es:
                eb = [self.sb(f"at_e{i}", [128, 512], F32, es) for i in range(3)]
                spb = [self.sb(f"at_sp{i}", [128, 512], F32, es) for i in range(3)]
                tmpb = [self.sb(f"at_tmp{i}", [128, 512], F32, es) for i in range(2)]
                Rs = [self.sb(f"at_rs{i}", [128, 512], F32, es) for i in range(2)]
                ATb = [self.sb(f"at_A{i}", [128, 512], BF16, es) for i in range(3)]
                self.mask4 = self.sb("mask4", [128, 4, 512], F32, es)
                self.dma('sp', self.mask4[:, :, :], self.W['mask4'][:, :, :], [], [('mask4',)])
                it_g = 0
                ai = 0
                pairs = []
                gi = 0
                for h in range(16):
                    for G in range(4):
                        kmax = 4 * G + 3
                        for kb in range(kmax, -1, -1):
                            pairs.append(dict(h=h, G=G, kb=kb, first=(kb == kmax), last=(kb == 0), diag=(kb >= 4 * G),
                                              gi=gi, i=len(pairs)))
                        gi += 1
                rstate = {'cur': 0}

                def opnd(p):
                    h, G, kb = p['h'], p['G'], p['kb']
                    c = h // 2
                    b0 = (h % 2) * 64
                    q_s = qT[b0:b0 + 64, c, G * 512:(G + 1) * 512]
                    k_s = kT[b0:b0 + 64, c, kb * 128:(kb + 1) * 128]
                    return c, b0, q_s, k_s, ('qT', c, G), ('kT', c, kb // 4)

                def stA(p):
                    c, b0, q_s, k_s, qr, kr = opnd(p)
                    b = p['i'] % 3
                    ps_z, pzr = self.next_ps()
                    self.mm(ps_z[:, :], [(k_s, q_s)], [kr, qr], [pzr])
                    self.act(eb[b][:, :], ps_z[:, :], AF.Exp, [pzr], [('at_e', b)])
                    self.act(spb[b][:, :], eb[b][:, :], AF.Ln, [('at_e', b)], [('at_sp', b)], bias=cc[:, 2:3])
                    if p['diag']:
                        self.tt('pool', spb[b][:, :], spb[b][:, :], self.mask4[:, p['kb'] - 4 * p['G'], :], ALU.mult,
                                [('at_sp', b), ('mask4',)], [('at_sp', b)])

                def stB(p):
                    c, b0, q_s, k_s, qr, kr = opnd(p)
                    b = p['i'] % 3
                    b2 = p['i'] % 2
                    ps_n, pnr = self.next_ps()
                    self.mm(ps_n[:, :], [(k_s, q_s), (self.uinclneg_f[:, :], spb[b][:, :])],
                            [kr, qr, ('at_sp', b), ('uinclneg',)], [pnr])
                    if p['first']:
                        self.act(ATb[b][:, :], ps_n[:, :], AF.Exp, [pnr], [('at_A', b)])
                    else:
                        rcur = rstate['cur']
                        self.tt('dve', tmpb[b2][:, :], ps_n[:, :], Rs[rcur][:, :], ALU.subtract,
                                [pnr, ('at_rs', rcur)], [('at_tmp', b2)])
                        self.act(ATb[b][:, :], tmpb[b2][:, :], AF.Exp, [('at_tmp', b2)], [('at_A', b)])
                    if p['diag']:
                        self.tt('pool', ATb[b][:, :], ATb[b][:, :], self.mask4[:, p['kb'] - 4 * p['G'], :], ALU.mult,
                                [('at_A', b), ('mask4',)], [('at_A', b)])
                    if not p['last']:
                        ps_r, prr = self.next_ps()
                        self.mm(ps_r[:, :], [(self.ones_f[:, :], spb[b][:, :])], [('at_sp', b), ('ones',)], [prr])
                        rcur = rstate['cur']
                        nxt = 1 - rcur
                        if p['first']:
                            self.copy('dve', Rs[nxt][:, :], ps_r[:, :], [prr], [('at_rs', nxt)])
                        else:
                            self.tt('dve', Rs[nxt][:, :], ps_r[:, :], Rs[rcur][:, :], ALU.add,
                                    [prr, ('at_rs', rcur)], [('at_rs', nxt)])
                        rstate['cur'] = nxt

                def stC(p):
                    c, b0, q_s, k_s, qr, kr = opnd(p)
                    h, G, kb = p['h'], p['G'], p['kb']
                    b = p['i'] % 3
                    acc = self.psacc[p['gi'] % 2]
                    accr = ('psacc', p['gi'] % 2)
                    v_s = H[:, c, kb * 128 + (h % 2) * 64:kb * 128 + (h % 2) * 64 + 64]
                    self.mm1(acc[b0:b0 + 64, :], v_s, ATb[b][:, :], p['first'], p['last'],
                             [('H', c, kb), ('at_A', b)], [accr])
                    if p['last']:
                        self.copy('act', qT[b0:b0 + 64, c, G * 512:(G + 1) * 512], acc[b0:b0 + 64, :], [accr], [qr])

                n = len(pairs)
                for t in range(n + 2):
                    if t < n:
                        stA(pairs[t])
                    if 0 <= t - 1 < n:
                        stB(pairs[t - 1])
                    if 0 <= t - 2 < n:
                        stC(pairs[t - 2])
                self.s.flush()
            with ExitStack() as es:
                wo = [self.sb(f"wo{i}", [128, 8, 128], BF16, es) for i in range(2)]
                for dt in range(8):
                    w = wo[dt % 2]
                    wr = ('wo', dt % 2)
                    self.dma('pool', w[:, :, :], Wo[:, dt * 128:(dt + 1) * 128].rearrange("(c p) n -> p c n", p=128),
                             [], [wr])
                    for tb in range(4):
                        t0, t1 = tb * 512, (tb + 1) * 512
                        ps, pr = self.next_ps()
                        self.mm(ps[:, :], [(w[:, c, :], qT[:, c, t0:t1]) for c in range(8)],
                                [wr] + [('qT', c, tb) for c in range(8)], [pr])
                        self.tt('dve', X[:, dt, t0:t1], ps[:, :], X[:, dt, t0:t1], ALU.add,
                                [pr] + XR(dt, t0, t1), XR(dt, t0, t1))
                self.s.flush()

    def ring(self, name, n, shape, dtype, es):
        tiles = [self.sb(f"{name}{i}", shape, dtype, es) for i in range(n)]
        state = {'i': 0}

        def nxt():
            i = state['i'] % n
            state['i'] += 1
            return tiles[i], (name, i)
        nxt.tiles = tiles
        return nxt

    def conv_tile(self, col0, cw, cb, R, silu_out=None, silu_reg=None):
        H = self.H
        Win = self.W['a_w_in']
        w, wr = R['w']()
        self.dma('pool', w[:, :, :], Win[:, col0:col0 + 128].rearrange("(c p) n -> p c n", p=128), [], [wr])
        raw, rr = R['raw']()
        for tb in range(4):
            t0, t1 = tb * 512, (tb + 1) * 512
            ps, pr = self.next_ps()
            self.mm(ps[:, :], [(w[:, kc, :], H[:, kc, t0:t1]) for kc in range(8)],
                    [wr] + [r for kc in range(8) for r in HR(kc, t0, t1)], [pr])
            self.copy('act', raw[:, 3 + t0:3 + t1], ps[:, :], [pr], [rr])
        acc, ar = R['acc']()
        if cb is not None:
            self.ts('dve', acc[:, :], raw[:, 3:3 + L], cw[:, 3:4], cb, ALU.mult, ALU.add, [rr, ('convw',)], [ar])
        else:
            self.ts('dve', acc[:, :], raw[:, 3:3 + L], cw[:, 3:4], None, ALU.mult, None, [rr, ('convw',)], [ar])
        for k in (2, 1, 0):
            self.stt(acc[:, :], raw[:, k:k + L], cw[:, k:k + 1], acc[:, :], ALU.mult, ALU.add, [rr, ar, ('convw',)], [ar])
        if silu_out is not None:
            self.act(silu_out, acc[:, :], AF.Silu, [ar], [silu_reg])
            return silu_out, silu_reg
        self.act(acc[:, :], acc[:, :], AF.Silu, [ar], [ar])
        return acc, ar

    def to_tok(self, src, sr, dst, dr_name, width_off, src_regs=None):
        for q in range(4):
            ps, pr = self.next_ps()
            psb = ps[:, :].bitcast(BF16)
            for j in range(4):
                tt = q * 4 + j
                self.tr(psb[:, j * 128:(j + 1) * 128], src[:, tt * 128:(tt + 1) * 128], self.ident_b[:, :],
                        (src_regs if src_regs is not None else [sr]) + [('ident_b',)], [pr])
            self.copy('act' if q % 2 == 0 else 'dve', dst[:, q * 4:q * 4 + 4, width_off:width_off + 128],
                      psb[:, 0:512].rearrange("p (c t) -> p c t", c=4), [pr],
                      [(dr_name, q * 4 + j) for j in range(4)])

    def stage_mix0(self):
        X, H = self.X, self.H
        W = self.W
        cc = self.cc
        Win = W['a_w_in']
        Wout = W['a_w_out']
        self.rmsnorm(0)
        with ExitStack() as es0:
            sb = lambda n, sh, dt=F32: self.sb(n, sh, dt, es0)
            convw_s = sb("convw_s", [128, 12, 4])
            convb_s = sb("convb_s", [128, 12])
            convw_g = sb("convw_g", [128, 24, 4])
            dsk = sb("dsk", [128, 16])
            gnw = sb("gnw", [128, 128])
            spl = sb("spl", [128, 16, 32])
            ag = sb("ag", [128, 16, 32])
            ea = sb("ea", [128, 16, 32])
            dte = sb("dte", [128, 16, 32])
            cdec = sb("cdec", [128, 16, 32])
            bet = sb("bet", [128, 16, 8])
            nbet = sb("nbet", [128, 16, 8])
            bg = sb("bg", [128, 16, 8])
            self.dma('sp', convw_s[:, :, :], W['convw_s'][:, :, :], [], [('convw',)])
            self.dma('sp', convb_s[:, :], W['convb_s'][:, :], [], [('convw',)])
            self.dma('sp', convw_g[:, :, :], W['convw_g'][:, :, :], [], [('convw',)])
            self.dma('sp', dsk[:, :], W['dsk'][:, :], [], [('dsk',)])
            self.dma('sp', gnw[:, :], W['gnw'][:, :], [], [('gnw',)])
            with ExitStack() as es:
                wsm = self.sb("wsm", [128, 8, 32], BF16, es)
                bias_bc = self.sb("bias_bc", [128, 32], F32, es)
                alog = self.sb("alog", [128, 32], F32, es)
                aneg = self.sb("aneg", [128, 32], F32, es)
                smraw = self.sb("smraw", [128, 16, 32], F32, es)
                e1 = self.sb("sm_e", [128, 16, 32], F32, es)
                acum = self.sb("acum", [128, 16, 32], F32, es)
                tot = self.sb("tot", [128, 16, 32], F32, es)
                tmp = self.sb("sm_tmp", [128, 16, 32], F32, es)
                self.dma('pool', wsm[:, :, 0:16], Win[:, 2560:2576].rearrange("(c p) n -> p c n", p=128), [], [('wsm', 0)])
                self.dma('pool', wsm[:, :, 16:32], Win[:, 6672:6688].rearrange("(c p) n -> p c n", p=128), [], [('wsm', 1)])
                self.dma('sp', bias_bc[:, :], W['bias_bc'][:, :], [], [('bias_bc',)])
                self.dma('sp', alog[:, :], W['alog_bc'][:, :], [], [('alog',)])
                self.act(aneg[:, :], alog[:, :], AF.Exp, [('alog',)], [('aneg',)])
                self.ts('dve', aneg[:, :], aneg[:, :], -1.0, None, ALU.mult, None, [('aneg',)], [('aneg',)])
                for tt in range(16):
                    ps, pr = self.next_ps()
                    self.mm(ps[:, 0:32], [(H[:, kc, tt * 128:(tt + 1) * 128], wsm[:, kc, :]) for kc in range(8)],
                            [('wsm', 0), ('wsm', 1)] + [('H', kc, tt) for kc in range(8)], [pr])
                    self.tt('dve', smraw[:, tt, :], ps[:, 0:32], bias_bc[:, :], ALU.add, [pr, ('bias_bc',)], [('smraw',)])
                f2 = lambda t: t[:, :, :].rearrange("p a b -> p (a b)")
                self.act(f2(e1), f2(smraw), AF.Exp, [('smraw',)], [('sm_e',)])
                self.act(f2(spl), f2(e1), AF.Ln, [('sm_e',)], [('spl',)], bias=cc[:, 2:3])
                self.ts('dve', tmp[:, :, 16:24], e1[:, :, 16:24], 1.0, None, ALU.add, None, [('sm_e',)], [('sm_tmp',)])
                self.s.op('dve', lambda e: e.reciprocal(tmp[:, :, 16:24], tmp[:, :, 16:24]), [('sm_tmp',)], [('sm_tmp',)])
                self.tt('dve', bet[:, :, :], e1[:, :, 16:24], tmp[:, :, 16:24], ALU.mult, [('sm_e',), ('sm_tmp',)], [('bet',)])
                self.ts('dve', nbet[:, :, :], bet[:, :, :], -1.0, None, ALU.mult, None, [('bet',)], [('nbet',)])
                self.tt('dve', ag[:, :, :], spl[:, :, :], aneg[:, :].unsqueeze(1).broadcast_to([128, 16, 32]), ALU.mult,
                        [('spl',), ('aneg',)], [('ag',)])
                ps, pr = self.next_ps()
                self.mm(ps[:, :], [(self.trile_f[:, :], f2(ag))], [('ag',), ('trile',)], [pr])
                self.copy('dve', f2(acum), ps[:, :], [pr], [('acum',)])
                self.act(f2(ea), ps[:, :], AF.Exp, [pr], [('ea',)])
                ps2, pr2 = self.next_ps()
                self.mm(ps2[:, :], [(self.ones_f[:, :], f2(ag))], [('ag',), ('ones',)], [pr2])
                self.act(f2(cdec), ps2[:, :], AF.Exp, [pr2], [('cdec',)])
                self.tt('dve', f2(tot), ps2[:, :], f2(acum), ALU.subtract, [pr2, ('acum',)], [('tot',)])
                self.act(f2(dte), f2(tot), AF.Exp, [('tot',)], [('dte',)])
                self.tt('dve', bg[:, :, :], bet[:, :, :], ea[:, :, 24:32], ALU.mult, [('bet',), ('ea',)], [('bg',)])
                self.s.flush()
            P = dict(convw_s=convw_s, convb_s=convb_s, convw_g=convw_g, dsk=dsk, gnw=gnw, spl=spl, ag=ag,
                     ea=ea, dte=dte, cdec=cdec, bet=bet, nbet=nbet, bg=bg)
            if os.environ.get("MIX_STOP") == "prelude":
                return
            MIXSEL = os.environ.get("MIX_SEL", "ssd,gdn").split(",")
            if 'ssd' in MIXSEL:
                for g in range(2):
                    self.ssd_group(g, P)
            if 'gdn' in MIXSEL:
                for hg in range(2):
                    self.gdn_group(hg, P)

    def ssd_group(self, g, P):
        X, H = self.X, self.H
        Win = self.W['a_w_in']
        Wout = self.W['a_w_out']
        cc = self.cc
        with ExitStack() as es0:
            xs_tok = self.sb("xs_tok", [128, 16, 512], BF16, es0)
            siluz = self.sb("siluz", [128, 16, 512], BF16, es0)
            BT = self.sb("BT", [128, L], BF16, es0)
            CT = self.sb("CT", [128, L], BF16, es0)
            B_tok = self.sb("B_tok", [128, 16, 128], BF16, es0)
            with ExitStack() as es:
                R = dict(w=self.ring("cw", 3, [128, 8, 128], BF16, es),
                         raw=self.ring("craw", 2, [128, L + 3], F32, es),
                         acc=self.ring("cacc", 1, [128, L], F32, es))
                xsb = self.ring("xsb", 2, [128, L], BF16, es)
                wz = self.sb("wz", [128, 8, 512], BF16, es)
                self.dma('pool', wz[:, :, :], Win[:, g * 512:(g + 1) * 512].rearrange("(c p) n -> p c n", p=128), [], [('wz',)])
                self._zero_halo(R, es)
                for j in range(4):
                    ti = g * 4 + j
                    xb, xr = xsb()
                    self.conv_tile(1024 + ti * 128, P['convw_s'][:, ti, :], P['convb_s'][:, ti:ti + 1], R,
                                   silu_out=xb[:, :], silu_reg=xr)
                    self.to_tok(xb, xr, xs_tok, 'xs_tok', j * 128)
                ti = 8 + g
                self.conv_tile(1024 + ti * 128, P['convw_s'][:, ti, :], P['convb_s'][:, ti:ti + 1], R,
                               silu_out=BT[:, :], silu_reg=('BT',))
                self.to_tok(BT, ('BT',), B_tok, 'B_tok', 0)
                ti = 10 + g
                self.conv_tile(1024 + ti * 128, P['convw_s'][:, ti, :], P['convb_s'][:, ti:ti + 1], R,
                               silu_out=CT[:, :], silu_reg=('CT',))
                for tt in range(16):
                    ps, pr = self.next_ps()
                    self.mm(ps[:, :], [(H[:, kc, tt * 128:(tt + 1) * 128], wz[:, kc, :]) for kc in range(8)],
                            [('wz',)] + [('H', kc, tt) for kc in range(8)], [pr])
                    self.act(siluz[:, tt, :], ps[:, :], AF.Silu, [pr], [('siluz', tt)])
                self.s.flush()
            if os.environ.get("MIX_STOP") == "ssd1":
                return
            yT = self.sb("yT", [128, 4, L], BF16, es0)
            with ExitStack() as es:
                ring = lambda n, k, sh, dt=F32: self.ring(n, k, sh, dt, es)
                cbm_r = ring("cbm", 2, [128, 128])
                lseg_r = ring("lseg", 2, [128, 4, 128])
                LT_r = ring("LT", 2, [128, 4, 128])
                MT_r = ring("MT", 1, [128, 8, 128], BF16)
                xdt_r = ring("xdt", 2, [128, 512], BF16)
                xdd_r = ring("xdd", 2, [128, 512], BF16)
                t1_r = ring("t1", 1, [128, 512])
                t3_r = ring("t3", 1, [128, 512])
                yg_r = ring("yg", 1, [128, 512])
                junk_r = ring("junk", 1, [128, 512], BF16)
                yn_r = ring("ynb", 2, [128, 512], BF16)
                sm_r = ring("ssm", 2, [128, 4])
                S = self.sb("ssdS", [128, 512], F32, es)
                Sb = self.sb("ssdSb", [128, 512], BF16, es)
                snw = self.sb("snw", [128, 512], F32, es)
                self.dma('sp', snw[:, :], self.W['snw'][:, g * 512:(g + 1) * 512], [], [('snw',)])
                hs = slice(g * 8, (g + 1) * 8)
                v8 = lambda ap: ap.rearrange("p (h d) -> p h d", h=8)
                bc8 = lambda ap: ap.unsqueeze(2).broadcast_to([128, 8, 64])
                for tt in range(16):
                    ts_ = slice(tt * 128, (tt + 1) * 128)
                    ps_cb, pcr = self.next_ps()
                    self.mm(ps_cb[:, 0:128], [(BT[:, ts_], CT[:, ts_])], [('BT',), ('CT',)], [pcr])
                    cbm, cbr = cbm_r()
                    self.tt('dve', cbm[:, :], ps_cb[:, 0:128], self.trile_f[:, :], ALU.mult, [pcr, ('trile',)], [cbr])
                    MT, mr = MT_r()
                    for half in range(2):
                        lseg, lr = lseg_r()
                        ps_s, psr = self.next_ps()
                        for i in range(4):
                            hh = g * 8 + half * 4 + i
                            self.ts('dve', lseg[:, i, :], self.gt_f[:, :], P['ag'][:, tt, hh:hh + 1], None,
                                    ALU.mult, None, [('gt',), ('ag',)], [(lr, i)])
                            self.mm(ps_s[:, i * 128:(i + 1) * 128], [(lseg[:, i, :], self.trile_f[:, :])],
                                    [(lr, i), ('trile',)], [psr])
                        LT, ltr = LT_r()
                        self.act(LT[:, :, :].rearrange("p a b -> p (a b)"), ps_s[:, :], AF.Exp, [psr], [ltr])
                        self.tt('dve', MT[:, half * 4:(half + 1) * 4, :], LT[:, :, :],
                                cbm[:, :].unsqueeze(1).broadcast_to([128, 4, 128]), ALU.mult, [ltr, cbr], [(mr, half)])
                    xdt, xdr = xdt_r()
                    self.tt('dve', v8(xdt[:, :]), v8(xs_tok[:, tt, :]), bc8(P['spl'][:, tt, hs]), ALU.mult,
                            [('xs_tok', tt), ('spl',)], [xdr])
                    xdd, xddr = xdd_r()
                    self.tt('pool', v8(xdd[:, :]), v8(xdt[:, :]), bc8(P['dte'][:, tt, hs]), ALU.mult,
                            [xdr, ('dte',)], [xddr])
                    ps_y, pyr = self.next_ps()
                    for i in range(8):
                        self.mm(ps_y[:, i * 64:(i + 1) * 64], [(MT[:, i, :], xdt[:, i * 64:(i + 1) * 64])],
                                [(mr, i // 4), xdr], [pyr])
                    t1, t1r = t1_r()
                    if tt > 0:
                        ps_o, por = self.next_ps()
                        self.mm(ps_o[:, :], [(CT[:, ts_], Sb[:, :])], [('CT',), ('ssdSb',)], [por])
                        self.tt('dve', v8(t1[:, :]), v8(ps_o[:, :]), bc8(P['ea'][:, tt, hs]), ALU.mult, [por, ('ea',)], [t1r])
                        self.tt('dve', t1[:, :], t1[:, :], ps_y[:, :], ALU.add, [t1r, pyr], [t1r])
                    else:
                        self.copy('dve', t1[:, :], ps_y[:, :], [pyr], [t1r])
                    t3, t3r = t3_r()
                    self.tt('pool', v8(t3[:, :]), v8(xs_tok[:, tt, :]), bc8(P['dsk'][:, hs]), ALU.mult,
                            [('xs_tok', tt), ('dsk',)], [t3r])
                    self.tt('dve', t3[:, :], t3[:, :], t1[:, :], ALU.add, [t3r, t1r], [t3r])
                    yg, ygr = yg_r()
                    self.tt('pool', yg[:, :], t3[:, :], siluz[:, tt, :], ALU.mult, [t3r, ('siluz', tt)], [ygr])
                    sm, smr = sm_r()
                    junk, jr = junk_r()
                    self.act(junk[:, :], yg[:, :], AF.Square, [ygr], [jr, (smr, 0)], accum_out=sm[:, 0:1])
                    self.act(sm[:, 1:2], sm[:, 0:1], AF.Ln, [(smr, 0)], [(smr, 1)], bias=cc[:, 0:1], scale=1.0 / 512)
                    self.act(sm[:, 2:3], sm[:, 1:2], AF.Exp, [(smr, 1)], [(smr, 2)], scale=-0.5)
                    yn, ynr = yn_r()
                    self.stt(yn[:, :], yg[:, :], sm[:, 2:3], snw[:, :], ALU.mult, ALU.mult,
                             [ygr, (smr, 2), ('snw',)], [ynr])
                    ps_t, ptr_ = self.next_ps()
                    psb = ps_t[:, :].bitcast(BF16)
                    for j in range(4):
                        self.tr(psb[:, j * 128:(j + 1) * 128], yn[:, j * 128:(j + 1) * 128], self.ident_b[:, :],
                                [ynr, ('ident_b',)], [ptr_])
                    self.copy('act', yT[:, 0:4, ts_], psb[:, 0:512].rearrange("p (c t) -> p c t", c=4), [ptr_],
                              [('yT', j, tt) for j in range(4)])
                    if tt < 15:
                        ps_st, pstr = self.next_ps()
                        self.mm(ps_st[:, :], [(B_tok[:, tt, :], xdd[:, :])], [('B_tok', tt), xddr], [pstr])
                        if tt == 0:
                            self.copy('dve', S[:, :], ps_st[:, :], [pstr], [('ssdS',)])
                        else:
                            self.tt('dve', v8(S[:, :]), v8(S[:, :]), bc8(P['cdec'][:, tt, hs]), ALU.mult,
                                    [('ssdS',), ('cdec',)], [('ssdS',)])
                            self.tt('dve', S[:, :], S[:, :], ps_st[:, :], ALU.add, [('ssdS',), pstr], [('ssdS',)])
                        self.copy('act', Sb[:, :], S[:, :], [('ssdS',)], [('ssdSb',)])
                self.s.flush()
            with ExitStack() as es:
                wo = self.sb("wo_s", [128, 4, D], BF16, es)
                self.dma('pool', wo[:, :, :], Wout[g * 512:(g + 1) * 512, :].rearrange("(c p) n -> p c n", p=128), [], [('wo_s',)])
                for tb in range(4):
                    t0, t1_ = tb * 512, (tb + 1) * 512
                    for dt in range(8):
                        ps, pr = self.next_ps()
                        self.mm(ps[:, :], [(wo[:, j, dt * 128:(dt + 1) * 128], yT[:, j, t0:t1_]) for j in range(4)],
                                [('wo_s',)] + [('yT', j, tt) for j in range(4) for tt in range(tb * 4, tb * 4 + 4)], [pr])
                        self.tt('dve', X[:, dt, t0:t1_], ps[:, :], X[:, dt, t0:t1_], ALU.add,
                                [pr] + XR(dt, t0, t1_), XR(dt, t0, t1_))
                self.s.flush()

    def _zero_halo(self, R, es):
        for i, t in enumerate(R['raw'].tiles):
            self.s.op('pool', (lambda e, t=t: e.memset(t[:, 0:3], 0.0)), [], [("craw", i)])

    def gdn_group(self, hg, P):
        X, H = self.X, self.H
        Win = self.W['a_w_in']
        Wout = self.W['a_w_out']
        cc = self.cc
        QOFF = 2576
        with ExitStack() as es0:
            siluzg = self.sb("siluzg", [128, 16, 512], BF16, es0)
            ogT = self.sb("ogT", [128, 4, L], BF16, es0)
            with ExitStack() as es:
                wzg = self.sb("wzg", [128, 8, 512], BF16, es)
                c0 = 5648 + hg * 512
                self.dma('pool', wzg[:, :, :], Win[:, c0:c0 + 512].rearrange("(c p) n -> p c n", p=128), [], [('wzg',)])
                for tt in range(16):
                    ps, pr = self.next_ps()
                    self.mm(ps[:, :], [(H[:, kc, tt * 128:(tt + 1) * 128], wzg[:, kc, :]) for kc in range(8)],
                            [('wzg',)] + [('H', kc, tt) for kc in range(8)], [pr])
                    self.act(siluzg[:, tt, :], ps[:, :], AF.Silu, [pr], [('siluzg', tt)])
                self.s.flush()
            for i in range(4):
                h = hg * 4 + i
                with ExitStack() as esh:
                    qTn = self.sb("qTn", [128, L], BF16, esh)
                    kTn = self.sb("kTn", [128, L], BF16, esh)
                    k_tok = self.sb("k_tok", [128, 16, 128], BF16, esh)
                    v_tok = self.sb("v_tok", [128, 16, 128], BF16, esh)
                    with ExitStack() as es:
                        R = dict(w=self.ring("cw", 2, [128, 8, 128], BF16, es),
                                 raw=self.ring("craw", 2, [128, L + 3], F32, es),
                                 acc=self.ring("cacc", 2, [128, L], F32, es))
                        kvb = ogT[:, i, :]
                        sqq = self.sb("gsqq", [128, 512], BF16, es)
                        lnv = self.sb("gln", [128, 512], F32, es)
                        rstd = self.sb("grs", [128, 512], F32, es)
                        self._zero_halo(R, es)
                        for which, dstT, dn in ((0, qTn, 'qTn'), (1, kTn, 'kTn')):
                            ti = which * 8 + h
                            acc, ar = self.conv_tile(QOFF + ti * 128, P['convw_g'][:, ti, :], None, R)
                            for tb in range(4):
                                t0, t1 = tb * 512, (tb + 1) * 512
                                self.act(sqq[:, :], acc[:, t0:t1], AF.Square, [ar], [('gsqq',)])
                                ps, pr = self.next_ps()
                                self.mm(ps[:, :], [(self.ones_b[:, :], sqq[:, :])], [('gsqq',), ('ones_b',)], [pr])
                                self.act(lnv[:, :], ps[:, :], AF.Ln, [pr], [('gln',)], bias=cc[:, 0:1])
                                self.act(rstd[:, :], lnv[:, :], AF.Exp, [('gln',)], [('grs',)],
                                         bias=cc[:, 4:5] if which == 0 else cc[:, 3:4], scale=-0.5)
                                self.tt('dve', dstT[:, t0:t1], acc[:, t0:t1], rstd[:, :], ALU.mult, [ar, ('grs',)], [(dn, tb)])
                        self.to_tok(kTn, None, k_tok, 'k_tok', 0, src_regs=[('kTn', tb) for tb in range(4)])
                        ti = 16 + h
                        self.conv_tile(QOFF + ti * 128, P['convw_g'][:, ti, :], None, R, silu_out=kvb, silu_reg=('kvb',))
                        self.to_tok(kvb, ('kvb',), v_tok, 'v_tok', 0)
                        self.s.flush()
                    with ExitStack() as es:
                        ub = self.sb("g_ub", [128, 16, 128], F32, es)
                        wT = self.sb("g_wT", [128, 16, 128], BF16, es)
                        qkT = self.sb("g_qkT", [128, 16, 128], BF16, es)
                        NCTX = int(os.environ.get("GDN_NCTX", "4"))
                        ctxs = []
                        for ci in range(NCTX):
                            t = lambda n, sh=[128, 128], dt=F32: self.sb(f"g_{n}{ci}", sh, dt, es)
                            ctxs.append(dict(gmask=t("gmask"), DLT=t("DLT", [128, 256]), PPa=t("PPa", [128, 256]),
                                             PPb=t("PPb", [128, 256]), Tt=t("Tt"), vb=t("vb"), kb2=t("kb2"), ci=ci))
                        gcol = lambda tt: P['ag'][:, tt, 24 + h:25 + h]
                        GPH = int(os.environ.get("GDN_PH", "3"))
                        for grp in range(16 // NCTX if GPH >= 2 else 0):
                            tts = tuple(range(grp * NCTX, (grp + 1) * NCTX))
                            rgs = [(lambda n, ci=c['ci']: ('g_' + n, ci)) for c in ctxs]
                            ps1s, ps3s, ps4s = [], [], []
                            for c, tt, rg in zip(ctxs, tts, rgs):
                                self.ts('dve', c['gmask'][:, :], self.gt_f[:, :], gcol(tt), None, ALU.mult, None,
                                        [('gt',), ('ag',)], [rg('gmask')])
                                self.act(c['vb'][:, :], v_tok[:, tt, :], AF.Identity, [('v_tok', tt), ('bet',)], [rg('vb')],
                                         scale=P['bet'][:, tt, h:h + 1])
                                self.act(c['kb2'][:, :], k_tok[:, tt, :], AF.Identity, [('k_tok', tt), ('bg',)], [rg('kb2')],
                                         scale=P['bg'][:, tt, h:h + 1])
                            for c, tt, rg in zip(ctxs, tts, rgs):
                                ts_ = slice(tt * 128, (tt + 1) * 128)
                                ps1, p1r = self.next_ps()
                                self.mm(ps1[:, 0:128], [(self.trile_f[:, :], c['gmask'][:, :])], [rg('gmask'), ('trile',)], [p1r])
                                self.mm(ps1[:, 128:256], [(c['gmask'][:, :], self.trile_f[:, :])], [rg('gmask'), ('trile',)], [p1r])
                                self.mm(ps1[:, 256:384], [(kTn[:, ts_], kTn[:, ts_])], [('kTn', tt // 4)], [p1r])
                                self.mm(ps1[:, 384:512], [(kTn[:, ts_], qTn[:, ts_])], [('kTn', tt // 4), ('qTn', tt // 4)], [p1r])
                                ps1s.append((ps1, p1r))
                            for c, tt, rg, (ps1, p1r) in zip(ctxs, tts, rgs, ps1s):
                                self.act(c['DLT'][:, :], ps1[:, 0:256], AF.Exp, [p1r], [rg('DLT')])
                                self.tt('dve', c['DLT'][:, :], c['DLT'][:, :], self.gtle_f[:, :], ALU.mult,
                                        [rg('DLT'), ('gtle',)], [rg('DLT')])
                            for c, tt, rg, (ps1, p1r) in zip(ctxs, tts, rgs, ps1s):
                                self.stt(c['PPa'][:, 0:128], ps1[:, 256:384], P['nbet'][:, tt, h:h + 1], c['DLT'][:, 0:128],
                                         ALU.mult, ALU.mult, [p1r, ('nbet',), rg('DLT')], [rg('PPa')])
                                self.tt('dve', qkT[:, tt, :], ps1[:, 384:512], c['DLT'][:, 128:256], ALU.mult,
                                        [p1r, rg('DLT')], [('g_qkT', tt)])
                            for c, tt, rg in zip(ctxs, tts, rgs):
                                ps4, p4r = self.next_ps()
                                self.tr(ps4[:, 0:128], c['PPa'][:, 0:128], self.ident_f[:, :], [rg('PPa'), ('ident',)], [p4r])
                                ps4s.append((ps4, p4r))
                            for c, tt, rg, (ps4, p4r) in zip(ctxs, tts, rgs, ps4s):
                                self.copy('act', c['PPa'][:, 128:256], ps4[:, 0:128], [p4r], [rg('PPa')])
                                self.tt('dve', c['Tt'][:, :], c['PPa'][:, 128:256], self.ident_f[:, :], ALU.add,
                                        [rg('PPa'), ('ident',)], [rg('Tt')])
                            cur, nxt = 'PPa', 'PPb'
                            for lvl in range(1, 7):
                                psas = []
                                for c, rg in zip(ctxs, rgs):
                                    psa, par = self.next_ps()
                                    pp = c[cur]
                                    self.mm(psa[:, 0:128], [(pp[:, 128:256], pp[:, 0:128])], [rg(cur)], [par])
                                    if lvl < 6:
                                        self.mm(psa[:, 128:256], [(pp[:, 0:128], pp[:, 128:256])], [rg(cur)], [par])
                                    psas.append((psa, par))
                                for c, rg, (psa, par) in zip(ctxs, rgs, psas):
                                    w = 256 if lvl < 6 else 128
                                    self.copy('act', c[nxt][:, 0:w], psa[:, 0:w], [par], [rg(nxt)])
                                psbs = []
                                for c, rg in zip(ctxs, rgs):
                                    psb_, pbr = self.next_ps()
                                    self.mm(psb_[:, 0:128], [(c[nxt][:, 0:128], c['Tt'][:, :])], [rg(nxt), rg('Tt')], [pbr])
                                    psbs.append((psb_, pbr))
                                for c, rg, (psb_, pbr) in zip(ctxs, rgs, psbs):
                                    self.tt('dve', c['Tt'][:, :], c['Tt'][:, :], psb_[:, 0:128], ALU.add, [rg('Tt'), pbr], [rg('Tt')])
                                cur, nxt = nxt, cur
                            psus = []
                            for c, tt, rg in zip(ctxs, tts, rgs):
                                psu, pur = self.next_ps()
                                self.mm(psu[:, 0:128], [(c['Tt'][:, :], c['vb'][:, :])], [rg('Tt'), rg('vb')], [pur])
                                self.mm(psu[:, 128:256], [(c['kb2'][:, :], c['Tt'][:, :])], [rg('Tt'), rg('kb2')], [pur])
                                psus.append((psu, pur))
                            for c, tt, rg, (psu, pur) in zip(ctxs, tts, rgs, psus):
                                self.copy('act', ub[:, tt, :], psu[:, 0:128], [pur], [('g_ub', tt)])
                                self.copy('dve', wT[:, tt, :], psu[:, 128:256], [pur], [('g_wT', tt)])
                        S = self.sb("g_S", [128, 128], F32, es)
                        Sb = self.sb("g_Sb", [128, 128], BF16, es)
                        u_r = self.ring("g_u", 2, [128, 128], BF16, es)
                        kd_r = self.ring("g_kd", 2, [128, 128], BF16, es)
                        t_r = self.ring("g_t", 2, [128, 128], F32, es)
                        o_r = self.ring("g_o", 2, [128, 128], F32, es)
                        on_r = self.ring("g_on", 2, [128, 128], F32, es)
                        og_r = self.ring("g_og", 2, [128, 128], BF16, es)
                        jk_r = self.ring("g_jk", 1, [128, 128], BF16, es)
                        sm_r = self.ring("g_sm", 2, [128, 4], F32, es)
                        for tt in range(16 if GPH >= 3 else 0):
                            ts_ = slice(tt * 128, (tt + 1) * 128)
                            u, ur = u_r()
                            if tt > 0:
                                ps_ws, pwr = self.next_ps()
                                self.mm(ps_ws[:, 0:128], [(wT[:, tt, :], Sb[:, :])], [('g_wT', tt), ('g_Sb',)], [pwr])
                                self.tt('dve', u[:, :], ub[:, tt, :], ps_ws[:, 0:128], ALU.subtract, [('g_ub', tt), pwr], [ur])
                                ps_o1, po1r = self.next_ps()
                                self.mm(ps_o1[:, 0:128], [(qTn[:, ts_], Sb[:, :])], [('qTn', tt // 4), ('g_Sb',)], [po1r])
                            else:
                                self.copy('dve', u[:, :], ub[:, tt, :], [('g_ub', tt)], [ur])
                            ps_o2, po2r = self.next_ps()
                            self.mm(ps_o2[:, 0:128], [(qkT[:, tt, :], u[:, :])], [('g_qkT', tt), ur], [po2r])
                            o, orr = o_r()
                            if tt > 0:
                                t, tr_ = t_r()
                                self.act(t[:, :], ps_o1[:, 0:128], AF.Identity, [po1r, ('ea',)], [tr_],
                                         scale=P['ea'][:, tt, 24 + h:25 + h])
                                self.tt('dve', o[:, :], t[:, :], ps_o2[:, 0:128], ALU.add, [tr_, po2r], [orr])
                            else:
                                self.copy('dve', o[:, :], ps_o2[:, 0:128], [po2r], [orr])
                            if tt < 15:
                                kd, kdr = kd_r()
                                self.act(kd[:, :], k_tok[:, tt, :], AF.Identity, [('k_tok', tt), ('dte',)], [kdr],
                                         scale=P['dte'][:, tt, 24 + h:25 + h])
                                ps_sk, pskr = self.next_ps()
                                self.mm(ps_sk[:, 0:128], [(kd[:, :], u[:, :])], [kdr, ur], [pskr])
                                if tt == 0:
                                    self.copy('dve', S[:, :], ps_sk[:, 0:128], [pskr], [('g_S',)])
                                else:
                                    self.stt(S[:, :], S[:, :], P['cdec'][:, tt, 24 + h:25 + h], ps_sk[:, 0:128], ALU.mult, ALU.add,
                                             [('g_S',), ('cdec',), pskr], [('g_S',)])
                                self.copy('act', Sb[:, :], S[:, :], [('g_S',)], [('g_Sb',)])
                            sm, smr = sm_r()
                            jk, jr = jk_r()
                            self.act(jk[:, :], o[:, :], AF.Square, [orr], [jr, (smr, 0)], accum_out=sm[:, 0:1])
                            self.act(sm[:, 1:2], sm[:, 0:1], AF.Ln, [(smr, 0)], [(smr, 1)], bias=cc[:, 0:1], scale=1.0 / 128)
                            self.act(sm[:, 2:3], sm[:, 1:2], AF.Exp, [(smr, 1)], [(smr, 2)], scale=-0.5)
                            on, onr = on_r()
                            self.stt(on[:, :], o[:, :], sm[:, 2:3], P['gnw'][:, :], ALU.mult, ALU.mult, [orr, (smr, 2), ('gnw',)], [onr])
                            og, ogr = og_r()
                            self.tt('dve', og[:, :], on[:, :], siluzg[:, tt, i * 128:(i + 1) * 128], ALU.mult,
                                    [onr, ('siluzg', tt)], [ogr])
                            ps_t, ptr_ = self.next_ps()
                            psb = ps_t[:, :].bitcast(BF16)
                            self.tr(psb[:, 0:128], og[:, :], self.ident_b[:, :], [ogr, ('ident_b',)], [ptr_])
                            self.copy('act', ogT[:, i, ts_], psb[:, 0:128], [ptr_], [('ogT', i, tt)])
                        self.s.flush()
            with ExitStack() as es:
                wo = self.sb("wo_g", [128, 4, D], BF16, es)
                r0 = 1024 + hg * 512
                self.dma('pool', wo[:, :, :], Wout[r0:r0 + 512, :].rearrange("(c p) n -> p c n", p=128), [], [('wo_g',)])
                for tb in range(4):
                    t0, t1_ = tb * 512, (tb + 1) * 512
                    for dt in range(8):
                        ps, pr = self.next_ps()
                        self.mm(ps[:, :], [(wo[:, j, dt * 128:(dt + 1) * 128], ogT[:, j, t0:t1_]) for j in range(4)],
                                [('wo_g',)] + [('ogT', j, tt) for j in range(4) for tt in range(tb * 4, tb * 4 + 4)], [pr])
                        self.tt('dve', X[:, dt, t0:t1_], ps[:, :], X[:, dt, t0:t1_], ALU.add,
                                [pr] + XR(dt, t0, t1_), XR(dt, t0, t1_))
                self.s.flush()


_CACHE = {}


def consts():
    i = np.arange(128)
    c = {}
    c['c_ident'] = np.eye(128, dtype=np.float32)
    c['c_ones'] = np.ones((128, 128), np.float32)
    c['c_uincl'] = (i[:, None] >= i[None, :]).astype(np.float32)
    c['c_trile'] = (i[:, None] <= i[None, :]).astype(np.float32)
    c['c_gt'] = (i[:, None] > i[None, :]).astype(np.float32)
    blk = np.zeros((128, 128), np.float32)
    blk[:64, :64] = 1
    blk[64:, 64:] = 1
    c['c_blk'] = blk
    col = np.arange(512)
    c['c_mask4'] = np.stack([(col[None, :] > (i[:, None] + 128 * k)) for k in range(4)], axis=1).astype(np.float32)
    return c


def fm_cols(v):
    return np.ascontiguousarray(np.asarray(v, np.float32).reshape(8, 128).T)


def make_inputs(inp, nseq, ncores):
    f = lambda a: np.ascontiguousarray(np.asarray(a, dtype=np.float32))
    shared = consts()
    normw = np.stack([fm_cols(inp['a_norm_w'][0]), fm_cols(inp['mlp_norm_w'][0]),
                      fm_cols(inp['c_norm_w'][0]), fm_cols(inp['mlp_norm_w'][1])], axis=1)
    shared['normw'] = np.ascontiguousarray(normw)
    shared['mlp_w1'] = f(inp['mlp_w1'])
    shared['mlp_w2'] = f(inp['mlp_w2'])
    shared['c_w_qkv'] = f(inp['c_w_qkv'][0])
    shared['c_w_o'] = f(inp['c_w_o'][0])
    shared['qkw'] = np.ascontiguousarray(np.stack([np.tile(f(inp['c_q_norm_w'][0]), 2),
                                                   np.tile(f(inp['c_k_norm_w'][0]), 2)], axis=1))
    shared['a_w_in'] = f(inp['a_w_in'][0])
    shared['a_w_out'] = f(inp['a_w_out'][0])
    bc = lambda v: np.ascontiguousarray(np.broadcast_to(np.asarray(v, np.float32)[None, :], (128, len(v))))
    cws = f(inp['ssd_conv_w'][0])
    shared['convw_s'] = np.ascontiguousarray(cws.reshape(4, 12, 128).transpose(2, 1, 0))
    shared['convb_s'] = np.ascontiguousarray(f(inp['ssd_conv_b'][0]).reshape(12, 128).T)
    cwg = f(inp['gdn_conv_w'][0])
    shared['convw_g'] = np.ascontiguousarray(cwg.reshape(4, 24, 128).transpose(2, 1, 0))
    shared['dsk'] = bc(f(inp['ssd_d_skip'][0]))
    shared['snw'] = bc(f(inp['ssd_norm_w'][0]))
    shared['gnw'] = bc(f(inp['gdn_norm_w'][0]))
    z8 = np.zeros(8, np.float32)
    shared['bias_bc'] = bc(np.concatenate([f(inp['ssd_dt_bias'][0]), z8, f(inp['gdn_dt_bias'][0])]))
    shared['alog_bc'] = bc(np.concatenate([f(inp['ssd_a_log'][0]), z8, f(inp['gdn_a_log'][0])]))
    x = f(inp['x'])
    maps = []
    for c in range(ncores):
        m = dict(shared)
        m['x'] = np.ascontiguousarray(x[c * nseq:(c + 1) * nseq])
        maps.append(m)
    return maps


ALL_STAGES = ('mix0', 'mlp0', 'attn', 'mlp1')


def run(inp, nseq=2, ncores=8, stages=ALL_STAGES, trace=False):
    key = (nseq, tuple(stages))
    if key not in _CACHE:
        _CACHE[key] = Builder(nseq, stages).build()
    nc = _CACHE[key]
    maps = make_inputs(inp, nseq, ncores)
    res = run_bass_kernel_spmd(nc, maps, core_ids=list(range(ncores)), trace=trace)
    outs = [r["out"] for r in res.results]
    return np.concatenate(outs, axis=0), res


def kernel(**inputs):
    out, _ = run(inputs)
    return out.astype(np.float32)
```

```python
import os
import numpy as np
from contextlib import ExitStack
import concourse.bass as bass
import concourse.mybir as mybir
from concourse.bass_utils import run_bass_kernel_spmd

F32 = mybir.dt.float32
BF16 = mybir.dt.bfloat16
F32R = mybir.dt.float32r


def r32(ap):
    return ap.bitcast(F32R)
AF = mybir.ActivationFunctionType
ALU = mybir.AluOpType

L = 2048
D = 1024
EPS = 1e-6
EPOCH = 30000
NDSEM = 8


class Sched:
    CE = ('pe', 'act', 'dve', 'pool')

    def __init__(self, nc, esem, dsem):
        self.nc = nc
        self.esem = esem
        self.dsem = dsem
        self.cnt = {e: 0 for e in self.CE}
        self.ndma = {'sp': 0, 'pool': 0}
        self.seen = {e: {} for e in ('pe', 'act', 'dve', 'pool', 'sp')}
        self.reset()

    def reset(self):
        self.ops = {e: [] for e in ('pe', 'act', 'dve', 'pool', 'sp')}
        self.last_w = {}
        self.readers = {}

    def op(self, eng, fn, reads=(), writes=(), dma=False):
        if eng == 'pool' and not dma and os.environ.get("POOL2DVE"):
            eng = 'dve'
        self.nop_total = getattr(self, 'nop_total', 0) + 1
        cut = os.environ.get("OPCUT")
        if cut and self.nop_total > int(cut):
            return None
        if os.environ.get("OPTRACE"):
            import traceback
            fr = traceback.extract_stack(limit=4)
            print("OP", self.nop_total, eng, "dma" if dma else "", [f"{f.name}:{f.lineno}" for f in fr[:-1]])
        writes = list(writes) + [r for r in reads if r[0] in ('ps', 'psacc') and r not in writes]
        idx = len(self.ops[eng])
        tok = (eng, idx)
        deps = {}
        for r in reads:
            w = self.last_w.get(r)
            if w is not None:
                deps[w] = True
        for r in writes:
            w = self.last_w.get(r)
            if w is not None and w not in deps:
                deps[w] = False
            for t in self.readers.get(r, ()):
                if t not in deps:
                    deps[t] = False
        for r in reads:
            self.readers.setdefault(r, []).append(tok)
        for r in writes:
            self.last_w[r] = tok
            self.readers[r] = []
        self.ops[eng].append(dict(fn=fn, deps=deps, dma=dma))
        return tok

    def flush(self):
        nc = self.nc
        ops = self.ops
        need = set()
        for e, lst in ops.items():
            for i, o in enumerate(lst):
                keep = []
                for (e2, i2), raw in o['deps'].items():
                    o2 = ops[e2][i2]
                    if o2['dma']:
                        keep.append((e2, i2))
                    elif e2 == e:
                        if e == 'pe':
                            continue
                        keep.append((e2, i2))
                    else:
                        keep.append((e2, i2))
                o['keep'] = keep
                for k in keep:
                    if not ops[k[0]][k[1]]['dma']:
                        need.add(k)
        for e in self.CE:
            for i, o in enumerate(ops[e]):
                if o['dma']:
                    continue
                if (e, i) in need:
                    self.cnt[e] += 1
                    c = self.cnt[e]
                    o['sig'] = (self.esem[e][(c - 1) // EPOCH], (c - 1) % EPOCH + 1)
                else:
                    o['sig'] = None
        pending = {'sp': [], 'pool': []}
        for q in ('sp', 'pool'):
            for o in ops[q]:
                if o['dma']:
                    n = self.ndma[q]
                    self.ndma[q] += 1
                    o['sig'] = (self.dsem[q][n % NDSEM], 16 * (n // NDSEM + 1))
                    o['prewait'] = (self.dsem[q][n % NDSEM], 16 * (n // NDSEM)) if n >= NDSEM else None
                    pending[q].append(o['sig'])

        def emit(e, eng):
            seen = self.seen[e]

            def wait(s, v):
                if seen.get(s[0], 0) >= v:
                    return
                seen[s[0]] = v
                eng.wait_ge(s[1], v)

            for o in ops[e]:
                if o.get('prewait') is not None:
                    wait(*o['prewait'])
                for k in o['keep']:
                    sg = ops[k[0]][k[1]]['sig']
                    wait(*sg)
                ins = o['fn'](eng)
                if o['sig'] is not None:
                    ins.then_inc(o['sig'][0][1], 16 if o['dma'] else 1)
            if e in pending:
                for sg in pending[e][-NDSEM:]:
                    wait(*sg)

        with nc.Block() as block:
            if ops['sp']:
                @block.sync
                def _(eng):
                    emit('sp', eng)
            if ops['pe']:
                @block.tensor
                def _(eng):
                    emit('pe', eng)
            if ops['act']:
                @block.scalar
                def _(eng):
                    emit('act', eng)
            if ops['dve']:
                @block.vector
                def _(eng):
                    emit('dve', eng)
            if ops['pool']:
                @block.gpsimd
                def _(eng):
                    emit('pool', eng)
        self.reset()


def XR(c, t0, t1):
    return [('X', c, tt) for tt in range(t0 // 128, (t1 + 127) // 128)]


def HR(c, t0, t1):
    return [('H', c, tt) for tt in range(t0 // 128, (t1 + 127) // 128)]


class Builder:
    def __init__(self, nseq, stages):
        self.nseq = nseq
        self.stages = stages
        nc = bass.Bass("TRN2", target_bir_lowering=False)
        self.nc = nc
        self.es = ExitStack()
        self.dram = {}
        self._uid = 0

    def din(self, name, shape, dtype=F32):
        t = self.nc.dram_tensor(name, list(shape), dtype, kind="ExternalInput").ap()
        self.dram[name] = t
        return t

    def sb(self, name, shape, dtype, es=None):
        es = es or self.es
        return es.enter_context(self.nc.sbuf_tensor(f"{name}_u{self.uid()}", list(shape), dtype))

    def psum(self, name, shape, dtype):
        return self.es.enter_context(self.nc.psum_tensor(name, list(shape), dtype))

    def uid(self):
        self._uid += 1
        return self._uid

    def mm(self, out, pairs, reads, writes):
        pairs = list(pairs)

        def fn(pe):
            n = len(pairs)
            ins = None
            for i, (l, r) in enumerate(pairs):
                ins = pe.matmul(out, l, r, start=(i == 0), stop=(i == n - 1))
            return ins
        self.s.op('pe', fn, reads, writes)

    def mm1(self, out, l, r, start, stop, reads, writes):
        self.s.op('pe', lambda pe: pe.matmul(out, l, r, start=start, stop=stop), reads, writes)

    def asel(self, out, in_, base, cm, n, reads, writes):
        self.s.op('pool', lambda e: e.affine_select(out, in_, [[1, n]], ALU.is_gt, 0.0, base=base,
                                                    channel_multiplier=cm), reads, writes)

    def tr(self, out, in_, ident, reads, writes):
        self.s.op('pe', lambda pe: pe.transpose(out, in_, ident), reads, writes)

    def act(self, out, in_, func, reads, writes, bias=None, scale=None, accum_out=None, eng='act'):
        kw = {}
        if bias is not None:
            kw['bias'] = bias
        if scale is not None:
            kw['scale'] = scale
        if accum_out is not None:
            kw['accum_out'] = accum_out
        self.s.op('act', lambda e: e.activation(out, in_, func, **kw), reads, writes)

    def tt(self, eng, out, a, b, op, reads, writes):
        self.s.op(eng, lambda e: e.tensor_tensor(out, a, b, op), reads, writes)

    def ts(self, eng, out, a, s1, s2, op0, op1, reads, writes):
        if op1 is None:
            self.s.op(eng, lambda e: e.tensor_scalar(out, a, s1, None, op0), reads, writes)
        else:
            self.s.op(eng, lambda e: e.tensor_scalar(out, a, s1, s2, op0, op1), reads, writes)

    def stt(self, out, a, sc, b, op0, op1, reads, writes):
        self.s.op('dve', lambda e: e.scalar_tensor_tensor(out, a, sc, b, op0, op1), reads, writes)

    def copy(self, eng, out, in_, reads, writes):
        if eng == 'act':
            self.s.op('act', lambda e: e.copy(out, in_), reads, writes)
        else:
            self.s.op(eng, lambda e: e.tensor_copy(out, in_), reads, writes)

    def dma(self, q, out, in_, reads, writes):
        self.s.op(q, lambda e: e.dma_start(out=out, in_=in_), reads, writes, dma=True)

    def next_ps(self):
        i = self._psi
        self._psi = (i + 1) % len(self.psr)
        return self.psr[i], ('ps', i)

    def build(self):
        nc = self.nc
        ns = self.nseq
        x = self.din("x", [ns, L, D])
        out = nc.dram_tensor("out", [ns, L, D], F32, kind="ExternalOutput").ap()
        self.x_d, self.out_d = x, out
        d = self.din
        W = {}
        W['ident'] = d("c_ident", [128, 128])
        W['ones'] = d("c_ones", [128, 128])
        W['uincl'] = d("c_uincl", [128, 128])
        W['trile'] = d("c_trile", [128, 128])
        W['gt'] = d("c_gt", [128, 128])
        W['blk'] = d("c_blk", [128, 128])
        W['mask4'] = d("c_mask4", [128, 4, 512])
        W['normw'] = d("normw", [128, 4, 8])
        W['mlp_w1'] = d("mlp_w1", [2, D, 4096])
        W['mlp_w2'] = d("mlp_w2", [2, 4096, D])
        W['c_w_qkv'] = d("c_w_qkv", [D, 3072])
        W['c_w_o'] = d("c_w_o", [D, D])
        W['qkw'] = d("qkw", [128, 2])
        W['a_w_in'] = d("a_w_in", [D, 6688])
        W['a_w_out'] = d("a_w_out", [2048, D])
        W['convw_s'] = d("convw_s", [128, 12, 4])
        W['convb_s'] = d("convb_s", [128, 12])
        W['convw_g'] = d("convw_g", [128, 24, 4])
        W['dsk'] = d("dsk", [128, 16])
        W['snw'] = d("snw", [128, 1024])
        W['gnw'] = d("gnw", [128, 128])
        W['bias_bc'] = d("bias_bc", [128, 32])
        W['alog_bc'] = d("alog_bc", [128, 32])
        self.W = W

        es = self.es
        sems = {}
        si = [0]

        def newsem(name):
            h = es.enter_context(nc.semaphore(name))
            si[0] += 1
            return (si[0], h)
        esem = {e: [newsem(f"s_{e}{i}") for i in range(3)] for e in Sched.CE}
        dsem = {q: [newsem(f"d_{q}{i}") for i in range(NDSEM)] for q in ('sp', 'pool')}
        self.s = Sched(nc, esem, dsem)

        self.X = self.sb("X", [128, 8, L], F32)
        self.H = self.sb("H", [128, 8, L], BF16)
        self.ident_f = self.sb("ident_f", [128, 128], F32)
        self.ones_f = self.sb("ones_f", [128, 128], F32)
        self.uinclneg_f = self.sb("uinclneg_f", [128, 128], F32)
        self.uincl_f = self.sb("uincl_f", [128, 128], F32)
        self.trile_f = self.sb("trile_f", [128, 128], F32)
        self.gt_f = self.sb("gt_f", [128, 128], F32)
        self.gtle_f = self.sb("gtle_f", [128, 256], F32)
        self.ones_r = self.sb("ones_r", [128, 128], F32)
        self.trile_r = self.sb("trile_r", [128, 128], F32)
        self.ones_b = self.sb("ones_b", [128, 128], BF16)
        self.ident_b = self.sb("ident_b", [128, 128], BF16)
        self.blk_b = self.sb("blk_b", [128, 128], BF16)
        self.normw = self.sb("normw_sb", [128, 4, 8], F32)
        self.qkw = self.sb("qkw_sb", [128, 2], F32)
        self.cc = self.sb("constcols", [128, 8], F32)
        self.psr = [self.psum(f"ps{i}", [128, 512], F32) for i in range(6)]
        self.psacc = [self.psum(f"psacc{i}", [128, 512], F32) for i in range(2)]
        self._psi = 0

        self.stage_consts()
        for sq in range(ns):
            self.stage_load(sq)
            if 'mix0' in self.stages:
                self.stage_mix0()
            if 'mlp0' in self.stages:
                self.stage_mlp(0)
            if 'attn' in self.stages:
                self.stage_attn()
            if 'mlp1' in self.stages:
                self.stage_mlp(1)
            self.stage_store(sq)
        self.es.close()
        return nc

    def stage_consts(self):
        W = self.W
        q = 'sp'
        for name, t in (('ident', self.ident_f), ('ones', self.ones_f), ('uincl', self.uincl_f),
                        ('trile', self.trile_f), ('gt', self.gt_f)):
            self.dma(q, t[:, :], W[name][:, :], [], [(name,)])
        self.dma(q, self.gtle_f[:, 0:128], W['gt'][:, :], [], [('gtle',)])
        self.dma(q, self.gtle_f[:, 128:256], W['trile'][:, :], [], [('gtle',)])
        self.dma(q, self.normw[:, :, :], W['normw'][:, :, :], [], [('normw',)])
        self.dma(q, self.qkw[:, :], W['qkw'][:, :], [], [('qkw',)])
        for i, v in enumerate((EPS, float(np.log(0.125)), 1.0, 0.0, float(-0.5 * np.log(128.0)))):
            self.s.op('pool', (lambda e, i=i, v=v: e.memset(self.cc[:, i:i + 1], v)), [], [('cc', i)])
        with ExitStack() as es:
            tmp = self.sb("ctmp", [128, 128], F32, es)
            self.dma(q, tmp[:, :], W['blk'][:, :], [], [('ctmp',)])
            self.copy('dve', self.blk_b[:, :], tmp[:, :], [('ctmp',)], [('blk_b',)])
            self.copy('dve', self.ones_b[:, :], self.ones_f[:, :], [('ones',)], [('ones_b',)])
            self.copy('dve', self.ident_b[:, :], self.ident_f[:, :], [('ident',)], [('ident_b',)])
            self.ts('dve', r32(self.uinclneg_f[:, :]), self.uincl_f[:, :], -1.0, None, ALU.mult, None,
                    [('uincl',)], [('uinclneg',)])
            self.copy('dve', r32(self.ones_r[:, :]), self.ones_f[:, :], [('ones',)], [('ones_r',)])
            self.copy('dve', r32(self.trile_r[:, :]), self.trile_f[:, :], [('trile',)], [('trile_r',)])
            self.s.flush()

    def stage_load(self, sq):
        X = self.X
        with ExitStack() as es:
            stg = [self.sb(f"ldstg{i}", [128, D], F32, es) for i in range(2)]
            for tt in range(16):
                st = stg[tt % 2]
                sr = ('ldstg', tt % 2)
                self.dma('sp', st[:, :], self.x_d[sq, tt * 128:(tt + 1) * 128, :], [], [sr])
                for half in range(2):
                    ps, pr = self.next_ps()
                    for j in range(4):
                        c = half * 4 + j
                        self.tr(ps[:, j * 128:(j + 1) * 128], st[:, c * 128:(c + 1) * 128], self.ident_f[:, :],
                                [sr, ('ident',)], [pr])
                    dst = X[:, half * 4:half * 4 + 4, tt * 128:(tt + 1) * 128]
                    src = ps[:, :].rearrange("p (c t) -> p c t", c=4)
                    wr = [('X', half * 4 + j, tt) for j in range(4)]
                    self.copy('act' if half == 0 else 'dve', dst, src, [pr], wr)
            self.s.flush()

    def stage_store(self, sq):
        X = self.X
        with ExitStack() as es:
            stg = [self.sb(f"ststg{i}", [128, D], F32, es) for i in range(2)]
            for tt in range(16):
                st = stg[tt % 2]
                sr = ('ststg', tt % 2)
                for half in range(2):
                    ps, pr = self.next_ps()
                    for j in range(4):
                        c = half * 4 + j
                        self.tr(ps[:, j * 128:(j + 1) * 128], X[:, c, tt * 128:(tt + 1) * 128], self.ident_f[:, :],
                                [('X', c, tt), ('ident',)], [pr])
                    self.copy('act' if half == 0 else 'dve', st[:, half * 512:(half + 1) * 512], ps[:, :], [pr], [sr])
                self.dma('sp', self.out_d[sq, tt * 128:(tt + 1) * 128, :], st[:, :], [sr], [('out', sq, tt)])
            self.s.flush()

    def rmsnorm(self, widx):
        with ExitStack() as es:
            self._rmsnorm(widx, es)
            self.s.flush()

    def _rmsnorm(self, widx, es):
        X, H = self.X, self.H
        sq = [self.sb(f"rn_sq{i}", [128, 8, 512], BF16, es) for i in range(2)]
        lnv = [self.sb(f"rn_ln{i}", [128, 512], F32, es) for i in range(2)]
        rstd = [self.sb(f"rn_rs{i}", [128, 512], F32, es) for i in range(2)]
        for tb in range(4):
            b = tb % 2
            t0, t1 = tb * 512, (tb + 1) * 512
            for c in range(8):
                if c % 2 == 0:
                    self.act(sq[b][:, c, :], X[:, c, t0:t1], AF.Square, XR(c, t0, t1), [('rn_sq', b, c)])
                else:
                    self.tt('pool', sq[b][:, c, :], X[:, c, t0:t1], X[:, c, t0:t1], ALU.mult,
                            XR(c, t0, t1), [('rn_sq', b, c)])
            ps, pr = self.next_ps()
            self.mm(ps[:, :], [(self.ones_b[:, :], sq[b][:, c, :]) for c in range(8)],
                    [('rn_sq', b, c) for c in range(8)] + [('ones_b',)], [pr])
            self.act(lnv[b][:, :], ps[:, :], AF.Ln, [pr], [('rn_ln', b)], bias=self.cc[:, 0:1], scale=1.0 / D)
            self.act(rstd[b][:, :], lnv[b][:, :], AF.Exp, [('rn_ln', b)], [('rn_rs', b)], scale=-0.5)
            for c in range(8):
                self.stt(H[:, c, t0:t1], X[:, c, t0:t1], self.normw[:, widx, c:c + 1], rstd[b][:, :],
                         ALU.mult, ALU.mult, XR(c, t0, t1) + [('rn_rs', b), ('normw',)], HR(c, t0, t1))

    def stage_mlp(self, layer):
        X, H = self.X, self.H
        W1 = self.W['mlp_w1']
        W2 = self.W['mlp_w2']
        self.rmsnorm(1 + 2 * layer)
        with ExitStack() as es:
            w1 = [self.sb(f"w1_{i}", [128, 8, 512], BF16, es) for i in range(2)]
            w2 = [self.sb(f"w2_{i}", [128, 4, D], BF16, es) for i in range(2)]
            A = [self.sb(f"mlpA{i}", [128, 4, L], BF16, es) for i in range(2)]
            R = [self.sb(f"mlpR{i}", [128, 512], F32, es) for i in range(3)]
            ri = 0
            for e in range(8):
                b = e % 2
                self.dma('pool', w1[b][:, :, :],
                         W1[layer, :, e * 512:(e + 1) * 512].rearrange("(c p) n -> p c n", p=128),
                         [], [('w1', b)])
                self.dma('pool', w2[b][:, :, :],
                         W2[layer, e * 512:(e + 1) * 512, :].rearrange("(c p) n -> p c n", p=128),
                         [], [('w2', b)])
                for tb in range(4):
                    t0, t1 = tb * 512, (tb + 1) * 512
                    for j in range(4):
                        ps, pr = self.next_ps()
                        self.mm(ps[:, :], [(w1[b][:, kc, j * 128:(j + 1) * 128], H[:, kc, t0:t1]) for kc in range(8)],
                                [('w1', b)] + [r for kc in range(8) for r in HR(kc, t0, t1)], [pr])
                        r = R[ri % 3]
                        rr = ('mlpR', ri % 3)
                        ri += 1
                        self.act(r[:, :], ps[:, :], AF.Relu, [pr], [rr])
                        self.tt('pool', A[b][:, j, t0:t1], r[:, :], r[:, :], ALU.mult, [rr], [('mlpA', b, j, tb)])
                for tb in range(4):
                    t0, t1 = tb * 512, (tb + 1) * 512
                    for dt in range(8):
                        ps, pr = self.next_ps()
                        self.mm(ps[:, :], [(w2[b][:, j, dt * 128:(dt + 1) * 128], A[b][:, j, t0:t1]) for j in range(4)],
                                [('w2', b)] + [('mlpA', b, j, tb) for j in range(4)], [pr])
                        self.tt('dve', X[:, dt, t0:t1], ps[:, :], X[:, dt, t0:t1], ALU.add,
                                [pr] + XR(dt, t0, t1), XR(dt, t0, t1))
            self.s.flush()

    def stage_attn(self):
        X, H = self.X, self.H
        Wqkv = self.W['c_w_qkv']
        Wo = self.W['c_w_o']
        cc = self.cc
        self.rmsnorm(2)
        with ExitStack() as es0:
            qT = self.sb("qT", [128, 8, L], BF16, es0)
            kT = self.sb("kT", [128, 8, L], BF16, es0)
            with ExitStack() as es:
                wq = [self.sb(f"wq{i}", [128, 8, 128], BF16, es) for i in range(3)]
                wv = [self.sb(f"wv{i}", [128, 8, 512], BF16, es) for i in range(2)]
                qraw = [self.sb(f"qraw{i}", [128, 512], F32, es) for i in range(2)]
                sqq = [self.sb(f"sqq{i}", [128, 512], BF16, es) for i in range(2)]
                lnv = [self.sb(f"qln{i}", [128, 512], F32, es) for i in range(2)]
                rstd = [self.sb(f"qrs{i}", [128, 512], F32, es) for i in range(2)]
                for half in range(2):
                    self.dma('pool', wv[half][:, :, :],
                             Wqkv[:, 2048 + half * 512:2048 + (half + 1) * 512].rearrange("(c p) n -> p c n", p=128),
                             [], [('wv', half)])
                wi = 0
                bi = 0
                PA = int(os.environ.get("ATT_PA", "3"))
                for c in range(8 if (PA & 1) else 0):
                    for which in range(2):
                        dst = qT if which == 0 else kT
                        dn = 'qT' if which == 0 else 'kT'
                        col0 = which * 1024 + c * 128
                        w = wq[wi % 3]
                        wr = ('wq', wi % 3)
                        wi += 1
                        self.dma('pool', w[:, :, :], Wqkv[:, col0:col0 + 128].rearrange("(c p) n -> p c n", p=128),
                                 [], [wr])
                        for tb in range(4):
                            t0, t1 = tb * 512, (tb + 1) * 512
                            b = bi % 2
                            bi += 1
                            ps, pr = self.next_ps()
                            self.mm(ps[:, :], [(w[:, kc, :], H[:, kc, t0:t1]) for kc in range(8)],
                                    [wr] + [r for kc in range(8) for r in HR(kc, t0, t1)], [pr])
                            self.act(sqq[b][:, :], ps[:, :], AF.Square, [pr], [('sqq', b)])
                            self.copy('dve', qraw[b][:, :], ps[:, :], [pr], [('qraw', b)])
                            ps2, pr2 = self.next_ps()
                            self.mm(ps2[:, :], [(self.blk_b[:, :], sqq[b][:, :])], [('sqq', b), ('blk_b',)], [pr2])
                            self.act(lnv[b][:, :], ps2[:, :], AF.Ln, [pr2], [('qln', b)], bias=cc[:, 0:1], scale=1.0 / 64)
                            self.act(rstd[b][:, :], lnv[b][:, :], AF.Exp, [('qln', b)], [('qrs', b)],
                                     bias=cc[:, 1:2] if which == 0 else cc[:, 3:4], scale=-0.5)
                            self.stt(dst[:, c, t0:t1], qraw[b][:, :], self.qkw[:, which:which + 1], rstd[b][:, :],
                                     ALU.mult, ALU.mult, [('qraw', b), ('qrs', b), ('qkw',)], [(dn, c, tb)])
                for tt in range(16 if (PA & 2) else 0):
                    t0, t1 = tt * 128, (tt + 1) * 128
                    pss = []
                    for half in range(2):
                        ps, pr = self.next_ps()
                        self.mm(ps[:, :], [(H[:, kc, t0:t1], wv[half][:, kc, :]) for kc in range(8)],
                                [('wv', half)] + [('H', kc, tt) for kc in range(8)], [pr])
                        pss.append((ps, pr))
                    for half in range(2):
                        ps, pr = pss[half]
                        self.copy('act' if half == 0 else 'dve', H[:, half * 4:half * 4 + 4, t0:t1],
                                  ps[:, :].rearrange("p (c t) -> p c t", c=4), [pr],
                                  [('H', half * 4 + j, tt) for j in range(4)])
                self.s.flush()
            with ExitStack() as es:
                eb = [self.sb(f"at_e{i}", [128, 512], F32, es) for i in range(3)]
                spb = [self.sb(f"at_sp{i}", [128, 512], F32, es) for i in range(3)]
                tmpb = [self.sb(f"at_tmp{i}", [128, 512], F32, es) for i in range(2)]
                Rs = [self.sb(f"at_rs{i}", [128, 512], F32, es) for i in range(2)]
                ATb = [self.sb(f"at_A{i}", [128, 512], BF16, es) for i in range(3)]
                qpad = [[self.sb(f"qpad{par}{j}", [128, 512], BF16, es) for j in range(2)] for par in range(2)]
                for par in range(2):
                    for j in range(2):
                        self.s.op('pool', (lambda e, t=qpad[par][j]: e.memset(t[:, :], 0.0)), [], [('qpad', par, j)])
                self.mask4 = self.sb("mask4", [128, 4, 512], F32, es)
                self.dma('sp', self.mask4[:, :, :], self.W['mask4'][:, :, :], [], [('mask4',)])
                it_g = 0
                ai = 0
                pairs = []
                gi = 0
                for h in range(16):
                    for G in range(4):
                        kmax = 4 * G + 3
                        for kb in range(kmax, -1, -1):
                            pairs.append(dict(h=h, G=G, kb=kb, first=(kb == kmax), last=(kb == 0), diag=(kb >= 4 * G),
                                              gi=gi, i=len(pairs)))
                        gi += 1
                rstate = {'cur': 0}

                def opnd(p):
                    h, G, kb = p['h'], p['G'], p['kb']
                    c = h // 2
                    b0 = (h % 2) * 64
                    par, j = h % 2, p['gi'] % 2
                    q_s = qpad[par][j][:, :]
                    k_s = kT[:, c, kb * 128:(kb + 1) * 128]
                    return c, b0, q_s, k_s, ('qpad', par, j), ('kT', c, kb // 4)

                def stA(p):
                    c, b0, q_s, k_s, qr, kr = opnd(p)
                    b = p['i'] % 3
                    if p['first']:
                        G = p['G']
                        self.copy('dve', q_s[b0:b0 + 64, :], qT[b0:b0 + 64, c, G * 512:(G + 1) * 512], [('qT', c, G)], [qr])
                    ps_z, pzr = self.next_ps()
                    self.mm(ps_z[:, :], [(k_s, q_s)], [kr, qr], [pzr])
                    self.act(eb[b][:, :], ps_z[:, :], AF.Exp, [pzr], [('at_e', b)])
                    self.act(r32(spb[b][:, :]), eb[b][:, :], AF.Ln, [('at_e', b)], [('at_sp', b)], bias=cc[:, 2:3])
                    if p['diag']:
                        self.tt('pool', r32(spb[b][:, :]), spb[b][:, :], self.mask4[:, p['kb'] - 4 * p['G'], :], ALU.mult,
                                [('at_sp', b), ('mask4',)], [('at_sp', b)])

                def stB(p):
                    c, b0, q_s, k_s, qr, kr = opnd(p)
                    b = p['i'] % 3
                    b2 = p['i'] % 2
                    ps_n, pnr = self.next_ps()
                    self.mm(ps_n[:, :], [(k_s, q_s), (r32(self.uinclneg_f[:, :]), r32(spb[b][:, :]))],
                            [kr, qr, ('at_sp', b), ('uinclneg',)], [pnr])
                    if p['first']:
                        self.act(ATb[b][:, :], ps_n[:, :], AF.Exp, [pnr], [('at_A', b)])
                    else:
                        rcur = rstate['cur']
                        self.tt('dve', tmpb[b2][:, :], ps_n[:, :], Rs[rcur][:, :], ALU.subtract,
                                [pnr, ('at_rs', rcur)], [('at_tmp', b2)])
                        self.act(ATb[b][:, :], tmpb[b2][:, :], AF.Exp, [('at_tmp', b2)], [('at_A', b)])
                    if p['diag']:
                        self.tt('pool', ATb[b][:, :], ATb[b][:, :], self.mask4[:, p['kb'] - 4 * p['G'], :], ALU.mult,
                                [('at_A', b), ('mask4',)], [('at_A', b)])
                    if not p['last']:
                        ps_r, prr = self.next_ps()
                        self.mm(ps_r[:, :], [(r32(self.ones_r[:, :]), r32(spb[b][:, :]))], [('at_sp', b), ('ones_r',)], [prr])
                        rcur = rstate['cur']
                        nxt = 1 - rcur
                        if p['first']:
                            self.copy('dve', Rs[nxt][:, :], ps_r[:, :], [prr], [('at_rs', nxt)])
                        else:
                            self.tt('dve', Rs[nxt][:, :], ps_r[:, :], Rs[rcur][:, :], ALU.add,
                                    [prr, ('at_rs', rcur)], [('at_rs', nxt)])
                        rstate['cur'] = nxt

                def stC(p):
                    c, b0, q_s, k_s, qr, kr = opnd(p)
                    h, G, kb = p['h'], p['G'], p['kb']
                    b = p['i'] % 3
                    acc = self.psacc[p['gi'] % 2]
                    accr = ('psacc', p['gi'] % 2)
                    v_s = H[:, c, kb * 128:(kb + 1) * 128]
                    self.mm1(acc[:, :], v_s, ATb[b][:, :], p['first'], p['last'],
                             [('H', c, kb), ('at_A', b)], [accr])
                    if p['last']:
                        self.copy('act', qT[b0:b0 + 64, c, G * 512:(G + 1) * 512], acc[b0:b0 + 64, :], [accr], [('qT', c, G)])

                n = len(pairs)
                for t in range(n + 2):
                    if t < n:
                        stA(pairs[t])
                    if 0 <= t - 1 < n:
                        stB(pairs[t - 1])
                    if 0 <= t - 2 < n:
                        stC(pairs[t - 2])
                self.s.flush()
            with ExitStack() as es:
                wo = [self.sb(f"wo{i}", [128, 8, 128], BF16, es) for i in range(2)]
                for dt in range(8):
                    w = wo[dt % 2]
                    wr = ('wo', dt % 2)
                    self.dma('pool', w[:, :, :], Wo[:, dt * 128:(dt + 1) * 128].rearrange("(c p) n -> p c n", p=128),
                             [], [wr])
                    for tb in range(4):
                        t0, t1 = tb * 512, (tb + 1) * 512
                        ps, pr = self.next_ps()
                        self.mm(ps[:, :], [(w[:, c, :], qT[:, c, t0:t1]) for c in range(8)],
                                [wr] + [('qT', c, tb) for c in range(8)], [pr])
                        self.tt('dve', X[:, dt, t0:t1], ps[:, :], X[:, dt, t0:t1], ALU.add,
                                [pr] + XR(dt, t0, t1), XR(dt, t0, t1))
                self.s.flush()

    def ring(self, name, n, shape, dtype, es):
        tiles = [self.sb(f"{name}{i}", shape, dtype, es) for i in range(n)]
        state = {'i': 0}

        def nxt():
            i = state['i'] % n
            state['i'] += 1
            return tiles[i], (name, i)
        nxt.tiles = tiles
        return nxt

    def conv_tile(self, col0, cw, cb, R, silu_out=None, silu_reg=None):
        H = self.H
        Win = self.W['a_w_in']
        w, wr = R['w']()
        self.dma('pool', w[:, :, :], Win[:, col0:col0 + 128].rearrange("(c p) n -> p c n", p=128), [], [wr])
        raw, rr = R['raw']()
        for tb in range(4):
            t0, t1 = tb * 512, (tb + 1) * 512
            ps, pr = self.next_ps()
            self.mm(ps[:, :], [(w[:, kc, :], H[:, kc, t0:t1]) for kc in range(8)],
                    [wr] + [r for kc in range(8) for r in HR(kc, t0, t1)], [pr])
            self.copy('act', raw[:, 3 + t0:3 + t1], ps[:, :], [pr], [rr])
        acc, ar = R['acc']()
        if cb is not None:
            self.ts('dve', acc[:, :], raw[:, 3:3 + L], cw[:, 3:4], cb, ALU.mult, ALU.add, [rr, ('convw',)], [ar])
        else:
            self.ts('dve', acc[:, :], raw[:, 3:3 + L], cw[:, 3:4], None, ALU.mult, None, [rr, ('convw',)], [ar])
        for k in (2, 1, 0):
            self.stt(acc[:, :], raw[:, k:k + L], cw[:, k:k + 1], acc[:, :], ALU.mult, ALU.add, [rr, ar, ('convw',)], [ar])
        if silu_out is not None:
            self.act(silu_out, acc[:, :], AF.Silu, [ar], [silu_reg])
            return silu_out, silu_reg
        self.act(acc[:, :], acc[:, :], AF.Silu, [ar], [ar])
        return acc, ar

    def to_tok(self, src, sr, dst, dr_name, width_off, src_regs=None):
        for q in range(4):
            ps, pr = self.next_ps()
            psb = ps[:, :].bitcast(BF16)
            for j in range(4):
                tt = q * 4 + j
                self.tr(psb[:, j * 128:(j + 1) * 128], src[:, tt * 128:(tt + 1) * 128], self.ident_b[:, :],
                        (src_regs if src_regs is not None else [sr]) + [('ident_b',)], [pr])
            self.copy('act' if q % 2 == 0 else 'dve', dst[:, q * 4:q * 4 + 4, width_off:width_off + 128],
                      psb[:, 0:512].rearrange("p (c t) -> p c t", c=4), [pr],
                      [(dr_name, q * 4 + j) for j in range(4)])

    def stage_mix0(self):
        X, H = self.X, self.H
        W = self.W
        cc = self.cc
        Win = W['a_w_in']
        Wout = W['a_w_out']
        self.rmsnorm(0)
        with ExitStack() as es0:
            sb = lambda n, sh, dt=F32: self.sb(n, sh, dt, es0)
            convw_s = sb("convw_s", [128, 12, 4])
            convb_s = sb("convb_s", [128, 12])
            convw_g = sb("convw_g", [128, 24, 4])
            dsk = sb("dsk", [128, 16])
            gnw = sb("gnw", [128, 128])
            spl = sb("spl", [128, 16, 32])
            ag = sb("ag", [128, 16, 32])
            ea = sb("ea", [128, 16, 32])
            dte = sb("dte", [128, 16, 32])
            cdec = sb("cdec", [128, 16, 32])
            bet = sb("bet", [128, 16, 8])
            nbet = sb("nbet", [128, 16, 8])
            bg = sb("bg", [128, 16, 8])
            self.dma('sp', convw_s[:, :, :], W['convw_s'][:, :, :], [], [('convw',)])
            self.dma('sp', convb_s[:, :], W['convb_s'][:, :], [], [('convw',)])
            self.dma('sp', convw_g[:, :, :], W['convw_g'][:, :, :], [], [('convw',)])
            self.dma('sp', dsk[:, :], W['dsk'][:, :], [], [('dsk',)])
            self.dma('sp', gnw[:, :], W['gnw'][:, :], [], [('gnw',)])
            with ExitStack() as es:
                wsm = self.sb("wsm", [128, 8, 32], BF16, es)
                bias_bc = self.sb("bias_bc", [128, 32], F32, es)
                alog = self.sb("alog", [128, 32], F32, es)
                aneg = self.sb("aneg", [128, 32], F32, es)
                smraw = self.sb("smraw", [128, 16, 32], F32, es)
                e1 = self.sb("sm_e", [128, 16, 32], F32, es)
                acum = self.sb("acum", [128, 16, 32], F32, es)
                tot = self.sb("tot", [128, 16, 32], F32, es)
                tmp = self.sb("sm_tmp", [128, 16, 32], F32, es)
                self.dma('pool', wsm[:, :, 0:16], Win[:, 2560:2576].rearrange("(c p) n -> p c n", p=128), [], [('wsm', 0)])
                self.dma('pool', wsm[:, :, 16:32], Win[:, 6672:6688].rearrange("(c p) n -> p c n", p=128), [], [('wsm', 1)])
                self.dma('sp', bias_bc[:, :], W['bias_bc'][:, :], [], [('bias_bc',)])
                self.dma('sp', alog[:, :], W['alog_bc'][:, :], [], [('alog',)])
                self.act(aneg[:, :], alog[:, :], AF.Exp, [('alog',)], [('aneg',)])
                self.ts('dve', aneg[:, :], aneg[:, :], -1.0, None, ALU.mult, None, [('aneg',)], [('aneg',)])
                for tt in range(16):
                    ps, pr = self.next_ps()
                    self.mm(ps[:, 0:32], [(H[:, kc, tt * 128:(tt + 1) * 128], wsm[:, kc, :]) for kc in range(8)],
                            [('wsm', 0), ('wsm', 1)] + [('H', kc, tt) for kc in range(8)], [pr])
                    self.tt('dve', smraw[:, tt, :], ps[:, 0:32], bias_bc[:, :], ALU.add, [pr, ('bias_bc',)], [('smraw',)])
                f2 = lambda t: t[:, :, :].rearrange("p a b -> p (a b)")
                self.act(f2(e1), f2(smraw), AF.Exp, [('smraw',)], [('sm_e',)])
                self.act(f2(spl), f2(e1), AF.Ln, [('sm_e',)], [('spl',)], bias=cc[:, 2:3])
                self.ts('dve', tmp[:, :, 16:24], e1[:, :, 16:24], 1.0, None, ALU.add, None, [('sm_e',)], [('sm_tmp',)])
                self.s.op('dve', lambda e: e.reciprocal(tmp[:, :, 16:24], tmp[:, :, 16:24]), [('sm_tmp',)], [('sm_tmp',)])
                self.tt('dve', bet[:, :, :], e1[:, :, 16:24], tmp[:, :, 16:24], ALU.mult, [('sm_e',), ('sm_tmp',)], [('bet',)])
                self.ts('dve', nbet[:, :, :], bet[:, :, :], -1.0, None, ALU.mult, None, [('bet',)], [('nbet',)])
                self.tt('dve', ag[:, :, :], spl[:, :, :], aneg[:, :].unsqueeze(1).broadcast_to([128, 16, 32]), ALU.mult,
                        [('spl',), ('aneg',)], [('ag',)])
                ps, pr = self.next_ps()
                self.mm(ps[:, :], [(self.trile_f[:, :], f2(ag))], [('ag',), ('trile',)], [pr])
                self.copy('dve', f2(acum), ps[:, :], [pr], [('acum',)])
                self.act(f2(ea), ps[:, :], AF.Exp, [pr], [('ea',)])
                ps2, pr2 = self.next_ps()
                self.mm(ps2[:, :], [(self.ones_f[:, :], f2(ag))], [('ag',), ('ones',)], [pr2])
                self.act(f2(cdec), ps2[:, :], AF.Exp, [pr2], [('cdec',)])
                self.tt('dve', f2(tot), ps2[:, :], f2(acum), ALU.subtract, [pr2, ('acum',)], [('tot',)])
                self.act(f2(dte), f2(tot), AF.Exp, [('tot',)], [('dte',)])
                self.tt('dve', bg[:, :, :], bet[:, :, :], ea[:, :, 24:32], ALU.mult, [('bet',), ('ea',)], [('bg',)])
                self.s.flush()
            P = dict(convw_s=convw_s, convb_s=convb_s, convw_g=convw_g, dsk=dsk, gnw=gnw, spl=spl, ag=ag,
                     ea=ea, dte=dte, cdec=cdec, bet=bet, nbet=nbet, bg=bg)
            if os.environ.get("MIX_STOP") == "prelude":
                return
            MIXSEL = os.environ.get("MIX_SEL", "ssd,gdn").split(",")
            if 'ssd' in MIXSEL:
                for g in range(2):
                    self.ssd_group(g, P)
            if 'gdn' in MIXSEL:
                for hg in range(2):
                    self.gdn_group(hg, P)

    def ssd_group(self, g, P):
        X, H = self.X, self.H
        Win = self.W['a_w_in']
        Wout = self.W['a_w_out']
        cc = self.cc
        with ExitStack() as es0:
            xs_tok = self.sb("xs_tok", [128, 16, 512], BF16, es0)
            siluz = self.sb("siluz", [128, 16, 512], BF16, es0)
            BT = self.sb("BT", [128, L], BF16, es0)
            CT = self.sb("CT", [128, L], BF16, es0)
            B_tok = self.sb("B_tok", [128, 16, 128], BF16, es0)
            with ExitStack() as es:
                R = dict(w=self.ring("cw", 3, [128, 8, 128], BF16, es),
                         raw=self.ring("craw", 2, [128, L + 3], F32, es),
                         acc=self.ring("cacc", 1, [128, L], F32, es))
                xsb = self.ring("xsb", 2, [128, L], BF16, es)
                wz = self.sb("wz", [128, 8, 512], BF16, es)
                self.dma('pool', wz[:, :, :], Win[:, g * 512:(g + 1) * 512].rearrange("(c p) n -> p c n", p=128), [], [('wz',)])
                self._zero_halo(R, es)
                for j in range(4):
                    ti = g * 4 + j
                    xb, xr = xsb()
                    self.conv_tile(1024 + ti * 128, P['convw_s'][:, ti, :], P['convb_s'][:, ti:ti + 1], R,
                                   silu_out=xb[:, :], silu_reg=xr)
                    self.to_tok(xb, xr, xs_tok, 'xs_tok', j * 128)
                ti = 8 + g
                self.conv_tile(1024 + ti * 128, P['convw_s'][:, ti, :], P['convb_s'][:, ti:ti + 1], R,
                               silu_out=BT[:, :], silu_reg=('BT',))
                self.to_tok(BT, ('BT',), B_tok, 'B_tok', 0)
                ti = 10 + g
                self.conv_tile(1024 + ti * 128, P['convw_s'][:, ti, :], P['convb_s'][:, ti:ti + 1], R,
                               silu_out=CT[:, :], silu_reg=('CT',))
                for tt in range(16):
                    ps, pr = self.next_ps()
                    self.mm(ps[:, :], [(H[:, kc, tt * 128:(tt + 1) * 128], wz[:, kc, :]) for kc in range(8)],
                            [('wz',)] + [('H', kc, tt) for kc in range(8)], [pr])
                    self.act(siluz[:, tt, :], ps[:, :], AF.Silu, [pr], [('siluz', tt)])
                self.s.flush()
            if os.environ.get("MIX_STOP") == "ssd1":
                return
            yT = self.sb("yT", [128, 4, L], BF16, es0)
            with ExitStack() as es:
                ring = lambda n, k, sh, dt=F32: self.ring(n, k, sh, dt, es)
                cbm_r = ring("cbm", 2, [128, 128])
                lseg_r = ring("lseg", 2, [128, 4, 128])
                LT_r = ring("LT", 2, [128, 4, 128])
                MT_r = ring("MT", 1, [128, 8, 128], BF16)
                xdt_r = ring("xdt", 2, [128, 512], BF16)
                xdd_r = ring("xdd", 2, [128, 512], BF16)
                t1_r = ring("t1", 1, [128, 512])
                t3_r = ring("t3", 1, [128, 512])
                yg_r = ring("yg", 1, [128, 512])
                junk_r = ring("junk", 1, [128, 512], BF16)
                yn_r = ring("ynb", 2, [128, 512], BF16)
                sm_r = ring("ssm", 2, [128, 4])
                S = self.sb("ssdS", [128, 512], F32, es)
                Sb = self.sb("ssdSb", [128, 512], BF16, es)
                snw = self.sb("snw", [128, 512], F32, es)
                self.dma('sp', snw[:, :], self.W['snw'][:, g * 512:(g + 1) * 512], [], [('snw',)])
                hs = slice(g * 8, (g + 1) * 8)
                v8 = lambda ap: ap.rearrange("p (h d) -> p h d", h=8)
                bc8 = lambda ap: ap.unsqueeze(2).broadcast_to([128, 8, 64])
                for tt in range(16):
                    ts_ = slice(tt * 128, (tt + 1) * 128)
                    ps_cb, pcr = self.next_ps()
                    self.mm(ps_cb[:, 0:128], [(BT[:, ts_], CT[:, ts_])], [('BT',), ('CT',)], [pcr])
                    cbm, cbr = cbm_r()
                    self.tt('dve', cbm[:, :], ps_cb[:, 0:128], self.trile_f[:, :], ALU.mult, [pcr, ('trile',)], [cbr])
                    MT, mr = MT_r()
                    for half in range(2):
                        lseg, lr = lseg_r()
                        ps_s, psr = self.next_ps()
                        for i in range(4):
                            hh = g * 8 + half * 4 + i
                            self.ts('dve', r32(lseg[:, i, :]), self.gt_f[:, :], P['ag'][:, tt, hh:hh + 1], None,
                                    ALU.mult, None, [('gt',), ('ag',)], [(lr, i)])
                            self.mm(ps_s[:, i * 128:(i + 1) * 128], [(r32(lseg[:, i, :]), r32(self.trile_r[:, :]))],
                                    [(lr, i), ('trile_r',)], [psr])
                        LT, ltr = LT_r()
                        self.act(LT[:, :, :].rearrange("p a b -> p (a b)"), ps_s[:, :], AF.Exp, [psr], [ltr])
                        self.tt('dve', MT[:, half * 4:(half + 1) * 4, :], LT[:, :, :],
                                cbm[:, :].unsqueeze(1).broadcast_to([128, 4, 128]), ALU.mult, [ltr, cbr], [(mr, half)])
                    xdt, xdr = xdt_r()
                    self.tt('dve', v8(xdt[:, :]), v8(xs_tok[:, tt, :]), bc8(P['spl'][:, tt, hs]), ALU.mult,
                            [('xs_tok', tt), ('spl',)], [xdr])
                    xdd, xddr = xdd_r()
                    self.tt('pool', v8(xdd[:, :]), v8(xdt[:, :]), bc8(P['dte'][:, tt, hs]), ALU.mult,
                            [xdr, ('dte',)], [xddr])
                    ps_y, pyr = self.next_ps()
                    for i in range(8):
                        self.mm(ps_y[:, i * 64:(i + 1) * 64], [(MT[:, i, :], xdt[:, i * 64:(i + 1) * 64])],
                                [(mr, i // 4), xdr], [pyr])
                    t1, t1r = t1_r()
                    if tt > 0:
                        ps_o, por = self.next_ps()
                        self.mm(ps_o[:, :], [(CT[:, ts_], Sb[:, :])], [('CT',), ('ssdSb',)], [por])
                        self.tt('dve', v8(t1[:, :]), v8(ps_o[:, :]), bc8(P['ea'][:, tt, hs]), ALU.mult, [por, ('ea',)], [t1r])
                        self.tt('dve', t1[:, :], t1[:, :], ps_y[:, :], ALU.add, [t1r, pyr], [t1r])
                    else:
                        self.copy('dve', t1[:, :], ps_y[:, :], [pyr], [t1r])
                    t3, t3r = t3_r()
                    self.tt('pool', v8(t3[:, :]), v8(xs_tok[:, tt, :]), bc8(P['dsk'][:, hs]), ALU.mult,
                            [('xs_tok', tt), ('dsk',)], [t3r])
                    self.tt('dve', t3[:, :], t3[:, :], t1[:, :], ALU.add, [t3r, t1r], [t3r])
                    yg, ygr = yg_r()
                    self.tt('pool', yg[:, :], t3[:, :], siluz[:, tt, :], ALU.mult, [t3r, ('siluz', tt)], [ygr])
                    sm, smr = sm_r()
                    junk, jr = junk_r()
                    self.act(junk[:, :], yg[:, :], AF.Square, [ygr], [jr, (smr, 0)], accum_out=sm[:, 0:1])
                    self.act(sm[:, 1:2], sm[:, 0:1], AF.Ln, [(smr, 0)], [(smr, 1)], bias=cc[:, 0:1], scale=1.0 / 512)
                    self.act(sm[:, 2:3], sm[:, 1:2], AF.Exp, [(smr, 1)], [(smr, 2)], scale=-0.5)
                    yn, ynr = yn_r()
                    self.stt(yn[:, :], yg[:, :], sm[:, 2:3], snw[:, :], ALU.mult, ALU.mult,
                             [ygr, (smr, 2), ('snw',)], [ynr])
                    ps_t, ptr_ = self.next_ps()
                    psb = ps_t[:, :].bitcast(BF16)
                    for j in range(4):
                        self.tr(psb[:, j * 128:(j + 1) * 128], yn[:, j * 128:(j + 1) * 128], self.ident_b[:, :],
                                [ynr, ('ident_b',)], [ptr_])
                    self.copy('act', yT[:, 0:4, ts_], psb[:, 0:512].rearrange("p (c t) -> p c t", c=4), [ptr_],
                              [('yT', j, tt) for j in range(4)])
                    if tt < 15:
                        ps_st, pstr = self.next_ps()
                        self.mm(ps_st[:, :], [(B_tok[:, tt, :], xdd[:, :])], [('B_tok', tt), xddr], [pstr])
                        if tt == 0:
                            self.copy('dve', S[:, :], ps_st[:, :], [pstr], [('ssdS',)])
                        else:
                            self.tt('dve', v8(S[:, :]), v8(S[:, :]), bc8(P['cdec'][:, tt, hs]), ALU.mult,
                                    [('ssdS',), ('cdec',)], [('ssdS',)])
                            self.tt('dve', S[:, :], S[:, :], ps_st[:, :], ALU.add, [('ssdS',), pstr], [('ssdS',)])
                        self.copy('act', Sb[:, :], S[:, :], [('ssdS',)], [('ssdSb',)])
                self.s.flush()
            with ExitStack() as es:
                wo = self.sb("wo_s", [128, 4, D], BF16, es)
                self.dma('pool', wo[:, :, :], Wout[g * 512:(g + 1) * 512, :].rearrange("(c p) n -> p c n", p=128), [], [('wo_s',)])
                for tb in range(4):
                    t0, t1_ = tb * 512, (tb + 1) * 512
                    for dt in range(8):
                        ps, pr = self.next_ps()
                        self.mm(ps[:, :], [(wo[:, j, dt * 128:(dt + 1) * 128], yT[:, j, t0:t1_]) for j in range(4)],
                                [('wo_s',)] + [('yT', j, tt) for j in range(4) for tt in range(tb * 4, tb * 4 + 4)], [pr])
                        self.tt('dve', X[:, dt, t0:t1_], ps[:, :], X[:, dt, t0:t1_], ALU.add,
                                [pr] + XR(dt, t0, t1_), XR(dt, t0, t1_))
                self.s.flush()

    def _zero_halo(self, R, es):
        for i, t in enumerate(R['raw'].tiles):
            self.s.op('pool', (lambda e, t=t: e.memset(t[:, 0:3], 0.0)), [], [("craw", i)])

    def gdn_group(self, hg, P):
        X, H = self.X, self.H
        Win = self.W['a_w_in']
        Wout = self.W['a_w_out']
        cc = self.cc
        QOFF = 2576
        with ExitStack() as es0:
            siluzg = self.sb("siluzg", [128, 16, 512], BF16, es0)
            ogT = self.sb("ogT", [128, 4, L], BF16, es0)
            with ExitStack() as es:
                wzg = self.sb("wzg", [128, 8, 512], BF16, es)
                c0 = 5648 + hg * 512
                self.dma('pool', wzg[:, :, :], Win[:, c0:c0 + 512].rearrange("(c p) n -> p c n", p=128), [], [('wzg',)])
                for tt in range(16):
                    ps, pr = self.next_ps()
                    self.mm(ps[:, :], [(H[:, kc, tt * 128:(tt + 1) * 128], wzg[:, kc, :]) for kc in range(8)],
                            [('wzg',)] + [('H', kc, tt) for kc in range(8)], [pr])
                    self.act(siluzg[:, tt, :], ps[:, :], AF.Silu, [pr], [('siluzg', tt)])
                self.s.flush()
            for i in range(4):
                h = hg * 4 + i
                with ExitStack() as esh:
                    qTn = self.sb("qTn", [128, L], BF16, esh)
                    kTn = self.sb("kTn", [128, L], BF16, esh)
                    k_tok = self.sb("k_tok", [128, 16, 128], BF16, esh)
                    v_tok = self.sb("v_tok", [128, 16, 128], BF16, esh)
                    with ExitStack() as es:
                        R = dict(w=self.ring("cw", 2, [128, 8, 128], BF16, es),
                                 raw=self.ring("craw", 2, [128, L + 3], F32, es),
                                 acc=self.ring("cacc", 2, [128, L], F32, es))
                        kvb = ogT[:, i, :]
                        sqq = self.sb("gsqq", [128, 512], BF16, es)
                        lnv = self.sb("gln", [128, 512], F32, es)
                        rstd = self.sb("grs", [128, 512], F32, es)
                        self._zero_halo(R, es)
                        for which, dstT, dn in ((0, qTn, 'qTn'), (1, kTn, 'kTn')):
                            ti = which * 8 + h
                            acc, ar = self.conv_tile(QOFF + ti * 128, P['convw_g'][:, ti, :], None, R)
                            for tb in range(4):
                                t0, t1 = tb * 512, (tb + 1) * 512
                                self.act(sqq[:, :], acc[:, t0:t1], AF.Square, [ar], [('gsqq',)])
                                ps, pr = self.next_ps()
                                self.mm(ps[:, :], [(self.ones_b[:, :], sqq[:, :])], [('gsqq',), ('ones_b',)], [pr])
                                self.act(lnv[:, :], ps[:, :], AF.Ln, [pr], [('gln',)], bias=cc[:, 0:1])
                                self.act(rstd[:, :], lnv[:, :], AF.Exp, [('gln',)], [('grs',)],
                                         bias=cc[:, 4:5] if which == 0 else cc[:, 3:4], scale=-0.5)
                                self.tt('dve', dstT[:, t0:t1], acc[:, t0:t1], rstd[:, :], ALU.mult, [ar, ('grs',)], [(dn, tb)])
                        self.to_tok(kTn, None, k_tok, 'k_tok', 0, src_regs=[('kTn', tb) for tb in range(4)])
                        ti = 16 + h
                        self.conv_tile(QOFF + ti * 128, P['convw_g'][:, ti, :], None, R, silu_out=kvb, silu_reg=('kvb',))
                        self.to_tok(kvb, ('kvb',), v_tok, 'v_tok', 0)
                        self.s.flush()
                    with ExitStack() as es:
                        ub = self.sb("g_ub", [128, 16, 128], F32, es)
                        wT = self.sb("g_wT", [128, 16, 128], BF16, es)
                        qkT = self.sb("g_qkT", [128, 16, 128], BF16, es)
                        NCTX = int(os.environ.get("GDN_NCTX", "4"))
                        ctxs = []
                        for ci in range(NCTX):
                            t = lambda n, sh=[128, 128], dt=F32: self.sb(f"g_{n}{ci}", sh, dt, es)
                            ctxs.append(dict(gmask=t("gmask"), DLT=t("DLT", [128, 256]), PPa=t("PPa", [128, 256]),
                                             PPb=t("PPb", [128, 256]), Tt=t("Tt"), vb=t("vb"), kb2=t("kb2"), ci=ci))
                        gcol = lambda tt: P['ag'][:, tt, 24 + h:25 + h]
                        GPH = int(os.environ.get("GDN_PH", "3"))
                        for grp in range(16 // NCTX if GPH >= 2 else 0):
                            tts = tuple(range(grp * NCTX, (grp + 1) * NCTX))
                            rgs = [(lambda n, ci=c['ci']: ('g_' + n, ci)) for c in ctxs]
                            ps1s, ps3s, ps4s = [], [], []
                            for c, tt, rg in zip(ctxs, tts, rgs):
                                self.ts('dve', r32(c['gmask'][:, :]), self.gt_f[:, :], gcol(tt), None, ALU.mult, None,
                                        [('gt',), ('ag',)], [rg('gmask')])
                                self.act(r32(c['vb'][:, :]), v_tok[:, tt, :], AF.Identity, [('v_tok', tt), ('bet',)], [rg('vb')],
                                         scale=P['bet'][:, tt, h:h + 1])
                                self.act(r32(c['kb2'][:, :]), k_tok[:, tt, :], AF.Identity, [('k_tok', tt), ('bg',)], [rg('kb2')],
                                         scale=P['bg'][:, tt, h:h + 1])
                            for c, tt, rg in zip(ctxs, tts, rgs):
                                ts_ = slice(tt * 128, (tt + 1) * 128)
                                ps1, p1r = self.next_ps()
                                self.mm(ps1[:, 0:128], [(r32(self.trile_r[:, :]), r32(c['gmask'][:, :]))], [rg('gmask'), ('trile_r',)], [p1r])
                                self.mm(ps1[:, 128:256], [(r32(c['gmask'][:, :]), r32(self.trile_r[:, :]))], [rg('gmask'), ('trile_r',)], [p1r])
                                self.mm(ps1[:, 256:384], [(kTn[:, ts_], kTn[:, ts_])], [('kTn', tt // 4)], [p1r])
                                self.mm(ps1[:, 384:512], [(kTn[:, ts_], qTn[:, ts_])], [('kTn', tt // 4), ('qTn', tt // 4)], [p1r])
                                ps1s.append((ps1, p1r))
                            for c, tt, rg, (ps1, p1r) in zip(ctxs, tts, rgs, ps1s):
                                self.act(c['DLT'][:, :], ps1[:, 0:256], AF.Exp, [p1r], [rg('DLT')])
                                self.tt('dve', c['DLT'][:, :], c['DLT'][:, :], self.gtle_f[:, :], ALU.mult,
                                        [rg('DLT'), ('gtle',)], [rg('DLT')])
                            for c, tt, rg, (ps1, p1r) in zip(ctxs, tts, rgs, ps1s):
                                self.stt(r32(c['PPa'][:, 0:128]), ps1[:, 256:384], P['nbet'][:, tt, h:h + 1], c['DLT'][:, 0:128],
                                         ALU.mult, ALU.mult, [p1r, ('nbet',), rg('DLT')], [rg('PPa')])
                                self.tt('dve', qkT[:, tt, :], ps1[:, 384:512], c['DLT'][:, 128:256], ALU.mult,
                                        [p1r, rg('DLT')], [('g_qkT', tt)])
                            for c, tt, rg in zip(ctxs, tts, rgs):
                                ps4, p4r = self.next_ps()
                                self.tr(ps4[:, 0:128], c['PPa'][:, 0:128], self.ident_f[:, :], [rg('PPa'), ('ident',)], [p4r])
                                ps4s.append((ps4, p4r))
                            for c, tt, rg, (ps4, p4r) in zip(ctxs, tts, rgs, ps4s):
                                self.copy('act', r32(c['PPa'][:, 128:256]), ps4[:, 0:128], [p4r], [rg('PPa')])
                                self.tt('dve', r32(c['Tt'][:, :]), c['PPa'][:, 128:256], self.ident_f[:, :], ALU.add,
                                        [rg('PPa'), ('ident',)], [rg('Tt')])
                            cur, nxt = 'PPa', 'PPb'
                            for lvl in range(1, 7):
                                psas = []
                                for c, rg in zip(ctxs, rgs):
                                    psa, par = self.next_ps()
                                    pp = c[cur]
                                    self.mm(psa[:, 0:128], [(r32(pp[:, 128:256]), r32(pp[:, 0:128]))], [rg(cur)], [par])
                                    if lvl < 6:
                                        self.mm(psa[:, 128:256], [(r32(pp[:, 0:128]), r32(pp[:, 128:256]))], [rg(cur)], [par])
                                    psas.append((psa, par))
                                for c, rg, (psa, par) in zip(ctxs, rgs, psas):
                                    w = 256 if lvl < 6 else 128
                                    self.copy('act', r32(c[nxt][:, 0:w]), psa[:, 0:w], [par], [rg(nxt)])
                                psbs = []
                                for c, rg in zip(ctxs, rgs):
                                    psb_, pbr = self.next_ps()
                                    self.mm(psb_[:, 0:128], [(r32(c[nxt][:, 0:128]), r32(c['Tt'][:, :]))], [rg(nxt), rg('Tt')], [pbr])
                                    psbs.append((psb_, pbr))
                                for c, rg, (psb_, pbr) in zip(ctxs, rgs, psbs):
                                    self.tt('dve', r32(c['Tt'][:, :]), c['Tt'][:, :], psb_[:, 0:128], ALU.add, [rg('Tt'), pbr], [rg('Tt')])
                                cur, nxt = nxt, cur
                            psus = []
                            for c, tt, rg in zip(ctxs, tts, rgs):
                                psu, pur = self.next_ps()
                                self.mm(psu[:, 0:128], [(r32(c['Tt'][:, :]), r32(c['vb'][:, :]))], [rg('Tt'), rg('vb')], [pur])
                                self.mm(psu[:, 128:256], [(r32(c['kb2'][:, :]), r32(c['Tt'][:, :]))], [rg('Tt'), rg('kb2')], [pur])
                                psus.append((psu, pur))
                            for c, tt, rg, (psu, pur) in zip(ctxs, tts, rgs, psus):
                                self.copy('act', ub[:, tt, :], psu[:, 0:128], [pur], [('g_ub', tt)])
                                self.copy('dve', wT[:, tt, :], psu[:, 128:256], [pur], [('g_wT', tt)])
                        S = self.sb("g_S", [128, 128], F32, es)
                        Sb = self.sb("g_Sb", [128, 128], BF16, es)
                        u_r = self.ring("g_u", 2, [128, 128], BF16, es)
                        kd_r = self.ring("g_kd", 2, [128, 128], BF16, es)
                        t_r = self.ring("g_t", 2, [128, 128], F32, es)
                        o_r = self.ring("g_o", 2, [128, 128], F32, es)
                        on_r = self.ring("g_on", 2, [128, 128], F32, es)
                        og_r = self.ring("g_og", 2, [128, 128], BF16, es)
                        jk_r = self.ring("g_jk", 1, [128, 128], BF16, es)
                        sm_r = self.ring("g_sm", 2, [128, 4], F32, es)
                        for tt in range(16 if GPH >= 3 else 0):
                            ts_ = slice(tt * 128, (tt + 1) * 128)
                            u, ur = u_r()
                            if tt > 0:
                                ps_ws, pwr = self.next_ps()
                                self.mm(ps_ws[:, 0:128], [(wT[:, tt, :], Sb[:, :])], [('g_wT', tt), ('g_Sb',)], [pwr])
                                self.tt('dve', u[:, :], ub[:, tt, :], ps_ws[:, 0:128], ALU.subtract, [('g_ub', tt), pwr], [ur])
                                ps_o1, po1r = self.next_ps()
                                self.mm(ps_o1[:, 0:128], [(qTn[:, ts_], Sb[:, :])], [('qTn', tt // 4), ('g_Sb',)], [po1r])
                            else:
                                self.copy('dve', u[:, :], ub[:, tt, :], [('g_ub', tt)], [ur])
                            ps_o2, po2r = self.next_ps()
                            self.mm(ps_o2[:, 0:128], [(qkT[:, tt, :], u[:, :])], [('g_qkT', tt), ur], [po2r])
                            o, orr = o_r()
                            if tt > 0:
                                t, tr_ = t_r()
                                self.act(t[:, :], ps_o1[:, 0:128], AF.Identity, [po1r, ('ea',)], [tr_],
                                         scale=P['ea'][:, tt, 24 + h:25 + h])
                                self.tt('dve', o[:, :], t[:, :], ps_o2[:, 0:128], ALU.add, [tr_, po2r], [orr])
                            else:
                                self.copy('dve', o[:, :], ps_o2[:, 0:128], [po2r], [orr])
                            if tt < 15:
                                kd, kdr = kd_r()
                                self.act(kd[:, :], k_tok[:, tt, :], AF.Identity, [('k_tok', tt), ('dte',)], [kdr],
                                         scale=P['dte'][:, tt, 24 + h:25 + h])
                                ps_sk, pskr = self.next_ps()
                                self.mm(ps_sk[:, 0:128], [(kd[:, :], u[:, :])], [kdr, ur], [pskr])
                                if tt == 0:
                                    self.copy('dve', S[:, :], ps_sk[:, 0:128], [pskr], [('g_S',)])
                                else:
                                    self.stt(S[:, :], S[:, :], P['cdec'][:, tt, 24 + h:25 + h], ps_sk[:, 0:128], ALU.mult, ALU.add,
                                             [('g_S',), ('cdec',), pskr], [('g_S',)])
                                self.copy('act', Sb[:, :], S[:, :], [('g_S',)], [('g_Sb',)])
                            sm, smr = sm_r()
                            jk, jr = jk_r()
                            self.act(jk[:, :], o[:, :], AF.Square, [orr], [jr, (smr, 0)], accum_out=sm[:, 0:1])
                            self.act(sm[:, 1:2], sm[:, 0:1], AF.Ln, [(smr, 0)], [(smr, 1)], bias=cc[:, 0:1], scale=1.0 / 128)
                            self.act(sm[:, 2:3], sm[:, 1:2], AF.Exp, [(smr, 1)], [(smr, 2)], scale=-0.5)
                            on, onr = on_r()
                            self.stt(on[:, :], o[:, :], sm[:, 2:3], P['gnw'][:, :], ALU.mult, ALU.mult, [orr, (smr, 2), ('gnw',)], [onr])
                            og, ogr = og_r()
                            self.tt('dve', og[:, :], on[:, :], siluzg[:, tt, i * 128:(i + 1) * 128], ALU.mult,
                                    [onr, ('siluzg', tt)], [ogr])
                            ps_t, ptr_ = self.next_ps()
                            psb = ps_t[:, :].bitcast(BF16)
                            self.tr(psb[:, 0:128], og[:, :], self.ident_b[:, :], [ogr, ('ident_b',)], [ptr_])
                            self.copy('act', ogT[:, i, ts_], psb[:, 0:128], [ptr_], [('ogT', i, tt)])
                        self.s.flush()
            with ExitStack() as es:
                wo = self.sb("wo_g", [128, 4, D], BF16, es)
                r0 = 1024 + hg * 512
                self.dma('pool', wo[:, :, :], Wout[r0:r0 + 512, :].rearrange("(c p) n -> p c n", p=128), [], [('wo_g',)])
                for tb in range(4):
                    t0, t1_ = tb * 512, (tb + 1) * 512
                    for dt in range(8):
                        ps, pr = self.next_ps()
                        self.mm(ps[:, :], [(wo[:, j, dt * 128:(dt + 1) * 128], ogT[:, j, t0:t1_]) for j in range(4)],
                                [('wo_g',)] + [('ogT', j, tt) for j in range(4) for tt in range(tb * 4, tb * 4 + 4)], [pr])
                        self.tt('dve', X[:, dt, t0:t1_], ps[:, :], X[:, dt, t0:t1_], ALU.add,
                                [pr] + XR(dt, t0, t1_), XR(dt, t0, t1_))
                self.s.flush()


_CACHE = {}


def consts():
    i = np.arange(128)
    c = {}
    c['c_ident'] = np.eye(128, dtype=np.float32)
    c['c_ones'] = np.ones((128, 128), np.float32)
    c['c_uincl'] = (i[:, None] >= i[None, :]).astype(np.float32)
    c['c_trile'] = (i[:, None] <= i[None, :]).astype(np.float32)
    c['c_gt'] = (i[:, None] > i[None, :]).astype(np.float32)
    blk = np.zeros((128, 128), np.float32)
    blk[:64, :64] = 1
    blk[64:, 64:] = 1
    c['c_blk'] = blk
    col = np.arange(512)
    c['c_mask4'] = np.stack([(col[None, :] > (i[:, None] + 128 * k)) for k in range(4)], axis=1).astype(np.float32)
    return c


def fm_cols(v):
    return np.ascontiguousarray(np.asarray(v, np.float32).reshape(8, 128).T)


def make_inputs(inp, nseq, ncores):
    f = lambda a: np.ascontiguousarray(np.asarray(a, dtype=np.float32))
    shared = consts()
    normw = np.stack([fm_cols(inp['a_norm_w'][0]), fm_cols(inp['mlp_norm_w'][0]),
                      fm_cols(inp['c_norm_w'][0]), fm_cols(inp['mlp_norm_w'][1])], axis=1)
    shared['normw'] = np.ascontiguousarray(normw)
    shared['mlp_w1'] = f(inp['mlp_w1'])
    shared['mlp_w2'] = f(inp['mlp_w2'])
    shared['c_w_qkv'] = f(inp['c_w_qkv'][0])
    shared['c_w_o'] = f(inp['c_w_o'][0])
    shared['qkw'] = np.ascontiguousarray(np.stack([np.tile(f(inp['c_q_norm_w'][0]), 2),
                                                   np.tile(f(inp['c_k_norm_w'][0]), 2)], axis=1))
    shared['a_w_in'] = f(inp['a_w_in'][0])
    shared['a_w_out'] = f(inp['a_w_out'][0])
    bc = lambda v: np.ascontiguousarray(np.broadcast_to(np.asarray(v, np.float32)[None, :], (128, len(v))))
    cws = f(inp['ssd_conv_w'][0])
    shared['convw_s'] = np.ascontiguousarray(cws.reshape(4, 12, 128).transpose(2, 1, 0))
    shared['convb_s'] = np.ascontiguousarray(f(inp['ssd_conv_b'][0]).reshape(12, 128).T)
    cwg = f(inp['gdn_conv_w'][0])
    shared['convw_g'] = np.ascontiguousarray(cwg.reshape(4, 24, 128).transpose(2, 1, 0))
    shared['dsk'] = bc(f(inp['ssd_d_skip'][0]))
    shared['snw'] = bc(f(inp['ssd_norm_w'][0]))
    shared['gnw'] = bc(f(inp['gdn_norm_w'][0]))
    z8 = np.zeros(8, np.float32)
    shared['bias_bc'] = bc(np.concatenate([f(inp['ssd_dt_bias'][0]), z8, f(inp['gdn_dt_bias'][0])]))
    shared['alog_bc'] = bc(np.concatenate([f(inp['ssd_a_log'][0]), z8, f(inp['gdn_a_log'][0])]))
    x = f(inp['x'])
    maps = []
    for c in range(ncores):
        m = dict(shared)
        m['x'] = np.ascontiguousarray(x[c * nseq:(c + 1) * nseq])
        maps.append(m)
    return maps


ALL_STAGES = ('mix0', 'mlp0', 'attn', 'mlp1')


def run(inp, nseq=2, ncores=8, stages=ALL_STAGES, trace=False):
    key = (nseq, tuple(stages))
    if key not in _CACHE:
        _CACHE[key] = Builder(nseq, stages).build()
    nc = _CACHE[key]
    maps = make_inputs(inp, nseq, ncores)
    res = run_bass_kernel_spmd(nc, maps, core_ids=list(range(ncores)), trace=trace)
    outs = [r["out"] for r in res.results]
    return np.concatenate(outs, axis=0), res


def kernel(**inputs):
    out, _ = run(inputs)
    return out.astype(np.float32)
```

```python
import os
import numpy as np
from contextlib import ExitStack
import concourse.bass as bass
import concourse.mybir as mybir
from concourse.bass_utils import run_bass_kernel_spmd

F32 = mybir.dt.float32
BF16 = mybir.dt.bfloat16
F32R = mybir.dt.float32r


def r32(ap):
    return ap.bitcast(F32R)
AF = mybir.ActivationFunctionType
ALU = mybir.AluOpType

L = 2048
D = 1024
EPS = 1e-6
EPOCH = 30000
NDSEM = 8


class Sched:
    CE = ('pe', 'act', 'dve', 'pool')

    def __init__(self, nc, esem, dsem):
        self.nc = nc
        self.esem = esem
        self.dsem = dsem
        self.cnt = {e: 0 for e in self.CE}
        self.ndma = {'sp': 0, 'pool': 0}
        self.seen = {e: {} for e in ('pe', 'act', 'dve', 'pool', 'sp')}
        self.reset()

    def reset(self):
        self.ops = {e: [] for e in ('pe', 'act', 'dve', 'pool', 'sp')}
        self.last_w = {}
        self.readers = {}

    def op(self, eng, fn, reads=(), writes=(), dma=False):
        if eng == 'pool' and not dma and os.environ.get("POOL2DVE"):
            eng = 'dve'
        self.nop_total = getattr(self, 'nop_total', 0) + 1
        cut = os.environ.get("OPCUT")
        if cut and self.nop_total > int(cut):
            return None
        if os.environ.get("OPTRACE"):
            import traceback
            fr = traceback.extract_stack(limit=4)
            print("OP", self.nop_total, eng, "dma" if dma else "", [f"{f.name}:{f.lineno}" for f in fr[:-1]])
        writes = list(writes) + [r for r in reads if r[0] in ('ps', 'psacc') and r not in writes]
        idx = len(self.ops[eng])
        tok = (eng, idx)
        deps = {}
        for r in reads:
            w = self.last_w.get(r)
            if w is not None:
                deps[w] = True
        for r in writes:
            w = self.last_w.get(r)
            if w is not None and w not in deps:
                deps[w] = False
            for t in self.readers.get(r, ()):
                if t not in deps:
                    deps[t] = False
        for r in reads:
            self.readers.setdefault(r, []).append(tok)
        for r in writes:
            self.last_w[r] = tok
            self.readers[r] = []
        self.ops[eng].append(dict(fn=fn, deps=deps, dma=dma))
        return tok

    def flush(self):
        nc = self.nc
        ops = self.ops
        need = set()
        for e, lst in ops.items():
            for i, o in enumerate(lst):
                keep = []
                for (e2, i2), raw in o['deps'].items():
                    o2 = ops[e2][i2]
                    if o2['dma']:
                        keep.append((e2, i2))
                    elif e2 == e:
                        if e == 'pe':
                            continue
                        keep.append((e2, i2))
                    else:
                        keep.append((e2, i2))
                o['keep'] = keep
                for k in keep:
                    if not ops[k[0]][k[1]]['dma']:
                        need.add(k)
        for e in self.CE:
            for i, o in enumerate(ops[e]):
                if o['dma']:
                    continue
                if (e, i) in need:
                    self.cnt[e] += 1
                    c = self.cnt[e]
                    o['sig'] = (self.esem[e][(c - 1) // EPOCH], (c - 1) % EPOCH + 1)
                else:
                    o['sig'] = None
        pending = {'sp': [], 'pool': []}
        for q in ('sp', 'pool'):
            for o in ops[q]:
                if o['dma']:
                    n = self.ndma[q]
                    self.ndma[q] += 1
                    o['sig'] = (self.dsem[q][n % NDSEM], 16 * (n // NDSEM + 1))
                    o['prewait'] = (self.dsem[q][n % NDSEM], 16 * (n // NDSEM)) if n >= NDSEM else None
                    pending[q].append(o['sig'])

        def emit(e, eng):
            seen = self.seen[e]

            def wait(s, v):
                if seen.get(s[0], 0) >= v:
                    return
                seen[s[0]] = v
                eng.wait_ge(s[1], v)

            for o in ops[e]:
                if o.get('prewait') is not None:
                    wait(*o['prewait'])
                for k in o['keep']:
                    sg = ops[k[0]][k[1]]['sig']
                    wait(*sg)
                ins = o['fn'](eng)
                if o['sig'] is not None:
                    ins.then_inc(o['sig'][0][1], 16 if o['dma'] else 1)
            if e in pending:
                for sg in pending[e][-NDSEM:]:
                    wait(*sg)

        with nc.Block() as block:
            if ops['sp']:
                @block.sync
                def _(eng):
                    emit('sp', eng)
            if ops['pe']:
                @block.tensor
                def _(eng):
                    emit('pe', eng)
            if ops['act']:
                @block.scalar
                def _(eng):
                    emit('act', eng)
            if ops['dve']:
                @block.vector
                def _(eng):
                    emit('dve', eng)
            if ops['pool']:
                @block.gpsimd
                def _(eng):
                    emit('pool', eng)
        self.reset()


def XR(c, t0, t1):
    return [('X', c, tt) for tt in range(t0 // 128, (t1 + 127) // 128)]


def HR(c, t0, t1):
    return [('H', c, tt) for tt in range(t0 // 128, (t1 + 127) // 128)]


class Builder:
    def __init__(self, nseq, stages):
        self.nseq = nseq
        self.stages = stages
        nc = bass.Bass("TRN2", target_bir_lowering=False)
        self.nc = nc
        self.es = ExitStack()
        self.dram = {}
        self._uid = 0

    def din(self, name, shape, dtype=F32):
        t = self.nc.dram_tensor(name, list(shape), dtype, kind="ExternalInput").ap()
        self.dram[name] = t
        return t

    def sb(self, name, shape, dtype, es=None):
        es = es or self.es
        return es.enter_context(self.nc.sbuf_tensor(f"{name}_u{self.uid()}", list(shape), dtype))

    def psum(self, name, shape, dtype):
        return self.es.enter_context(self.nc.psum_tensor(name, list(shape), dtype))

    def uid(self):
        self._uid += 1
        return self._uid

    def mm(self, out, pairs, reads, writes):
        pairs = list(pairs)

        def fn(pe):
            n = len(pairs)
            ins = None
            for i, (l, r) in enumerate(pairs):
                ins = pe.matmul(out, l, r, start=(i == 0), stop=(i == n - 1))
            return ins
        self.s.op('pe', fn, reads, writes)

    def mm1(self, out, l, r, start, stop, reads, writes):
        self.s.op('pe', lambda pe: pe.matmul(out, l, r, start=start, stop=stop), reads, writes)

    def asel(self, out, in_, base, cm, n, reads, writes):
        self.s.op('pool', lambda e: e.affine_select(out, in_, [[1, n]], ALU.is_gt, 0.0, base=base,
                                                    channel_multiplier=cm), reads, writes)

    def tr(self, out, in_, ident, reads, writes):
        self.s.op('pe', lambda pe: pe.transpose(out, in_, ident), reads, writes)

    def act(self, out, in_, func, reads, writes, bias=None, scale=None, accum_out=None, eng='act'):
        kw = {}
        if bias is not None:
            kw['bias'] = bias
        if scale is not None:
            kw['scale'] = scale
        if accum_out is not None:
            kw['accum_out'] = accum_out
        self.s.op('act', lambda e: e.activation(out, in_, func, **kw), reads, writes)

    def tt(self, eng, out, a, b, op, reads, writes):
        self.s.op(eng, lambda e: e.tensor_tensor(out, a, b, op), reads, writes)

    def ts(self, eng, out, a, s1, s2, op0, op1, reads, writes):
        if op1 is None:
            self.s.op(eng, lambda e: e.tensor_scalar(out, a, s1, None, op0), reads, writes)
        else:
            self.s.op(eng, lambda e: e.tensor_scalar(out, a, s1, s2, op0, op1), reads, writes)

    def stt(self, out, a, sc, b, op0, op1, reads, writes):
        self.s.op('dve', lambda e: e.scalar_tensor_tensor(out, a, sc, b, op0, op1), reads, writes)

    def copy(self, eng, out, in_, reads, writes):
        if eng == 'act':
            self.s.op('act', lambda e: e.copy(out, in_), reads, writes)
        else:
            self.s.op(eng, lambda e: e.tensor_copy(out, in_), reads, writes)

    def dma(self, q, out, in_, reads, writes):
        self.s.op(q, lambda e: e.dma_start(out=out, in_=in_), reads, writes, dma=True)

    def next_ps(self):
        i = self._psi
        self._psi = (i + 1) % len(self.psr)
        return self.psr[i], ('ps', i)

    def build(self):
        nc = self.nc
        ns = self.nseq
        x = self.din("x", [ns, L, D])
        out = nc.dram_tensor("out", [ns, L, D], F32, kind="ExternalOutput").ap()
        self.x_d, self.out_d = x, out
        d = self.din
        W = {}
        W['ident'] = d("c_ident", [128, 128])
        W['ones'] = d("c_ones", [128, 128])
        W['uincl'] = d("c_uincl", [128, 128])
        W['trile'] = d("c_trile", [128, 128])
        W['gt'] = d("c_gt", [128, 128])
        W['blk'] = d("c_blk", [128, 128])
        W['mask4'] = d("c_mask4", [128, 4, 512])
        W['normw'] = d("normw", [128, 4, 8])
        W['mlp_w1'] = d("mlp_w1", [2, D, 4096])
        W['mlp_w2'] = d("mlp_w2", [2, 4096, D])
        W['c_w_qkv'] = d("c_w_qkv", [D, 3072])
        W['c_w_o'] = d("c_w_o", [D, D])
        W['qkw'] = d("qkw", [128, 2])
        W['a_w_in'] = d("a_w_in", [D, 6688])
        W['a_w_out'] = d("a_w_out", [2048, D])
        W['convw_s'] = d("convw_s", [128, 12, 4])
        W['convb_s'] = d("convb_s", [128, 12])
        W['convw_g'] = d("convw_g", [128, 24, 4])
        W['dsk'] = d("dsk", [128, 16])
        W['snw'] = d("snw", [128, 1024])
        W['gnw'] = d("gnw", [128, 128])
        W['bias_bc'] = d("bias_bc", [128, 32])
        W['alog_bc'] = d("alog_bc", [128, 32])
        self.W = W

        es = self.es
        sems = {}
        si = [0]

        def newsem(name):
            h = es.enter_context(nc.semaphore(name))
            si[0] += 1
            return (si[0], h)
        esem = {e: [newsem(f"s_{e}{i}") for i in range(3)] for e in Sched.CE}
        dsem = {q: [newsem(f"d_{q}{i}") for i in range(NDSEM)] for q in ('sp', 'pool')}
        self.s = Sched(nc, esem, dsem)

        self.X = self.sb("X", [128, 8, L], F32)
        self.H = self.sb("H", [128, 8, L], BF16)
        self.ident_f = self.sb("ident_f", [128, 128], F32)
        self.ones_f = self.sb("ones_f", [128, 128], F32)
        self.uinclneg_f = self.sb("uinclneg_f", [128, 128], F32)
        self.uincl_f = self.sb("uincl_f", [128, 128], F32)
        self.trile_f = self.sb("trile_f", [128, 128], F32)
        self.gt_f = self.sb("gt_f", [128, 128], F32)
        self.gtle_f = self.sb("gtle_f", [128, 256], F32)
        self.ones_r = self.sb("ones_r", [128, 128], F32)
        self.trile_r = self.sb("trile_r", [128, 128], F32)
        self.ones_b = self.sb("ones_b", [128, 128], BF16)
        self.ident_b = self.sb("ident_b", [128, 128], BF16)
        self.blk_b = self.sb("blk_b", [128, 128], BF16)
        self.normw = self.sb("normw_sb", [128, 4, 8], F32)
        self.qkw = self.sb("qkw_sb", [128, 2], F32)
        self.cc = self.sb("constcols", [128, 8], F32)
        self.psr = [self.psum(f"ps{i}", [128, 512], F32) for i in range(6)]
        self.psacc = [self.psum(f"psacc{i}", [128, 512], F32) for i in range(2)]
        self._psi = 0

        self.stage_consts()
        for sq in range(ns):
            self.stage_load(sq)
            if 'mix0' in self.stages:
                self.stage_mix0()
            if 'mlp0' in self.stages:
                self.stage_mlp(0)
            if 'attn' in self.stages:
                self.stage_attn()
            if 'mlp1' in self.stages:
                self.stage_mlp(1)
            self.stage_store(sq)
        self.es.close()
        return nc

    def stage_consts(self):
        W = self.W
        q = 'sp'
        for name, t in (('ident', self.ident_f), ('ones', self.ones_f), ('uincl', self.uincl_f),
                        ('trile', self.trile_f), ('gt', self.gt_f)):
            self.dma(q, t[:, :], W[name][:, :], [], [(name,)])
        self.dma(q, self.gtle_f[:, 0:128], W['gt'][:, :], [], [('gtle',)])
        self.dma(q, self.gtle_f[:, 128:256], W['trile'][:, :], [], [('gtle',)])
        self.dma(q, self.normw[:, :, :], W['normw'][:, :, :], [], [('normw',)])
        self.dma(q, self.qkw[:, :], W['qkw'][:, :], [], [('qkw',)])
        for i, v in enumerate((EPS, float(np.log(0.125)), 1.0, 0.0, float(-0.5 * np.log(128.0)))):
            self.s.op('pool', (lambda e, i=i, v=v: e.memset(self.cc[:, i:i + 1], v)), [], [('cc', i)])
        with ExitStack() as es:
            tmp = self.sb("ctmp", [128, 128], F32, es)
            self.dma(q, tmp[:, :], W['blk'][:, :], [], [('ctmp',)])
            self.copy('dve', self.blk_b[:, :], tmp[:, :], [('ctmp',)], [('blk_b',)])
            self.copy('dve', self.ones_b[:, :], self.ones_f[:, :], [('ones',)], [('ones_b',)])
            self.copy('dve', self.ident_b[:, :], self.ident_f[:, :], [('ident',)], [('ident_b',)])
            self.ts('dve', r32(self.uinclneg_f[:, :]), self.uincl_f[:, :], -1.0, None, ALU.mult, None,
                    [('uincl',)], [('uinclneg',)])
            self.copy('dve', r32(self.ones_r[:, :]), self.ones_f[:, :], [('ones',)], [('ones_r',)])
            self.copy('dve', r32(self.trile_r[:, :]), self.trile_f[:, :], [('trile',)], [('trile_r',)])
            self.s.flush()

    def stage_load(self, sq):
        X = self.X
        with ExitStack() as es:
            stg = [self.sb(f"ldstg{i}", [128, D], F32, es) for i in range(2)]
            for tt in range(16):
                st = stg[tt % 2]
                sr = ('ldstg', tt % 2)
                self.dma('sp', st[:, :], self.x_d[sq, tt * 128:(tt + 1) * 128, :], [], [sr])
                for half in range(2):
                    ps, pr = self.next_ps()
                    for j in range(4):
                        c = half * 4 + j
                        self.tr(ps[:, j * 128:(j + 1) * 128], st[:, c * 128:(c + 1) * 128], self.ident_f[:, :],
                                [sr, ('ident',)], [pr])
                    dst = X[:, half * 4:half * 4 + 4, tt * 128:(tt + 1) * 128]
                    src = ps[:, :].rearrange("p (c t) -> p c t", c=4)
                    wr = [('X', half * 4 + j, tt) for j in range(4)]
                    self.copy('act' if half == 0 else 'dve', dst, src, [pr], wr)
            self.s.flush()

    def stage_store(self, sq):
        X = self.X
        with ExitStack() as es:
            stg = [self.sb(f"ststg{i}", [128, D], F32, es) for i in range(2)]
            for tt in range(16):
                st = stg[tt % 2]
                sr = ('ststg', tt % 2)
                for half in range(2):
                    ps, pr = self.next_ps()
                    for j in range(4):
                        c = half * 4 + j
                        self.tr(ps[:, j * 128:(j + 1) * 128], X[:, c, tt * 128:(tt + 1) * 128], self.ident_f[:, :],
                                [('X', c, tt), ('ident',)], [pr])
                    self.copy('act' if half == 0 else 'dve', st[:, half * 512:(half + 1) * 512], ps[:, :], [pr], [sr])
                self.dma('sp', self.out_d[sq, tt * 128:(tt + 1) * 128, :], st[:, :], [sr], [('out', sq, tt)])
            self.s.flush()

    def rmsnorm(self, widx):
        with ExitStack() as es:
            self._rmsnorm(widx, es)
            self.s.flush()

    def _rmsnorm(self, widx, es):
        X, H = self.X, self.H
        sq = [self.sb(f"rn_sq{i}", [128, 8, 512], BF16, es) for i in range(2)]
        lnv = [self.sb(f"rn_ln{i}", [128, 512], F32, es) for i in range(2)]
        rstd = [self.sb(f"rn_rs{i}", [128, 512], F32, es) for i in range(2)]
        for tb in range(4):
            b = tb % 2
            t0, t1 = tb * 512, (tb + 1) * 512
            for c in range(8):
                if c % 2 == 0:
                    self.act(sq[b][:, c, :], X[:, c, t0:t1], AF.Square, XR(c, t0, t1), [('rn_sq', b, c)])
                else:
                    self.tt('pool', sq[b][:, c, :], X[:, c, t0:t1], X[:, c, t0:t1], ALU.mult,
                            XR(c, t0, t1), [('rn_sq', b, c)])
            ps, pr = self.next_ps()
            self.mm(ps[:, :], [(self.ones_b[:, :], sq[b][:, c, :]) for c in range(8)],
                    [('rn_sq', b, c) for c in range(8)] + [('ones_b',)], [pr])
            self.act(lnv[b][:, :], ps[:, :], AF.Ln, [pr], [('rn_ln', b)], bias=self.cc[:, 0:1], scale=1.0 / D)
            self.act(rstd[b][:, :], lnv[b][:, :], AF.Exp, [('rn_ln', b)], [('rn_rs', b)], scale=-0.5)
            for c in range(8):
                self.stt(H[:, c, t0:t1], X[:, c, t0:t1], self.normw[:, widx, c:c + 1], rstd[b][:, :],
                         ALU.mult, ALU.mult, XR(c, t0, t1) + [('rn_rs', b), ('normw',)], HR(c, t0, t1))

    def stage_mlp(self, layer):
        X, H = self.X, self.H
        W1 = self.W['mlp_w1']
        W2 = self.W['mlp_w2']
        self.rmsnorm(1 + 2 * layer)
        with ExitStack() as es:
            w1 = [self.sb(f"w1_{i}", [128, 8, 512], BF16, es) for i in range(2)]
            w2 = [self.sb(f"w2_{i}", [128, 4, D], BF16, es) for i in range(2)]
            A = [self.sb(f"mlpA{i}", [128, 4, L], BF16, es) for i in range(2)]
            R = [self.sb(f"mlpR{i}", [128, 512], F32, es) for i in range(3)]
            ri = 0
            for e in range(8):
                b = e % 2
                self.dma('pool', w1[b][:, :, :],
                         W1[layer, :, e * 512:(e + 1) * 512].rearrange("(c p) n -> p c n", p=128),
                         [], [('w1', b)])
                self.dma('pool', w2[b][:, :, :],
                         W2[layer, e * 512:(e + 1) * 512, :].rearrange("(c p) n -> p c n", p=128),
                         [], [('w2', b)])
                for tb in range(4):
                    t0, t1 = tb * 512, (tb + 1) * 512
                    for j in range(4):
                        ps, pr = self.next_ps()
                        self.mm(ps[:, :], [(w1[b][:, kc, j * 128:(j + 1) * 128], H[:, kc, t0:t1]) for kc in range(8)],
                                [('w1', b)] + [r for kc in range(8) for r in HR(kc, t0, t1)], [pr])
                        r = R[ri % 3]
                        rr = ('mlpR', ri % 3)
                        ri += 1
                        self.act(r[:, :], ps[:, :], AF.Relu, [pr], [rr])
                        self.tt('pool', A[b][:, j, t0:t1], r[:, :], r[:, :], ALU.mult, [rr], [('mlpA', b, j, tb)])
                for tb in range(4):
                    t0, t1 = tb * 512, (tb + 1) * 512
                    for dt in range(8):
                        ps, pr = self.next_ps()
                        self.mm(ps[:, :], [(w2[b][:, j, dt * 128:(dt + 1) * 128], A[b][:, j, t0:t1]) for j in range(4)],
                                [('w2', b)] + [('mlpA', b, j, tb) for j in range(4)], [pr])
                        self.tt('dve', X[:, dt, t0:t1], ps[:, :], X[:, dt, t0:t1], ALU.add,
                                [pr] + XR(dt, t0, t1), XR(dt, t0, t1))
            self.s.flush()

    def stage_attn(self):
        X, H = self.X, self.H
        Wqkv = self.W['c_w_qkv']
        Wo = self.W['c_w_o']
        cc = self.cc
        self.rmsnorm(2)
        with ExitStack() as es0:
            qT = self.sb("qT", [128, 8, L], BF16, es0)
            kT = self.sb("kT", [128, 8, L], BF16, es0)
            with ExitStack() as es:
                wq = [self.sb(f"wq{i}", [128, 8, 128], BF16, es) for i in range(3)]
                wv = [self.sb(f"wv{i}", [128, 8, 512], BF16, es) for i in range(2)]
                qraw = [self.sb(f"qraw{i}", [128, 512], F32, es) for i in range(2)]
                sqq = [self.sb(f"sqq{i}", [128, 512], BF16, es) for i in range(2)]
                lnv = [self.sb(f"qln{i}", [128, 512], F32, es) for i in range(2)]
                rstd = [self.sb(f"qrs{i}", [128, 512], F32, es) for i in range(2)]
                for half in range(2):
                    self.dma('pool', wv[half][:, :, :],
                             Wqkv[:, 2048 + half * 512:2048 + (half + 1) * 512].rearrange("(c p) n -> p c n", p=128),
                             [], [('wv', half)])
                wi = 0
                bi = 0
                PA = int(os.environ.get("ATT_PA", "3"))
                for c in range(8 if (PA & 1) else 0):
                    for which in range(2):
                        dst = qT if which == 0 else kT
                        dn = 'qT' if which == 0 else 'kT'
                        col0 = which * 1024 + c * 128
                        w = wq[wi % 3]
                        wr = ('wq', wi % 3)
                        wi += 1
                        self.dma('pool', w[:, :, :], Wqkv[:, col0:col0 + 128].rearrange("(c p) n -> p c n", p=128),
                                 [], [wr])
                        for tb in range(4):
                            t0, t1 = tb * 512, (tb + 1) * 512
                            b = bi % 2
                            bi += 1
                            ps, pr = self.next_ps()
                            self.mm(ps[:, :], [(w[:, kc, :], H[:, kc, t0:t1]) for kc in range(8)],
                                    [wr] + [r for kc in range(8) for r in HR(kc, t0, t1)], [pr])
                            self.act(sqq[b][:, :], ps[:, :], AF.Square, [pr], [('sqq', b)])
                            self.copy('dve', qraw[b][:, :], ps[:, :], [pr], [('qraw', b)])
                            ps2, pr2 = self.next_ps()
                            self.mm(ps2[:, :], [(self.blk_b[:, :], sqq[b][:, :])], [('sqq', b), ('blk_b',)], [pr2])
                            self.act(lnv[b][:, :], ps2[:, :], AF.Ln, [pr2], [('qln', b)], bias=cc[:, 0:1], scale=1.0 / 64)
                            self.act(rstd[b][:, :], lnv[b][:, :], AF.Exp, [('qln', b)], [('qrs', b)],
                                     bias=cc[:, 1:2] if which == 0 else cc[:, 3:4], scale=-0.5)
                            self.stt(dst[:, c, t0:t1], qraw[b][:, :], self.qkw[:, which:which + 1], rstd[b][:, :],
                                     ALU.mult, ALU.mult, [('qraw', b), ('qrs', b), ('qkw',)], [(dn, c, tb)])
                for tt in range(16 if (PA & 2) else 0):
                    t0, t1 = tt * 128, (tt + 1) * 128
                    pss = []
                    for half in range(2):
                        ps, pr = self.next_ps()
                        self.mm(ps[:, :], [(H[:, kc, t0:t1], wv[half][:, kc, :]) for kc in range(8)],
                                [('wv', half)] + [('H', kc, tt) for kc in range(8)], [pr])
                        pss.append((ps, pr))
                    for half in range(2):
                        ps, pr = pss[half]
                        self.copy('act' if half == 0 else 'dve', H[:, half * 4:half * 4 + 4, t0:t1],
                                  ps[:, :].rearrange("p (c t) -> p c t", c=4), [pr],
                                  [('H', half * 4 + j, tt) for j in range(4)])
                self.s.flush()
            with ExitStack() as es:
                eb = [self.sb(f"at_e{i}", [128, 512], F32, es) for i in range(3)]
                spb = [self.sb(f"at_sp{i}", [128, 512], F32, es) for i in range(3)]
                tmpb = [self.sb(f"at_tmp{i}", [128, 512], F32, es) for i in range(2)]
                Rs = [self.sb(f"at_rs{i}", [128, 512], F32, es) for i in range(2)]
                ATb = [self.sb(f"at_A{i}", [128, 512], BF16, es) for i in range(3)]
                qpad = [[self.sb(f"qpad{par}{j}", [128, 512], BF16, es) for j in range(2)] for par in range(2)]
                for par in range(2):
                    for j in range(2):
                        self.s.op('pool', (lambda e, t=qpad[par][j]: e.memset(t[:, :], 0.0)), [], [('qpad', par, j)])
                self.mask4 = self.sb("mask4", [128, 4, 512], F32, es)
                self.dma('sp', self.mask4[:, :, :], self.W['mask4'][:, :, :], [], [('mask4',)])
                it_g = 0
                ai = 0
                pairs = []
                gi = 0
                for h in range(16):
                    for G in range(4):
                        kmax = 4 * G + 3
                        for kb in range(kmax, -1, -1):
                            pairs.append(dict(h=h, G=G, kb=kb, first=(kb == kmax), last=(kb == 0), diag=(kb >= 4 * G),
                                              gi=gi, i=len(pairs)))
                        gi += 1
                rstate = {'cur': 0}

                def opnd(p):
                    h, G, kb = p['h'], p['G'], p['kb']
                    c = h // 2
                    b0 = (h % 2) * 64
                    par, j = h % 2, p['gi'] % 2
                    q_s = qpad[par][j][:, :]
                    k_s = kT[:, c, kb * 128:(kb + 1) * 128]
                    return c, b0, q_s, k_s, ('qpad', par, j), ('kT', c, kb // 4)

                def stA(p):
                    c, b0, q_s, k_s, qr, kr = opnd(p)
                    b = p['i'] % 3
                    if p['first']:
                        G = p['G']
                        self.copy('dve', q_s[b0:b0 + 64, :], qT[b0:b0 + 64, c, G * 512:(G + 1) * 512], [('qT', c, G)], [qr])
                    ps_z, pzr = self.next_ps()
                    self.mm(ps_z[:, :], [(k_s, q_s)], [kr, qr], [pzr])
                    self.act(eb[b][:, :], ps_z[:, :], AF.Exp, [pzr], [('at_e', b)])
                    self.act(r32(spb[b][:, :]), eb[b][:, :], AF.Ln, [('at_e', b)], [('at_sp', b)], bias=cc[:, 2:3])
                    if p['diag']:
                        self.tt('pool', r32(spb[b][:, :]), spb[b][:, :], self.mask4[:, p['kb'] - 4 * p['G'], :], ALU.mult,
                                [('at_sp', b), ('mask4',)], [('at_sp', b)])

                def stB(p):
                    c, b0, q_s, k_s, qr, kr = opnd(p)
                    b = p['i'] % 3
                    b2 = p['i'] % 2
                    ps_n, pnr = self.next_ps()
                    self.mm(ps_n[:, :], [(k_s, q_s), (r32(self.uinclneg_f[:, :]), r32(spb[b][:, :]))],
                            [kr, qr, ('at_sp', b), ('uinclneg',)], [pnr])
                    if p['first']:
                        self.act(ATb[b][:, :], ps_n[:, :], AF.Exp, [pnr], [('at_A', b)])
                    else:
                        rcur = rstate['cur']
                        self.tt('dve', tmpb[b2][:, :], ps_n[:, :], Rs[rcur][:, :], ALU.subtract,
                                [pnr, ('at_rs', rcur)], [('at_tmp', b2)])
                        self.act(ATb[b][:, :], tmpb[b2][:, :], AF.Exp, [('at_tmp', b2)], [('at_A', b)])
                    if p['diag']:
                        self.tt('pool', ATb[b][:, :], ATb[b][:, :], self.mask4[:, p['kb'] - 4 * p['G'], :], ALU.mult,
                                [('at_A', b), ('mask4',)], [('at_A', b)])
                    if not p['last']:
                        ps_r, prr = self.next_ps()
                        self.mm(ps_r[:, :], [(r32(self.ones_r[:, :]), r32(spb[b][:, :]))], [('at_sp', b), ('ones_r',)], [prr])
                        rcur = rstate['cur']
                        nxt = 1 - rcur
                        if p['first']:
                            self.copy('dve', Rs[nxt][:, :], ps_r[:, :], [prr], [('at_rs', nxt)])
                        else:
                            self.tt('dve', Rs[nxt][:, :], ps_r[:, :], Rs[rcur][:, :], ALU.add,
                                    [prr, ('at_rs', rcur)], [('at_rs', nxt)])
                        rstate['cur'] = nxt

                def stC(p):
                    c, b0, q_s, k_s, qr, kr = opnd(p)
                    h, G, kb = p['h'], p['G'], p['kb']
                    b = p['i'] % 3
                    acc = self.psacc[p['gi'] % 2]
                    accr = ('psacc', p['gi'] % 2)
                    v_s = H[:, c, kb * 128:(kb + 1) * 128]
                    self.mm1(acc[:, :], v_s, ATb[b][:, :], p['first'], p['last'],
                             [('H', c, kb), ('at_A', b)], [accr])
                    if p['last']:
                        self.copy('act', qT[b0:b0 + 64, c, G * 512:(G + 1) * 512], acc[b0:b0 + 64, :], [accr], [('qT', c, G)])

                n = len(pairs)
                for t in range(n + 2):
                    if t < n:
                        stA(pairs[t])
                    if 0 <= t - 1 < n:
                        stB(pairs[t - 1])
                    if 0 <= t - 2 < n:
                        stC(pairs[t - 2])
                self.s.flush()
            with ExitStack() as es:
                wo = [self.sb(f"wo{i}", [128, 8, 128], BF16, es) for i in range(2)]
                for dt in range(8):
                    w = wo[dt % 2]
                    wr = ('wo', dt % 2)
                    self.dma('pool', w[:, :, :], Wo[:, dt * 128:(dt + 1) * 128].rearrange("(c p) n -> p c n", p=128),
                             [], [wr])
                    for tb in range(4):
                        t0, t1 = tb * 512, (tb + 1) * 512
                        ps, pr = self.next_ps()
                        self.mm(ps[:, :], [(w[:, c, :], qT[:, c, t0:t1]) for c in range(8)],
                                [wr] + [('qT', c, tb) for c in range(8)], [pr])
                        self.tt('dve', X[:, dt, t0:t1], ps[:, :], X[:, dt, t0:t1], ALU.add,
                                [pr] + XR(dt, t0, t1), XR(dt, t0, t1))
                self.s.flush()

    def ring(self, name, n, shape, dtype, es):
        tiles = [self.sb(f"{name}{i}", shape, dtype, es) for i in range(n)]
        state = {'i': 0}

        def nxt():
            i = state['i'] % n
            state['i'] += 1
            return tiles[i], (name, i)
        nxt.tiles = tiles
        return nxt

    def conv_proj(self, col0, R):
        H = self.H
        Win = self.W['a_w_in']
        w, wr = R['w']()
        self.dma('pool', w[:, :, :], Win[:, col0:col0 + 128].rearrange("(c p) n -> p c n", p=128), [], [wr])
        raw, rr = R['raw']()
        for tb in range(4):
            t0, t1 = tb * 512, (tb + 1) * 512
            ps, pr = self.next_ps()
            self.mm(ps[:, :], [(w[:, kc, :], H[:, kc, t0:t1]) for kc in range(8)],
                    [wr] + [r for kc in range(8) for r in HR(kc, t0, t1)], [pr])
            self.copy('act', raw[:, 3 + t0:3 + t1], ps[:, :], [pr], [rr])
        return raw, rr

    def conv_act(self, raw, rr, cw, cb, R, silu_out=None, silu_reg=None):
        acc, ar = R['acc']()
        if cb is not None:
            self.ts('dve', acc[:, :], raw[:, 3:3 + L], cw[:, 3:4], cb, ALU.mult, ALU.add, [rr, ('convw',)], [ar])
        else:
            self.ts('dve', acc[:, :], raw[:, 3:3 + L], cw[:, 3:4], None, ALU.mult, None, [rr, ('convw',)], [ar])
        for k in (2, 1, 0):
            self.stt(acc[:, :], raw[:, k:k + L], cw[:, k:k + 1], acc[:, :], ALU.mult, ALU.add, [rr, ar, ('convw',)], [ar])
        if silu_out is not None:
            self.act(silu_out, acc[:, :], AF.Silu, [ar], [silu_reg])
            return silu_out, silu_reg
        self.act(acc[:, :], acc[:, :], AF.Silu, [ar], [ar])
        return acc, ar

    def conv_pipeline(self, tiles, R):
        nxt = self.conv_proj(tiles[0]['col'], R)
        for j, t in enumerate(tiles):
            cur = nxt
            if j + 1 < len(tiles):
                nxt = self.conv_proj(tiles[j + 1]['col'], R)
            out, outr = self.conv_act(cur[0], cur[1], t['cw'], t['cb'], R, t.get('silu_out'), t.get('silu_reg'))
            if t.get('post'):
                t['post'](out, outr)

    def to_tok(self, src, sr, dst, dr_name, width_off, src_regs=None):
        for q in range(4):
            ps, pr = self.next_ps()
            psb = ps[:, :].bitcast(BF16)
            for j in range(4):
                tt = q * 4 + j
                self.tr(psb[:, j * 128:(j + 1) * 128], src[:, tt * 128:(tt + 1) * 128], self.ident_b[:, :],
                        (src_regs if src_regs is not None else [sr]) + [('ident_b',)], [pr])
            self.copy('act' if q % 2 == 0 else 'dve', dst[:, q * 4:q * 4 + 4, width_off:width_off + 128],
                      psb[:, 0:512].rearrange("p (c t) -> p c t", c=4), [pr],
                      [(dr_name, q * 4 + j) for j in range(4)])

    def stage_mix0(self):
        X, H = self.X, self.H
        W = self.W
        cc = self.cc
        Win = W['a_w_in']
        Wout = W['a_w_out']
        self.rmsnorm(0)
        with ExitStack() as es0:
            sb = lambda n, sh, dt=F32: self.sb(n, sh, dt, es0)
            convw_s = sb("convw_s", [128, 12, 4])
            convb_s = sb("convb_s", [128, 12])
            convw_g = sb("convw_g", [128, 24, 4])
            dsk = sb("dsk", [128, 16])
            gnw = sb("gnw", [128, 128])
            spl = sb("spl", [128, 16, 32])
            ag = sb("ag", [128, 16, 32])
            ea = sb("ea", [128, 16, 32])
            dte = sb("dte", [128, 16, 32])
            cdec = sb("cdec", [128, 16, 32])
            bet = sb("bet", [128, 16, 8])
            nbet = sb("nbet", [128, 16, 8])
            bg = sb("bg", [128, 16, 8])
            self.dma('sp', convw_s[:, :, :], W['convw_s'][:, :, :], [], [('convw',)])
            self.dma('sp', convb_s[:, :], W['convb_s'][:, :], [], [('convw',)])
            self.dma('sp', convw_g[:, :, :], W['convw_g'][:, :, :], [], [('convw',)])
            self.dma('sp', dsk[:, :], W['dsk'][:, :], [], [('dsk',)])
            self.dma('sp', gnw[:, :], W['gnw'][:, :], [], [('gnw',)])
            with ExitStack() as es:
                wsm = self.sb("wsm", [128, 8, 32], BF16, es)
                bias_bc = self.sb("bias_bc", [128, 32], F32, es)
                alog = self.sb("alog", [128, 32], F32, es)
                aneg = self.sb("aneg", [128, 32], F32, es)
                smraw = self.sb("smraw", [128, 16, 32], F32, es)
                e1 = self.sb("sm_e", [128, 16, 32], F32, es)
                acum = self.sb("acum", [128, 16, 32], F32, es)
                tot = self.sb("tot", [128, 16, 32], F32, es)
                tmp = self.sb("sm_tmp", [128, 16, 32], F32, es)
                self.dma('pool', wsm[:, :, 0:16], Win[:, 2560:2576].rearrange("(c p) n -> p c n", p=128), [], [('wsm', 0)])
                self.dma('pool', wsm[:, :, 16:32], Win[:, 6672:6688].rearrange("(c p) n -> p c n", p=128), [], [('wsm', 1)])
                self.dma('sp', bias_bc[:, :], W['bias_bc'][:, :], [], [('bias_bc',)])
                self.dma('sp', alog[:, :], W['alog_bc'][:, :], [], [('alog',)])
                self.act(aneg[:, :], alog[:, :], AF.Exp, [('alog',)], [('aneg',)])
                self.ts('dve', aneg[:, :], aneg[:, :], -1.0, None, ALU.mult, None, [('aneg',)], [('aneg',)])
                for tt in range(16):
                    ps, pr = self.next_ps()
                    self.mm(ps[:, 0:32], [(H[:, kc, tt * 128:(tt + 1) * 128], wsm[:, kc, :]) for kc in range(8)],
                            [('wsm', 0), ('wsm', 1)] + [('H', kc, tt) for kc in range(8)], [pr])
                    self.tt('dve', smraw[:, tt, :], ps[:, 0:32], bias_bc[:, :], ALU.add, [pr, ('bias_bc',)], [('smraw',)])
                f2 = lambda t: t[:, :, :].rearrange("p a b -> p (a b)")
                self.act(f2(e1), f2(smraw), AF.Exp, [('smraw',)], [('sm_e',)])
                self.act(f2(spl), f2(e1), AF.Ln, [('sm_e',)], [('spl',)], bias=cc[:, 2:3])
                self.ts('dve', tmp[:, :, 16:24], e1[:, :, 16:24], 1.0, None, ALU.add, None, [('sm_e',)], [('sm_tmp',)])
                self.s.op('dve', lambda e: e.reciprocal(tmp[:, :, 16:24], tmp[:, :, 16:24]), [('sm_tmp',)], [('sm_tmp',)])
                self.tt('dve', bet[:, :, :], e1[:, :, 16:24], tmp[:, :, 16:24], ALU.mult, [('sm_e',), ('sm_tmp',)], [('bet',)])
                self.ts('dve', nbet[:, :, :], bet[:, :, :], -1.0, None, ALU.mult, None, [('bet',)], [('nbet',)])
                self.tt('dve', ag[:, :, :], spl[:, :, :], aneg[:, :].unsqueeze(1).broadcast_to([128, 16, 32]), ALU.mult,
                        [('spl',), ('aneg',)], [('ag',)])
                ps, pr = self.next_ps()
                self.mm(ps[:, :], [(self.trile_f[:, :], f2(ag))], [('ag',), ('trile',)], [pr])
                self.copy('dve', f2(acum), ps[:, :], [pr], [('acum',)])
                self.act(f2(ea), ps[:, :], AF.Exp, [pr], [('ea',)])
                ps2, pr2 = self.next_ps()
                self.mm(ps2[:, :], [(self.ones_f[:, :], f2(ag))], [('ag',), ('ones',)], [pr2])
                self.act(f2(cdec), ps2[:, :], AF.Exp, [pr2], [('cdec',)])
                self.tt('dve', f2(tot), ps2[:, :], f2(acum), ALU.subtract, [pr2, ('acum',)], [('tot',)])
                self.act(f2(dte), f2(tot), AF.Exp, [('tot',)], [('dte',)])
                self.tt('dve', bg[:, :, :], bet[:, :, :], ea[:, :, 24:32], ALU.mult, [('bet',), ('ea',)], [('bg',)])
                self.s.flush()
            P = dict(convw_s=convw_s, convb_s=convb_s, convw_g=convw_g, dsk=dsk, gnw=gnw, spl=spl, ag=ag,
                     ea=ea, dte=dte, cdec=cdec, bet=bet, nbet=nbet, bg=bg)
            if os.environ.get("MIX_STOP") == "prelude":
                return
            MIXSEL = os.environ.get("MIX_SEL", "ssd,gdn").split(",")
            if 'ssd' in MIXSEL:
                for g in range(2):
                    self.ssd_group(g, P)
            if 'gdn' in MIXSEL:
                for hg in range(2):
                    self.gdn_group(hg, P)

    def ssd_group(self, g, P):
        X, H = self.X, self.H
        Win = self.W['a_w_in']
        Wout = self.W['a_w_out']
        cc = self.cc
        with ExitStack() as es0:
            xs_tok = self.sb("xs_tok", [128, 16, 512], BF16, es0)
            siluz = self.sb("siluz", [128, 16, 512], BF16, es0)
            BT = self.sb("BT", [128, L], BF16, es0)
            CT = self.sb("CT", [128, L], BF16, es0)
            B_tok = self.sb("B_tok", [128, 16, 128], BF16, es0)
            with ExitStack() as es:
                R = dict(w=self.ring("cw", 3, [128, 8, 128], BF16, es),
                         raw=self.ring("craw", 2, [128, L + 3], F32, es),
                         acc=self.ring("cacc", 1, [128, L], F32, es))
                xsb = self.ring("xsb", 2, [128, L], BF16, es)
                wz = self.sb("wz", [128, 8, 512], BF16, es)
                self.dma('pool', wz[:, :, :], Win[:, g * 512:(g + 1) * 512].rearrange("(c p) n -> p c n", p=128), [], [('wz',)])
                self._zero_halo(R, es)
                tiles = []
                for j in range(4):
                    ti = g * 4 + j
                    xb, xr = xsb()
                    tiles.append(dict(col=1024 + ti * 128, cw=P['convw_s'][:, ti, :], cb=P['convb_s'][:, ti:ti + 1],
                                      silu_out=xb[:, :], silu_reg=xr,
                                      post=(lambda o, r, xb=xb, xr=xr, j=j: self.to_tok(xb, xr, xs_tok, 'xs_tok', j * 128))))
                ti = 8 + g
                tiles.append(dict(col=1024 + ti * 128, cw=P['convw_s'][:, ti, :], cb=P['convb_s'][:, ti:ti + 1],
                                  silu_out=BT[:, :], silu_reg=('BT',),
                                  post=(lambda o, r: self.to_tok(BT, ('BT',), B_tok, 'B_tok', 0))))
                ti = 10 + g
                tiles.append(dict(col=1024 + ti * 128, cw=P['convw_s'][:, ti, :], cb=P['convb_s'][:, ti:ti + 1],
                                  silu_out=CT[:, :], silu_reg=('CT',)))
                self.conv_pipeline(tiles, R)
                for tt in range(16):
                    ps, pr = self.next_ps()
                    self.mm(ps[:, :], [(H[:, kc, tt * 128:(tt + 1) * 128], wz[:, kc, :]) for kc in range(8)],
                            [('wz',)] + [('H', kc, tt) for kc in range(8)], [pr])
                    self.act(siluz[:, tt, :], ps[:, :], AF.Silu, [pr], [('siluz', tt)])
                self.s.flush()
            if os.environ.get("MIX_STOP") == "ssd1":
                return
            yT = self.sb("yT", [128, 4, L], BF16, es0)
            with ExitStack() as es:
                ring = lambda n, k, sh, dt=F32: self.ring(n, k, sh, dt, es)
                cbm_r = ring("cbm", 2, [128, 128])
                lseg_r = ring("lseg", 2, [128, 4, 128])
                LT_r = ring("LT", 2, [128, 4, 128])
                MT_r = ring("MT", 1, [128, 8, 128], BF16)
                xdt_r = ring("xdt", 2, [128, 512], BF16)
                xdd_r = ring("xdd", 2, [128, 512], BF16)
                t1_r = ring("t1", 1, [128, 512])
                t3_r = ring("t3", 1, [128, 512])
                yg_r = ring("yg", 1, [128, 512])
                junk_r = ring("junk", 1, [128, 512], BF16)
                yn_r = ring("ynb", 2, [128, 512], BF16)
                sm_r = ring("ssm", 2, [128, 4])
                S = self.sb("ssdS", [128, 512], F32, es)
                Sb = self.sb("ssdSb", [128, 512], BF16, es)
                snw = self.sb("snw", [128, 512], F32, es)
                self.dma('sp', snw[:, :], self.W['snw'][:, g * 512:(g + 1) * 512], [], [('snw',)])
                hs = slice(g * 8, (g + 1) * 8)
                v8 = lambda ap: ap.rearrange("p (h d) -> p h d", h=8)
                bc8 = lambda ap: ap.unsqueeze(2).broadcast_to([128, 8, 64])
                for tt in range(16):
                    ts_ = slice(tt * 128, (tt + 1) * 128)
                    ps_cb, pcr = self.next_ps()
                    self.mm(ps_cb[:, 0:128], [(BT[:, ts_], CT[:, ts_])], [('BT',), ('CT',)], [pcr])
                    cbm, cbr = cbm_r()
                    self.tt('dve', cbm[:, :], ps_cb[:, 0:128], self.trile_f[:, :], ALU.mult, [pcr, ('trile',)], [cbr])
                    MT, mr = MT_r()
                    for half in range(2):
                        lseg, lr = lseg_r()
                        ps_s, psr = self.next_ps()
                        for i in range(4):
                            hh = g * 8 + half * 4 + i
                            self.ts('dve', r32(lseg[:, i, :]), self.gt_f[:, :], P['ag'][:, tt, hh:hh + 1], None,
                                    ALU.mult, None, [('gt',), ('ag',)], [(lr, i)])
                            self.mm(ps_s[:, i * 128:(i + 1) * 128], [(r32(lseg[:, i, :]), r32(self.trile_r[:, :]))],
                                    [(lr, i), ('trile_r',)], [psr])
                        LT, ltr = LT_r()
                        self.act(LT[:, :, :].rearrange("p a b -> p (a b)"), ps_s[:, :], AF.Exp, [psr], [ltr])
                        self.tt('dve', MT[:, half * 4:(half + 1) * 4, :], LT[:, :, :],
                                cbm[:, :].unsqueeze(1).broadcast_to([128, 4, 128]), ALU.mult, [ltr, cbr], [(mr, half)])
                    xdt, xdr = xdt_r()
                    self.tt('dve', v8(xdt[:, :]), v8(xs_tok[:, tt, :]), bc8(P['spl'][:, tt, hs]), ALU.mult,
                            [('xs_tok', tt), ('spl',)], [xdr])
                    xdd, xddr = xdd_r()
                    self.tt('pool', v8(xdd[:, :]), v8(xdt[:, :]), bc8(P['dte'][:, tt, hs]), ALU.mult,
                            [xdr, ('dte',)], [xddr])
                    ps_y, pyr = self.next_ps()
                    for i in range(8):
                        self.mm(ps_y[:, i * 64:(i + 1) * 64], [(MT[:, i, :], xdt[:, i * 64:(i + 1) * 64])],
                                [(mr, i // 4), xdr], [pyr])
                    t1, t1r = t1_r()
                    if tt > 0:
                        ps_o, por = self.next_ps()
                        self.mm(ps_o[:, :], [(CT[:, ts_], Sb[:, :])], [('CT',), ('ssdSb',)], [por])
                        self.tt('dve', v8(t1[:, :]), v8(ps_o[:, :]), bc8(P['ea'][:, tt, hs]), ALU.mult, [por, ('ea',)], [t1r])
                        self.tt('dve', t1[:, :], t1[:, :], ps_y[:, :], ALU.add, [t1r, pyr], [t1r])
                    else:
                        self.copy('dve', t1[:, :], ps_y[:, :], [pyr], [t1r])
                    t3, t3r = t3_r()
                    self.tt('pool', v8(t3[:, :]), v8(xs_tok[:, tt, :]), bc8(P['dsk'][:, hs]), ALU.mult,
                            [('xs_tok', tt), ('dsk',)], [t3r])
                    self.tt('dve', t3[:, :], t3[:, :], t1[:, :], ALU.add, [t3r, t1r], [t3r])
                    yg, ygr = yg_r()
                    self.tt('pool', yg[:, :], t3[:, :], siluz[:, tt, :], ALU.mult, [t3r, ('siluz', tt)], [ygr])
                    sm, smr = sm_r()
                    junk, jr = junk_r()
                    self.act(junk[:, :], yg[:, :], AF.Square, [ygr], [jr, (smr, 0)], accum_out=sm[:, 0:1])
                    self.act(sm[:, 1:2], sm[:, 0:1], AF.Ln, [(smr, 0)], [(smr, 1)], bias=cc[:, 0:1], scale=1.0 / 512)
                    self.act(sm[:, 2:3], sm[:, 1:2], AF.Exp, [(smr, 1)], [(smr, 2)], scale=-0.5)
                    yn, ynr = yn_r()
                    self.stt(yn[:, :], yg[:, :], sm[:, 2:3], snw[:, :], ALU.mult, ALU.mult,
                             [ygr, (smr, 2), ('snw',)], [ynr])
                    ps_t, ptr_ = self.next_ps()
                    psb = ps_t[:, :].bitcast(BF16)
                    for j in range(4):
                        self.tr(psb[:, j * 128:(j + 1) * 128], yn[:, j * 128:(j + 1) * 128], self.ident_b[:, :],
                                [ynr, ('ident_b',)], [ptr_])
                    self.copy('act', yT[:, 0:4, ts_], psb[:, 0:512].rearrange("p (c t) -> p c t", c=4), [ptr_],
                              [('yT', j, tt) for j in range(4)])
                    if tt < 15:
                        ps_st, pstr = self.next_ps()
                        self.mm(ps_st[:, :], [(B_tok[:, tt, :], xdd[:, :])], [('B_tok', tt), xddr], [pstr])
                        if tt == 0:
                            self.copy('dve', S[:, :], ps_st[:, :], [pstr], [('ssdS',)])
                        else:
                            self.tt('dve', v8(S[:, :]), v8(S[:, :]), bc8(P['cdec'][:, tt, hs]), ALU.mult,
                                    [('ssdS',), ('cdec',)], [('ssdS',)])
                            self.tt('dve', S[:, :], S[:, :], ps_st[:, :], ALU.add, [('ssdS',), pstr], [('ssdS',)])
                        self.copy('act', Sb[:, :], S[:, :], [('ssdS',)], [('ssdSb',)])
                self.s.flush()
            with ExitStack() as es:
                wo = self.sb("wo_s", [128, 4, D], BF16, es)
                self.dma('pool', wo[:, :, :], Wout[g * 512:(g + 1) * 512, :].rearrange("(c p) n -> p c n", p=128), [], [('wo_s',)])
                for tb in range(4):
                    t0, t1_ = tb * 512, (tb + 1) * 512
                    for dt in range(8):
                        ps, pr = self.next_ps()
                        self.mm(ps[:, :], [(wo[:, j, dt * 128:(dt + 1) * 128], yT[:, j, t0:t1_]) for j in range(4)],
                                [('wo_s',)] + [('yT', j, tt) for j in range(4) for tt in range(tb * 4, tb * 4 + 4)], [pr])
                        self.tt('dve', X[:, dt, t0:t1_], ps[:, :], X[:, dt, t0:t1_], ALU.add,
                                [pr] + XR(dt, t0, t1_), XR(dt, t0, t1_))
                self.s.flush()

    def _zero_halo(self, R, es):
        for i, t in enumerate(R['raw'].tiles):
            self.s.op('pool', (lambda e, t=t: e.memset(t[:, 0:3], 0.0)), [], [("craw", i)])

    def gdn_group(self, hg, P):
        X, H = self.X, self.H
        Win = self.W['a_w_in']
        Wout = self.W['a_w_out']
        cc = self.cc
        QOFF = 2576
        with ExitStack() as es0:
            siluzg = self.sb("siluzg", [128, 16, 512], BF16, es0)
            ogT = self.sb("ogT", [128, 4, L], BF16, es0)
            with ExitStack() as es:
                wzg = self.sb("wzg", [128, 8, 512], BF16, es)
                c0 = 5648 + hg * 512
                self.dma('pool', wzg[:, :, :], Win[:, c0:c0 + 512].rearrange("(c p) n -> p c n", p=128), [], [('wzg',)])
                for tt in range(16):
                    ps, pr = self.next_ps()
                    self.mm(ps[:, :], [(H[:, kc, tt * 128:(tt + 1) * 128], wzg[:, kc, :]) for kc in range(8)],
                            [('wzg',)] + [('H', kc, tt) for kc in range(8)], [pr])
                    self.act(siluzg[:, tt, :], ps[:, :], AF.Silu, [pr], [('siluzg', tt)])
                self.s.flush()
            for i in range(4):
                h = hg * 4 + i
                with ExitStack() as esh:
                    qTn = self.sb("qTn", [128, L], BF16, esh)
                    kTn = self.sb("kTn", [128, L], BF16, esh)
                    k_tok = self.sb("k_tok", [128, 16, 128], BF16, esh)
                    v_tok = self.sb("v_tok", [128, 16, 128], BF16, esh)
                    with ExitStack() as es:
                        R = dict(w=self.ring("cw", 2, [128, 8, 128], BF16, es),
                                 raw=self.ring("craw", 2, [128, L + 3], F32, es),
                                 acc=self.ring("cacc", 2, [128, L], F32, es))
                        kvb = ogT[:, i, :]
                        sqq = self.sb("gsqq", [128, 512], BF16, es)
                        lnv = self.sb("gln", [128, 512], F32, es)
                        rstd = self.sb("grs", [128, 512], F32, es)
                        self._zero_halo(R, es)
                        def l2post(which, dstT, dn):
                            def post(acc, ar):
                                for tb in range(4):
                                    t0, t1 = tb * 512, (tb + 1) * 512
                                    self.act(sqq[:, :], acc[:, t0:t1], AF.Square, [ar], [('gsqq',)])
                                    ps, pr = self.next_ps()
                                    self.mm(ps[:, :], [(self.ones_b[:, :], sqq[:, :])], [('gsqq',), ('ones_b',)], [pr])
                                    self.act(lnv[:, :], ps[:, :], AF.Ln, [pr], [('gln',)], bias=cc[:, 0:1])
                                    self.act(rstd[:, :], lnv[:, :], AF.Exp, [('gln',)], [('grs',)],
                                             bias=cc[:, 4:5] if which == 0 else cc[:, 3:4], scale=-0.5)
                                    self.tt('dve', dstT[:, t0:t1], acc[:, t0:t1], rstd[:, :], ALU.mult, [ar, ('grs',)], [(dn, tb)])
                                if which == 1:
                                    self.to_tok(kTn, None, k_tok, 'k_tok', 0, src_regs=[('kTn', tb) for tb in range(4)])
                            return post
                        tiles = []
                        for which, dstT, dn in ((0, qTn, 'qTn'), (1, kTn, 'kTn')):
                            ti = which * 8 + h
                            tiles.append(dict(col=QOFF + ti * 128, cw=P['convw_g'][:, ti, :], cb=None, post=l2post(which, dstT, dn)))
                        ti = 16 + h
                        tiles.append(dict(col=QOFF + ti * 128, cw=P['convw_g'][:, ti, :], cb=None, silu_out=kvb, silu_reg=('kvb',),
                                          post=(lambda o, r: self.to_tok(kvb, ('kvb',), v_tok, 'v_tok', 0))))
                        self.conv_pipeline(tiles, R)
                        self.s.flush()
                    with ExitStack() as es:
                        ub = self.sb("g_ub", [128, 16, 128], F32, es)
                        wT = self.sb("g_wT", [128, 16, 128], BF16, es)
                        qkT = self.sb("g_qkT", [128, 16, 128], BF16, es)
                        NCTX = int(os.environ.get("GDN_NCTX", "4"))
                        ctxs = []
                        for ci in range(NCTX):
                            t = lambda n, sh=[128, 128], dt=F32: self.sb(f"g_{n}{ci}", sh, dt, es)
                            ctxs.append(dict(gmask=t("gmask"), DLT=t("DLT", [128, 256]), PPa=t("PPa", [128, 256]),
                                             PPb=t("PPb", [128, 256]), Tt=t("Tt"), vb=t("vb"), kb2=t("kb2"), ci=ci))
                        gcol = lambda tt: P['ag'][:, tt, 24 + h:25 + h]
                        GPH = int(os.environ.get("GDN_PH", "3"))

                        def p2_group(grp):
                            tts = tuple(range(grp * NCTX, (grp + 1) * NCTX))
                            rgs = [(lambda n, ci=c['ci']: ('g_' + n, ci)) for c in ctxs]
                            ps1s, ps3s, ps4s = [], [], []
                            for c, tt, rg in zip(ctxs, tts, rgs):
                                self.ts('dve', r32(c['gmask'][:, :]), self.gt_f[:, :], gcol(tt), None, ALU.mult, None,
                                        [('gt',), ('ag',)], [rg('gmask')])
                                self.act(r32(c['vb'][:, :]), v_tok[:, tt, :], AF.Identity, [('v_tok', tt), ('bet',)], [rg('vb')],
                                         scale=P['bet'][:, tt, h:h + 1])
                                self.act(r32(c['kb2'][:, :]), k_tok[:, tt, :], AF.Identity, [('k_tok', tt), ('bg',)], [rg('kb2')],
                                         scale=P['bg'][:, tt, h:h + 1])
                            for c, tt, rg in zip(ctxs, tts, rgs):
                                ts_ = slice(tt * 128, (tt + 1) * 128)
                                ps1, p1r = self.next_ps()
                                self.mm(ps1[:, 0:128], [(r32(self.trile_r[:, :]), r32(c['gmask'][:, :]))], [rg('gmask'), ('trile_r',)], [p1r])
                                self.mm(ps1[:, 128:256], [(r32(c['gmask'][:, :]), r32(self.trile_r[:, :]))], [rg('gmask'), ('trile_r',)], [p1r])
                                self.mm(ps1[:, 256:384], [(kTn[:, ts_], kTn[:, ts_])], [('kTn', tt // 4)], [p1r])
                                self.mm(ps1[:, 384:512], [(kTn[:, ts_], qTn[:, ts_])], [('kTn', tt // 4), ('qTn', tt // 4)], [p1r])
                                ps1s.append((ps1, p1r))
                            yield
                            for c, tt, rg, (ps1, p1r) in zip(ctxs, tts, rgs, ps1s):
                                self.act(c['DLT'][:, :], ps1[:, 0:256], AF.Exp, [p1r], [rg('DLT')])
                                self.tt('dve', c['DLT'][:, :], c['DLT'][:, :], self.gtle_f[:, :], ALU.mult,
                                        [rg('DLT'), ('gtle',)], [rg('DLT')])
                            for c, tt, rg, (ps1, p1r) in zip(ctxs, tts, rgs, ps1s):
                                self.stt(r32(c['PPa'][:, 0:128]), ps1[:, 256:384], P['nbet'][:, tt, h:h + 1], c['DLT'][:, 0:128],
                                         ALU.mult, ALU.mult, [p1r, ('nbet',), rg('DLT')], [rg('PPa')])
                                self.tt('dve', qkT[:, tt, :], ps1[:, 384:512], c['DLT'][:, 128:256], ALU.mult,
                                        [p1r, rg('DLT')], [('g_qkT', tt)])
                            yield
                            for c, tt, rg in zip(ctxs, tts, rgs):
                                ps4, p4r = self.next_ps()
                                self.tr(ps4[:, 0:128], c['PPa'][:, 0:128], self.ident_f[:, :], [rg('PPa'), ('ident',)], [p4r])
                                ps4s.append((ps4, p4r))
                            for c, tt, rg, (ps4, p4r) in zip(ctxs, tts, rgs, ps4s):
                                self.copy('act', r32(c['PPa'][:, 128:256]), ps4[:, 0:128], [p4r], [rg('PPa')])
                                self.tt('dve', r32(c['Tt'][:, :]), c['PPa'][:, 128:256], self.ident_f[:, :], ALU.add,
                                        [rg('PPa'), ('ident',)], [rg('Tt')])
                            cur, nxt = 'PPa', 'PPb'
                            for lvl in range(1, 7):
                                psas = []
                                for c, rg in zip(ctxs, rgs):
                                    psa, par = self.next_ps()
                                    pp = c[cur]
                                    self.mm(psa[:, 0:128], [(r32(pp[:, 128:256]), r32(pp[:, 0:128]))], [rg(cur)], [par])
                                    if lvl < 6:
                                        self.mm(psa[:, 128:256], [(r32(pp[:, 0:128]), r32(pp[:, 128:256]))], [rg(cur)], [par])
                                    psas.append((psa, par))
                                for c, rg, (psa, par) in zip(ctxs, rgs, psas):
                                    w = 256 if lvl < 6 else 128
                                    self.copy('act', r32(c[nxt][:, 0:w]), psa[:, 0:w], [par], [rg(nxt)])
                                yield
                                psbs = []
                                for c, rg in zip(ctxs, rgs):
                                    psb_, pbr = self.next_ps()
                                    self.mm(psb_[:, 0:128], [(r32(c[nxt][:, 0:128]), r32(c['Tt'][:, :]))], [rg(nxt), rg('Tt')], [pbr])
                                    psbs.append((psb_, pbr))
                                for c, rg, (psb_, pbr) in zip(ctxs, rgs, psbs):
                                    self.tt('dve', r32(c['Tt'][:, :]), c['Tt'][:, :], psb_[:, 0:128], ALU.add, [rg('Tt'), pbr], [rg('Tt')])
                                cur, nxt = nxt, cur
                                yield
                            psus = []
                            for c, tt, rg in zip(ctxs, tts, rgs):
                                psu, pur = self.next_ps()
                                self.mm(psu[:, 0:128], [(r32(c['Tt'][:, :]), r32(c['vb'][:, :]))], [rg('Tt'), rg('vb')], [pur])
                                self.mm(psu[:, 128:256], [(r32(c['kb2'][:, :]), r32(c['Tt'][:, :]))], [rg('Tt'), rg('kb2')], [pur])
                                psus.append((psu, pur))
                            for c, tt, rg, (psu, pur) in zip(ctxs, tts, rgs, psus):
                                self.copy('act', ub[:, tt, :], psu[:, 0:128], [pur], [('g_ub', tt)])
                                self.copy('dve', wT[:, tt, :], psu[:, 128:256], [pur], [('g_wT', tt)])
                        S = self.sb("g_S", [128, 128], F32, es)
                        Sb = self.sb("g_Sb", [128, 128], BF16, es)
                        u_r = self.ring("g_u", 2, [128, 128], BF16, es)
                        kd_r = self.ring("g_kd", 2, [128, 128], BF16, es)
                        t_r = self.ring("g_t", 2, [128, 128], F32, es)
                        o_r = self.ring("g_o", 2, [128, 128], F32, es)
                        on_r = self.ring("g_on", 2, [128, 128], F32, es)
                        og_r = self.ring("g_og", 2, [128, 128], BF16, es)
                        jk_r = self.ring("g_jk", 1, [128, 128], BF16, es)
                        sm_r = self.ring("g_sm", 2, [128, 4], F32, es)
                        sst = {}

                        def scan_a(tt):
                            ts_ = slice(tt * 128, (tt + 1) * 128)
                            u, ur = u_r()
                            d = sst[tt] = dict(u=u, ur=ur)
                            if tt > 0:
                                ps_ws, pwr = self.next_ps()
                                self.mm(ps_ws[:, 0:128], [(wT[:, tt, :], Sb[:, :])], [('g_wT', tt), ('g_Sb',)], [pwr])
                                self.tt('dve', u[:, :], ub[:, tt, :], ps_ws[:, 0:128], ALU.subtract, [('g_ub', tt), pwr], [ur])
                            else:
                                self.copy('dve', u[:, :], ub[:, tt, :], [('g_ub', tt)], [ur])

                        def scan_b(tt):
                            ts_ = slice(tt * 128, (tt + 1) * 128)
                            d = sst.pop(tt)
                            u, ur = d['u'], d['ur']
                            if tt > 0:
                                ps_o1, po1r = self.next_ps()
                                self.mm(ps_o1[:, 0:128], [(qTn[:, ts_], Sb[:, :])], [('qTn', tt // 4), ('g_Sb',)], [po1r])
                            ps_o2, po2r = self.next_ps()
                            self.mm(ps_o2[:, 0:128], [(qkT[:, tt, :], u[:, :])], [('g_qkT', tt), ur], [po2r])
                            o, orr = o_r()
                            if tt > 0:
                                t, tr_ = t_r()
                                self.act(t[:, :], ps_o1[:, 0:128], AF.Identity, [po1r, ('ea',)], [tr_],
                                         scale=P['ea'][:, tt, 24 + h:25 + h])
                                self.tt('dve', o[:, :], t[:, :], ps_o2[:, 0:128], ALU.add, [tr_, po2r], [orr])
                            else:
                                self.copy('dve', o[:, :], ps_o2[:, 0:128], [po2r], [orr])
                            if tt < 15:
                                kd, kdr = kd_r()
                                self.act(kd[:, :], k_tok[:, tt, :], AF.Identity, [('k_tok', tt), ('dte',)], [kdr],
                                         scale=P['dte'][:, tt, 24 + h:25 + h])
                                ps_sk, pskr = self.next_ps()
                                self.mm(ps_sk[:, 0:128], [(kd[:, :], u[:, :])], [kdr, ur], [pskr])
                                if tt == 0:
                                    self.copy('dve', S[:, :], ps_sk[:, 0:128], [pskr], [('g_S',)])
                                else:
                                    self.stt(S[:, :], S[:, :], P['cdec'][:, tt, 24 + h:25 + h], ps_sk[:, 0:128], ALU.mult, ALU.add,
                                             [('g_S',), ('cdec',), pskr], [('g_S',)])
                                self.copy('act', Sb[:, :], S[:, :], [('g_S',)], [('g_Sb',)])
                            sm, smr = sm_r()
                            jk, jr = jk_r()
                            self.act(jk[:, :], o[:, :], AF.Square, [orr], [jr, (smr, 0)], accum_out=sm[:, 0:1])
                            self.act(sm[:, 1:2], sm[:, 0:1], AF.Ln, [(smr, 0)], [(smr, 1)], bias=cc[:, 0:1], scale=1.0 / 128)
                            self.act(sm[:, 2:3], sm[:, 1:2], AF.Exp, [(smr, 1)], [(smr, 2)], scale=-0.5)
                            on, onr = on_r()
                            self.stt(on[:, :], o[:, :], sm[:, 2:3], P['gnw'][:, :], ALU.mult, ALU.mult, [orr, (smr, 2), ('gnw',)], [onr])
                            og, ogr = og_r()
                            self.tt('dve', og[:, :], on[:, :], siluzg[:, tt, i * 128:(i + 1) * 128], ALU.mult,
                                    [onr, ('siluzg', tt)], [ogr])
                            ps_t, ptr_ = self.next_ps()
                            psb = ps_t[:, :].bitcast(BF16)
                            self.tr(psb[:, 0:128], og[:, :], self.ident_b[:, :], [ogr, ('ident_b',)], [ptr_])
                            self.copy('act', ogT[:, i, ts_], psb[:, 0:128], [ptr_], [('ogT', i, tt)])

                        queue = []
                        for grp in range(16 // NCTX if GPH >= 2 else 0):
                            for _ in p2_group(grp):
                                if queue:
                                    queue.pop(0)()
                            if GPH >= 3:
                                for tt in range(grp * NCTX, (grp + 1) * NCTX):
                                    queue.append(lambda tt=tt: scan_a(tt))
                                    queue.append(lambda tt=tt: scan_b(tt))
                        while queue:
                            queue.pop(0)()
                        self.s.flush()
            with ExitStack() as es:
                wo = self.sb("wo_g", [128, 4, D], BF16, es)
                r0 = 1024 + hg * 512
                self.dma('pool', wo[:, :, :], Wout[r0:r0 + 512, :].rearrange("(c p) n -> p c n", p=128), [], [('wo_g',)])
                for tb in range(4):
                    t0, t1_ = tb * 512, (tb + 1) * 512
                    for dt in range(8):
                        ps, pr = self.next_ps()
                        self.mm(ps[:, :], [(wo[:, j, dt * 128:(dt + 1) * 128], ogT[:, j, t0:t1_]) for j in range(4)],
                                [('wo_g',)] + [('ogT', j, tt) for j in range(4) for tt in range(tb * 4, tb * 4 + 4)], [pr])
                        self.tt('dve', X[:, dt, t0:t1_], ps[:, :], X[:, dt, t0:t1_], ALU.add,
                                [pr] + XR(dt, t0, t1_), XR(dt, t0, t1_))
                self.s.flush()


_CACHE = {}


def consts():
    i = np.arange(128)
    c = {}
    c['c_ident'] = np.eye(128, dtype=np.float32)
    c['c_ones'] = np.ones((128, 128), np.float32)
    c['c_uincl'] = (i[:, None] >= i[None, :]).astype(np.float32)
    c['c_trile'] = (i[:, None] <= i[None, :]).astype(np.float32)
    c['c_gt'] = (i[:, None] > i[None, :]).astype(np.float32)
    blk = np.zeros((128, 128), np.float32)
    blk[:64, :64] = 1
    blk[64:, 64:] = 1
    c['c_blk'] = blk
    col = np.arange(512)
    c['c_mask4'] = np.stack([(col[None, :] > (i[:, None] + 128 * k)) for k in range(4)], axis=1).astype(np.float32)
    return c


def fm_cols(v):
    return np.ascontiguousarray(np.asarray(v, np.float32).reshape(8, 128).T)


def make_inputs(inp, nseq, ncores):
    f = lambda a: np.ascontiguousarray(np.asarray(a, dtype=np.float32))
    shared = consts()
    normw = np.stack([fm_cols(inp['a_norm_w'][0]), fm_cols(inp['mlp_norm_w'][0]),
                      fm_cols(inp['c_norm_w'][0]), fm_cols(inp['mlp_norm_w'][1])], axis=1)
    shared['normw'] = np.ascontiguousarray(normw)
    shared['mlp_w1'] = f(inp['mlp_w1'])
    shared['mlp_w2'] = f(inp['mlp_w2'])
    shared['c_w_qkv'] = f(inp['c_w_qkv'][0])
    shared['c_w_o'] = f(inp['c_w_o'][0])
    shared['qkw'] = np.ascontiguousarray(np.stack([np.tile(f(inp['c_q_norm_w'][0]), 2),
                                                   np.tile(f(inp['c_k_norm_w'][0]), 2)], axis=1))
    shared['a_w_in'] = f(inp['a_w_in'][0])
    shared['a_w_out'] = f(inp['a_w_out'][0])
    bc = lambda v: np.ascontiguousarray(np.broadcast_to(np.asarray(v, np.float32)[None, :], (128, len(v))))
    cws = f(inp['ssd_conv_w'][0])
    shared['convw_s'] = np.ascontiguousarray(cws.reshape(4, 12, 128).transpose(2, 1, 0))
    shared['convb_s'] = np.ascontiguousarray(f(inp['ssd_conv_b'][0]).reshape(12, 128).T)
    cwg = f(inp['gdn_conv_w'][0])
    shared['convw_g'] = np.ascontiguousarray(cwg.reshape(4, 24, 128).transpose(2, 1, 0))
    shared['dsk'] = bc(f(inp['ssd_d_skip'][0]))
    shared['snw'] = bc(f(inp['ssd_norm_w'][0]))
    shared['gnw'] = bc(f(inp['gdn_norm_w'][0]))
    z8 = np.zeros(8, np.float32)
    shared['bias_bc'] = bc(np.concatenate([f(inp['ssd_dt_bias'][0]), z8, f(inp['gdn_dt_bias'][0])]))
    shared['alog_bc'] = bc(np.concatenate([f(inp['ssd_a_log'][0]), z8, f(inp['gdn_a_log'][0])]))
    x = f(inp['x'])
    maps = []
    for c in range(ncores):
        m = dict(shared)
        m['x'] = np.ascontiguousarray(x[c * nseq:(c + 1) * nseq])
        maps.append(m)
    return maps


ALL_STAGES = ('mix0', 'mlp0', 'attn', 'mlp1')


def run(inp, nseq=2, ncores=8, stages=ALL_STAGES, trace=False):
    key = (nseq, tuple(stages))
    if key not in _CACHE:
        _CACHE[key] = Builder(nseq, stages).build()
    nc = _CACHE[key]
    maps = make_inputs(inp, nseq, ncores)
    res = run_bass_kernel_spmd(nc, maps, core_ids=list(range(ncores)), trace=trace)
    outs = [r["out"] for r in res.results]
    return np.concatenate(outs, axis=0), res


def kernel(**inputs):
    out, _ = run(inputs)
    return out.astype(np.float32)
```

```python
import os
import numpy as np
from contextlib import ExitStack
import concourse.bass as bass
import concourse.mybir as mybir
from concourse.bass_utils import run_bass_kernel_spmd

F32 = mybir.dt.float32
BF16 = mybir.dt.bfloat16
F32R = mybir.dt.float32r


def r32(ap):
    return ap.bitcast(F32R)
AF = mybir.ActivationFunctionType
ALU = mybir.AluOpType

L = 2048
D = 1024
EPS = 1e-6
EPOCH = 30000
NDSEM = 8


class Sched:
    CE = ('pe', 'act', 'dve', 'pool')

    def __init__(self, nc, esem, dsem):
        self.nc = nc
        self.esem = esem
        self.dsem = dsem
        self.cnt = {e: 0 for e in self.CE}
        self.ndma = {'sp': 0, 'pool': 0}
        self.seen = {e: {} for e in ('pe', 'act', 'dve', 'pool', 'sp')}
        self.reset()

    def reset(self):
        self.ops = {e: [] for e in ('pe', 'act', 'dve', 'pool', 'sp')}
        self.last_w = {}
        self.readers = {}

    def op(self, eng, fn, reads=(), writes=(), dma=False):
        if eng == 'pool' and not dma and os.environ.get("POOL2DVE"):
            eng = 'dve'
        self.nop_total = getattr(self, 'nop_total', 0) + 1
        cut = os.environ.get("OPCUT")
        if cut and self.nop_total > int(cut):
            return None
        if os.environ.get("OPTRACE"):
            import traceback
            fr = traceback.extract_stack(limit=4)
            print("OP", self.nop_total, eng, "dma" if dma else "", [f"{f.name}:{f.lineno}" for f in fr[:-1]])
        writes = list(writes) + [r for r in reads if r[0] in ('ps', 'psacc') and r not in writes]
        idx = len(self.ops[eng])
        tok = (eng, idx)
        deps = {}
        for r in reads:
            w = self.last_w.get(r)
            if w is not None:
                deps[w] = True
        for r in writes:
            w = self.last_w.get(r)
            if w is not None and w not in deps:
                deps[w] = False
            for t in self.readers.get(r, ()):
                if t not in deps:
                    deps[t] = False
        for r in reads:
            self.readers.setdefault(r, []).append(tok)
        for r in writes:
            self.last_w[r] = tok
            self.readers[r] = []
        self.ops[eng].append(dict(fn=fn, deps=deps, dma=dma))
        return tok

    def flush(self):
        nc = self.nc
        ops = self.ops
        need = set()
        for e, lst in ops.items():
            for i, o in enumerate(lst):
                keep = []
                for (e2, i2), raw in o['deps'].items():
                    o2 = ops[e2][i2]
                    if o2['dma']:
                        keep.append((e2, i2))
                    elif e2 == e:
                        if e == 'pe':
                            continue
                        keep.append((e2, i2))
                    else:
                        keep.append((e2, i2))
                o['keep'] = keep
                for k in keep:
                    if not ops[k[0]][k[1]]['dma']:
                        need.add(k)
        for e in self.CE:
            for i, o in enumerate(ops[e]):
                if o['dma']:
                    continue
                if (e, i) in need:
                    self.cnt[e] += 1
                    c = self.cnt[e]
                    o['sig'] = (self.esem[e][(c - 1) // EPOCH], (c - 1) % EPOCH + 1)
                else:
                    o['sig'] = None
        pending = {'sp': [], 'pool': []}
        for q in ('sp', 'pool'):
            for o in ops[q]:
                if o['dma']:
                    n = self.ndma[q]
                    self.ndma[q] += 1
                    o['sig'] = (self.dsem[q][n % NDSEM], 16 * (n // NDSEM + 1))
                    o['prewait'] = (self.dsem[q][n % NDSEM], 16 * (n // NDSEM)) if n >= NDSEM else None
                    pending[q].append(o['sig'])

        def emit(e, eng):
            seen = self.seen[e]

            def wait(s, v):
                if seen.get(s[0], 0) >= v:
                    return
                seen[s[0]] = v
                eng.wait_ge(s[1], v)

            for o in ops[e]:
                if o.get('prewait') is not None:
                    wait(*o['prewait'])
                for k in o['keep']:
                    sg = ops[k[0]][k[1]]['sig']
                    wait(*sg)
                ins = o['fn'](eng)
                if o['sig'] is not None:
                    ins.then_inc(o['sig'][0][1], 16 if o['dma'] else 1)
            if e in pending:
                for sg in pending[e][-NDSEM:]:
                    wait(*sg)

        with nc.Block() as block:
            if ops['sp']:
                @block.sync
                def _(eng):
                    emit('sp', eng)
            if ops['pe']:
                @block.tensor
                def _(eng):
                    emit('pe', eng)
            if ops['act']:
                @block.scalar
                def _(eng):
                    emit('act', eng)
            if ops['dve']:
                @block.vector
                def _(eng):
                    emit('dve', eng)
            if ops['pool']:
                @block.gpsimd
                def _(eng):
                    emit('pool', eng)
        self.reset()


def XR(c, t0, t1):
    return [('X', c, tt) for tt in range(t0 // 128, (t1 + 127) // 128)]


def HR(c, t0, t1):
    return [('H', c, tt) for tt in range(t0 // 128, (t1 + 127) // 128)]


class Builder:
    def __init__(self, nseq, stages):
        self.nseq = nseq
        self.stages = stages
        nc = bass.Bass("TRN2", target_bir_lowering=False)
        self.nc = nc
        self.es = ExitStack()
        self.dram = {}
        self._uid = 0

    def din(self, name, shape, dtype=F32):
        t = self.nc.dram_tensor(name, list(shape), dtype, kind="ExternalInput").ap()
        self.dram[name] = t
        return t

    def sb(self, name, shape, dtype, es=None):
        es = es or self.es
        return es.enter_context(self.nc.sbuf_tensor(f"{name}_u{self.uid()}", list(shape), dtype))

    def psum(self, name, shape, dtype):
        return self.es.enter_context(self.nc.psum_tensor(name, list(shape), dtype))

    def uid(self):
        self._uid += 1
        return self._uid

    def mm(self, out, pairs, reads, writes):
        pairs = list(pairs)

        def fn(pe):
            n = len(pairs)
            ins = None
            for i, (l, r) in enumerate(pairs):
                ins = pe.matmul(out, l, r, start=(i == 0), stop=(i == n - 1))
            return ins
        self.s.op('pe', fn, reads, writes)

    def mm1(self, out, l, r, start, stop, reads, writes):
        self.s.op('pe', lambda pe: pe.matmul(out, l, r, start=start, stop=stop), reads, writes)

    def asel(self, out, in_, base, cm, n, reads, writes):
        self.s.op('pool', lambda e: e.affine_select(out, in_, [[1, n]], ALU.is_gt, 0.0, base=base,
                                                    channel_multiplier=cm), reads, writes)

    def tr(self, out, in_, ident, reads, writes):
        self.s.op('pe', lambda pe: pe.transpose(out, in_, ident), reads, writes)

    def act(self, out, in_, func, reads, writes, bias=None, scale=None, accum_out=None, eng='act'):
        kw = {}
        if bias is not None:
            kw['bias'] = bias
        if scale is not None:
            kw['scale'] = scale
        if accum_out is not None:
            kw['accum_out'] = accum_out
        self.s.op('act', lambda e: e.activation(out, in_, func, **kw), reads, writes)

    def tt(self, eng, out, a, b, op, reads, writes):
        self.s.op(eng, lambda e: e.tensor_tensor(out, a, b, op), reads, writes)

    def ts(self, eng, out, a, s1, s2, op0, op1, reads, writes):
        if op1 is None:
            self.s.op(eng, lambda e: e.tensor_scalar(out, a, s1, None, op0), reads, writes)
        else:
            self.s.op(eng, lambda e: e.tensor_scalar(out, a, s1, s2, op0, op1), reads, writes)

    def stt(self, out, a, sc, b, op0, op1, reads, writes):
        self.s.op('dve', lambda e: e.scalar_tensor_tensor(out, a, sc, b, op0, op1), reads, writes)

    def copy(self, eng, out, in_, reads, writes):
        if eng == 'act':
            self.s.op('act', lambda e: e.copy(out, in_), reads, writes)
        else:
            self.s.op(eng, lambda e: e.tensor_copy(out, in_), reads, writes)

    def dma(self, q, out, in_, reads, writes):
        self.s.op(q, lambda e: e.dma_start(out=out, in_=in_), reads, writes, dma=True)

    def next_ps(self):
        i = self._psi
        self._psi = (i + 1) % len(self.psr)
        return self.psr[i], ('ps', i)

    def build(self):
        nc = self.nc
        ns = self.nseq
        x = self.din("x", [ns, L, D])
        out = nc.dram_tensor("out", [ns, L, D], F32, kind="ExternalOutput").ap()
        self.x_d, self.out_d = x, out
        d = self.din
        W = {}
        W['ident'] = d("c_ident", [128, 128])
        W['ones'] = d("c_ones", [128, 128])
        W['uincl'] = d("c_uincl", [128, 128])
        W['trile'] = d("c_trile", [128, 128])
        W['gt'] = d("c_gt", [128, 128])
        W['blk'] = d("c_blk", [128, 128])
        W['mask4'] = d("c_mask4", [128, 4, 512])
        W['normw'] = d("normw", [128, 4, 8])
        W['mlp_w1'] = d("mlp_w1", [2, D, 4096])
        W['mlp_w2'] = d("mlp_w2", [2, 4096, D])
        W['c_w_qkv'] = d("c_w_qkv", [D, 3072])
        W['c_w_o'] = d("c_w_o", [D, D])
        W['qkw'] = d("qkw", [128, 2])
        W['a_w_in'] = d("a_w_in", [D, 6688])
        W['a_w_out'] = d("a_w_out", [2048, D])
        W['convw_s'] = d("convw_s", [128, 12, 4])
        W['convb_s'] = d("convb_s", [128, 12])
        W['convw_g'] = d("convw_g", [128, 24, 4])
        W['dsk'] = d("dsk", [128, 16])
        W['snw'] = d("snw", [128, 1024])
        W['gnw'] = d("gnw", [128, 128])
        W['bias_bc'] = d("bias_bc", [128, 32])
        W['alog_bc'] = d("alog_bc", [128, 32])
        self.W = W

        es = self.es
        sems = {}
        si = [0]

        def newsem(name):
            h = es.enter_context(nc.semaphore(name))
            si[0] += 1
            return (si[0], h)
        esem = {e: [newsem(f"s_{e}{i}") for i in range(3)] for e in Sched.CE}
        dsem = {q: [newsem(f"d_{q}{i}") for i in range(NDSEM)] for q in ('sp', 'pool')}
        self.s = Sched(nc, esem, dsem)

        self.X = self.sb("X", [128, 8, L], F32)
        self.H = self.sb("H", [128, 8, L], BF16)
        self.ident_f = self.sb("ident_f", [128, 128], F32)
        self.ones_f = self.sb("ones_f", [128, 128], F32)
        self.uinclneg_f = self.sb("uinclneg_f", [128, 128], F32)
        self.uincl_f = self.sb("uincl_f", [128, 128], F32)
        self.trile_f = self.sb("trile_f", [128, 128], F32)
        self.gt_f = self.sb("gt_f", [128, 128], F32)
        self.gtle_f = self.sb("gtle_f", [128, 256], F32)
        self.ones_r = self.sb("ones_r", [128, 128], F32)
        self.trile_r = self.sb("trile_r", [128, 128], F32)
        self.ones_b = self.sb("ones_b", [128, 128], BF16)
        self.ident_b = self.sb("ident_b", [128, 128], BF16)
        self.blk_b = self.sb("blk_b", [128, 128], BF16)
        self.normw = self.sb("normw_sb", [128, 4, 8], F32)
        self.qkw = self.sb("qkw_sb", [128, 2], F32)
        self.cc = self.sb("constcols", [128, 8], F32)
        self.psr = [self.psum(f"ps{i}", [128, 512], F32) for i in range(6)]
        self.psacc = [self.psum(f"psacc{i}", [128, 512], F32) for i in range(2)]
        self._psi = 0

        self.stage_consts()
        for sq in range(ns):
            self.stage_load(sq)
            if 'mix0' in self.stages:
                self.stage_mix0()
            if 'mlp0' in self.stages:
                self.stage_mlp(0)
            if 'attn' in self.stages:
                self.stage_attn()
            if 'mlp1' in self.stages:
                self.stage_mlp(1)
            self.stage_store(sq)
        self.es.close()
        return nc

    def stage_consts(self):
        W = self.W
        q = 'sp'
        for name, t in (('ident', self.ident_f), ('ones', self.ones_f), ('uincl', self.uincl_f),
                        ('trile', self.trile_f), ('gt', self.gt_f)):
            self.dma(q, t[:, :], W[name][:, :], [], [(name,)])
        self.dma(q, self.gtle_f[:, 0:128], W['gt'][:, :], [], [('gtle',)])
        self.dma(q, self.gtle_f[:, 128:256], W['trile'][:, :], [], [('gtle',)])
        self.dma(q, self.normw[:, :, :], W['normw'][:, :, :], [], [('normw',)])
        self.dma(q, self.qkw[:, :], W['qkw'][:, :], [], [('qkw',)])
        for i, v in enumerate((EPS, float(np.log(0.125)), 1.0, 0.0, float(-0.5 * np.log(128.0)))):
            self.s.op('pool', (lambda e, i=i, v=v: e.memset(self.cc[:, i:i + 1], v)), [], [('cc', i)])
        with ExitStack() as es:
            tmp = self.sb("ctmp", [128, 128], F32, es)
            self.dma(q, tmp[:, :], W['blk'][:, :], [], [('ctmp',)])
            self.copy('dve', self.blk_b[:, :], tmp[:, :], [('ctmp',)], [('blk_b',)])
            self.copy('dve', self.ones_b[:, :], self.ones_f[:, :], [('ones',)], [('ones_b',)])
            self.copy('dve', self.ident_b[:, :], self.ident_f[:, :], [('ident',)], [('ident_b',)])
            self.ts('dve', r32(self.uinclneg_f[:, :]), self.uincl_f[:, :], -1.0, None, ALU.mult, None,
                    [('uincl',)], [('uinclneg',)])
            self.copy('dve', r32(self.ones_r[:, :]), self.ones_f[:, :], [('ones',)], [('ones_r',)])
            self.copy('dve', r32(self.trile_r[:, :]), self.trile_f[:, :], [('trile',)], [('trile_r',)])
            self.s.flush()

    def stage_load(self, sq):
        X = self.X
        with ExitStack() as es:
            stg = [self.sb(f"ldstg{i}", [128, D], F32, es) for i in range(2)]
            for tt in range(16):
                st = stg[tt % 2]
                sr = ('ldstg', tt % 2)
                self.dma('sp', st[:, :], self.x_d[sq, tt * 128:(tt + 1) * 128, :], [], [sr])
                for half in range(2):
                    ps, pr = self.next_ps()
                    for j in range(4):
                        c = half * 4 + j
                        self.tr(ps[:, j * 128:(j + 1) * 128], st[:, c * 128:(c + 1) * 128], self.ident_f[:, :],
                                [sr, ('ident',)], [pr])
                    dst = X[:, half * 4:half * 4 + 4, tt * 128:(tt + 1) * 128]
                    src = ps[:, :].rearrange("p (c t) -> p c t", c=4)
                    wr = [('X', half * 4 + j, tt) for j in range(4)]
                    self.copy('act' if half == 0 else 'dve', dst, src, [pr], wr)
            self.s.flush()

    def stage_store(self, sq):
        X = self.X
        with ExitStack() as es:
            stg = [self.sb(f"ststg{i}", [128, D], F32, es) for i in range(2)]
            for tt in range(16):
                st = stg[tt % 2]
                sr = ('ststg', tt % 2)
                for half in range(2):
                    ps, pr = self.next_ps()
                    for j in range(4):
                        c = half * 4 + j
                        self.tr(ps[:, j * 128:(j + 1) * 128], X[:, c, tt * 128:(tt + 1) * 128], self.ident_f[:, :],
                                [('X', c, tt), ('ident',)], [pr])
                    self.copy('act' if half == 0 else 'dve', st[:, half * 512:(half + 1) * 512], ps[:, :], [pr], [sr])
                self.dma('sp', self.out_d[sq, tt * 128:(tt + 1) * 128, :], st[:, :], [sr], [('out', sq, tt)])
            self.s.flush()

    def rmsnorm(self, widx):
        with ExitStack() as es:
            self._rmsnorm(widx, es)
            self.s.flush()

    def _rmsnorm(self, widx, es):
        X, H = self.X, self.H
        sq = [self.sb(f"rn_sq{i}", [128, 8, 512], BF16, es) for i in range(2)]
        lnv = [self.sb(f"rn_ln{i}", [128, 512], F32, es) for i in range(2)]
        rstd = [self.sb(f"rn_rs{i}", [128, 512], F32, es) for i in range(2)]
        for tb in range(4):
            b = tb % 2
            t0, t1 = tb * 512, (tb + 1) * 512
            for c in range(8):
                if c % 2 == 0:
                    self.act(sq[b][:, c, :], X[:, c, t0:t1], AF.Square, XR(c, t0, t1), [('rn_sq', b, c)])
                else:
                    self.tt('pool', sq[b][:, c, :], X[:, c, t0:t1], X[:, c, t0:t1], ALU.mult,
                            XR(c, t0, t1), [('rn_sq', b, c)])
            ps, pr = self.next_ps()
            self.mm(ps[:, :], [(self.ones_b[:, :], sq[b][:, c, :]) for c in range(8)],
                    [('rn_sq', b, c) for c in range(8)] + [('ones_b',)], [pr])
            self.act(lnv[b][:, :], ps[:, :], AF.Ln, [pr], [('rn_ln', b)], bias=self.cc[:, 0:1], scale=1.0 / D)
            self.act(rstd[b][:, :], lnv[b][:, :], AF.Exp, [('rn_ln', b)], [('rn_rs', b)], scale=-0.5)
            for c in range(8):
                self.stt(H[:, c, t0:t1], X[:, c, t0:t1], self.normw[:, widx, c:c + 1], rstd[b][:, :],
                         ALU.mult, ALU.mult, XR(c, t0, t1) + [('rn_rs', b), ('normw',)], HR(c, t0, t1))

    def stage_mlp(self, layer):
        X, H = self.X, self.H
        W1 = self.W['mlp_w1']
        W2 = self.W['mlp_w2']
        with ExitStack() as es:
            self._rmsnorm(1 + 2 * layer, es)
            w1 = [self.sb(f"w1_{i}", [128, 8, 512], BF16, es) for i in range(2)]
            w2 = [self.sb(f"w2_{i}", [128, 4, D], BF16, es) for i in range(2)]
            A = [self.sb(f"mlpA{i}", [128, 4, L], BF16, es) for i in range(2)]
            R = [self.sb(f"mlpR{i}", [128, 512], F32, es) for i in range(3)]
            ri = 0
            for e in range(8):
                b = e % 2
                self.dma('pool', w1[b][:, :, :],
                         W1[layer, :, e * 512:(e + 1) * 512].rearrange("(c p) n -> p c n", p=128),
                         [], [('w1', b)])
                self.dma('pool', w2[b][:, :, :],
                         W2[layer, e * 512:(e + 1) * 512, :].rearrange("(c p) n -> p c n", p=128),
                         [], [('w2', b)])
                for tb in range(4):
                    t0, t1 = tb * 512, (tb + 1) * 512
                    for j in range(4):
                        ps, pr = self.next_ps()
                        self.mm(ps[:, :], [(w1[b][:, kc, j * 128:(j + 1) * 128], H[:, kc, t0:t1]) for kc in range(8)],
                                [('w1', b)] + [r for kc in range(8) for r in HR(kc, t0, t1)], [pr])
                        r = R[ri % 3]
                        rr = ('mlpR', ri % 3)
                        ri += 1
                        self.act(r[:, :], ps[:, :], AF.Relu, [pr], [rr])
                        self.tt('pool', A[b][:, j, t0:t1], r[:, :], r[:, :], ALU.mult, [rr], [('mlpA', b, j, tb)])
                for tb in range(4):
                    t0, t1 = tb * 512, (tb + 1) * 512
                    for dt in range(8):
                        ps, pr = self.next_ps()
                        self.mm(ps[:, :], [(w2[b][:, j, dt * 128:(dt + 1) * 128], A[b][:, j, t0:t1]) for j in range(4)],
                                [('w2', b)] + [('mlpA', b, j, tb) for j in range(4)], [pr])
                        self.tt('dve', X[:, dt, t0:t1], ps[:, :], X[:, dt, t0:t1], ALU.add,
                                [pr] + XR(dt, t0, t1), XR(dt, t0, t1))
            self.s.flush()

    def stage_attn(self):
        X, H = self.X, self.H
        Wqkv = self.W['c_w_qkv']
        Wo = self.W['c_w_o']
        cc = self.cc
        self.rmsnorm(2)
        with ExitStack() as es0:
            qT = self.sb("qT", [128, 8, L], BF16, es0)
            kT = self.sb("kT", [128, 8, L], BF16, es0)
            with ExitStack() as es:
                wq = [self.sb(f"wq{i}", [128, 8, 128], BF16, es) for i in range(3)]
                wv = [self.sb(f"wv{i}", [128, 8, 512], BF16, es) for i in range(2)]
                qraw = [self.sb(f"qraw{i}", [128, 512], F32, es) for i in range(2)]
                sqq = [self.sb(f"sqq{i}", [128, 512], BF16, es) for i in range(2)]
                lnv = [self.sb(f"qln{i}", [128, 512], F32, es) for i in range(2)]
                rstd = [self.sb(f"qrs{i}", [128, 512], F32, es) for i in range(2)]
                for half in range(2):
                    self.dma('pool', wv[half][:, :, :],
                             Wqkv[:, 2048 + half * 512:2048 + (half + 1) * 512].rearrange("(c p) n -> p c n", p=128),
                             [], [('wv', half)])
                wi = 0
                PA = int(os.environ.get("ATT_PA", "3"))
                items = []
                for c in range(8 if (PA & 1) else 0):
                    for which in range(2):
                        col0 = which * 1024 + c * 128
                        w = wq[wi % 3]
                        wr = ('wq', wi % 3)
                        wi += 1
                        for tb in range(4):
                            items.append(dict(c=c, which=which, col0=col0, w=w, wr=wr, tb=tb, i=len(items)))

                def qk_a(it):
                    if it['tb'] == 0:
                        self.dma('pool', it['w'][:, :, :],
                                 Wqkv[:, it['col0']:it['col0'] + 128].rearrange("(c p) n -> p c n", p=128), [], [it['wr']])
                    t0, t1 = it['tb'] * 512, (it['tb'] + 1) * 512
                    b = it['i'] % 2
                    ps, pr = self.next_ps()
                    self.mm(ps[:, :], [(it['w'][:, kc, :], H[:, kc, t0:t1]) for kc in range(8)],
                            [it['wr']] + [r for kc in range(8) for r in HR(kc, t0, t1)], [pr])
                    self.act(sqq[b][:, :], ps[:, :], AF.Square, [pr], [('sqq', b)])
                    self.copy('dve', qraw[b][:, :], ps[:, :], [pr], [('qraw', b)])

                def qk_b(it):
                    which, c, tb = it['which'], it['c'], it['tb']
                    t0, t1 = tb * 512, (tb + 1) * 512
                    b = it['i'] % 2
                    dst = qT if which == 0 else kT
                    dn = 'qT' if which == 0 else 'kT'
                    ps2, pr2 = self.next_ps()
                    self.mm(ps2[:, :], [(self.blk_b[:, :], sqq[b][:, :])], [('sqq', b), ('blk_b',)], [pr2])
                    self.act(lnv[b][:, :], ps2[:, :], AF.Ln, [pr2], [('qln', b)], bias=cc[:, 0:1], scale=1.0 / 64)
                    self.act(rstd[b][:, :], lnv[b][:, :], AF.Exp, [('qln', b)], [('qrs', b)],
                             bias=cc[:, 1:2] if which == 0 else cc[:, 3:4], scale=-0.5)
                    self.stt(dst[:, c, t0:t1], qraw[b][:, :], self.qkw[:, which:which + 1], rstd[b][:, :],
                             ALU.mult, ALU.mult, [('qraw', b), ('qrs', b), ('qkw',)], [(dn, c, tb)])

                for t in range(len(items) + 1):
                    if t < len(items):
                        qk_a(items[t])
                    if t >= 1:
                        qk_b(items[t - 1])
                for tt in range(16 if (PA & 2) else 0):
                    t0, t1 = tt * 128, (tt + 1) * 128
                    pss = []
                    for half in range(2):
                        ps, pr = self.next_ps()
                        self.mm(ps[:, :], [(H[:, kc, t0:t1], wv[half][:, kc, :]) for kc in range(8)],
                                [('wv', half)] + [('H', kc, tt) for kc in range(8)], [pr])
                        pss.append((ps, pr))
                    for half in range(2):
                        ps, pr = pss[half]
                        self.copy('act' if half == 0 else 'dve', H[:, half * 4:half * 4 + 4, t0:t1],
                                  ps[:, :].rearrange("p (c t) -> p c t", c=4), [pr],
                                  [('H', half * 4 + j, tt) for j in range(4)])
                self.s.flush()
            with ExitStack() as es:
                eb = [self.sb(f"at_e{i}", [128, 512], F32, es) for i in range(3)]
                spb = [self.sb(f"at_sp{i}", [128, 512], F32, es) for i in range(3)]
                tmpb = [self.sb(f"at_tmp{i}", [128, 512], F32, es) for i in range(2)]
                Rs = [self.sb(f"at_rs{i}", [128, 512], F32, es) for i in range(2)]
                ATb = [self.sb(f"at_A{i}", [128, 512], BF16, es) for i in range(3)]
                qpad = [[self.sb(f"qpad{par}{j}", [128, 512], BF16, es) for j in range(2)] for par in range(2)]
                for par in range(2):
                    for j in range(2):
                        self.s.op('pool', (lambda e, t=qpad[par][j]: e.memset(t[:, :], 0.0)), [], [('qpad', par, j)])
                self.mask4 = self.sb("mask4", [128, 4, 512], F32, es)
                self.dma('sp', self.mask4[:, :, :], self.W['mask4'][:, :, :], [], [('mask4',)])
                it_g = 0
                ai = 0
                pairs = []
                gi = 0
                for h in range(16):
                    for G in range(4):
                        kmax = 4 * G + 3
                        for kb in range(kmax, -1, -1):
                            pairs.append(dict(h=h, G=G, kb=kb, first=(kb == kmax), last=(kb == 0), diag=(kb >= 4 * G),
                                              gi=gi, i=len(pairs)))
                        gi += 1
                rstate = {'cur': 0}

                def opnd(p):
                    h, G, kb = p['h'], p['G'], p['kb']
                    c = h // 2
                    b0 = (h % 2) * 64
                    par, j = h % 2, p['gi'] % 2
                    q_s = qpad[par][j][:, :]
                    k_s = kT[:, c, kb * 128:(kb + 1) * 128]
                    return c, b0, q_s, k_s, ('qpad', par, j), ('kT', c, kb // 4)

                def stA(p):
                    c, b0, q_s, k_s, qr, kr = opnd(p)
                    b = p['i'] % 3
                    if p['first']:
                        G = p['G']
                        self.copy('dve', q_s[b0:b0 + 64, :], qT[b0:b0 + 64, c, G * 512:(G + 1) * 512], [('qT', c, G)], [qr])
                    ps_z, pzr = self.next_ps()
                    self.mm(ps_z[:, :], [(k_s, q_s)], [kr, qr], [pzr])
                    self.act(eb[b][:, :], ps_z[:, :], AF.Exp, [pzr], [('at_e', b)])
                    self.act(r32(spb[b][:, :]), eb[b][:, :], AF.Ln, [('at_e', b)], [('at_sp', b)], bias=cc[:, 2:3])
                    if p['diag']:
                        self.tt('pool', r32(spb[b][:, :]), spb[b][:, :], self.mask4[:, p['kb'] - 4 * p['G'], :], ALU.mult,
                                [('at_sp', b), ('mask4',)], [('at_sp', b)])

                def stB(p):
                    c, b0, q_s, k_s, qr, kr = opnd(p)
                    b = p['i'] % 3
                    b2 = p['i'] % 2
                    ps_n, pnr = self.next_ps()
                    self.mm(ps_n[:, :], [(k_s, q_s), (r32(self.uinclneg_f[:, :]), r32(spb[b][:, :]))],
                            [kr, qr, ('at_sp', b), ('uinclneg',)], [pnr])
                    if p['first']:
                        self.act(ATb[b][:, :], ps_n[:, :], AF.Exp, [pnr], [('at_A', b)])
                    else:
                        rcur = rstate['cur']
                        self.tt('dve', tmpb[b2][:, :], ps_n[:, :], Rs[rcur][:, :], ALU.subtract,
                                [pnr, ('at_rs', rcur)], [('at_tmp', b2)])
                        self.act(ATb[b][:, :], tmpb[b2][:, :], AF.Exp, [('at_tmp', b2)], [('at_A', b)])
                    if p['diag']:
                        self.tt('pool', ATb[b][:, :], ATb[b][:, :], self.mask4[:, p['kb'] - 4 * p['G'], :], ALU.mult,
                                [('at_A', b), ('mask4',)], [('at_A', b)])
                    if not p['last']:
                        ps_r, prr = self.next_ps()
                        self.mm(ps_r[:, :], [(r32(self.ones_r[:, :]), r32(spb[b][:, :]))], [('at_sp', b), ('ones_r',)], [prr])
                        rcur = rstate['cur']
                        nxt = 1 - rcur
                        if p['first']:
                            self.copy('dve', Rs[nxt][:, :], ps_r[:, :], [prr], [('at_rs', nxt)])
                        else:
                            self.tt('dve', Rs[nxt][:, :], ps_r[:, :], Rs[rcur][:, :], ALU.add,
                                    [prr, ('at_rs', rcur)], [('at_rs', nxt)])
                        rstate['cur'] = nxt

                def stC(p):
                    c, b0, q_s, k_s, qr, kr = opnd(p)
                    h, G, kb = p['h'], p['G'], p['kb']
                    b = p['i'] % 3
                    acc = self.psacc[p['gi'] % 2]
                    accr = ('psacc', p['gi'] % 2)
                    v_s = H[:, c, kb * 128:(kb + 1) * 128]
                    self.mm1(acc[:, :], v_s, ATb[b][:, :], p['first'], p['last'],
                             [('H', c, kb), ('at_A', b)], [accr])
                    if p['last']:
                        self.copy('act', qT[b0:b0 + 64, c, G * 512:(G + 1) * 512], acc[b0:b0 + 64, :], [accr], [('qT', c, G)])

                n = len(pairs)
                for t in range(n + 2):
                    if t < n:
                        stA(pairs[t])
                    if 0 <= t - 1 < n:
                        stB(pairs[t - 1])
                    if 0 <= t - 2 < n:
                        stC(pairs[t - 2])
                self.s.flush()
            with ExitStack() as es:
                wo = [self.sb(f"wo{i}", [128, 8, 128], BF16, es) for i in range(2)]
                for dt in range(8):
                    w = wo[dt % 2]
                    wr = ('wo', dt % 2)
                    self.dma('pool', w[:, :, :], Wo[:, dt * 128:(dt + 1) * 128].rearrange("(c p) n -> p c n", p=128),
                             [], [wr])
                    for tb in range(4):
                        t0, t1 = tb * 512, (tb + 1) * 512
                        ps, pr = self.next_ps()
                        self.mm(ps[:, :], [(w[:, c, :], qT[:, c, t0:t1]) for c in range(8)],
                                [wr] + [('qT', c, tb) for c in range(8)], [pr])
                        self.tt('dve', X[:, dt, t0:t1], ps[:, :], X[:, dt, t0:t1], ALU.add,
                                [pr] + XR(dt, t0, t1), XR(dt, t0, t1))
                self.s.flush()

    def ring(self, name, n, shape, dtype, es):
        tiles = [self.sb(f"{name}{i}", shape, dtype, es) for i in range(n)]
        state = {'i': 0}

        def nxt():
            i = state['i'] % n
            state['i'] += 1
            return tiles[i], (name, i)
        nxt.tiles = tiles
        return nxt

    def conv_proj(self, col0, R):
        H = self.H
        Win = self.W['a_w_in']
        w, wr = R['w']()
        self.dma('pool', w[:, :, :], Win[:, col0:col0 + 128].rearrange("(c p) n -> p c n", p=128), [], [wr])
        raw, rr = R['raw']()
        for tb in range(4):
            t0, t1 = tb * 512, (tb + 1) * 512
            ps, pr = self.next_ps()
            self.mm(ps[:, :], [(w[:, kc, :], H[:, kc, t0:t1]) for kc in range(8)],
                    [wr] + [r for kc in range(8) for r in HR(kc, t0, t1)], [pr])
            self.copy('act', raw[:, 3 + t0:3 + t1], ps[:, :], [pr], [rr])
        return raw, rr

    def conv_act(self, raw, rr, cw, cb, R, silu_out=None, silu_reg=None):
        acc, ar = R['acc']()
        if cb is not None:
            self.ts('dve', acc[:, :], raw[:, 3:3 + L], cw[:, 3:4], cb, ALU.mult, ALU.add, [rr, ('convw',)], [ar])
        else:
            self.ts('dve', acc[:, :], raw[:, 3:3 + L], cw[:, 3:4], None, ALU.mult, None, [rr, ('convw',)], [ar])
        for k in (2, 1, 0):
            self.stt(acc[:, :], raw[:, k:k + L], cw[:, k:k + 1], acc[:, :], ALU.mult, ALU.add, [rr, ar, ('convw',)], [ar])
        if silu_out is not None:
            self.act(silu_out, acc[:, :], AF.Silu, [ar], [silu_reg])
            return silu_out, silu_reg
        self.act(acc[:, :], acc[:, :], AF.Silu, [ar], [ar])
        return acc, ar

    def conv_pipeline(self, tiles, R):
        nxt = self.conv_proj(tiles[0]['col'], R)
        for j, t in enumerate(tiles):
            cur = nxt
            if j + 1 < len(tiles):
                nxt = self.conv_proj(tiles[j + 1]['col'], R)
            out, outr = self.conv_act(cur[0], cur[1], t['cw'], t['cb'], R, t.get('silu_out'), t.get('silu_reg'))
            if t.get('post'):
                t['post'](out, outr)

    def to_tok(self, src, sr, dst, dr_name, width_off, src_regs=None):
        for q in range(4):
            ps, pr = self.next_ps()
            psb = ps[:, :].bitcast(BF16)
            for j in range(4):
                tt = q * 4 + j
                self.tr(psb[:, j * 128:(j + 1) * 128], src[:, tt * 128:(tt + 1) * 128], self.ident_b[:, :],
                        (src_regs if src_regs is not None else [sr]) + [('ident_b',)], [pr])
            self.copy('act' if q % 2 == 0 else 'dve', dst[:, q * 4:q * 4 + 4, width_off:width_off + 128],
                      psb[:, 0:512].rearrange("p (c t) -> p c t", c=4), [pr],
                      [(dr_name, q * 4 + j) for j in range(4)])

    def stage_mix0(self):
        X, H = self.X, self.H
        W = self.W
        cc = self.cc
        Win = W['a_w_in']
        Wout = W['a_w_out']
        self.rmsnorm(0)
        with ExitStack() as es0:
            sb = lambda n, sh, dt=F32: self.sb(n, sh, dt, es0)
            convw_s = sb("convw_s", [128, 12, 4])
            convb_s = sb("convb_s", [128, 12])
            convw_g = sb("convw_g", [128, 24, 4])
            dsk = sb("dsk", [128, 16])
            gnw = sb("gnw", [128, 128])
            spl = sb("spl", [128, 16, 32])
            ag = sb("ag", [128, 16, 32])
            ea = sb("ea", [128, 16, 32])
            dte = sb("dte", [128, 16, 32])
            cdec = sb("cdec", [128, 16, 32])
            bet = sb("bet", [128, 16, 8])
            nbet = sb("nbet", [128, 16, 8])
            bg = sb("bg", [128, 16, 8])
            self.dma('sp', convw_s[:, :, :], W['convw_s'][:, :, :], [], [('convw',)])
            self.dma('sp', convb_s[:, :], W['convb_s'][:, :], [], [('convw',)])
            self.dma('sp', convw_g[:, :, :], W['convw_g'][:, :, :], [], [('convw',)])
            self.dma('sp', dsk[:, :], W['dsk'][:, :], [], [('dsk',)])
            self.dma('sp', gnw[:, :], W['gnw'][:, :], [], [('gnw',)])
            with ExitStack() as es:
                wsm = self.sb("wsm", [128, 8, 32], BF16, es)
                bias_bc = self.sb("bias_bc", [128, 32], F32, es)
                alog = self.sb("alog", [128, 32], F32, es)
                aneg = self.sb("aneg", [128, 32], F32, es)
                smraw = self.sb("smraw", [128, 16, 32], F32, es)
                e1 = self.sb("sm_e", [128, 16, 32], F32, es)
                acum = self.sb("acum", [128, 16, 32], F32, es)
                tot = self.sb("tot", [128, 16, 32], F32, es)
                tmp = self.sb("sm_tmp", [128, 16, 32], F32, es)
                self.dma('pool', wsm[:, :, 0:16], Win[:, 2560:2576].rearrange("(c p) n -> p c n", p=128), [], [('wsm', 0)])
                self.dma('pool', wsm[:, :, 16:32], Win[:, 6672:6688].rearrange("(c p) n -> p c n", p=128), [], [('wsm', 1)])
                self.dma('sp', bias_bc[:, :], W['bias_bc'][:, :], [], [('bias_bc',)])
                self.dma('sp', alog[:, :], W['alog_bc'][:, :], [], [('alog',)])
                self.act(aneg[:, :], alog[:, :], AF.Exp, [('alog',)], [('aneg',)])
                self.ts('dve', aneg[:, :], aneg[:, :], -1.0, None, ALU.mult, None, [('aneg',)], [('aneg',)])
                for tt in range(16):
                    ps, pr = self.next_ps()
                    self.mm(ps[:, 0:32], [(H[:, kc, tt * 128:(tt + 1) * 128], wsm[:, kc, :]) for kc in range(8)],
                            [('wsm', 0), ('wsm', 1)] + [('H', kc, tt) for kc in range(8)], [pr])
                    self.tt('dve', smraw[:, tt, :], ps[:, 0:32], bias_bc[:, :], ALU.add, [pr, ('bias_bc',)], [('smraw',)])
                f2 = lambda t: t[:, :, :].rearrange("p a b -> p (a b)")
                self.act(f2(e1), f2(smraw), AF.Exp, [('smraw',)], [('sm_e',)])
                self.act(f2(spl), f2(e1), AF.Ln, [('sm_e',)], [('spl',)], bias=cc[:, 2:3])
                self.ts('dve', tmp[:, :, 16:24], e1[:, :, 16:24], 1.0, None, ALU.add, None, [('sm_e',)], [('sm_tmp',)])
                self.s.op('dve', lambda e: e.reciprocal(tmp[:, :, 16:24], tmp[:, :, 16:24]), [('sm_tmp',)], [('sm_tmp',)])
                self.tt('dve', bet[:, :, :], e1[:, :, 16:24], tmp[:, :, 16:24], ALU.mult, [('sm_e',), ('sm_tmp',)], [('bet',)])
                self.ts('dve', nbet[:, :, :], bet[:, :, :], -1.0, None, ALU.mult, None, [('bet',)], [('nbet',)])
                self.tt('dve', ag[:, :, :], spl[:, :, :], aneg[:, :].unsqueeze(1).broadcast_to([128, 16, 32]), ALU.mult,
                        [('spl',), ('aneg',)], [('ag',)])
                ps, pr = self.next_ps()
                self.mm(ps[:, :], [(self.trile_f[:, :], f2(ag))], [('ag',), ('trile',)], [pr])
                self.copy('dve', f2(acum), ps[:, :], [pr], [('acum',)])
                self.act(f2(ea), ps[:, :], AF.Exp, [pr], [('ea',)])
                ps2, pr2 = self.next_ps()
                self.mm(ps2[:, :], [(self.ones_f[:, :], f2(ag))], [('ag',), ('ones',)], [pr2])
                self.act(f2(cdec), ps2[:, :], AF.Exp, [pr2], [('cdec',)])
                self.tt('dve', f2(tot), ps2[:, :], f2(acum), ALU.subtract, [pr2, ('acum',)], [('tot',)])
                self.act(f2(dte), f2(tot), AF.Exp, [('tot',)], [('dte',)])
                self.tt('dve', bg[:, :, :], bet[:, :, :], ea[:, :, 24:32], ALU.mult, [('bet',), ('ea',)], [('bg',)])
                self.s.flush()
            P = dict(convw_s=convw_s, convb_s=convb_s, convw_g=convw_g, dsk=dsk, gnw=gnw, spl=spl, ag=ag,
                     ea=ea, dte=dte, cdec=cdec, bet=bet, nbet=nbet, bg=bg)
            if os.environ.get("MIX_STOP") == "prelude":
                return
            MIXSEL = os.environ.get("MIX_SEL", "ssd,gdn").split(",")
            if 'ssd' in MIXSEL:
                for g in range(2):
                    self.ssd_group(g, P)
            if 'gdn' in MIXSEL:
                for hg in range(2):
                    self.gdn_group(hg, P)

    def ssd_group(self, g, P):
        X, H = self.X, self.H
        Win = self.W['a_w_in']
        Wout = self.W['a_w_out']
        cc = self.cc
        with ExitStack() as es0:
            xs_tok = self.sb("xs_tok", [128, 16, 512], BF16, es0)
            siluz = self.sb("siluz", [128, 16, 512], BF16, es0)
            BT = self.sb("BT", [128, L], BF16, es0)
            CT = self.sb("CT", [128, L], BF16, es0)
            B_tok = self.sb("B_tok", [128, 16, 128], BF16, es0)
            with ExitStack() as es:
                R = dict(w=self.ring("cw", 3, [128, 8, 128], BF16, es),
                         raw=self.ring("craw", 2, [128, L + 3], F32, es),
                         acc=self.ring("cacc", 1, [128, L], F32, es))
                xsb = self.ring("xsb", 2, [128, L], BF16, es)
                wz = self.sb("wz", [128, 8, 512], BF16, es)
                self.dma('pool', wz[:, :, :], Win[:, g * 512:(g + 1) * 512].rearrange("(c p) n -> p c n", p=128), [], [('wz',)])
                self._zero_halo(R, es)
                tiles = []
                for j in range(4):
                    ti = g * 4 + j
                    xb, xr = xsb()
                    tiles.append(dict(col=1024 + ti * 128, cw=P['convw_s'][:, ti, :], cb=P['convb_s'][:, ti:ti + 1],
                                      silu_out=xb[:, :], silu_reg=xr,
                                      post=(lambda o, r, xb=xb, xr=xr, j=j: self.to_tok(xb, xr, xs_tok, 'xs_tok', j * 128))))
                ti = 8 + g
                tiles.append(dict(col=1024 + ti * 128, cw=P['convw_s'][:, ti, :], cb=P['convb_s'][:, ti:ti + 1],
                                  silu_out=BT[:, :], silu_reg=('BT',),
                                  post=(lambda o, r: self.to_tok(BT, ('BT',), B_tok, 'B_tok', 0))))
                ti = 10 + g
                tiles.append(dict(col=1024 + ti * 128, cw=P['convw_s'][:, ti, :], cb=P['convb_s'][:, ti:ti + 1],
                                  silu_out=CT[:, :], silu_reg=('CT',)))
                self.conv_pipeline(tiles, R)
                for tt in range(16):
                    ps, pr = self.next_ps()
                    self.mm(ps[:, :], [(H[:, kc, tt * 128:(tt + 1) * 128], wz[:, kc, :]) for kc in range(8)],
                            [('wz',)] + [('H', kc, tt) for kc in range(8)], [pr])
                    self.act(siluz[:, tt, :], ps[:, :], AF.Silu, [pr], [('siluz', tt)])
                self.s.flush()
            if os.environ.get("MIX_STOP") == "ssd1":
                return
            yT = self.sb("yT", [128, 4, L], BF16, es0)
            with ExitStack() as es:
                ring = lambda n, k, sh, dt=F32: self.ring(n, k, sh, dt, es)
                cbm_r = ring("cbm", 2, [128, 128])
                lseg_r = ring("lseg", 2, [128, 4, 128])
                LT_r = ring("LT", 2, [128, 4, 128])
                MT_r = ring("MT", 2, [128, 8, 128], BF16)
                xdt_r = ring("xdt", 2, [128, 512], BF16)
                xdd_r = ring("xdd", 2, [128, 512], BF16)
                t1_r = ring("t1", 1, [128, 512])
                t3_r = ring("t3", 1, [128, 512])
                yg_r = ring("yg", 2, [128, 512])
                junk_r = ring("junk", 1, [128, 512], BF16)
                yn_r = ring("ynb", 2, [128, 512], BF16)
                sm_r = ring("ssm", 2, [128, 4])
                S = self.sb("ssdS", [128, 512], F32, es)
                Sb = self.sb("ssdSb", [128, 512], BF16, es)
                snw = self.sb("snw", [128, 512], F32, es)
                self.dma('sp', snw[:, :], self.W['snw'][:, g * 512:(g + 1) * 512], [], [('snw',)])
                hs = slice(g * 8, (g + 1) * 8)
                v8 = lambda ap: ap.rearrange("p (h d) -> p h d", h=8)
                bc8 = lambda ap: ap.unsqueeze(2).broadcast_to([128, 8, 64])
                sA = {}
                sB = {}

                def ssd_a(tt):
                    ts_ = slice(tt * 128, (tt + 1) * 128)
                    ps_cb, pcr = self.next_ps()
                    self.mm(ps_cb[:, 0:128], [(BT[:, ts_], CT[:, ts_])], [('BT',), ('CT',)], [pcr])
                    cbm, cbr = cbm_r()
                    self.tt('dve', cbm[:, :], ps_cb[:, 0:128], self.trile_f[:, :], ALU.mult, [pcr, ('trile',)], [cbr])
                    MT, mr = MT_r()
                    for half in range(2):
                        lseg, lr = lseg_r()
                        ps_s, psr = self.next_ps()
                        for i in range(4):
                            hh = g * 8 + half * 4 + i
                            self.ts('dve', r32(lseg[:, i, :]), self.gt_f[:, :], P['ag'][:, tt, hh:hh + 1], None,
                                    ALU.mult, None, [('gt',), ('ag',)], [(lr, i)])
                            self.mm(ps_s[:, i * 128:(i + 1) * 128], [(r32(lseg[:, i, :]), r32(self.trile_r[:, :]))],
                                    [(lr, i), ('trile_r',)], [psr])
                        LT, ltr = LT_r()
                        self.act(LT[:, :, :].rearrange("p a b -> p (a b)"), ps_s[:, :], AF.Exp, [psr], [ltr])
                        self.tt('dve', MT[:, half * 4:(half + 1) * 4, :], LT[:, :, :],
                                cbm[:, :].unsqueeze(1).broadcast_to([128, 4, 128]), ALU.mult, [ltr, cbr], [(mr, half)])
                    xdt, xdr = xdt_r()
                    self.tt('dve', v8(xdt[:, :]), v8(xs_tok[:, tt, :]), bc8(P['spl'][:, tt, hs]), ALU.mult,
                            [('xs_tok', tt), ('spl',)], [xdr])
                    xdd, xddr = xdd_r()
                    self.tt('pool', v8(xdd[:, :]), v8(xdt[:, :]), bc8(P['dte'][:, tt, hs]), ALU.mult,
                            [xdr, ('dte',)], [xddr])
                    sA[tt] = dict(MT=MT, mr=mr, xdt=xdt, xdr=xdr, xdd=xdd, xddr=xddr, ts_=ts_)

                def ssd_b(tt):
                    d = sA.pop(tt)
                    MT, mr, xdt, xdr, xdd, xddr, ts_ = d['MT'], d['mr'], d['xdt'], d['xdr'], d['xdd'], d['xddr'], d['ts_']
                    ps_y, pyr = self.next_ps()
                    for i in range(8):
                        self.mm(ps_y[:, i * 64:(i + 1) * 64], [(MT[:, i, :], xdt[:, i * 64:(i + 1) * 64])],
                                [(mr, i // 4), xdr], [pyr])
                    t1, t1r = t1_r()
                    if tt > 0:
                        ps_o, por = self.next_ps()
                        self.mm(ps_o[:, :], [(CT[:, ts_], Sb[:, :])], [('CT',), ('ssdSb',)], [por])
                        self.tt('dve', v8(t1[:, :]), v8(ps_o[:, :]), bc8(P['ea'][:, tt, hs]), ALU.mult, [por, ('ea',)], [t1r])
                        self.tt('dve', t1[:, :], t1[:, :], ps_y[:, :], ALU.add, [t1r, pyr], [t1r])
                    else:
                        self.copy('dve', t1[:, :], ps_y[:, :], [pyr], [t1r])
                    t3, t3r = t3_r()
                    self.tt('pool', v8(t3[:, :]), v8(xs_tok[:, tt, :]), bc8(P['dsk'][:, hs]), ALU.mult,
                            [('xs_tok', tt), ('dsk',)], [t3r])
                    self.tt('dve', t3[:, :], t3[:, :], t1[:, :], ALU.add, [t3r, t1r], [t3r])
                    yg, ygr = yg_r()
                    self.tt('pool', yg[:, :], t3[:, :], siluz[:, tt, :], ALU.mult, [t3r, ('siluz', tt)], [ygr])
                    sm, smr = sm_r()
                    junk, jr = junk_r()
                    self.act(junk[:, :], yg[:, :], AF.Square, [ygr], [jr, (smr, 0)], accum_out=sm[:, 0:1])
                    if tt < 15:
                        ps_st, pstr = self.next_ps()
                        self.mm(ps_st[:, :], [(B_tok[:, tt, :], xdd[:, :])], [('B_tok', tt), xddr], [pstr])
                        if tt == 0:
                            self.copy('dve', S[:, :], ps_st[:, :], [pstr], [('ssdS',)])
                        else:
                            self.tt('dve', v8(S[:, :]), v8(S[:, :]), bc8(P['cdec'][:, tt, hs]), ALU.mult,
                                    [('ssdS',), ('cdec',)], [('ssdS',)])
                            self.tt('dve', S[:, :], S[:, :], ps_st[:, :], ALU.add, [('ssdS',), pstr], [('ssdS',)])
                        self.copy('act', Sb[:, :], S[:, :], [('ssdS',)], [('ssdSb',)])
                    sB[tt] = dict(yg=yg, ygr=ygr, sm=sm, smr=smr, ts_=ts_)

                def ssd_c(tt):
                    d = sB.pop(tt)
                    yg, ygr, sm, smr, ts_ = d['yg'], d['ygr'], d['sm'], d['smr'], d['ts_']
                    self.act(sm[:, 1:2], sm[:, 0:1], AF.Ln, [(smr, 0)], [(smr, 1)], bias=cc[:, 0:1], scale=1.0 / 512)
                    self.act(sm[:, 2:3], sm[:, 1:2], AF.Exp, [(smr, 1)], [(smr, 2)], scale=-0.5)
                    yn, ynr = yn_r()
                    self.stt(yn[:, :], yg[:, :], sm[:, 2:3], snw[:, :], ALU.mult, ALU.mult,
                             [ygr, (smr, 2), ('snw',)], [ynr])
                    ps_t, ptr_ = self.next_ps()
                    psb = ps_t[:, :].bitcast(BF16)
                    for j in range(4):
                        self.tr(psb[:, j * 128:(j + 1) * 128], yn[:, j * 128:(j + 1) * 128], self.ident_b[:, :],
                                [ynr, ('ident_b',)], [ptr_])
                    self.copy('act', yT[:, 0:4, ts_], psb[:, 0:512].rearrange("p (c t) -> p c t", c=4), [ptr_],
                              [('yT', j, tt) for j in range(4)])

                for t in range(18):
                    if t < 16:
                        ssd_a(t)
                    if 1 <= t <= 16:
                        ssd_b(t - 1)
                    if t >= 2:
                        ssd_c(t - 2)
                self.s.flush()
            with ExitStack() as es:
                wo = self.sb("wo_s", [128, 4, D], BF16, es)
                self.dma('pool', wo[:, :, :], Wout[g * 512:(g + 1) * 512, :].rearrange("(c p) n -> p c n", p=128), [], [('wo_s',)])
                for tb in range(4):
                    t0, t1_ = tb * 512, (tb + 1) * 512
                    for dt in range(8):
                        ps, pr = self.next_ps()
                        self.mm(ps[:, :], [(wo[:, j, dt * 128:(dt + 1) * 128], yT[:, j, t0:t1_]) for j in range(4)],
                                [('wo_s',)] + [('yT', j, tt) for j in range(4) for tt in range(tb * 4, tb * 4 + 4)], [pr])
                        self.tt('dve', X[:, dt, t0:t1_], ps[:, :], X[:, dt, t0:t1_], ALU.add,
                                [pr] + XR(dt, t0, t1_), XR(dt, t0, t1_))
                self.s.flush()

    def _zero_halo(self, R, es):
        for i, t in enumerate(R['raw'].tiles):
            self.s.op('pool', (lambda e, t=t: e.memset(t[:, 0:3], 0.0)), [], [("craw", i)])

    def gdn_group(self, hg, P):
        X, H = self.X, self.H
        Win = self.W['a_w_in']
        Wout = self.W['a_w_out']
        cc = self.cc
        QOFF = 2576
        with ExitStack() as es0:
            siluzg = self.sb("siluzg", [128, 16, 512], BF16, es0)
            ogT = self.sb("ogT", [128, 4, L], BF16, es0)
            with ExitStack() as es:
                wzg = self.sb("wzg", [128, 8, 512], BF16, es)
                c0 = 5648 + hg * 512
                self.dma('pool', wzg[:, :, :], Win[:, c0:c0 + 512].rearrange("(c p) n -> p c n", p=128), [], [('wzg',)])
                for tt in range(16):
                    ps, pr = self.next_ps()
                    self.mm(ps[:, :], [(H[:, kc, tt * 128:(tt + 1) * 128], wzg[:, kc, :]) for kc in range(8)],
                            [('wzg',)] + [('H', kc, tt) for kc in range(8)], [pr])
                    self.act(siluzg[:, tt, :], ps[:, :], AF.Silu, [pr], [('siluzg', tt)])
                self.s.flush()
            for i in range(4):
                h = hg * 4 + i
                with ExitStack() as esh:
                    qTn = self.sb("qTn", [128, L], BF16, esh)
                    kTn = self.sb("kTn", [128, L], BF16, esh)
                    k_tok = self.sb("k_tok", [128, 16, 128], BF16, esh)
                    v_tok = self.sb("v_tok", [128, 16, 128], BF16, esh)
                    with ExitStack() as es:
                        R = dict(w=self.ring("cw", 2, [128, 8, 128], BF16, es),
                                 raw=self.ring("craw", 2, [128, L + 3], F32, es),
                                 acc=self.ring("cacc", 2, [128, L], F32, es))
                        kvb = ogT[:, i, :]
                        sqq = self.sb("gsqq", [128, 512], BF16, es)
                        lnv = self.sb("gln", [128, 512], F32, es)
                        rstd = self.sb("grs", [128, 512], F32, es)
                        self._zero_halo(R, es)
                        def l2post(which, dstT, dn):
                            def post(acc, ar):
                                for tb in range(4):
                                    t0, t1 = tb * 512, (tb + 1) * 512
                                    self.act(sqq[:, :], acc[:, t0:t1], AF.Square, [ar], [('gsqq',)])
                                    ps, pr = self.next_ps()
                                    self.mm(ps[:, :], [(self.ones_b[:, :], sqq[:, :])], [('gsqq',), ('ones_b',)], [pr])
                                    self.act(lnv[:, :], ps[:, :], AF.Ln, [pr], [('gln',)], bias=cc[:, 0:1])
                                    self.act(rstd[:, :], lnv[:, :], AF.Exp, [('gln',)], [('grs',)],
                                             bias=cc[:, 4:5] if which == 0 else cc[:, 3:4], scale=-0.5)
                                    self.tt('dve', dstT[:, t0:t1], acc[:, t0:t1], rstd[:, :], ALU.mult, [ar, ('grs',)], [(dn, tb)])
                                if which == 1:
                                    self.to_tok(kTn, None, k_tok, 'k_tok', 0, src_regs=[('kTn', tb) for tb in range(4)])
                            return post
                        tiles = []
                        for which, dstT, dn in ((0, qTn, 'qTn'), (1, kTn, 'kTn')):
                            ti = which * 8 + h
                            tiles.append(dict(col=QOFF + ti * 128, cw=P['convw_g'][:, ti, :], cb=None, post=l2post(which, dstT, dn)))
                        ti = 16 + h
                        tiles.append(dict(col=QOFF + ti * 128, cw=P['convw_g'][:, ti, :], cb=None, silu_out=kvb, silu_reg=('kvb',),
                                          post=(lambda o, r: self.to_tok(kvb, ('kvb',), v_tok, 'v_tok', 0))))
                        self.conv_pipeline(tiles, R)
                        self.s.flush()
                    with ExitStack() as es:
                        ub = self.sb("g_ub", [128, 16, 128], F32, es)
                        wT = self.sb("g_wT", [128, 16, 128], BF16, es)
                        qkT = self.sb("g_qkT", [128, 16, 128], BF16, es)
                        NCTX = int(os.environ.get("GDN_NCTX", "4"))
                        ctxs = []
                        for ci in range(NCTX):
                            t = lambda n, sh=[128, 128], dt=F32: self.sb(f"g_{n}{ci}", sh, dt, es)
                            ctxs.append(dict(gmask=t("gmask"), DLT=t("DLT", [128, 256]), PPa=t("PPa", [128, 384]),
                                             PPb=t("PPb", [128, 384]), vb=t("vb"), kb2=t("kb2"), ci=ci))
                        gcol = lambda tt: P['ag'][:, tt, 24 + h:25 + h]
                        GPH = int(os.environ.get("GDN_PH", "3"))

                        def p2_group(grp):
                            tts = tuple(range(grp * NCTX, (grp + 1) * NCTX))
                            rgs = [(lambda n, ci=c['ci']: ('g_' + n, ci)) for c in ctxs]
                            ps1s, ps3s, ps4s = [], [], []
                            for c, tt, rg in zip(ctxs, tts, rgs):
                                self.ts('dve', r32(c['gmask'][:, :]), self.gt_f[:, :], gcol(tt), None, ALU.mult, None,
                                        [('gt',), ('ag',)], [rg('gmask')])
                                self.act(r32(c['vb'][:, :]), v_tok[:, tt, :], AF.Identity, [('v_tok', tt), ('bet',)], [rg('vb')],
                                         scale=P['bet'][:, tt, h:h + 1])
                                self.act(r32(c['kb2'][:, :]), k_tok[:, tt, :], AF.Identity, [('k_tok', tt), ('bg',)], [rg('kb2')],
                                         scale=P['bg'][:, tt, h:h + 1])
                            for c, tt, rg in zip(ctxs, tts, rgs):
                                ts_ = slice(tt * 128, (tt + 1) * 128)
                                ps1, p1r = self.next_ps()
                                self.mm(ps1[:, 0:128], [(r32(self.trile_r[:, :]), r32(c['gmask'][:, :]))], [rg('gmask'), ('trile_r',)], [p1r])
                                self.mm(ps1[:, 128:256], [(r32(c['gmask'][:, :]), r32(self.trile_r[:, :]))], [rg('gmask'), ('trile_r',)], [p1r])
                                self.mm(ps1[:, 256:384], [(kTn[:, ts_], kTn[:, ts_])], [('kTn', tt // 4)], [p1r])
                                self.mm(ps1[:, 384:512], [(kTn[:, ts_], qTn[:, ts_])], [('kTn', tt // 4), ('qTn', tt // 4)], [p1r])
                                ps1s.append((ps1, p1r))
                            yield
                            for c, tt, rg, (ps1, p1r) in zip(ctxs, tts, rgs, ps1s):
                                self.act(c['DLT'][:, :], ps1[:, 0:256], AF.Exp, [p1r], [rg('DLT')])
                                self.tt('dve', c['DLT'][:, :], c['DLT'][:, :], self.gtle_f[:, :], ALU.mult,
                                        [rg('DLT'), ('gtle',)], [rg('DLT')])
                            for c, tt, rg, (ps1, p1r) in zip(ctxs, tts, rgs, ps1s):
                                self.stt(r32(c['PPa'][:, 0:128]), ps1[:, 256:384], P['nbet'][:, tt, h:h + 1], c['DLT'][:, 0:128],
                                         ALU.mult, ALU.mult, [p1r, ('nbet',), rg('DLT')], [rg('PPa')])
                                self.tt('dve', qkT[:, tt, :], ps1[:, 384:512], c['DLT'][:, 128:256], ALU.mult,
                                        [p1r, rg('DLT')], [('g_qkT', tt)])
                            yield
                            for c, tt, rg in zip(ctxs, tts, rgs):
                                ps4, p4r = self.next_ps()
                                self.tr(ps4[:, 0:128], c['PPa'][:, 0:128], self.ident_f[:, :], [rg('PPa'), ('ident',)], [p4r])
                                ps4s.append((ps4, p4r))
                            for c, tt, rg, (ps4, p4r) in zip(ctxs, tts, rgs, ps4s):
                                self.copy('act', r32(c['PPa'][:, 128:256]), ps4[:, 0:128], [p4r], [rg('PPa')])
                                self.tt('dve', r32(c['PPa'][:, 256:384]), c['PPa'][:, 128:256], self.ident_f[:, :], ALU.add,
                                        [rg('PPa'), ('ident',)], [rg('PPa')])
                            cur, nxt = 'PPa', 'PPb'
                            for k in range(6):
                                psas = []
                                for c, rg in zip(ctxs, rgs):
                                    psa, par = self.next_ps()
                                    pp = c[cur]
                                    self.mm(psa[:, 0:128], [(r32(pp[:, 128:256]), r32(pp[:, 0:128]))], [rg(cur)], [par])
                                    if k == 0:
                                        self.mm(psa[:, 128:256], [(r32(pp[:, 0:128]), r32(pp[:, 128:256]))], [rg(cur)], [par])
                                    elif k <= 4:
                                        self.mm(psa[:, 128:384], [(r32(pp[:, 0:128]), r32(pp[:, 128:384]))], [rg(cur)], [par])
                                    else:
                                        self.mm(psa[:, 256:384], [(r32(pp[:, 0:128]), r32(pp[:, 256:384]))], [rg(cur)], [par])
                                    psas.append((psa, par))
                                yield
                                for c, rg, (psa, par) in zip(ctxs, rgs, psas):
                                    pp, pn = c[cur], c[nxt]
                                    w = 256 if k < 5 else 128
                                    self.copy('act', r32(pn[:, 0:w]), psa[:, 0:w], [par], [rg(nxt)])
                                    if k == 0:
                                        self.copy('dve', r32(pn[:, 256:384]), pp[:, 256:384], [rg(cur)], [rg(nxt)])
                                    else:
                                        self.tt('dve', r32(pn[:, 256:384]), pp[:, 256:384], psa[:, 256:384], ALU.add,
                                                [rg(cur), par], [rg(nxt)])
                                cur, nxt = nxt, cur
                                yield
                            psbs = []
                            for c, rg in zip(ctxs, rgs):
                                psb_, pbr = self.next_ps()
                                pp = c[cur]
                                self.mm(psb_[:, 0:128], [(r32(pp[:, 0:128]), r32(pp[:, 256:384]))], [rg(cur)], [pbr])
                                psbs.append((psb_, pbr))
                            yield
                            for c, rg, (psb_, pbr) in zip(ctxs, rgs, psbs):
                                pp = c[cur]
                                self.tt('dve', r32(pp[:, 256:384]), pp[:, 256:384], psb_[:, 0:128], ALU.add, [rg(cur), pbr], [rg(cur)])
                            yield
                            TT = cur
                            psus = []
                            for c, tt, rg in zip(ctxs, tts, rgs):
                                psu, pur = self.next_ps()
                                self.mm(psu[:, 0:128], [(r32(c[TT][:, 256:384]), r32(c['vb'][:, :]))], [rg(TT), rg('vb')], [pur])
                                self.mm(psu[:, 128:256], [(r32(c['kb2'][:, :]), r32(c[TT][:, 256:384]))], [rg(TT), rg('kb2')], [pur])
                                psus.append((psu, pur))
                            for c, tt, rg, (psu, pur) in zip(ctxs, tts, rgs, psus):
                                self.copy('act', ub[:, tt, :], psu[:, 0:128], [pur], [('g_ub', tt)])
                                self.copy('dve', wT[:, tt, :], psu[:, 128:256], [pur], [('g_wT', tt)])
                        S = self.sb("g_S", [128, 128], F32, es)
                        Sb = self.sb("g_Sb", [128, 128], BF16, es)
                        u_r = self.ring("g_u", 2, [128, 128], BF16, es)
                        kd_r = self.ring("g_kd", 2, [128, 128], BF16, es)
                        t_r = self.ring("g_t", 2, [128, 128], F32, es)
                        o_r = self.ring("g_o", 2, [128, 128], F32, es)
                        on_r = self.ring("g_on", 2, [128, 128], F32, es)
                        og_r = self.ring("g_og", 2, [128, 128], BF16, es)
                        jk_r = self.ring("g_jk", 1, [128, 128], BF16, es)
                        sm_r = self.ring("g_sm", 2, [128, 4], F32, es)
                        sst = {}

                        def scan_a(tt):
                            ts_ = slice(tt * 128, (tt + 1) * 128)
                            u, ur = u_r()
                            d = sst[tt] = dict(u=u, ur=ur)
                            if tt > 0:
                                ps_ws, pwr = self.next_ps()
                                self.mm(ps_ws[:, 0:128], [(wT[:, tt, :], Sb[:, :])], [('g_wT', tt), ('g_Sb',)], [pwr])
                                self.tt('dve', u[:, :], ub[:, tt, :], ps_ws[:, 0:128], ALU.subtract, [('g_ub', tt), pwr], [ur])
                            else:
                                self.copy('dve', u[:, :], ub[:, tt, :], [('g_ub', tt)], [ur])

                        def scan_b(tt):
                            ts_ = slice(tt * 128, (tt + 1) * 128)
                            d = sst.pop(tt)
                            u, ur = d['u'], d['ur']
                            if tt > 0:
                                ps_o1, po1r = self.next_ps()
                                self.mm(ps_o1[:, 0:128], [(qTn[:, ts_], Sb[:, :])], [('qTn', tt // 4), ('g_Sb',)], [po1r])
                            ps_o2, po2r = self.next_ps()
                            self.mm(ps_o2[:, 0:128], [(qkT[:, tt, :], u[:, :])], [('g_qkT', tt), ur], [po2r])
                            o, orr = o_r()
                            if tt > 0:
                                t, tr_ = t_r()
                                self.act(t[:, :], ps_o1[:, 0:128], AF.Identity, [po1r, ('ea',)], [tr_],
                                         scale=P['ea'][:, tt, 24 + h:25 + h])
                                self.tt('dve', o[:, :], t[:, :], ps_o2[:, 0:128], ALU.add, [tr_, po2r], [orr])
                            else:
                                self.copy('dve', o[:, :], ps_o2[:, 0:128], [po2r], [orr])
                            if tt < 15:
                                kd, kdr = kd_r()
                                self.act(kd[:, :], k_tok[:, tt, :], AF.Identity, [('k_tok', tt), ('dte',)], [kdr],
                                         scale=P['dte'][:, tt, 24 + h:25 + h])
                                ps_sk, pskr = self.next_ps()
                                self.mm(ps_sk[:, 0:128], [(kd[:, :], u[:, :])], [kdr, ur], [pskr])
                                if tt == 0:
                                    self.copy('dve', S[:, :], ps_sk[:, 0:128], [pskr], [('g_S',)])
                                else:
                                    self.stt(S[:, :], S[:, :], P['cdec'][:, tt, 24 + h:25 + h], ps_sk[:, 0:128], ALU.mult, ALU.add,
                                             [('g_S',), ('cdec',), pskr], [('g_S',)])
                                self.copy('act', Sb[:, :], S[:, :], [('g_S',)], [('g_Sb',)])
                            sm, smr = sm_r()
                            jk, jr = jk_r()
                            self.act(jk[:, :], o[:, :], AF.Square, [orr], [jr, (smr, 0)], accum_out=sm[:, 0:1])
                            self.act(sm[:, 1:2], sm[:, 0:1], AF.Ln, [(smr, 0)], [(smr, 1)], bias=cc[:, 0:1], scale=1.0 / 128)
                            self.act(sm[:, 2:3], sm[:, 1:2], AF.Exp, [(smr, 1)], [(smr, 2)], scale=-0.5)
                            on, onr = on_r()
                            self.stt(on[:, :], o[:, :], sm[:, 2:3], P['gnw'][:, :], ALU.mult, ALU.mult, [orr, (smr, 2), ('gnw',)], [onr])
                            og, ogr = og_r()
                            self.tt('dve', og[:, :], on[:, :], siluzg[:, tt, i * 128:(i + 1) * 128], ALU.mult,
                                    [onr, ('siluzg', tt)], [ogr])
                            ps_t, ptr_ = self.next_ps()
                            psb = ps_t[:, :].bitcast(BF16)
                            self.tr(psb[:, 0:128], og[:, :], self.ident_b[:, :], [ogr, ('ident_b',)], [ptr_])
                            self.copy('act', ogT[:, i, ts_], psb[:, 0:128], [ptr_], [('ogT', i, tt)])

                        queue = []
                        for grp in range(16 // NCTX if GPH >= 2 else 0):
                            for _ in p2_group(grp):
                                if queue:
                                    queue.pop(0)()
                            if GPH >= 3:
                                for tt in range(grp * NCTX, (grp + 1) * NCTX):
                                    queue.append(lambda tt=tt: scan_a(tt))
                                    queue.append(lambda tt=tt: scan_b(tt))
                        while queue:
                            queue.pop(0)()
                        self.s.flush()
            with ExitStack() as es:
                wo = self.sb("wo_g", [128, 4, D], BF16, es)
                r0 = 1024 + hg * 512
                self.dma('pool', wo[:, :, :], Wout[r0:r0 + 512, :].rearrange("(c p) n -> p c n", p=128), [], [('wo_g',)])
                for tb in range(4):
                    t0, t1_ = tb * 512, (tb + 1) * 512
                    for dt in range(8):
                        ps, pr = self.next_ps()
                        self.mm(ps[:, :], [(wo[:, j, dt * 128:(dt + 1) * 128], ogT[:, j, t0:t1_]) for j in range(4)],
                                [('wo_g',)] + [('ogT', j, tt) for j in range(4) for tt in range(tb * 4, tb * 4 + 4)], [pr])
                        self.tt('dve', X[:, dt, t0:t1_], ps[:, :], X[:, dt, t0:t1_], ALU.add,
                                [pr] + XR(dt, t0, t1_), XR(dt, t0, t1_))
                self.s.flush()


_CACHE = {}


def consts():
    i = np.arange(128)
    c = {}
    c['c_ident'] = np.eye(128, dtype=np.float32)
    c['c_ones'] = np.ones((128, 128), np.float32)
    c['c_uincl'] = (i[:, None] >= i[None, :]).astype(np.float32)
    c['c_trile'] = (i[:, None] <= i[None, :]).astype(np.float32)
    c['c_gt'] = (i[:, None] > i[None, :]).astype(np.float32)
    blk = np.zeros((128, 128), np.float32)
    blk[:64, :64] = 1
    blk[64:, 64:] = 1
    c['c_blk'] = blk
    col = np.arange(512)
    c['c_mask4'] = np.stack([(col[None, :] > (i[:, None] + 128 * k)) for k in range(4)], axis=1).astype(np.float32)
    return c


def fm_cols(v):
    return np.ascontiguousarray(np.asarray(v, np.float32).reshape(8, 128).T)


def make_inputs(inp, nseq, ncores):
    f = lambda a: np.ascontiguousarray(np.asarray(a, dtype=np.float32))
    shared = consts()
    normw = np.stack([fm_cols(inp['a_norm_w'][0]), fm_cols(inp['mlp_norm_w'][0]),
                      fm_cols(inp['c_norm_w'][0]), fm_cols(inp['mlp_norm_w'][1])], axis=1)
    shared['normw'] = np.ascontiguousarray(normw)
    shared['mlp_w1'] = f(inp['mlp_w1'])
    shared['mlp_w2'] = f(inp['mlp_w2'])
    shared['c_w_qkv'] = f(inp['c_w_qkv'][0])
    shared['c_w_o'] = f(inp['c_w_o'][0])
    shared['qkw'] = np.ascontiguousarray(np.stack([np.tile(f(inp['c_q_norm_w'][0]), 2),
                                                   np.tile(f(inp['c_k_norm_w'][0]), 2)], axis=1))
    shared['a_w_in'] = f(inp['a_w_in'][0])
    shared['a_w_out'] = f(inp['a_w_out'][0])
    bc = lambda v: np.ascontiguousarray(np.broadcast_to(np.asarray(v, np.float32)[None, :], (128, len(v))))
    cws = f(inp['ssd_conv_w'][0])
    shared['convw_s'] = np.ascontiguousarray(cws.reshape(4, 12, 128).transpose(2, 1, 0))
    shared['convb_s'] = np.ascontiguousarray(f(inp['ssd_conv_b'][0]).reshape(12, 128).T)
    cwg = f(inp['gdn_conv_w'][0])
    shared['convw_g'] = np.ascontiguousarray(cwg.reshape(4, 24, 128).transpose(2, 1, 0))
    shared['dsk'] = bc(f(inp['ssd_d_skip'][0]))
    shared['snw'] = bc(f(inp['ssd_norm_w'][0]))
    shared['gnw'] = bc(f(inp['gdn_norm_w'][0]))
    z8 = np.zeros(8, np.float32)
    shared['bias_bc'] = bc(np.concatenate([f(inp['ssd_dt_bias'][0]), z8, f(inp['gdn_dt_bias'][0])]))
    shared['alog_bc'] = bc(np.concatenate([f(inp['ssd_a_log'][0]), z8, f(inp['gdn_a_log'][0])]))
    x = f(inp['x'])
    maps = []
    for c in range(ncores):
        m = dict(shared)
        m['x'] = np.ascontiguousarray(x[c * nseq:(c + 1) * nseq])
        maps.append(m)
    return maps


ALL_STAGES = ('mix0', 'mlp0', 'attn', 'mlp1')


def run(inp, nseq=2, ncores=8, stages=ALL_STAGES, trace=False):
    key = (nseq, tuple(stages))
    if key not in _CACHE:
        _CACHE[key] = Builder(nseq, stages).build()
    nc = _CACHE[key]
    maps = make_inputs(inp, nseq, ncores)
    res = run_bass_kernel_spmd(nc, maps, core_ids=list(range(ncores)), trace=trace)
    outs = [r["out"] for r in res.results]
    return np.concatenate(outs, axis=0), res


def kernel(**inputs):
    out, _ = run(inputs)
    return out.astype(np.float32)
```

```python
import os
import numpy as np
from contextlib import ExitStack
import concourse.bass as bass
import concourse.mybir as mybir
from concourse.bass_utils import run_bass_kernel_spmd

F32 = mybir.dt.float32
BF16 = mybir.dt.bfloat16
F32R = mybir.dt.float32r


def r32(ap):
    return ap.bitcast(F32R)
AF = mybir.ActivationFunctionType
ALU = mybir.AluOpType

L = 2048
D = 1024
EPS = 1e-6
EPOCH = 30000
NDSEM = 8


class Sched:
    CE = ('pe', 'act', 'dve', 'pool')

    def __init__(self, nc, esem, dsem):
        self.nc = nc
        self.esem = esem
        self.dsem = dsem
        self.cnt = {e: 0 for e in self.CE}
        self.ndma = {'sp': 0, 'pool': 0}
        self.seen = {e: {} for e in ('pe', 'act', 'dve', 'pool', 'sp')}
        self.reset()

    def reset(self):
        self.ops = {e: [] for e in ('pe', 'act', 'dve', 'pool', 'sp')}
        self.last_w = {}
        self.readers = {}

    def op(self, eng, fn, reads=(), writes=(), dma=False):
        if eng == 'pool' and not dma and os.environ.get("POOL2DVE"):
            eng = 'dve'
        self.nop_total = getattr(self, 'nop_total', 0) + 1
        cut = os.environ.get("OPCUT")
        if cut and self.nop_total > int(cut):
            return None
        if os.environ.get("OPTRACE"):
            import traceback
            fr = traceback.extract_stack(limit=4)
            print("OP", self.nop_total, eng, "dma" if dma else "", [f"{f.name}:{f.lineno}" for f in fr[:-1]])
        writes = list(writes) + [r for r in reads if r[0] in ('ps', 'psacc') and r not in writes]
        idx = len(self.ops[eng])
        tok = (eng, idx)
        deps = {}
        for r in reads:
            w = self.last_w.get(r)
            if w is not None:
                deps[w] = True
        for r in writes:
            w = self.last_w.get(r)
            if w is not None and w not in deps:
                deps[w] = False
            for t in self.readers.get(r, ()):
                if t not in deps:
                    deps[t] = False
        for r in reads:
            self.readers.setdefault(r, []).append(tok)
        for r in writes:
            self.last_w[r] = tok
            self.readers[r] = []
        self.ops[eng].append(dict(fn=fn, deps=deps, dma=dma))
        return tok

    def flush(self):
        nc = self.nc
        ops = self.ops
        need = set()
        for e, lst in ops.items():
            for i, o in enumerate(lst):
                keep = []
                for (e2, i2), raw in o['deps'].items():
                    o2 = ops[e2][i2]
                    if o2['dma']:
                        keep.append((e2, i2))
                    elif e2 == e:
                        if e == 'pe':
                            continue
                        keep.append((e2, i2))
                    else:
                        keep.append((e2, i2))
                o['keep'] = keep
                for k in keep:
                    if not ops[k[0]][k[1]]['dma']:
                        need.add(k)
        for e in self.CE:
            for i, o in enumerate(ops[e]):
                if o['dma']:
                    continue
                if (e, i) in need:
                    self.cnt[e] += 1
                    c = self.cnt[e]
                    o['sig'] = (self.esem[e][(c - 1) // EPOCH], (c - 1) % EPOCH + 1)
                else:
                    o['sig'] = None
        pending = {'sp': [], 'pool': []}
        for q in ('sp', 'pool'):
            for o in ops[q]:
                if o['dma']:
                    n = self.ndma[q]
                    self.ndma[q] += 1
                    o['sig'] = (self.dsem[q][n % NDSEM], 16 * (n // NDSEM + 1))
                    o['prewait'] = (self.dsem[q][n % NDSEM], 16 * (n // NDSEM)) if n >= NDSEM else None
                    pending[q].append(o['sig'])

        def emit(e, eng):
            seen = self.seen[e]

            def wait(s, v):
                if seen.get(s[0], 0) >= v:
                    return
                seen[s[0]] = v
                eng.wait_ge(s[1], v)

            for o in ops[e]:
                if o.get('prewait') is not None:
                    wait(*o['prewait'])
                for k in o['keep']:
                    sg = ops[k[0]][k[1]]['sig']
                    wait(*sg)
                ins = o['fn'](eng)
                if o['sig'] is not None:
                    ins.then_inc(o['sig'][0][1], 16 if o['dma'] else 1)
            if e in pending:
                for sg in pending[e][-NDSEM:]:
                    wait(*sg)

        with nc.Block() as block:
            if ops['sp']:
                @block.sync
                def _(eng):
                    emit('sp', eng)
            if ops['pe']:
                @block.tensor
                def _(eng):
                    emit('pe', eng)
            if ops['act']:
                @block.scalar
                def _(eng):
                    emit('act', eng)
            if ops['dve']:
                @block.vector
                def _(eng):
                    emit('dve', eng)
            if ops['pool']:
                @block.gpsimd
                def _(eng):
                    emit('pool', eng)
        self.reset()


def XR(c, t0, t1):
    return [('X', c, tt) for tt in range(t0 // 128, (t1 + 127) // 128)]


def HR(c, t0, t1):
    return [('H', c, tt) for tt in range(t0 // 128, (t1 + 127) // 128)]


class Builder:
    def __init__(self, nseq, stages):
        self.nseq = nseq
        self.stages = stages
        nc = bass.Bass("TRN2", target_bir_lowering=False)
        self.nc = nc
        self.es = ExitStack()
        self.dram = {}
        self._uid = 0

    def din(self, name, shape, dtype=F32):
        t = self.nc.dram_tensor(name, list(shape), dtype, kind="ExternalInput").ap()
        self.dram[name] = t
        return t

    def sb(self, name, shape, dtype, es=None):
        es = es or self.es
        return es.enter_context(self.nc.sbuf_tensor(f"{name}_u{self.uid()}", list(shape), dtype))

    def psum(self, name, shape, dtype):
        return self.es.enter_context(self.nc.psum_tensor(name, list(shape), dtype))

    def uid(self):
        self._uid += 1
        return self._uid

    def mm(self, out, pairs, reads, writes):
        pairs = list(pairs)

        def fn(pe):
            n = len(pairs)
            ins = None
            for i, (l, r) in enumerate(pairs):
                ins = pe.matmul(out, l, r, start=(i == 0), stop=(i == n - 1))
            return ins
        self.s.op('pe', fn, reads, writes)

    def mm1(self, out, l, r, start, stop, reads, writes):
        self.s.op('pe', lambda pe: pe.matmul(out, l, r, start=start, stop=stop), reads, writes)

    def asel(self, out, in_, base, cm, n, reads, writes):
        self.s.op('pool', lambda e: e.affine_select(out, in_, [[1, n]], ALU.is_gt, 0.0, base=base,
                                                    channel_multiplier=cm), reads, writes)

    def tr(self, out, in_, ident, reads, writes):
        self.s.op('pe', lambda pe: pe.transpose(out, in_, ident), reads, writes)

    def act(self, out, in_, func, reads, writes, bias=None, scale=None, accum_out=None, eng='act'):
        kw = {}
        if bias is not None:
            kw['bias'] = bias
        if scale is not None:
            kw['scale'] = scale
        if accum_out is not None:
            kw['accum_out'] = accum_out
        self.s.op('act', lambda e: e.activation(out, in_, func, **kw), reads, writes)

    def tt(self, eng, out, a, b, op, reads, writes):
        self.s.op(eng, lambda e: e.tensor_tensor(out, a, b, op), reads, writes)

    def ts(self, eng, out, a, s1, s2, op0, op1, reads, writes):
        if op1 is None:
            self.s.op(eng, lambda e: e.tensor_scalar(out, a, s1, None, op0), reads, writes)
        else:
            self.s.op(eng, lambda e: e.tensor_scalar(out, a, s1, s2, op0, op1), reads, writes)

    def stt(self, out, a, sc, b, op0, op1, reads, writes):
        self.s.op('dve', lambda e: e.scalar_tensor_tensor(out, a, sc, b, op0, op1), reads, writes)

    def copy(self, eng, out, in_, reads, writes):
        if eng == 'act':
            self.s.op('act', lambda e: e.copy(out, in_), reads, writes)
        else:
            self.s.op(eng, lambda e: e.tensor_copy(out, in_), reads, writes)

    def dma(self, q, out, in_, reads, writes):
        self.s.op(q, lambda e: e.dma_start(out=out, in_=in_), reads, writes, dma=True)

    def next_ps(self):
        i = self._psi
        self._psi = (i + 1) % len(self.psr)
        return self.psr[i], ('ps', i)

    def build(self):
        nc = self.nc
        ns = self.nseq
        x = self.din("x", [ns, L, D])
        out = nc.dram_tensor("out", [ns, L, D], F32, kind="ExternalOutput").ap()
        self.x_d, self.out_d = x, out
        d = self.din
        W = {}
        W['ident'] = d("c_ident", [128, 128])
        W['ones'] = d("c_ones", [128, 128])
        W['uincl'] = d("c_uincl", [128, 128])
        W['trile'] = d("c_trile", [128, 128])
        W['gt'] = d("c_gt", [128, 128])
        W['blk'] = d("c_blk", [128, 128])
        W['mask4'] = d("c_mask4", [128, 4, 512])
        W['normw'] = d("normw", [128, 4, 8])
        W['mlp_w1'] = d("mlp_w1", [2, D, 4096])
        W['mlp_w2'] = d("mlp_w2", [2, 4096, D])
        W['c_w_qkv'] = d("c_w_qkv", [D, 3072])
        W['c_w_o'] = d("c_w_o", [D, D])
        W['qkw'] = d("qkw", [128, 2])
        W['a_w_in'] = d("a_w_in", [D, 6688])
        W['a_w_out'] = d("a_w_out", [2048, D])
        W['convw_s'] = d("convw_s", [128, 12, 4])
        W['convb_s'] = d("convb_s", [128, 12])
        W['convw_g'] = d("convw_g", [128, 24, 4])
        W['dsk'] = d("dsk", [128, 16])
        W['snw'] = d("snw", [128, 1024])
        W['gnw'] = d("gnw", [128, 128])
        W['bias_bc'] = d("bias_bc", [128, 32])
        W['alog_bc'] = d("alog_bc", [128, 32])
        self.W = W

        es = self.es
        sems = {}
        si = [0]

        def newsem(name):
            h = es.enter_context(nc.semaphore(name))
            si[0] += 1
            return (si[0], h)
        esem = {e: [newsem(f"s_{e}{i}") for i in range(3)] for e in Sched.CE}
        dsem = {q: [newsem(f"d_{q}{i}") for i in range(NDSEM)] for q in ('sp', 'pool')}
        self.s = Sched(nc, esem, dsem)

        self.X = self.sb("X", [128, 8, L], F32)
        self.H = self.sb("H", [128, 8, L], BF16)
        self.ident_f = self.sb("ident_f", [128, 128], F32)
        self.ones_f = self.sb("ones_f", [128, 128], F32)
        self.uinclneg_f = self.sb("uinclneg_f", [128, 128], F32)
        self.uincl_f = self.sb("uincl_f", [128, 128], F32)
        self.trile_f = self.sb("trile_f", [128, 128], F32)
        self.gt_f = self.sb("gt_f", [128, 128], F32)
        self.gtle_f = self.sb("gtle_f", [128, 256], F32)
        self.ones_r = self.sb("ones_r", [128, 128], F32)
        self.trile_r = self.sb("trile_r", [128, 128], F32)
        self.ones_b = self.sb("ones_b", [128, 128], BF16)
        self.ident_b = self.sb("ident_b", [128, 128], BF16)
        self.blk_b = self.sb("blk_b", [128, 128], BF16)
        self.normw = self.sb("normw_sb", [128, 4, 8], F32)
        self.qkw = self.sb("qkw_sb", [128, 2], F32)
        self.cc = self.sb("constcols", [128, 8], F32)
        self.psr = [self.psum(f"ps{i}", [128, 512], F32) for i in range(6)]
        self.psacc = [self.psum(f"psacc{i}", [128, 512], F32) for i in range(2)]
        self._psi = 0

        self.stage_consts()
        for sq in range(ns):
            self.stage_load(sq)
            if 'mix0' in self.stages:
                self.stage_mix0()
            if 'mlp0' in self.stages:
                self.stage_mlp(0)
            if 'attn' in self.stages:
                self.stage_attn()
            if 'mlp1' in self.stages:
                self.stage_mlp(1)
            self.stage_store(sq)
        self.es.close()
        return nc

    def stage_consts(self):
        W = self.W
        q = 'sp'
        for name, t in (('ident', self.ident_f), ('ones', self.ones_f), ('uincl', self.uincl_f),
                        ('trile', self.trile_f), ('gt', self.gt_f)):
            self.dma(q, t[:, :], W[name][:, :], [], [(name,)])
        self.dma(q, self.gtle_f[:, 0:128], W['gt'][:, :], [], [('gtle',)])
        self.dma(q, self.gtle_f[:, 128:256], W['trile'][:, :], [], [('gtle',)])
        self.dma(q, self.normw[:, :, :], W['normw'][:, :, :], [], [('normw',)])
        self.dma(q, self.qkw[:, :], W['qkw'][:, :], [], [('qkw',)])
        for i, v in enumerate((EPS, float(np.log(0.125)), 1.0, 0.0, float(-0.5 * np.log(128.0)))):
            self.s.op('pool', (lambda e, i=i, v=v: e.memset(self.cc[:, i:i + 1], v)), [], [('cc', i)])
        with ExitStack() as es:
            tmp = self.sb("ctmp", [128, 128], F32, es)
            self.dma(q, tmp[:, :], W['blk'][:, :], [], [('ctmp',)])
            self.copy('dve', self.blk_b[:, :], tmp[:, :], [('ctmp',)], [('blk_b',)])
            self.copy('dve', self.ones_b[:, :], self.ones_f[:, :], [('ones',)], [('ones_b',)])
            self.copy('dve', self.ident_b[:, :], self.ident_f[:, :], [('ident',)], [('ident_b',)])
            self.ts('dve', r32(self.uinclneg_f[:, :]), self.uincl_f[:, :], -1.0, None, ALU.mult, None,
                    [('uincl',)], [('uinclneg',)])
            self.copy('dve', r32(self.ones_r[:, :]), self.ones_f[:, :], [('ones',)], [('ones_r',)])
            self.copy('dve', r32(self.trile_r[:, :]), self.trile_f[:, :], [('trile',)], [('trile_r',)])
            self.s.flush()

    def stage_load(self, sq):
        X = self.X
        with ExitStack() as es:
            stg = [self.sb(f"ldstg{i}", [128, D], F32, es) for i in range(2)]
            for tt in range(16):
                st = stg[tt % 2]
                sr = ('ldstg', tt % 2)
                self.dma('sp', st[:, :], self.x_d[sq, tt * 128:(tt + 1) * 128, :], [], [sr])
                for half in range(2):
                    ps, pr = self.next_ps()
                    for j in range(4):
                        c = half * 4 + j
                        self.tr(ps[:, j * 128:(j + 1) * 128], st[:, c * 128:(c + 1) * 128], self.ident_f[:, :],
                                [sr, ('ident',)], [pr])
                    dst = X[:, half * 4:half * 4 + 4, tt * 128:(tt + 1) * 128]
                    src = ps[:, :].rearrange("p (c t) -> p c t", c=4)
                    wr = [('X', half * 4 + j, tt) for j in range(4)]
                    self.copy('act' if half == 0 else 'dve', dst, src, [pr], wr)
            self.s.flush()

    def stage_store(self, sq):
        X = self.X
        with ExitStack() as es:
            stg = [self.sb(f"ststg{i}", [128, D], F32, es) for i in range(2)]
            for tt in range(16):
                st = stg[tt % 2]
                sr = ('ststg', tt % 2)
                for half in range(2):
                    ps, pr = self.next_ps()
                    for j in range(4):
                        c = half * 4 + j
                        self.tr(ps[:, j * 128:(j + 1) * 128], X[:, c, tt * 128:(tt + 1) * 128], self.ident_f[:, :],
                                [('X', c, tt), ('ident',)], [pr])
                    self.copy('act' if half == 0 else 'dve', st[:, half * 512:(half + 1) * 512], ps[:, :], [pr], [sr])
                self.dma('sp', self.out_d[sq, tt * 128:(tt + 1) * 128, :], st[:, :], [sr], [('out', sq, tt)])
            self.s.flush()

    def rmsnorm(self, widx):
        with ExitStack() as es:
            self._rmsnorm(widx, es)
            self.s.flush()

    def _rmsnorm(self, widx, es):
        X, H = self.X, self.H
        sq = [self.sb(f"rn_sq{i}", [128, 8, 512], BF16, es) for i in range(2)]
        lnv = [self.sb(f"rn_ln{i}", [128, 512], F32, es) for i in range(2)]
        rstd = [self.sb(f"rn_rs{i}", [128, 512], F32, es) for i in range(2)]
        for tb in range(4):
            b = tb % 2
            t0, t1 = tb * 512, (tb + 1) * 512
            for c in range(8):
                if c % 2 == 0:
                    self.act(sq[b][:, c, :], X[:, c, t0:t1], AF.Square, XR(c, t0, t1), [('rn_sq', b, c)])
                else:
                    self.tt('pool', sq[b][:, c, :], X[:, c, t0:t1], X[:, c, t0:t1], ALU.mult,
                            XR(c, t0, t1), [('rn_sq', b, c)])
            ps, pr = self.next_ps()
            self.mm(ps[:, :], [(self.ones_b[:, :], sq[b][:, c, :]) for c in range(8)],
                    [('rn_sq', b, c) for c in range(8)] + [('ones_b',)], [pr])
            self.act(lnv[b][:, :], ps[:, :], AF.Ln, [pr], [('rn_ln', b)], bias=self.cc[:, 0:1], scale=1.0 / D)
            self.act(rstd[b][:, :], lnv[b][:, :], AF.Exp, [('rn_ln', b)], [('rn_rs', b)], scale=-0.5)
            for c in range(8):
                self.stt(H[:, c, t0:t1], X[:, c, t0:t1], self.normw[:, widx, c:c + 1], rstd[b][:, :],
                         ALU.mult, ALU.mult, XR(c, t0, t1) + [('rn_rs', b), ('normw',)], HR(c, t0, t1))

    def stage_mlp(self, layer):
        X, H = self.X, self.H
        W1 = self.W['mlp_w1']
        W2 = self.W['mlp_w2']
        with ExitStack() as es:
            self._rmsnorm(1 + 2 * layer, es)
            w1 = [self.sb(f"w1_{i}", [128, 8, 512], BF16, es) for i in range(2)]
            w2 = [self.sb(f"w2_{i}", [128, 4, D], BF16, es) for i in range(2)]
            A = [self.sb(f"mlpA{i}", [128, 4, L], BF16, es) for i in range(2)]
            R = [self.sb(f"mlpR{i}", [128, 512], F32, es) for i in range(3)]
            ri = 0
            for e in range(8):
                b = e % 2
                self.dma('pool', w1[b][:, :, :],
                         W1[layer, :, e * 512:(e + 1) * 512].rearrange("(c p) n -> p c n", p=128),
                         [], [('w1', b)])
                self.dma('pool', w2[b][:, :, :],
                         W2[layer, e * 512:(e + 1) * 512, :].rearrange("(c p) n -> p c n", p=128),
                         [], [('w2', b)])
                for tb in range(4):
                    t0, t1 = tb * 512, (tb + 1) * 512
                    for j in range(4):
                        ps, pr = self.next_ps()
                        self.mm(ps[:, :], [(w1[b][:, kc, j * 128:(j + 1) * 128], H[:, kc, t0:t1]) for kc in range(8)],
                                [('w1', b)] + [r for kc in range(8) for r in HR(kc, t0, t1)], [pr])
                        r = R[ri % 3]
                        rr = ('mlpR', ri % 3)
                        ri += 1
                        self.act(r[:, :], ps[:, :], AF.Relu, [pr], [rr])
                        self.tt('pool', A[b][:, j, t0:t1], r[:, :], r[:, :], ALU.mult, [rr], [('mlpA', b, j, tb)])
                for tb in range(4):
                    t0, t1 = tb * 512, (tb + 1) * 512
                    for dt in range(8):
                        ps, pr = self.next_ps()
                        self.mm(ps[:, :], [(w2[b][:, j, dt * 128:(dt + 1) * 128], A[b][:, j, t0:t1]) for j in range(4)],
                                [('w2', b)] + [('mlpA', b, j, tb) for j in range(4)], [pr])
                        self.tt('dve', X[:, dt, t0:t1], ps[:, :], X[:, dt, t0:t1], ALU.add,
                                [pr] + XR(dt, t0, t1), XR(dt, t0, t1))
            self.s.flush()

    def stage_attn(self):
        X, H = self.X, self.H
        Wqkv = self.W['c_w_qkv']
        Wo = self.W['c_w_o']
        cc = self.cc
        self.rmsnorm(2)
        with ExitStack() as es0:
            qT = self.sb("qT", [128, 8, L], BF16, es0)
            kT = self.sb("kT", [128, 8, L], BF16, es0)
            with ExitStack() as es:
                wq = [self.sb(f"wq{i}", [128, 8, 128], BF16, es) for i in range(3)]
                wv = [self.sb(f"wv{i}", [128, 8, 512], BF16, es) for i in range(2)]
                qraw = [self.sb(f"qraw{i}", [128, 512], F32, es) for i in range(2)]
                sqq = [self.sb(f"sqq{i}", [128, 512], BF16, es) for i in range(2)]
                lnv = [self.sb(f"qln{i}", [128, 512], F32, es) for i in range(2)]
                rstd = [self.sb(f"qrs{i}", [128, 512], F32, es) for i in range(2)]
                for half in range(2):
                    self.dma('pool', wv[half][:, :, :],
                             Wqkv[:, 2048 + half * 512:2048 + (half + 1) * 512].rearrange("(c p) n -> p c n", p=128),
                             [], [('wv', half)])
                wi = 0
                PA = int(os.environ.get("ATT_PA", "3"))
                items = []
                for c in range(8 if (PA & 1) else 0):
                    for which in range(2):
                        col0 = which * 1024 + c * 128
                        w = wq[wi % 3]
                        wr = ('wq', wi % 3)
                        wi += 1
                        for tb in range(4):
                            items.append(dict(c=c, which=which, col0=col0, w=w, wr=wr, tb=tb, i=len(items)))

                def qk_a(it):
                    if it['tb'] == 0:
                        self.dma('pool', it['w'][:, :, :],
                                 Wqkv[:, it['col0']:it['col0'] + 128].rearrange("(c p) n -> p c n", p=128), [], [it['wr']])
                    t0, t1 = it['tb'] * 512, (it['tb'] + 1) * 512
                    b = it['i'] % 2
                    ps, pr = self.next_ps()
                    self.mm(ps[:, :], [(it['w'][:, kc, :], H[:, kc, t0:t1]) for kc in range(8)],
                            [it['wr']] + [r for kc in range(8) for r in HR(kc, t0, t1)], [pr])
                    self.act(sqq[b][:, :], ps[:, :], AF.Square, [pr], [('sqq', b)])
                    self.copy('dve', qraw[b][:, :], ps[:, :], [pr], [('qraw', b)])

                def qk_b(it):
                    which, c, tb = it['which'], it['c'], it['tb']
                    t0, t1 = tb * 512, (tb + 1) * 512
                    b = it['i'] % 2
                    dst = qT if which == 0 else kT
                    dn = 'qT' if which == 0 else 'kT'
                    ps2, pr2 = self.next_ps()
                    self.mm(ps2[:, :], [(self.blk_b[:, :], sqq[b][:, :])], [('sqq', b), ('blk_b',)], [pr2])
                    self.act(lnv[b][:, :], ps2[:, :], AF.Ln, [pr2], [('qln', b)], bias=cc[:, 0:1], scale=1.0 / 64)
                    self.act(rstd[b][:, :], lnv[b][:, :], AF.Exp, [('qln', b)], [('qrs', b)],
                             bias=cc[:, 1:2] if which == 0 else cc[:, 3:4], scale=-0.5)
                    self.stt(dst[:, c, t0:t1], qraw[b][:, :], self.qkw[:, which:which + 1], rstd[b][:, :],
                             ALU.mult, ALU.mult, [('qraw', b), ('qrs', b), ('qkw',)], [(dn, c, tb)])

                for t in range(len(items) + 1):
                    if t < len(items):
                        qk_a(items[t])
                    if t >= 1:
                        qk_b(items[t - 1])
                for tt in range(16 if (PA & 2) else 0):
                    t0, t1 = tt * 128, (tt + 1) * 128
                    pss = []
                    for half in range(2):
                        ps, pr = self.next_ps()
                        self.mm(ps[:, :], [(H[:, kc, t0:t1], wv[half][:, kc, :]) for kc in range(8)],
                                [('wv', half)] + [('H', kc, tt) for kc in range(8)], [pr])
                        pss.append((ps, pr))
                    for half in range(2):
                        ps, pr = pss[half]
                        self.copy('act' if half == 0 else 'dve', H[:, half * 4:half * 4 + 4, t0:t1],
                                  ps[:, :].rearrange("p (c t) -> p c t", c=4), [pr],
                                  [('H', half * 4 + j, tt) for j in range(4)])
                self.s.flush()
            with ExitStack() as es:
                eb = [self.sb(f"at_e{i}", [128, 512], F32, es) for i in range(3)]
                spb = [self.sb(f"at_sp{i}", [128, 512], F32, es) for i in range(3)]
                tmpb = [self.sb(f"at_tmp{i}", [128, 512], F32, es) for i in range(2)]
                Rs = [self.sb(f"at_rs{i}", [128, 512], F32, es) for i in range(2)]
                ATb = [self.sb(f"at_A{i}", [128, 512], BF16, es) for i in range(3)]
                qpad = [[self.sb(f"qpad{par}{j}", [128, 512], BF16, es) for j in range(2)] for par in range(2)]
                for par in range(2):
                    for j in range(2):
                        self.s.op('pool', (lambda e, t=qpad[par][j]: e.memset(t[:, :], 0.0)), [], [('qpad', par, j)])
                self.mask4 = self.sb("mask4", [128, 4, 512], F32, es)
                self.dma('sp', self.mask4[:, :, :], self.W['mask4'][:, :, :], [], [('mask4',)])
                it_g = 0
                ai = 0
                pairs = []
                gi = 0
                for h in range(16):
                    for G in range(4):
                        kmax = 4 * G + 3
                        for kb in range(kmax, -1, -1):
                            pairs.append(dict(h=h, G=G, kb=kb, first=(kb == kmax), last=(kb == 0), diag=(kb >= 4 * G),
                                              gi=gi, i=len(pairs)))
                        gi += 1
                rstate = {'cur': 0}

                def opnd(p):
                    h, G, kb = p['h'], p['G'], p['kb']
                    c = h // 2
                    b0 = (h % 2) * 64
                    par, j = h % 2, p['gi'] % 2
                    q_s = qpad[par][j][:, :]
                    k_s = kT[:, c, kb * 128:(kb + 1) * 128]
                    return c, b0, q_s, k_s, ('qpad', par, j), ('kT', c, kb // 4)

                def stA(p):
                    c, b0, q_s, k_s, qr, kr = opnd(p)
                    b = p['i'] % 3
                    if p['first']:
                        G = p['G']
                        self.copy('dve', q_s[b0:b0 + 64, :], qT[b0:b0 + 64, c, G * 512:(G + 1) * 512], [('qT', c, G)], [qr])
                    ps_z, pzr = self.next_ps()
                    self.mm(ps_z[:, :], [(k_s, q_s)], [kr, qr], [pzr])
                    self.act(eb[b][:, :], ps_z[:, :], AF.Exp, [pzr], [('at_e', b)])
                    self.act(r32(spb[b][:, :]), eb[b][:, :], AF.Ln, [('at_e', b)], [('at_sp', b)], bias=cc[:, 2:3])
                    if p['diag']:
                        self.tt('pool', r32(spb[b][:, :]), spb[b][:, :], self.mask4[:, p['kb'] - 4 * p['G'], :], ALU.mult,
                                [('at_sp', b), ('mask4',)], [('at_sp', b)])

                def stB(p):
                    c, b0, q_s, k_s, qr, kr = opnd(p)
                    b = p['i'] % 3
                    b2 = p['i'] % 2
                    ps_n, pnr = self.next_ps()
                    self.mm(ps_n[:, :], [(k_s, q_s), (r32(self.uinclneg_f[:, :]), r32(spb[b][:, :]))],
                            [kr, qr, ('at_sp', b), ('uinclneg',)], [pnr])
                    if p['first']:
                        self.act(ATb[b][:, :], ps_n[:, :], AF.Exp, [pnr], [('at_A', b)])
                    else:
                        rcur = rstate['cur']
                        self.tt('dve', tmpb[b2][:, :], ps_n[:, :], Rs[rcur][:, :], ALU.subtract,
                                [pnr, ('at_rs', rcur)], [('at_tmp', b2)])
                        self.act(ATb[b][:, :], tmpb[b2][:, :], AF.Exp, [('at_tmp', b2)], [('at_A', b)])
                    if p['diag']:
                        self.tt('pool', ATb[b][:, :], ATb[b][:, :], self.mask4[:, p['kb'] - 4 * p['G'], :], ALU.mult,
                                [('at_A', b), ('mask4',)], [('at_A', b)])
                    if not p['last']:
                        ps_r, prr = self.next_ps()
                        self.mm(ps_r[:, :], [(r32(self.ones_r[:, :]), r32(spb[b][:, :]))], [('at_sp', b), ('ones_r',)], [prr])
                        rcur = rstate['cur']
                        nxt = 1 - rcur
                        if p['first']:
                            self.copy('dve', Rs[nxt][:, :], ps_r[:, :], [prr], [('at_rs', nxt)])
                        else:
                            self.tt('dve', Rs[nxt][:, :], ps_r[:, :], Rs[rcur][:, :], ALU.add,
                                    [prr, ('at_rs', rcur)], [('at_rs', nxt)])
                        rstate['cur'] = nxt

                def stC(p):
                    c, b0, q_s, k_s, qr, kr = opnd(p)
                    h, G, kb = p['h'], p['G'], p['kb']
                    b = p['i'] % 3
                    acc = self.psacc[p['gi'] % 2]
                    accr = ('psacc', p['gi'] % 2)
                    v_s = H[:, c, kb * 128:(kb + 1) * 128]
                    self.mm1(acc[:, :], v_s, ATb[b][:, :], p['first'], p['last'],
                             [('H', c, kb), ('at_A', b)], [accr])
                    if p['last']:
                        self.copy('act', qT[b0:b0 + 64, c, G * 512:(G + 1) * 512], acc[b0:b0 + 64, :], [accr], [('qT', c, G)])

                n = len(pairs)
                for t in range(n + 2):
                    if t < n:
                        stA(pairs[t])
                    if 0 <= t - 1 < n:
                        stB(pairs[t - 1])
                    if 0 <= t - 2 < n:
                        stC(pairs[t - 2])
                self.s.flush()
            with ExitStack() as es:
                wo = [self.sb(f"wo{i}", [128, 8, 128], BF16, es) for i in range(2)]
                for dt in range(8):
                    w = wo[dt % 2]
                    wr = ('wo', dt % 2)
                    self.dma('pool', w[:, :, :], Wo[:, dt * 128:(dt + 1) * 128].rearrange("(c p) n -> p c n", p=128),
                             [], [wr])
                    for tb in range(4):
                        t0, t1 = tb * 512, (tb + 1) * 512
                        ps, pr = self.next_ps()
                        self.mm(ps[:, :], [(w[:, c, :], qT[:, c, t0:t1]) for c in range(8)],
                                [wr] + [('qT', c, tb) for c in range(8)], [pr])
                        self.tt('dve', X[:, dt, t0:t1], ps[:, :], X[:, dt, t0:t1], ALU.add,
                                [pr] + XR(dt, t0, t1), XR(dt, t0, t1))
                self.s.flush()

    def ring(self, name, n, shape, dtype, es):
        tiles = [self.sb(f"{name}{i}", shape, dtype, es) for i in range(n)]
        state = {'i': 0}

        def nxt():
            i = state['i'] % n
            state['i'] += 1
            return tiles[i], (name, i)
        nxt.tiles = tiles
        return nxt

    def conv_proj(self, col0, R):
        H = self.H
        Win = self.W['a_w_in']
        w, wr = R['w']()
        self.dma('pool', w[:, :, :], Win[:, col0:col0 + 128].rearrange("(c p) n -> p c n", p=128), [], [wr])
        raw, rr = R['raw']()
        for tb in range(4):
            t0, t1 = tb * 512, (tb + 1) * 512
            ps, pr = self.next_ps()
            self.mm(ps[:, :], [(w[:, kc, :], H[:, kc, t0:t1]) for kc in range(8)],
                    [wr] + [r for kc in range(8) for r in HR(kc, t0, t1)], [pr])
            self.copy('act', raw[:, 3 + t0:3 + t1], ps[:, :], [pr], [rr])
        return raw, rr

    def conv_act(self, raw, rr, cw, cb, R, silu_out=None, silu_reg=None):
        acc, ar = R['acc']()
        if cb is not None:
            self.ts('dve', acc[:, :], raw[:, 3:3 + L], cw[:, 3:4], cb, ALU.mult, ALU.add, [rr, ('convw',)], [ar])
        else:
            self.ts('dve', acc[:, :], raw[:, 3:3 + L], cw[:, 3:4], None, ALU.mult, None, [rr, ('convw',)], [ar])
        for k in (2, 1, 0):
            self.stt(acc[:, :], raw[:, k:k + L], cw[:, k:k + 1], acc[:, :], ALU.mult, ALU.add, [rr, ar, ('convw',)], [ar])
        if silu_out is not None:
            self.act(silu_out, acc[:, :], AF.Silu, [ar], [silu_reg])
            return silu_out, silu_reg
        self.act(acc[:, :], acc[:, :], AF.Silu, [ar], [ar])
        return acc, ar

    def conv_pipeline(self, tiles, R):
        nxt = self.conv_proj(tiles[0]['col'], R)
        for j, t in enumerate(tiles):
            cur = nxt
            if j + 1 < len(tiles):
                nxt = self.conv_proj(tiles[j + 1]['col'], R)
            out, outr = self.conv_act(cur[0], cur[1], t['cw'], t['cb'], R, t.get('silu_out'), t.get('silu_reg'))
            if t.get('post'):
                t['post'](out, outr)

    def to_tok(self, src, sr, dst, dr_name, width_off, src_regs=None):
        for q in range(4):
            ps, pr = self.next_ps()
            psb = ps[:, :].bitcast(BF16)
            for j in range(4):
                tt = q * 4 + j
                self.tr(psb[:, j * 128:(j + 1) * 128], src[:, tt * 128:(tt + 1) * 128], self.ident_b[:, :],
                        (src_regs if src_regs is not None else [sr]) + [('ident_b',)], [pr])
            self.copy('act' if q % 2 == 0 else 'dve', dst[:, q * 4:q * 4 + 4, width_off:width_off + 128],
                      psb[:, 0:512].rearrange("p (c t) -> p c t", c=4), [pr],
                      [(dr_name, q * 4 + j) for j in range(4)])

    def stage_mix0(self):
        X, H = self.X, self.H
        W = self.W
        cc = self.cc
        Win = W['a_w_in']
        Wout = W['a_w_out']
        self.rmsnorm(0)
        with ExitStack() as es0:
            sb = lambda n, sh, dt=F32: self.sb(n, sh, dt, es0)
            convw_s = sb("convw_s", [128, 12, 4])
            convb_s = sb("convb_s", [128, 12])
            convw_g = sb("convw_g", [128, 24, 4])
            dsk = sb("dsk", [128, 16])
            gnw = sb("gnw", [128, 128])
            spl = sb("spl", [128, 16, 32])
            ag = sb("ag", [128, 16, 32])
            ea = sb("ea", [128, 16, 32])
            dte = sb("dte", [128, 16, 32])
            cdec = sb("cdec", [128, 16, 32])
            bet = sb("bet", [128, 16, 8])
            nbet = sb("nbet", [128, 16, 8])
            bg = sb("bg", [128, 16, 8])
            self.dma('sp', convw_s[:, :, :], W['convw_s'][:, :, :], [], [('convw',)])
            self.dma('sp', convb_s[:, :], W['convb_s'][:, :], [], [('convw',)])
            self.dma('sp', convw_g[:, :, :], W['convw_g'][:, :, :], [], [('convw',)])
            self.dma('sp', dsk[:, :], W['dsk'][:, :], [], [('dsk',)])
            self.dma('sp', gnw[:, :], W['gnw'][:, :], [], [('gnw',)])
            with ExitStack() as es:
                wsm = self.sb("wsm", [128, 8, 32], BF16, es)
                bias_bc = self.sb("bias_bc", [128, 32], F32, es)
                alog = self.sb("alog", [128, 32], F32, es)
                aneg = self.sb("aneg", [128, 32], F32, es)
                smraw = self.sb("smraw", [128, 16, 32], F32, es)
                e1 = self.sb("sm_e", [128, 16, 32], F32, es)
                acum = self.sb("acum", [128, 16, 32], F32, es)
                tot = self.sb("tot", [128, 16, 32], F32, es)
                tmp = self.sb("sm_tmp", [128, 16, 32], F32, es)
                self.dma('pool', wsm[:, :, 0:16], Win[:, 2560:2576].rearrange("(c p) n -> p c n", p=128), [], [('wsm', 0)])
                self.dma('pool', wsm[:, :, 16:32], Win[:, 6672:6688].rearrange("(c p) n -> p c n", p=128), [], [('wsm', 1)])
                self.dma('sp', bias_bc[:, :], W['bias_bc'][:, :], [], [('bias_bc',)])
                self.dma('sp', alog[:, :], W['alog_bc'][:, :], [], [('alog',)])
                self.act(aneg[:, :], alog[:, :], AF.Exp, [('alog',)], [('aneg',)])
                self.ts('dve', aneg[:, :], aneg[:, :], -1.0, None, ALU.mult, None, [('aneg',)], [('aneg',)])
                for tt in range(16):
                    ps, pr = self.next_ps()
                    self.mm(ps[:, 0:32], [(H[:, kc, tt * 128:(tt + 1) * 128], wsm[:, kc, :]) for kc in range(8)],
                            [('wsm', 0), ('wsm', 1)] + [('H', kc, tt) for kc in range(8)], [pr])
                    self.tt('dve', smraw[:, tt, :], ps[:, 0:32], bias_bc[:, :], ALU.add, [pr, ('bias_bc',)], [('smraw',)])
                f2 = lambda t: t[:, :, :].rearrange("p a b -> p (a b)")
                self.act(f2(e1), f2(smraw), AF.Exp, [('smraw',)], [('sm_e',)])
                self.act(f2(spl), f2(e1), AF.Ln, [('sm_e',)], [('spl',)], bias=cc[:, 2:3])
                self.ts('dve', tmp[:, :, 16:24], e1[:, :, 16:24], 1.0, None, ALU.add, None, [('sm_e',)], [('sm_tmp',)])
                self.s.op('dve', lambda e: e.reciprocal(tmp[:, :, 16:24], tmp[:, :, 16:24]), [('sm_tmp',)], [('sm_tmp',)])
                self.tt('dve', bet[:, :, :], e1[:, :, 16:24], tmp[:, :, 16:24], ALU.mult, [('sm_e',), ('sm_tmp',)], [('bet',)])
                self.ts('dve', nbet[:, :, :], bet[:, :, :], -1.0, None, ALU.mult, None, [('bet',)], [('nbet',)])
                self.tt('dve', ag[:, :, :], spl[:, :, :], aneg[:, :].unsqueeze(1).broadcast_to([128, 16, 32]), ALU.mult,
                        [('spl',), ('aneg',)], [('ag',)])
                ps, pr = self.next_ps()
                self.mm(ps[:, :], [(self.trile_f[:, :], f2(ag))], [('ag',), ('trile',)], [pr])
                self.copy('dve', f2(acum), ps[:, :], [pr], [('acum',)])
                self.act(f2(ea), ps[:, :], AF.Exp, [pr], [('ea',)])
                ps2, pr2 = self.next_ps()
                self.mm(ps2[:, :], [(self.ones_f[:, :], f2(ag))], [('ag',), ('ones',)], [pr2])
                self.act(f2(cdec), ps2[:, :], AF.Exp, [pr2], [('cdec',)])
                self.tt('dve', f2(tot), ps2[:, :], f2(acum), ALU.subtract, [pr2, ('acum',)], [('tot',)])
                self.act(f2(dte), f2(tot), AF.Exp, [('tot',)], [('dte',)])
                self.tt('dve', bg[:, :, :], bet[:, :, :], ea[:, :, 24:32], ALU.mult, [('bet',), ('ea',)], [('bg',)])
                self.s.flush()
            P = dict(convw_s=convw_s, convb_s=convb_s, convw_g=convw_g, dsk=dsk, gnw=gnw, spl=spl, ag=ag,
                     ea=ea, dte=dte, cdec=cdec, bet=bet, nbet=nbet, bg=bg)
            if os.environ.get("MIX_STOP") == "prelude":
                return
            MIXSEL = os.environ.get("MIX_SEL", "ssd,gdn").split(",")
            if 'ssd' in MIXSEL:
                for g in range(2):
                    self.ssd_group(g, P)
            if 'gdn' in MIXSEL:
                for hg in range(2):
                    self.gdn_group(hg, P)

    def ssd_group(self, g, P):
        X, H = self.X, self.H
        Win = self.W['a_w_in']
        Wout = self.W['a_w_out']
        cc = self.cc
        with ExitStack() as es0:
            xs_tok = self.sb("xs_tok", [128, 16, 512], BF16, es0)
            siluz = self.sb("siluz", [128, 16, 512], BF16, es0)
            BT = self.sb("BT", [128, L], BF16, es0)
            CT = self.sb("CT", [128, L], BF16, es0)
            B_tok = self.sb("B_tok", [128, 16, 128], BF16, es0)
            with ExitStack() as es:
                R = dict(w=self.ring("cw", 3, [128, 8, 128], BF16, es),
                         raw=self.ring("craw", 2, [128, L + 3], F32, es),
                         acc=self.ring("cacc", 1, [128, L], F32, es))
                xsb = self.ring("xsb", 2, [128, L], BF16, es)
                wz = self.sb("wz", [128, 8, 512], BF16, es)
                self.dma('pool', wz[:, :, :], Win[:, g * 512:(g + 1) * 512].rearrange("(c p) n -> p c n", p=128), [], [('wz',)])
                self._zero_halo(R, es)
                tiles = []
                for j in range(4):
                    ti = g * 4 + j
                    xb, xr = xsb()
                    tiles.append(dict(col=1024 + ti * 128, cw=P['convw_s'][:, ti, :], cb=P['convb_s'][:, ti:ti + 1],
                                      silu_out=xb[:, :], silu_reg=xr,
                                      post=(lambda o, r, xb=xb, xr=xr, j=j: self.to_tok(xb, xr, xs_tok, 'xs_tok', j * 128))))
                ti = 8 + g
                tiles.append(dict(col=1024 + ti * 128, cw=P['convw_s'][:, ti, :], cb=P['convb_s'][:, ti:ti + 1],
                                  silu_out=BT[:, :], silu_reg=('BT',),
                                  post=(lambda o, r: self.to_tok(BT, ('BT',), B_tok, 'B_tok', 0))))
                ti = 10 + g
                tiles.append(dict(col=1024 + ti * 128, cw=P['convw_s'][:, ti, :], cb=P['convb_s'][:, ti:ti + 1],
                                  silu_out=CT[:, :], silu_reg=('CT',)))
                self.conv_pipeline(tiles, R)
                for tt in range(16):
                    ps, pr = self.next_ps()
                    self.mm(ps[:, :], [(H[:, kc, tt * 128:(tt + 1) * 128], wz[:, kc, :]) for kc in range(8)],
                            [('wz',)] + [('H', kc, tt) for kc in range(8)], [pr])
                    self.act(siluz[:, tt, :], ps[:, :], AF.Silu, [pr], [('siluz', tt)])
                self.s.flush()
            if os.environ.get("MIX_STOP") == "ssd1":
                return
            yT = self.sb("yT", [128, 4, L], BF16, es0)
            with ExitStack() as es:
                ring = lambda n, k, sh, dt=F32: self.ring(n, k, sh, dt, es)
                cbm_r = ring("cbm", 2, [128, 128])
                lseg_r = ring("lseg", 2, [128, 4, 128])
                LT_r = ring("LT", 2, [128, 4, 128])
                MT_r = ring("MT", 2, [128, 8, 128], BF16)
                xdt_r = ring("xdt", 2, [128, 512], BF16)
                xdd_r = ring("xdd", 2, [128, 512], BF16)
                t1_r = ring("t1", 1, [128, 512])
                t3_r = ring("t3", 1, [128, 512])
                yg_r = ring("yg", 2, [128, 512])
                junk_r = ring("junk", 1, [128, 512], BF16)
                yn_r = ring("ynb", 2, [128, 512], BF16)
                sm_r = ring("ssm", 2, [128, 4])
                S = self.sb("ssdS", [128, 512], F32, es)
                Sb = self.sb("ssdSb", [128, 512], BF16, es)
                snw = self.sb("snw", [128, 512], F32, es)
                self.dma('sp', snw[:, :], self.W['snw'][:, g * 512:(g + 1) * 512], [], [('snw',)])
                hs = slice(g * 8, (g + 1) * 8)
                v8 = lambda ap: ap.rearrange("p (h d) -> p h d", h=8)
                bc8 = lambda ap: ap.unsqueeze(2).broadcast_to([128, 8, 64])
                sA = {}
                sB = {}

                def ssd_a(tt):
                    ts_ = slice(tt * 128, (tt + 1) * 128)
                    ps_cb, pcr = self.next_ps()
                    self.mm(ps_cb[:, 0:128], [(BT[:, ts_], CT[:, ts_])], [('BT',), ('CT',)], [pcr])
                    cbm, cbr = cbm_r()
                    self.tt('dve', cbm[:, :], ps_cb[:, 0:128], self.trile_f[:, :], ALU.mult, [pcr, ('trile',)], [cbr])
                    MT, mr = MT_r()
                    halves = []
                    for half in range(2):
                        lseg, lr = lseg_r()
                        ps_s, psr = self.next_ps()
                        for i in range(4):
                            hh = g * 8 + half * 4 + i
                            self.ts('dve', r32(lseg[:, i, :]), self.gt_f[:, :], P['ag'][:, tt, hh:hh + 1], None,
                                    ALU.mult, None, [('gt',), ('ag',)], [(lr, i)])
                            self.mm(ps_s[:, i * 128:(i + 1) * 128], [(r32(lseg[:, i, :]), r32(self.trile_r[:, :]))],
                                    [(lr, i), ('trile_r',)], [psr])
                        halves.append((ps_s, psr))
                    for half, (ps_s, psr) in enumerate(halves):
                        LT, ltr = LT_r()
                        self.act(LT[:, :, :].rearrange("p a b -> p (a b)"), ps_s[:, :], AF.Exp, [psr], [ltr])
                        self.tt('dve', MT[:, half * 4:(half + 1) * 4, :], LT[:, :, :],
                                cbm[:, :].unsqueeze(1).broadcast_to([128, 4, 128]), ALU.mult, [ltr, cbr], [(mr, half)])
                    xdt, xdr = xdt_r()
                    self.tt('dve', v8(xdt[:, :]), v8(xs_tok[:, tt, :]), bc8(P['spl'][:, tt, hs]), ALU.mult,
                            [('xs_tok', tt), ('spl',)], [xdr])
                    xdd, xddr = xdd_r()
                    self.tt('pool', v8(xdd[:, :]), v8(xdt[:, :]), bc8(P['dte'][:, tt, hs]), ALU.mult,
                            [xdr, ('dte',)], [xddr])
                    sA[tt] = dict(MT=MT, mr=mr, xdt=xdt, xdr=xdr, xdd=xdd, xddr=xddr, ts_=ts_)

                def ssd_b(tt):
                    d = sA.pop(tt)
                    MT, mr, xdt, xdr, xdd, xddr, ts_ = d['MT'], d['mr'], d['xdt'], d['xdr'], d['xdd'], d['xddr'], d['ts_']
                    ps_y, pyr = self.next_ps()
                    for i in range(8):
                        self.mm(ps_y[:, i * 64:(i + 1) * 64], [(MT[:, i, :], xdt[:, i * 64:(i + 1) * 64])],
                                [(mr, i // 4), xdr], [pyr])
                    t1, t1r = t1_r()
                    if tt > 0:
                        ps_o, por = self.next_ps()
                        self.mm(ps_o[:, :], [(CT[:, ts_], Sb[:, :])], [('CT',), ('ssdSb',)], [por])
                        self.tt('dve', v8(t1[:, :]), v8(ps_o[:, :]), bc8(P['ea'][:, tt, hs]), ALU.mult, [por, ('ea',)], [t1r])
                        self.tt('dve', t1[:, :], t1[:, :], ps_y[:, :], ALU.add, [t1r, pyr], [t1r])
                    else:
                        self.copy('dve', t1[:, :], ps_y[:, :], [pyr], [t1r])
                    t3, t3r = t3_r()
                    self.tt('pool', v8(t3[:, :]), v8(xs_tok[:, tt, :]), bc8(P['dsk'][:, hs]), ALU.mult,
                            [('xs_tok', tt), ('dsk',)], [t3r])
                    self.tt('dve', t3[:, :], t3[:, :], t1[:, :], ALU.add, [t3r, t1r], [t3r])
                    yg, ygr = yg_r()
                    self.tt('pool', yg[:, :], t3[:, :], siluz[:, tt, :], ALU.mult, [t3r, ('siluz', tt)], [ygr])
                    sm, smr = sm_r()
                    junk, jr = junk_r()
                    self.act(junk[:, :], yg[:, :], AF.Square, [ygr], [jr, (smr, 0)], accum_out=sm[:, 0:1])
                    if tt < 15:
                        ps_st, pstr = self.next_ps()
                        self.mm(ps_st[:, :], [(B_tok[:, tt, :], xdd[:, :])], [('B_tok', tt), xddr], [pstr])
                        if tt == 0:
                            self.copy('dve', S[:, :], ps_st[:, :], [pstr], [('ssdS',)])
                        else:
                            self.tt('dve', v8(S[:, :]), v8(S[:, :]), bc8(P['cdec'][:, tt, hs]), ALU.mult,
                                    [('ssdS',), ('cdec',)], [('ssdS',)])
                            self.tt('dve', S[:, :], S[:, :], ps_st[:, :], ALU.add, [('ssdS',), pstr], [('ssdS',)])
                        self.copy('act', Sb[:, :], S[:, :], [('ssdS',)], [('ssdSb',)])
                    sB[tt] = dict(yg=yg, ygr=ygr, sm=sm, smr=smr, ts_=ts_)

                def ssd_c(tt):
                    d = sB.pop(tt)
                    yg, ygr, sm, smr, ts_ = d['yg'], d['ygr'], d['sm'], d['smr'], d['ts_']
                    self.act(sm[:, 1:2], sm[:, 0:1], AF.Ln, [(smr, 0)], [(smr, 1)], bias=cc[:, 0:1], scale=1.0 / 512)
                    self.act(sm[:, 2:3], sm[:, 1:2], AF.Exp, [(smr, 1)], [(smr, 2)], scale=-0.5)
                    yn, ynr = yn_r()
                    self.stt(yn[:, :], yg[:, :], sm[:, 2:3], snw[:, :], ALU.mult, ALU.mult,
                             [ygr, (smr, 2), ('snw',)], [ynr])
                    ps_t, ptr_ = self.next_ps()
                    psb = ps_t[:, :].bitcast(BF16)
                    for j in range(4):
                        self.tr(psb[:, j * 128:(j + 1) * 128], yn[:, j * 128:(j + 1) * 128], self.ident_b[:, :],
                                [ynr, ('ident_b',)], [ptr_])
                    self.copy('act', yT[:, 0:4, ts_], psb[:, 0:512].rearrange("p (c t) -> p c t", c=4), [ptr_],
                              [('yT', j, tt) for j in range(4)])

                for t in range(18):
                    if t < 16:
                        ssd_a(t)
                    if 1 <= t <= 16:
                        ssd_b(t - 1)
                    if t >= 2:
                        ssd_c(t - 2)
                self.s.flush()
            with ExitStack() as es:
                wo = self.sb("wo_s", [128, 4, D], BF16, es)
                self.dma('pool', wo[:, :, :], Wout[g * 512:(g + 1) * 512, :].rearrange("(c p) n -> p c n", p=128), [], [('wo_s',)])
                for tb in range(4):
                    t0, t1_ = tb * 512, (tb + 1) * 512
                    for dt in range(8):
                        ps, pr = self.next_ps()
                        self.mm(ps[:, :], [(wo[:, j, dt * 128:(dt + 1) * 128], yT[:, j, t0:t1_]) for j in range(4)],
                                [('wo_s',)] + [('yT', j, tt) for j in range(4) for tt in range(tb * 4, tb * 4 + 4)], [pr])
                        self.tt('dve', X[:, dt, t0:t1_], ps[:, :], X[:, dt, t0:t1_], ALU.add,
                                [pr] + XR(dt, t0, t1_), XR(dt, t0, t1_))
                self.s.flush()

    def _zero_halo(self, R, es):
        for i, t in enumerate(R['raw'].tiles):
            self.s.op('pool', (lambda e, t=t: e.memset(t[:, 0:3], 0.0)), [], [("craw", i)])

    def gdn_group(self, hg, P):
        X, H = self.X, self.H
        Win = self.W['a_w_in']
        Wout = self.W['a_w_out']
        cc = self.cc
        QOFF = 2576
        with ExitStack() as es0:
            siluzg = self.sb("siluzg", [128, 16, 512], BF16, es0)
            ogT = self.sb("ogT", [128, 4, L], BF16, es0)
            with ExitStack() as es:
                wzg = self.sb("wzg", [128, 8, 512], BF16, es)
                c0 = 5648 + hg * 512
                self.dma('pool', wzg[:, :, :], Win[:, c0:c0 + 512].rearrange("(c p) n -> p c n", p=128), [], [('wzg',)])
                for tt in range(16):
                    ps, pr = self.next_ps()
                    self.mm(ps[:, :], [(H[:, kc, tt * 128:(tt + 1) * 128], wzg[:, kc, :]) for kc in range(8)],
                            [('wzg',)] + [('H', kc, tt) for kc in range(8)], [pr])
                    self.act(siluzg[:, tt, :], ps[:, :], AF.Silu, [pr], [('siluzg', tt)])
                self.s.flush()
            for i in range(4):
                h = hg * 4 + i
                with ExitStack() as esh:
                    qTn = self.sb("qTn", [128, L], BF16, esh)
                    kTn = self.sb("kTn", [128, L], BF16, esh)
                    k_tok = self.sb("k_tok", [128, 16, 128], BF16, esh)
                    v_tok = self.sb("v_tok", [128, 16, 128], BF16, esh)
                    with ExitStack() as es:
                        R = dict(w=self.ring("cw", 2, [128, 8, 128], BF16, es),
                                 raw=self.ring("craw", 2, [128, L + 3], F32, es),
                                 acc=self.ring("cacc", 2, [128, L], F32, es))
                        kvb = ogT[:, i, :]
                        sqq4 = [self.sb(f"gsqq{j}", [128, 512], BF16, es) for j in range(4)]
                        self._zero_halo(R, es)
                        def l2post(which, dstT, dn):
                            def post(acc, ar):
                                pss = []
                                for tb in range(4):
                                    t0, t1 = tb * 512, (tb + 1) * 512
                                    self.act(sqq4[tb][:, :], acc[:, t0:t1], AF.Square, [ar], [('gsqq', tb)])
                                for tb in range(4):
                                    ps, pr = self.next_ps()
                                    self.mm(ps[:, :], [(self.ones_b[:, :], sqq4[tb][:, :])], [('gsqq', tb), ('ones_b',)], [pr])
                                    pss.append((ps, pr))
                                for tb in range(4):
                                    ps, pr = pss[tb]
                                    self.act(ps[:, :], ps[:, :], AF.Ln, [pr], [pr], bias=cc[:, 0:1])
                                for tb in range(4):
                                    ps, pr = pss[tb]
                                    self.act(ps[:, :], ps[:, :], AF.Exp, [pr], [pr],
                                             bias=cc[:, 4:5] if which == 0 else cc[:, 3:4], scale=-0.5)
                                for tb in range(4):
                                    ps, pr = pss[tb]
                                    t0, t1 = tb * 512, (tb + 1) * 512
                                    self.tt('dve', dstT[:, t0:t1], acc[:, t0:t1], ps[:, :], ALU.mult, [ar, pr], [(dn, tb)])
                                if which == 1:
                                    self.to_tok(kTn, None, k_tok, 'k_tok', 0, src_regs=[('kTn', tb) for tb in range(4)])
                            return post
                        tiles = []
                        for which, dstT, dn in ((0, qTn, 'qTn'), (1, kTn, 'kTn')):
                            ti = which * 8 + h
                            tiles.append(dict(col=QOFF + ti * 128, cw=P['convw_g'][:, ti, :], cb=None, post=l2post(which, dstT, dn)))
                        ti = 16 + h
                        tiles.append(dict(col=QOFF + ti * 128, cw=P['convw_g'][:, ti, :], cb=None, silu_out=kvb, silu_reg=('kvb',),
                                          post=(lambda o, r: self.to_tok(kvb, ('kvb',), v_tok, 'v_tok', 0))))
                        self.conv_pipeline(tiles, R)
                        self.s.flush()
                    with ExitStack() as es:
                        ub = self.sb("g_ub", [128, 16, 128], F32, es)
                        wT = self.sb("g_wT", [128, 16, 128], BF16, es)
                        qkT = self.sb("g_qkT", [128, 16, 128], BF16, es)
                        NCTX = int(os.environ.get("GDN_NCTX", "4"))
                        ctxs = []
                        for ci in range(NCTX):
                            t = lambda n, sh=[128, 128], dt=F32: self.sb(f"g_{n}{ci}", sh, dt, es)
                            ctxs.append(dict(gmask=t("gmask"), DLT=t("DLT", [128, 256]), PPa=t("PPa", [128, 384]),
                                             PPb=t("PPb", [128, 384]), vb=t("vb"), kb2=t("kb2"), ci=ci))
                        gcol = lambda tt: P['ag'][:, tt, 24 + h:25 + h]
                        GPH = int(os.environ.get("GDN_PH", "3"))

                        def p2_group(grp):
                            tts = tuple(range(grp * NCTX, (grp + 1) * NCTX))
                            rgs = [(lambda n, ci=c['ci']: ('g_' + n, ci)) for c in ctxs]
                            ps1s, ps3s, ps4s = [], [], []
                            for c, tt, rg in zip(ctxs, tts, rgs):
                                self.ts('dve', r32(c['gmask'][:, :]), self.gt_f[:, :], gcol(tt), None, ALU.mult, None,
                                        [('gt',), ('ag',)], [rg('gmask')])
                                self.act(r32(c['vb'][:, :]), v_tok[:, tt, :], AF.Identity, [('v_tok', tt), ('bet',)], [rg('vb')],
                                         scale=P['bet'][:, tt, h:h + 1])
                                self.act(r32(c['kb2'][:, :]), k_tok[:, tt, :], AF.Identity, [('k_tok', tt), ('bg',)], [rg('kb2')],
                                         scale=P['bg'][:, tt, h:h + 1])
                            for c, tt, rg in zip(ctxs, tts, rgs):
                                ts_ = slice(tt * 128, (tt + 1) * 128)
                                ps1, p1r = self.next_ps()
                                self.mm(ps1[:, 0:128], [(r32(self.trile_r[:, :]), r32(c['gmask'][:, :]))], [rg('gmask'), ('trile_r',)], [p1r])
                                self.mm(ps1[:, 128:256], [(r32(c['gmask'][:, :]), r32(self.trile_r[:, :]))], [rg('gmask'), ('trile_r',)], [p1r])
                                self.mm(ps1[:, 256:384], [(kTn[:, ts_], kTn[:, ts_])], [('kTn', tt // 4)], [p1r])
                                self.mm(ps1[:, 384:512], [(kTn[:, ts_], qTn[:, ts_])], [('kTn', tt // 4), ('qTn', tt // 4)], [p1r])
                                ps1s.append((ps1, p1r))
                            yield
                            for c, tt, rg, (ps1, p1r) in zip(ctxs, tts, rgs, ps1s):
                                self.act(c['DLT'][:, :], ps1[:, 0:256], AF.Exp, [p1r], [rg('DLT')])
                                self.tt('dve', c['DLT'][:, :], c['DLT'][:, :], self.gtle_f[:, :], ALU.mult,
                                        [rg('DLT'), ('gtle',)], [rg('DLT')])
                            for c, tt, rg, (ps1, p1r) in zip(ctxs, tts, rgs, ps1s):
                                self.stt(r32(c['PPa'][:, 0:128]), ps1[:, 256:384], P['nbet'][:, tt, h:h + 1], c['DLT'][:, 0:128],
                                         ALU.mult, ALU.mult, [p1r, ('nbet',), rg('DLT')], [rg('PPa')])
                                self.tt('dve', qkT[:, tt, :], ps1[:, 384:512], c['DLT'][:, 128:256], ALU.mult,
                                        [p1r, rg('DLT')], [('g_qkT', tt)])
                            yield
                            for c, tt, rg in zip(ctxs, tts, rgs):
                                ps4, p4r = self.next_ps()
                                self.tr(ps4[:, 0:128], c['PPa'][:, 0:128], self.ident_f[:, :], [rg('PPa'), ('ident',)], [p4r])
                                ps4s.append((ps4, p4r))
                            for c, tt, rg, (ps4, p4r) in zip(ctxs, tts, rgs, ps4s):
                                self.copy('act', r32(c['PPa'][:, 128:256]), ps4[:, 0:128], [p4r], [rg('PPa')])
                                self.tt('dve', r32(c['PPa'][:, 256:384]), c['PPa'][:, 128:256], self.ident_f[:, :], ALU.add,
                                        [rg('PPa'), ('ident',)], [rg('PPa')])
                            cur, nxt = 'PPa', 'PPb'
                            for k in range(6):
                                psas = []
                                for c, rg in zip(ctxs, rgs):
                                    psa, par = self.next_ps()
                                    pp = c[cur]
                                    self.mm(psa[:, 0:128], [(r32(pp[:, 128:256]), r32(pp[:, 0:128]))], [rg(cur)], [par])
                                    if k == 0:
                                        self.mm(psa[:, 128:256], [(r32(pp[:, 0:128]), r32(pp[:, 128:256]))], [rg(cur)], [par])
                                    elif k <= 4:
                                        self.mm(psa[:, 128:384], [(r32(pp[:, 0:128]), r32(pp[:, 128:384]))], [rg(cur)], [par])
                                    else:
                                        self.mm(psa[:, 256:384], [(r32(pp[:, 0:128]), r32(pp[:, 256:384]))], [rg(cur)], [par])
                                    psas.append((psa, par))
                                yield
                                for c, rg, (psa, par) in zip(ctxs, rgs, psas):
                                    pp, pn = c[cur], c[nxt]
                                    w = 256 if k < 5 else 128
                                    self.copy('act', r32(pn[:, 0:w]), psa[:, 0:w], [par], [rg(nxt)])
                                    if k == 0:
                                        self.copy('dve', r32(pn[:, 256:384]), pp[:, 256:384], [rg(cur)], [rg(nxt)])
                                    else:
                                        self.tt('dve', r32(pn[:, 256:384]), pp[:, 256:384], psa[:, 256:384], ALU.add,
                                                [rg(cur), par], [rg(nxt)])
                                cur, nxt = nxt, cur
                                yield
                            psbs = []
                            for c, rg in zip(ctxs, rgs):
                                psb_, pbr = self.next_ps()
                                pp = c[cur]
                                self.mm(psb_[:, 0:128], [(r32(pp[:, 0:128]), r32(pp[:, 256:384]))], [rg(cur)], [pbr])
                                psbs.append((psb_, pbr))
                            yield
                            for c, rg, (psb_, pbr) in zip(ctxs, rgs, psbs):
                                pp = c[cur]
                                self.tt('dve', r32(pp[:, 256:384]), pp[:, 256:384], psb_[:, 0:128], ALU.add, [rg(cur), pbr], [rg(cur)])
                            yield
                            TT = cur
                            psus = []
                            for c, tt, rg in zip(ctxs, tts, rgs):
                                psu, pur = self.next_ps()
                                self.mm(psu[:, 0:128], [(r32(c[TT][:, 256:384]), r32(c['vb'][:, :]))], [rg(TT), rg('vb')], [pur])
                                self.mm(psu[:, 128:256], [(r32(c['kb2'][:, :]), r32(c[TT][:, 256:384]))], [rg(TT), rg('kb2')], [pur])
                                psus.append((psu, pur))
                            for c, tt, rg, (psu, pur) in zip(ctxs, tts, rgs, psus):
                                self.copy('act', ub[:, tt, :], psu[:, 0:128], [pur], [('g_ub', tt)])
                                self.copy('dve', wT[:, tt, :], psu[:, 128:256], [pur], [('g_wT', tt)])
                        S = self.sb("g_S", [128, 128], F32, es)
                        Sb = self.sb("g_Sb", [128, 128], BF16, es)
                        u_r = self.ring("g_u", 2, [128, 128], BF16, es)
                        kd_r = self.ring("g_kd", 2, [128, 128], BF16, es)
                        t_r = self.ring("g_t", 2, [128, 128], F32, es)
                        o_r = self.ring("g_o", 2, [128, 128], F32, es)
                        on_r = self.ring("g_on", 2, [128, 128], F32, es)
                        og_r = self.ring("g_og", 2, [128, 128], BF16, es)
                        jk_r = self.ring("g_jk", 1, [128, 128], BF16, es)
                        sm_r = self.ring("g_sm", 2, [128, 4], F32, es)
                        sst = {}

                        def scan_a(tt):
                            ts_ = slice(tt * 128, (tt + 1) * 128)
                            u, ur = u_r()
                            d = sst[tt] = dict(u=u, ur=ur)
                            if tt > 0:
                                ps_ws, pwr = self.next_ps()
                                self.mm(ps_ws[:, 0:128], [(wT[:, tt, :], Sb[:, :])], [('g_wT', tt), ('g_Sb',)], [pwr])
                                self.tt('dve', u[:, :], ub[:, tt, :], ps_ws[:, 0:128], ALU.subtract, [('g_ub', tt), pwr], [ur])
                            else:
                                self.copy('dve', u[:, :], ub[:, tt, :], [('g_ub', tt)], [ur])

                        def scan_b(tt):
                            ts_ = slice(tt * 128, (tt + 1) * 128)
                            d = sst.pop(tt)
                            u, ur = d['u'], d['ur']
                            if tt > 0:
                                ps_o1, po1r = self.next_ps()
                                self.mm(ps_o1[:, 0:128], [(qTn[:, ts_], Sb[:, :])], [('qTn', tt // 4), ('g_Sb',)], [po1r])
                            ps_o2, po2r = self.next_ps()
                            self.mm(ps_o2[:, 0:128], [(qkT[:, tt, :], u[:, :])], [('g_qkT', tt), ur], [po2r])
                            o, orr = o_r()
                            if tt > 0:
                                t, tr_ = t_r()
                                self.act(t[:, :], ps_o1[:, 0:128], AF.Identity, [po1r, ('ea',)], [tr_],
                                         scale=P['ea'][:, tt, 24 + h:25 + h])
                                self.tt('dve', o[:, :], t[:, :], ps_o2[:, 0:128], ALU.add, [tr_, po2r], [orr])
                            else:
                                self.copy('dve', o[:, :], ps_o2[:, 0:128], [po2r], [orr])
                            if tt < 15:
                                kd, kdr = kd_r()
                                self.act(kd[:, :], k_tok[:, tt, :], AF.Identity, [('k_tok', tt), ('dte',)], [kdr],
                                         scale=P['dte'][:, tt, 24 + h:25 + h])
                                ps_sk, pskr = self.next_ps()
                                self.mm(ps_sk[:, 0:128], [(kd[:, :], u[:, :])], [kdr, ur], [pskr])
                                if tt == 0:
                                    self.copy('dve', S[:, :], ps_sk[:, 0:128], [pskr], [('g_S',)])
                                else:
                                    self.stt(S[:, :], S[:, :], P['cdec'][:, tt, 24 + h:25 + h], ps_sk[:, 0:128], ALU.mult, ALU.add,
                                             [('g_S',), ('cdec',), pskr], [('g_S',)])
                                self.copy('act', Sb[:, :], S[:, :], [('g_S',)], [('g_Sb',)])
                            sm, smr = sm_r()
                            jk, jr = jk_r()
                            self.act(jk[:, :], o[:, :], AF.Square, [orr], [jr, (smr, 0)], accum_out=sm[:, 0:1])
                            self.act(sm[:, 1:2], sm[:, 0:1], AF.Ln, [(smr, 0)], [(smr, 1)], bias=cc[:, 0:1], scale=1.0 / 128)
                            self.act(sm[:, 2:3], sm[:, 1:2], AF.Exp, [(smr, 1)], [(smr, 2)], scale=-0.5)
                            on, onr = on_r()
                            self.stt(on[:, :], o[:, :], sm[:, 2:3], P['gnw'][:, :], ALU.mult, ALU.mult, [orr, (smr, 2), ('gnw',)], [onr])
                            og, ogr = og_r()
                            self.tt('dve', og[:, :], on[:, :], siluzg[:, tt, i * 128:(i + 1) * 128], ALU.mult,
                                    [onr, ('siluzg', tt)], [ogr])
                            ps_t, ptr_ = self.next_ps()
                            psb = ps_t[:, :].bitcast(BF16)
                            self.tr(psb[:, 0:128], og[:, :], self.ident_b[:, :], [ogr, ('ident_b',)], [ptr_])
                            self.copy('act', ogT[:, i, ts_], psb[:, 0:128], [ptr_], [('ogT', i, tt)])

                        queue = []
                        for grp in range(16 // NCTX if GPH >= 2 else 0):
                            for _ in p2_group(grp):
                                if queue:
                                    queue.pop(0)()
                            if GPH >= 3:
                                for tt in range(grp * NCTX, (grp + 1) * NCTX):
                                    queue.append(lambda tt=tt: scan_a(tt))
                                    queue.append(lambda tt=tt: scan_b(tt))
                        while queue:
                            queue.pop(0)()
                        self.s.flush()
            with ExitStack() as es:
                wo = self.sb("wo_g", [128, 4, D], BF16, es)
                r0 = 1024 + hg * 512
                self.dma('pool', wo[:, :, :], Wout[r0:r0 + 512, :].rearrange("(c p) n -> p c n", p=128), [], [('wo_g',)])
                for tb in range(4):
                    t0, t1_ = tb * 512, (tb + 1) * 512
                    for dt in range(8):
                        ps, pr = self.next_ps()
                        self.mm(ps[:, :], [(wo[:, j, dt * 128:(dt + 1) * 128], ogT[:, j, t0:t1_]) for j in range(4)],
                                [('wo_g',)] + [('ogT', j, tt) for j in range(4) for tt in range(tb * 4, tb * 4 + 4)], [pr])
                        self.tt('dve', X[:, dt, t0:t1_], ps[:, :], X[:, dt, t0:t1_], ALU.add,
                                [pr] + XR(dt, t0, t1_), XR(dt, t0, t1_))
                self.s.flush()


_CACHE = {}


def consts():
    i = np.arange(128)
    c = {}
    c['c_ident'] = np.eye(128, dtype=np.float32)
    c['c_ones'] = np.ones((128, 128), np.float32)
    c['c_uincl'] = (i[:, None] >= i[None, :]).astype(np.float32)
    c['c_trile'] = (i[:, None] <= i[None, :]).astype(np.float32)
    c['c_gt'] = (i[:, None] > i[None, :]).astype(np.float32)
    blk = np.zeros((128, 128), np.float32)
    blk[:64, :64] = 1
    blk[64:, 64:] = 1
    c['c_blk'] = blk
    col = np.arange(512)
    c['c_mask4'] = np.stack([(col[None, :] > (i[:, None] + 128 * k)) for k in range(4)], axis=1).astype(np.float32)
    return c


def fm_cols(v):
    return np.ascontiguousarray(np.asarray(v, np.float32).reshape(8, 128).T)


def make_inputs(inp, nseq, ncores):
    f = lambda a: np.ascontiguousarray(np.asarray(a, dtype=np.float32))
    shared = consts()
    normw = np.stack([fm_cols(inp['a_norm_w'][0]), fm_cols(inp['mlp_norm_w'][0]),
                      fm_cols(inp['c_norm_w'][0]), fm_cols(inp['mlp_norm_w'][1])], axis=1)
    shared['normw'] = np.ascontiguousarray(normw)
    shared['mlp_w1'] = f(inp['mlp_w1'])
    shared['mlp_w2'] = f(inp['mlp_w2'])
    shared['c_w_qkv'] = f(inp['c_w_qkv'][0])
    shared['c_w_o'] = f(inp['c_w_o'][0])
    shared['qkw'] = np.ascontiguousarray(np.stack([np.tile(f(inp['c_q_norm_w'][0]), 2),
                                                   np.tile(f(inp['c_k_norm_w'][0]), 2)], axis=1))
    shared['a_w_in'] = f(inp['a_w_in'][0])
    shared['a_w_out'] = f(inp['a_w_out'][0])
    bc = lambda v: np.ascontiguousarray(np.broadcast_to(np.asarray(v, np.float32)[None, :], (128, len(v))))
    cws = f(inp['ssd_conv_w'][0])
    shared['convw_s'] = np.ascontiguousarray(cws.reshape(4, 12, 128).transpose(2, 1, 0))
    shared['convb_s'] = np.ascontiguousarray(f(inp['ssd_conv_b'][0]).reshape(12, 128).T)
    cwg = f(inp['gdn_conv_w'][0])
    shared['convw_g'] = np.ascontiguousarray(cwg.reshape(4, 24, 128).transpose(2, 1, 0))
    shared['dsk'] = bc(f(inp['ssd_d_skip'][0]))
    shared['snw'] = bc(f(inp['ssd_norm_w'][0]))
    shared['gnw'] = bc(f(inp['gdn_norm_w'][0]))
    z8 = np.zeros(8, np.float32)
    shared['bias_bc'] = bc(np.concatenate([f(inp['ssd_dt_bias'][0]), z8, f(inp['gdn_dt_bias'][0])]))
    shared['alog_bc'] = bc(np.concatenate([f(inp['ssd_a_log'][0]), z8, f(inp['gdn_a_log'][0])]))
    x = f(inp['x'])
    maps = []
    for c in range(ncores):
        m = dict(shared)
        m['x'] = np.ascontiguousarray(x[c * nseq:(c + 1) * nseq])
        maps.append(m)
    return maps


ALL_STAGES = ('mix0', 'mlp0', 'attn', 'mlp1')


def run(inp, nseq=2, ncores=8, stages=ALL_STAGES, trace=False):
    key = (nseq, tuple(stages))
    if key not in _CACHE:
        _CACHE[key] = Builder(nseq, stages).build()
    nc = _CACHE[key]
    maps = make_inputs(inp, nseq, ncores)
    res = run_bass_kernel_spmd(nc, maps, core_ids=list(range(ncores)), trace=trace)
    outs = [r["out"] for r in res.results]
    return np.concatenate(outs, axis=0), res


def kernel(**inputs):
    out, _ = run(inputs)
    return out.astype(np.float32)
```

```python
import os
import numpy as np
from contextlib import ExitStack
import concourse.bass as bass
import concourse.mybir as mybir
from concourse.bass_utils import run_bass_kernel_spmd

F32 = mybir.dt.float32
BF16 = mybir.dt.bfloat16
F32R = mybir.dt.float32r


def r32(ap):
    return ap.bitcast(F32R)
AF = mybir.ActivationFunctionType
ALU = mybir.AluOpType

L = 2048
D = 1024
EPS = 1e-6
EPOCH = 30000
NDSEM = 8


class Sched:
    CE = ('pe', 'act', 'dve', 'pool')

    def __init__(self, nc, esem, dsem):
        self.nc = nc
        self.esem = esem
        self.dsem = dsem
        self.cnt = {e: 0 for e in self.CE}
        self.ndma = {'sp': 0, 'pool': 0}
        self.seen = {e: {} for e in ('pe', 'act', 'dve', 'pool', 'sp')}
        self.reset()

    def reset(self):
        self.ops = {e: [] for e in ('pe', 'act', 'dve', 'pool', 'sp')}
        self.last_w = {}
        self.readers = {}

    def op(self, eng, fn, reads=(), writes=(), dma=False):
        if eng == 'pool' and not dma and os.environ.get("POOL2DVE"):
            eng = 'dve'
        self.nop_total = getattr(self, 'nop_total', 0) + 1
        cut = os.environ.get("OPCUT")
        if cut and self.nop_total > int(cut):
            return None
        if os.environ.get("OPTRACE"):
            import traceback
            fr = traceback.extract_stack(limit=4)
            print("OP", self.nop_total, eng, "dma" if dma else "", [f"{f.name}:{f.lineno}" for f in fr[:-1]])
        writes = list(writes) + [r for r in reads if r[0] in ('ps', 'psacc') and r not in writes]
        idx = len(self.ops[eng])
        tok = (eng, idx)
        deps = {}
        for r in reads:
            w = self.last_w.get(r)
            if w is not None:
                deps[w] = True
        for r in writes:
            w = self.last_w.get(r)
            if w is not None and w not in deps:
                deps[w] = False
            for t in self.readers.get(r, ()):
                if t not in deps:
                    deps[t] = False
        for r in reads:
            self.readers.setdefault(r, []).append(tok)
        for r in writes:
            self.last_w[r] = tok
            self.readers[r] = []
        self.ops[eng].append(dict(fn=fn, deps=deps, dma=dma))
        return tok

    def flush(self):
        nc = self.nc
        ops = self.ops
        need = set()
        for e, lst in ops.items():
            for i, o in enumerate(lst):
                keep = []
                for (e2, i2), raw in o['deps'].items():
                    o2 = ops[e2][i2]
                    if o2['dma']:
                        keep.append((e2, i2))
                    elif e2 == e:
                        if e == 'pe':
                            continue
                        keep.append((e2, i2))
                    else:
                        keep.append((e2, i2))
                o['keep'] = keep
                for k in keep:
                    if not ops[k[0]][k[1]]['dma']:
                        need.add(k)
        for e in self.CE:
            for i, o in enumerate(ops[e]):
                if o['dma']:
                    continue
                if (e, i) in need:
                    self.cnt[e] += 1
                    c = self.cnt[e]
                    o['sig'] = (self.esem[e][(c - 1) // EPOCH], (c - 1) % EPOCH + 1)
                else:
                    o['sig'] = None
        pending = {'sp': [], 'pool': []}
        for q in ('sp', 'pool'):
            for o in ops[q]:
                if o['dma']:
                    n = self.ndma[q]
                    self.ndma[q] += 1
                    o['sig'] = (self.dsem[q][n % NDSEM], 16 * (n // NDSEM + 1))
                    o['prewait'] = (self.dsem[q][n % NDSEM], 16 * (n // NDSEM)) if n >= NDSEM else None
                    pending[q].append(o['sig'])

        def emit(e, eng):
            seen = self.seen[e]

            def wait(s, v):
                if seen.get(s[0], 0) >= v:
                    return
                seen[s[0]] = v
                eng.wait_ge(s[1], v)

            for o in ops[e]:
                if o.get('prewait') is not None:
                    wait(*o['prewait'])
                for k in o['keep']:
                    sg = ops[k[0]][k[1]]['sig']
                    wait(*sg)
                ins = o['fn'](eng)
                if o['sig'] is not None:
                    ins.then_inc(o['sig'][0][1], 16 if o['dma'] else 1)
            if e in pending:
                for sg in pending[e][-NDSEM:]:
                    wait(*sg)

        with nc.Block() as block:
            if ops['sp']:
                @block.sync
                def _(eng):
                    emit('sp', eng)
            if ops['pe']:
                @block.tensor
                def _(eng):
                    emit('pe', eng)
            if ops['act']:
                @block.scalar
                def _(eng):
                    emit('act', eng)
            if ops['dve']:
                @block.vector
                def _(eng):
                    emit('dve', eng)
            if ops['pool']:
                @block.gpsimd
                def _(eng):
                    emit('pool', eng)
        self.reset()


def XR(c, t0, t1):
    return [('X', c, tt) for tt in range(t0 // 128, (t1 + 127) // 128)]


def HR(c, t0, t1):
    return [('H', c, tt) for tt in range(t0 // 128, (t1 + 127) // 128)]


class Builder:
    def __init__(self, nseq, stages):
        self.nseq = nseq
        self.stages = stages
        nc = bass.Bass("TRN2", target_bir_lowering=False)
        self.nc = nc
        self.es = ExitStack()
        self.dram = {}
        self._uid = 0

    def din(self, name, shape, dtype=F32):
        t = self.nc.dram_tensor(name, list(shape), dtype, kind="ExternalInput").ap()
        self.dram[name] = t
        return t

    def sb(self, name, shape, dtype, es=None):
        es = es or self.es
        return es.enter_context(self.nc.sbuf_tensor(f"{name}_u{self.uid()}", list(shape), dtype))

    def psum(self, name, shape, dtype):
        return self.es.enter_context(self.nc.psum_tensor(name, list(shape), dtype))

    def uid(self):
        self._uid += 1
        return self._uid

    def mm(self, out, pairs, reads, writes):
        pairs = list(pairs)

        def fn(pe):
            n = len(pairs)
            ins = None
            for i, (l, r) in enumerate(pairs):
                ins = pe.matmul(out, l, r, start=(i == 0), stop=(i == n - 1))
            return ins
        self.s.op('pe', fn, reads, writes)

    def mm1(self, out, l, r, start, stop, reads, writes):
        self.s.op('pe', lambda pe: pe.matmul(out, l, r, start=start, stop=stop), reads, writes)

    def asel(self, out, in_, base, cm, n, reads, writes):
        self.s.op('pool', lambda e: e.affine_select(out, in_, [[1, n]], ALU.is_gt, 0.0, base=base,
                                                    channel_multiplier=cm), reads, writes)

    def tr(self, out, in_, ident, reads, writes):
        self.s.op('pe', lambda pe: pe.transpose(out, in_, ident), reads, writes)

    def act(self, out, in_, func, reads, writes, bias=None, scale=None, accum_out=None, eng='act'):
        kw = {}
        if bias is not None:
            kw['bias'] = bias
        if scale is not None:
            kw['scale'] = scale
        if accum_out is not None:
            kw['accum_out'] = accum_out
        self.s.op('act', lambda e: e.activation(out, in_, func, **kw), reads, writes)

    def tt(self, eng, out, a, b, op, reads, writes):
        self.s.op(eng, lambda e: e.tensor_tensor(out, a, b, op), reads, writes)

    def ts(self, eng, out, a, s1, s2, op0, op1, reads, writes):
        if op1 is None:
            self.s.op(eng, lambda e: e.tensor_scalar(out, a, s1, None, op0), reads, writes)
        else:
            self.s.op(eng, lambda e: e.tensor_scalar(out, a, s1, s2, op0, op1), reads, writes)

    def stt(self, out, a, sc, b, op0, op1, reads, writes):
        self.s.op('dve', lambda e: e.scalar_tensor_tensor(out, a, sc, b, op0, op1), reads, writes)

    def copy(self, eng, out, in_, reads, writes):
        if eng == 'act':
            self.s.op('act', lambda e: e.copy(out, in_), reads, writes)
        else:
            self.s.op(eng, lambda e: e.tensor_copy(out, in_), reads, writes)

    def dma(self, q, out, in_, reads, writes):
        self.s.op(q, lambda e: e.dma_start(out=out, in_=in_), reads, writes, dma=True)

    def next_ps(self):
        i = self._psi
        self._psi = (i + 1) % len(self.psr)
        return self.psr[i], ('ps', i)

    def build(self):
        nc = self.nc
        ns = self.nseq
        x = self.din("x", [ns, L, D])
        out = nc.dram_tensor("out", [ns, L, D], F32, kind="ExternalOutput").ap()
        self.x_d, self.out_d = x, out
        d = self.din
        W = {}
        W['ident'] = d("c_ident", [128, 128])
        W['ones'] = d("c_ones", [128, 128])
        W['uincl'] = d("c_uincl", [128, 128])
        W['trile'] = d("c_trile", [128, 128])
        W['gt'] = d("c_gt", [128, 128])
        W['blk'] = d("c_blk", [128, 128])
        W['mask4'] = d("c_mask4", [128, 4, 512])
        W['normw'] = d("normw", [128, 4, 8])
        W['mlp_w1'] = d("mlp_w1", [2, D, 4096])
        W['mlp_w2'] = d("mlp_w2", [2, 4096, D])
        W['c_w_qkv'] = d("c_w_qkv", [D, 3072])
        W['c_w_o'] = d("c_w_o", [D, D])
        W['qkw'] = d("qkw", [128, 2])
        W['a_w_in'] = d("a_w_in", [D, 6688])
        W['a_w_out'] = d("a_w_out", [2048, D])
        W['convw_s'] = d("convw_s", [128, 12, 4])
        W['convb_s'] = d("convb_s", [128, 12])
        W['convw_g'] = d("convw_g", [128, 24, 4])
        W['dsk'] = d("dsk", [128, 16])
        W['snw'] = d("snw", [128, 1024])
        W['gnw'] = d("gnw", [128, 128])
        W['bias_bc'] = d("bias_bc", [128, 32])
        W['alog_bc'] = d("alog_bc", [128, 32])
        self.W = W

        es = self.es
        sems = {}
        si = [0]

        def newsem(name):
            h = es.enter_context(nc.semaphore(name))
            si[0] += 1
            return (si[0], h)
        esem = {e: [newsem(f"s_{e}{i}") for i in range(3)] for e in Sched.CE}
        dsem = {q: [newsem(f"d_{q}{i}") for i in range(NDSEM)] for q in ('sp', 'pool')}
        self.s = Sched(nc, esem, dsem)

        self.X = self.sb("X", [128, 8, L], F32)
        self.H = self.sb("H", [128, 8, L], BF16)
        self.ident_f = self.sb("ident_f", [128, 128], F32)
        self.ones_f = self.sb("ones_f", [128, 128], F32)
        self.uinclneg_f = self.sb("uinclneg_f", [128, 128], F32)
        self.uincl_f = self.sb("uincl_f", [128, 128], F32)
        self.trile_f = self.sb("trile_f", [128, 128], F32)
        self.gt_f = self.sb("gt_f", [128, 128], F32)
        self.gtle_f = self.sb("gtle_f", [128, 256], F32)
        self.ones_r = self.sb("ones_r", [128, 128], F32)
        self.trile_r = self.sb("trile_r", [128, 128], F32)
        self.ones_b = self.sb("ones_b", [128, 128], BF16)
        self.ident_b = self.sb("ident_b", [128, 128], BF16)
        self.blk_b = self.sb("blk_b", [128, 128], BF16)
        self.normw = self.sb("normw_sb", [128, 4, 8], F32)
        self.qkw = self.sb("qkw_sb", [128, 2], F32)
        self.cc = self.sb("constcols", [128, 8], F32)
        self.psr = [self.psum(f"ps{i}", [128, 512], F32) for i in range(6)]
        self.psacc = [self.psum(f"psacc{i}", [128, 512], F32) for i in range(2)]
        self._psi = 0

        self.stage_consts()
        for sq in range(ns):
            self.stage_load(sq)
            if 'mix0' in self.stages:
                self.stage_mix0()
            if 'mlp0' in self.stages:
                self.stage_mlp(0)
            if 'attn' in self.stages:
                self.stage_attn()
            if 'mlp1' in self.stages:
                self.stage_mlp(1)
            self.stage_store(sq)
        self.es.close()
        return nc

    def stage_consts(self):
        W = self.W
        q = 'sp'
        for name, t in (('ident', self.ident_f), ('ones', self.ones_f), ('uincl', self.uincl_f),
                        ('trile', self.trile_f), ('gt', self.gt_f)):
            self.dma(q, t[:, :], W[name][:, :], [], [(name,)])
        self.dma(q, self.gtle_f[:, 0:128], W['gt'][:, :], [], [('gtle',)])
        self.dma(q, self.gtle_f[:, 128:256], W['trile'][:, :], [], [('gtle',)])
        self.dma(q, self.normw[:, :, :], W['normw'][:, :, :], [], [('normw',)])
        self.dma(q, self.qkw[:, :], W['qkw'][:, :], [], [('qkw',)])
        for i, v in enumerate((EPS, float(np.log(0.125)), 1.0, 0.0, float(-0.5 * np.log(128.0)))):
            self.s.op('pool', (lambda e, i=i, v=v: e.memset(self.cc[:, i:i + 1], v)), [], [('cc', i)])
        with ExitStack() as es:
            tmp = self.sb("ctmp", [128, 128], F32, es)
            self.dma(q, tmp[:, :], W['blk'][:, :], [], [('ctmp',)])
            self.copy('dve', self.blk_b[:, :], tmp[:, :], [('ctmp',)], [('blk_b',)])
            self.copy('dve', self.ones_b[:, :], self.ones_f[:, :], [('ones',)], [('ones_b',)])
            self.copy('dve', self.ident_b[:, :], self.ident_f[:, :], [('ident',)], [('ident_b',)])
            self.ts('dve', r32(self.uinclneg_f[:, :]), self.uincl_f[:, :], -1.0, None, ALU.mult, None,
                    [('uincl',)], [('uinclneg',)])
            self.copy('dve', r32(self.ones_r[:, :]), self.ones_f[:, :], [('ones',)], [('ones_r',)])
            self.copy('dve', r32(self.trile_r[:, :]), self.trile_f[:, :], [('trile',)], [('trile_r',)])
            self.s.flush()

    def stage_load(self, sq):
        X = self.X
        with ExitStack() as es:
            stg = [self.sb(f"ldstg{i}", [128, D], F32, es) for i in range(2)]
            for tt in range(16):
                st = stg[tt % 2]
                sr = ('ldstg', tt % 2)
                self.dma('sp', st[:, :], self.x_d[sq, tt * 128:(tt + 1) * 128, :], [], [sr])
                for half in range(2):
                    ps, pr = self.next_ps()
                    for j in range(4):
                        c = half * 4 + j
                        self.tr(ps[:, j * 128:(j + 1) * 128], st[:, c * 128:(c + 1) * 128], self.ident_f[:, :],
                                [sr, ('ident',)], [pr])
                    dst = X[:, half * 4:half * 4 + 4, tt * 128:(tt + 1) * 128]
                    src = ps[:, :].rearrange("p (c t) -> p c t", c=4)
                    wr = [('X', half * 4 + j, tt) for j in range(4)]
                    self.copy('act' if half == 0 else 'dve', dst, src, [pr], wr)
            self.s.flush()

    def stage_store(self, sq):
        X = self.X
        with ExitStack() as es:
            stg = [self.sb(f"ststg{i}", [128, D], F32, es) for i in range(2)]
            for tt in range(16):
                st = stg[tt % 2]
                sr = ('ststg', tt % 2)
                for half in range(2):
                    ps, pr = self.next_ps()
                    for j in range(4):
                        c = half * 4 + j
                        self.tr(ps[:, j * 128:(j + 1) * 128], X[:, c, tt * 128:(tt + 1) * 128], self.ident_f[:, :],
                                [('X', c, tt), ('ident',)], [pr])
                    self.copy('act' if half == 0 else 'dve', st[:, half * 512:(half + 1) * 512], ps[:, :], [pr], [sr])
                self.dma('sp', self.out_d[sq, tt * 128:(tt + 1) * 128, :], st[:, :], [sr], [('out', sq, tt)])
            self.s.flush()

    def rmsnorm(self, widx):
        with ExitStack() as es:
            self._rmsnorm(widx, es)
            self.s.flush()

    def _rmsnorm(self, widx, es):
        X, H = self.X, self.H
        sq = [self.sb(f"rn_sq{i}", [128, 8, 512], BF16, es) for i in range(2)]
        lnv = [self.sb(f"rn_ln{i}", [128, 512], F32, es) for i in range(2)]
        rstd = [self.sb(f"rn_rs{i}", [128, 512], F32, es) for i in range(2)]
        for tb in range(4):
            b = tb % 2
            t0, t1 = tb * 512, (tb + 1) * 512
            for c in range(8):
                if c % 2 == 0:
                    self.act(sq[b][:, c, :], X[:, c, t0:t1], AF.Square, XR(c, t0, t1), [('rn_sq', b, c)])
                else:
                    self.tt('pool', sq[b][:, c, :], X[:, c, t0:t1], X[:, c, t0:t1], ALU.mult,
                            XR(c, t0, t1), [('rn_sq', b, c)])
            ps, pr = self.next_ps()
            self.mm(ps[:, :], [(self.ones_b[:, :], sq[b][:, c, :]) for c in range(8)],
                    [('rn_sq', b, c) for c in range(8)] + [('ones_b',)], [pr])
            self.act(lnv[b][:, :], ps[:, :], AF.Ln, [pr], [('rn_ln', b)], bias=self.cc[:, 0:1], scale=1.0 / D)
            self.act(rstd[b][:, :], lnv[b][:, :], AF.Exp, [('rn_ln', b)], [('rn_rs', b)], scale=-0.5)
            for c in range(8):
                self.stt(H[:, c, t0:t1], X[:, c, t0:t1], self.normw[:, widx, c:c + 1], rstd[b][:, :],
                         ALU.mult, ALU.mult, XR(c, t0, t1) + [('rn_rs', b), ('normw',)], HR(c, t0, t1))

    def stage_mlp(self, layer):
        X, H = self.X, self.H
        W1 = self.W['mlp_w1']
        W2 = self.W['mlp_w2']
        with ExitStack() as es:
            self._rmsnorm(1 + 2 * layer, es)
            w1 = [self.sb(f"w1_{i}", [128, 8, 512], BF16, es) for i in range(2)]
            w2 = [self.sb(f"w2_{i}", [128, 4, D], BF16, es) for i in range(2)]
            A = [self.sb(f"mlpA{i}", [128, 4, L], BF16, es) for i in range(2)]
            R = [self.sb(f"mlpR{i}", [128, 512], F32, es) for i in range(3)]
            ri = 0
            for e in range(8):
                b = e % 2
                self.dma('pool', w1[b][:, :, :],
                         W1[layer, :, e * 512:(e + 1) * 512].rearrange("(c p) n -> p c n", p=128),
                         [], [('w1', b)])
                self.dma('pool', w2[b][:, :, :],
                         W2[layer, e * 512:(e + 1) * 512, :].rearrange("(c p) n -> p c n", p=128),
                         [], [('w2', b)])
                for tb in range(4):
                    t0, t1 = tb * 512, (tb + 1) * 512
                    for j in range(4):
                        ps, pr = self.next_ps()
                        self.mm(ps[:, :], [(w1[b][:, kc, j * 128:(j + 1) * 128], H[:, kc, t0:t1]) for kc in range(8)],
                                [('w1', b)] + [r for kc in range(8) for r in HR(kc, t0, t1)], [pr])
                        r = R[ri % 3]
                        rr = ('mlpR', ri % 3)
                        ri += 1
                        self.act(r[:, :], ps[:, :], AF.Relu, [pr], [rr])
                        self.tt('pool', A[b][:, j, t0:t1], r[:, :], r[:, :], ALU.mult, [rr], [('mlpA', b, j, tb)])
                for tb in range(4):
                    t0, t1 = tb * 512, (tb + 1) * 512
                    for dt in range(8):
                        ps, pr = self.next_ps()
                        self.mm(ps[:, :], [(w2[b][:, j, dt * 128:(dt + 1) * 128], A[b][:, j, t0:t1]) for j in range(4)],
                                [('w2', b)] + [('mlpA', b, j, tb) for j in range(4)], [pr])
                        self.tt('dve', X[:, dt, t0:t1], ps[:, :], X[:, dt, t0:t1], ALU.add,
                                [pr] + XR(dt, t0, t1), XR(dt, t0, t1))
            self.s.flush()

    def stage_attn(self):
        X, H = self.X, self.H
        Wqkv = self.W['c_w_qkv']
        Wo = self.W['c_w_o']
        cc = self.cc
        self.rmsnorm(2)
        with ExitStack() as es0:
            qT = self.sb("qT", [128, 8, L], BF16, es0)
            kT = self.sb("kT", [128, 8, L], BF16, es0)
            with ExitStack() as es:
                wq = [self.sb(f"wq{i}", [128, 8, 128], BF16, es) for i in range(3)]
                wv = [self.sb(f"wv{i}", [128, 8, 512], BF16, es) for i in range(2)]
                qraw = [self.sb(f"qraw{i}", [128, 512], F32, es) for i in range(2)]
                sqq = [self.sb(f"sqq{i}", [128, 512], BF16, es) for i in range(2)]
                lnv = [self.sb(f"qln{i}", [128, 512], F32, es) for i in range(2)]
                rstd = [self.sb(f"qrs{i}", [128, 512], F32, es) for i in range(2)]
                for half in range(2):
                    self.dma('pool', wv[half][:, :, :],
                             Wqkv[:, 2048 + half * 512:2048 + (half + 1) * 512].rearrange("(c p) n -> p c n", p=128),
                             [], [('wv', half)])
                wi = 0
                PA = int(os.environ.get("ATT_PA", "3"))
                items = []
                for c in range(8 if (PA & 1) else 0):
                    for which in range(2):
                        col0 = which * 1024 + c * 128
                        w = wq[wi % 3]
                        wr = ('wq', wi % 3)
                        wi += 1
                        for tb in range(4):
                            items.append(dict(c=c, which=which, col0=col0, w=w, wr=wr, tb=tb, i=len(items)))

                def qk_a(it):
                    if it['tb'] == 0:
                        self.dma('pool', it['w'][:, :, :],
                                 Wqkv[:, it['col0']:it['col0'] + 128].rearrange("(c p) n -> p c n", p=128), [], [it['wr']])
                    t0, t1 = it['tb'] * 512, (it['tb'] + 1) * 512
                    b = it['i'] % 2
                    ps, pr = self.next_ps()
                    self.mm(ps[:, :], [(it['w'][:, kc, :], H[:, kc, t0:t1]) for kc in range(8)],
                            [it['wr']] + [r for kc in range(8) for r in HR(kc, t0, t1)], [pr])
                    self.act(sqq[b][:, :], ps[:, :], AF.Square, [pr], [('sqq', b)])
                    self.copy('dve', qraw[b][:, :], ps[:, :], [pr], [('qraw', b)])

                def qk_b(it):
                    which, c, tb = it['which'], it['c'], it['tb']
                    t0, t1 = tb * 512, (tb + 1) * 512
                    b = it['i'] % 2
                    dst = qT if which == 0 else kT
                    dn = 'qT' if which == 0 else 'kT'
                    ps2, pr2 = self.next_ps()
                    self.mm(ps2[:, :], [(self.blk_b[:, :], sqq[b][:, :])], [('sqq', b), ('blk_b',)], [pr2])
                    self.act(lnv[b][:, :], ps2[:, :], AF.Ln, [pr2], [('qln', b)], bias=cc[:, 0:1], scale=1.0 / 64)
                    self.act(rstd[b][:, :], lnv[b][:, :], AF.Exp, [('qln', b)], [('qrs', b)],
                             bias=cc[:, 1:2] if which == 0 else cc[:, 3:4], scale=-0.5)
                    self.stt(dst[:, c, t0:t1], qraw[b][:, :], self.qkw[:, which:which + 1], rstd[b][:, :],
                             ALU.mult, ALU.mult, [('qraw', b), ('qrs', b), ('qkw',)], [(dn, c, tb)])

                for t in range(len(items) + 1):
                    if t < len(items):
                        qk_a(items[t])
                    if t >= 1:
                        qk_b(items[t - 1])
                for tt in range(16 if (PA & 2) else 0):
                    t0, t1 = tt * 128, (tt + 1) * 128
                    pss = []
                    for half in range(2):
                        ps, pr = self.next_ps()
                        self.mm(ps[:, :], [(H[:, kc, t0:t1], wv[half][:, kc, :]) for kc in range(8)],
                                [('wv', half)] + [('H', kc, tt) for kc in range(8)], [pr])
                        pss.append((ps, pr))
                    for half in range(2):
                        ps, pr = pss[half]
                        self.copy('act' if half == 0 else 'dve', H[:, half * 4:half * 4 + 4, t0:t1],
                                  ps[:, :].rearrange("p (c t) -> p c t", c=4), [pr],
                                  [('H', half * 4 + j, tt) for j in range(4)])
                self.s.flush()
            with ExitStack() as es:
                eb = [self.sb(f"at_e{i}", [128, 512], F32, es) for i in range(3)]
                spb = [self.sb(f"at_sp{i}", [128, 512], F32, es) for i in range(3)]
                tmpb = [self.sb(f"at_tmp{i}", [128, 512], F32, es) for i in range(2)]
                Rs = [self.sb(f"at_rs{i}", [128, 512], F32, es) for i in range(2)]
                ATb = [self.sb(f"at_A{i}", [128, 512], BF16, es) for i in range(3)]
                qpad = [[self.sb(f"qpad{par}{j}", [128, 512], BF16, es) for j in range(2)] for par in range(2)]
                for par in range(2):
                    for j in range(2):
                        self.s.op('pool', (lambda e, t=qpad[par][j]: e.memset(t[:, :], 0.0)), [], [('qpad', par, j)])
                self.mask4 = self.sb("mask4", [128, 4, 512], F32, es)
                self.dma('sp', self.mask4[:, :, :], self.W['mask4'][:, :, :], [], [('mask4',)])
                it_g = 0
                ai = 0
                pairs = []
                gi = 0
                for h in range(16):
                    for G in range(4):
                        kmax = 4 * G + 3
                        for kb in range(kmax, -1, -1):
                            pairs.append(dict(h=h, G=G, kb=kb, first=(kb == kmax), last=(kb == 0), diag=(kb >= 4 * G),
                                              gi=gi, i=len(pairs)))
                        gi += 1
                rstate = {'cur': 0}

                def opnd(p):
                    h, G, kb = p['h'], p['G'], p['kb']
                    c = h // 2
                    b0 = (h % 2) * 64
                    par, j = h % 2, p['gi'] % 2
                    q_s = qpad[par][j][:, :]
                    k_s = kT[:, c, kb * 128:(kb + 1) * 128]
                    return c, b0, q_s, k_s, ('qpad', par, j), ('kT', c, kb // 4)

                def stA(p):
                    c, b0, q_s, k_s, qr, kr = opnd(p)
                    b = p['i'] % 3
                    if p['first']:
                        G = p['G']
                        self.copy('dve', q_s[b0:b0 + 64, :], qT[b0:b0 + 64, c, G * 512:(G + 1) * 512], [('qT', c, G)], [qr])
                    ps_z, pzr = self.next_ps()
                    self.mm(ps_z[:, :], [(k_s, q_s)], [kr, qr], [pzr])
                    self.act(eb[b][:, :], ps_z[:, :], AF.Exp, [pzr], [('at_e', b)])
                    self.act(r32(spb[b][:, :]), eb[b][:, :], AF.Ln, [('at_e', b)], [('at_sp', b)], bias=cc[:, 2:3])
                    if p['diag']:
                        self.tt('pool', r32(spb[b][:, :]), spb[b][:, :], self.mask4[:, p['kb'] - 4 * p['G'], :], ALU.mult,
                                [('at_sp', b), ('mask4',)], [('at_sp', b)])

                def stB(p):
                    c, b0, q_s, k_s, qr, kr = opnd(p)
                    b = p['i'] % 3
                    b2 = p['i'] % 2
                    c0 = 128 * (p['kb'] - 4 * p['G']) if p['diag'] else 0
                    if p['first']:
                        self.s.op('pool', lambda e: e.memset(Rs[0][:, :], 0.0), [], [('at_rs', 0)])
                    ps_n, pnr = self.next_ps()
                    self.mm1(ps_n[:, :], k_s, q_s, True, False, [kr, qr], [pnr])
                    self.mm1(ps_n[:, c0:512], r32(self.uinclneg_f[:, :]), r32(spb[b][:, c0:512]), False, True,
                             [('at_sp', b), ('uinclneg',)], [pnr])
                    if p['first']:
                        self.act(ATb[b][:, :], ps_n[:, :], AF.Exp, [pnr], [('at_A', b)])
                    else:
                        self.tt('dve', tmpb[b2][:, :], ps_n[:, :], Rs[0][:, :], ALU.subtract,
                                [pnr, ('at_rs', 0)], [('at_tmp', b2)])
                        self.act(ATb[b][:, :], tmpb[b2][:, :], AF.Exp, [('at_tmp', b2)], [('at_A', b)])
                    if p['diag']:
                        self.tt('pool', ATb[b][:, :], ATb[b][:, :], self.mask4[:, p['kb'] - 4 * p['G'], :], ALU.mult,
                                [('at_A', b), ('mask4',)], [('at_A', b)])
                    if not p['last']:
                        ps_r, prr = self.next_ps()
                        self.mm(ps_r[:, c0:512], [(r32(self.ones_r[:, :]), r32(spb[b][:, c0:512]))], [('at_sp', b), ('ones_r',)], [prr])
                        self.tt('dve', Rs[0][:, c0:512], ps_r[:, c0:512], Rs[0][:, c0:512], ALU.add,
                                [prr, ('at_rs', 0)], [('at_rs', 0)])

                def stC(p):
                    c, b0, q_s, k_s, qr, kr = opnd(p)
                    h, G, kb = p['h'], p['G'], p['kb']
                    b = p['i'] % 3
                    acc = self.psacc[p['gi'] % 2]
                    accr = ('psacc', p['gi'] % 2)
                    v_s = H[:, c, kb * 128:(kb + 1) * 128]
                    self.mm1(acc[:, :], v_s, ATb[b][:, :], p['first'], p['last'],
                             [('H', c, kb), ('at_A', b)], [accr])
                    if p['last']:
                        self.copy('act', qT[b0:b0 + 64, c, G * 512:(G + 1) * 512], acc[b0:b0 + 64, :], [accr], [('qT', c, G)])

                n = len(pairs)
                for t in range(n + 2):
                    if t < n:
                        stA(pairs[t])
                    if 0 <= t - 1 < n:
                        stB(pairs[t - 1])
                    if 0 <= t - 2 < n:
                        stC(pairs[t - 2])
                self.s.flush()
            with ExitStack() as es:
                wo = [self.sb(f"wo{i}", [128, 8, 128], BF16, es) for i in range(2)]
                for dt in range(8):
                    w = wo[dt % 2]
                    wr = ('wo', dt % 2)
                    self.dma('pool', w[:, :, :], Wo[:, dt * 128:(dt + 1) * 128].rearrange("(c p) n -> p c n", p=128),
                             [], [wr])
                    for tb in range(4):
                        t0, t1 = tb * 512, (tb + 1) * 512
                        ps, pr = self.next_ps()
                        self.mm(ps[:, :], [(w[:, c, :], qT[:, c, t0:t1]) for c in range(8)],
                                [wr] + [('qT', c, tb) for c in range(8)], [pr])
                        self.tt('dve', X[:, dt, t0:t1], ps[:, :], X[:, dt, t0:t1], ALU.add,
                                [pr] + XR(dt, t0, t1), XR(dt, t0, t1))
                self.s.flush()

    def ring(self, name, n, shape, dtype, es):
        tiles = [self.sb(f"{name}{i}", shape, dtype, es) for i in range(n)]
        state = {'i': 0}

        def nxt():
            i = state['i'] % n
            state['i'] += 1
            return tiles[i], (name, i)
        nxt.tiles = tiles
        return nxt

    def conv_proj(self, col0, R):
        H = self.H
        Win = self.W['a_w_in']
        w, wr = R['w']()
        self.dma('pool', w[:, :, :], Win[:, col0:col0 + 128].rearrange("(c p) n -> p c n", p=128), [], [wr])
        raw, rr = R['raw']()
        for tb in range(4):
            t0, t1 = tb * 512, (tb + 1) * 512
            ps, pr = self.next_ps()
            self.mm(ps[:, :], [(w[:, kc, :], H[:, kc, t0:t1]) for kc in range(8)],
                    [wr] + [r for kc in range(8) for r in HR(kc, t0, t1)], [pr])
            self.copy('act', raw[:, 3 + t0:3 + t1], ps[:, :], [pr], [rr])
        return raw, rr

    def conv_act(self, raw, rr, cw, cb, R, silu_out=None, silu_reg=None):
        acc, ar = R['acc']()
        if cb is not None:
            self.ts('dve', acc[:, :], raw[:, 3:3 + L], cw[:, 3:4], cb, ALU.mult, ALU.add, [rr, ('convw',)], [ar])
        else:
            self.ts('dve', acc[:, :], raw[:, 3:3 + L], cw[:, 3:4], None, ALU.mult, None, [rr, ('convw',)], [ar])
        for k in (2, 1, 0):
            self.stt(acc[:, :], raw[:, k:k + L], cw[:, k:k + 1], acc[:, :], ALU.mult, ALU.add, [rr, ar, ('convw',)], [ar])
        if silu_out is not None:
            self.act(silu_out, acc[:, :], AF.Silu, [ar], [silu_reg])
            return silu_out, silu_reg
        self.act(acc[:, :], acc[:, :], AF.Silu, [ar], [ar])
        return acc, ar

    def conv_pipeline(self, tiles, R):
        nxt = self.conv_proj(tiles[0]['col'], R)
        for j, t in enumerate(tiles):
            cur = nxt
            if j + 1 < len(tiles):
                nxt = self.conv_proj(tiles[j + 1]['col'], R)
            out, outr = self.conv_act(cur[0], cur[1], t['cw'], t['cb'], R, t.get('silu_out'), t.get('silu_reg'))
            if t.get('post'):
                t['post'](out, outr)

    def to_tok(self, src, sr, dst, dr_name, width_off, src_regs=None):
        for q in range(4):
            ps, pr = self.next_ps()
            psb = ps[:, :].bitcast(BF16)
            for j in range(4):
                tt = q * 4 + j
                self.tr(psb[:, j * 128:(j + 1) * 128], src[:, tt * 128:(tt + 1) * 128], self.ident_b[:, :],
                        (src_regs if src_regs is not None else [sr]) + [('ident_b',)], [pr])
            self.copy('act' if q % 2 == 0 else 'dve', dst[:, q * 4:q * 4 + 4, width_off:width_off + 128],
                      psb[:, 0:512].rearrange("p (c t) -> p c t", c=4), [pr],
                      [(dr_name, q * 4 + j) for j in range(4)])

    def stage_mix0(self):
        X, H = self.X, self.H
        W = self.W
        cc = self.cc
        Win = W['a_w_in']
        Wout = W['a_w_out']
        self.rmsnorm(0)
        with ExitStack() as es0:
            sb = lambda n, sh, dt=F32: self.sb(n, sh, dt, es0)
            convw_s = sb("convw_s", [128, 12, 4])
            convb_s = sb("convb_s", [128, 12])
            convw_g = sb("convw_g", [128, 24, 4])
            dsk = sb("dsk", [128, 16])
            gnw = sb("gnw", [128, 128])
            spl = sb("spl", [128, 16, 32])
            ag = sb("ag", [128, 16, 32])
            ea = sb("ea", [128, 16, 32])
            dte = sb("dte", [128, 16, 32])
            cdec = sb("cdec", [128, 16, 32])
            bet = sb("bet", [128, 16, 8])
            nbet = sb("nbet", [128, 16, 8])
            bg = sb("bg", [128, 16, 8])
            self.dma('sp', convw_s[:, :, :], W['convw_s'][:, :, :], [], [('convw',)])
            self.dma('sp', convb_s[:, :], W['convb_s'][:, :], [], [('convw',)])
            self.dma('sp', convw_g[:, :, :], W['convw_g'][:, :, :], [], [('convw',)])
            self.dma('sp', dsk[:, :], W['dsk'][:, :], [], [('dsk',)])
            self.dma('sp', gnw[:, :], W['gnw'][:, :], [], [('gnw',)])
            with ExitStack() as es:
                wsm = self.sb("wsm", [128, 8, 32], BF16, es)
                bias_bc = self.sb("bias_bc", [128, 32], F32, es)
                alog = self.sb("alog", [128, 32], F32, es)
                aneg = self.sb("aneg", [128, 32], F32, es)
                smraw = self.sb("smraw", [128, 16, 32], F32, es)
                e1 = self.sb("sm_e", [128, 16, 32], F32, es)
                acum = self.sb("acum", [128, 16, 32], F32, es)
                tot = self.sb("tot", [128, 16, 32], F32, es)
                tmp = self.sb("sm_tmp", [128, 16, 32], F32, es)
                self.dma('pool', wsm[:, :, 0:16], Win[:, 2560:2576].rearrange("(c p) n -> p c n", p=128), [], [('wsm', 0)])
                self.dma('pool', wsm[:, :, 16:32], Win[:, 6672:6688].rearrange("(c p) n -> p c n", p=128), [], [('wsm', 1)])
                self.dma('sp', bias_bc[:, :], W['bias_bc'][:, :], [], [('bias_bc',)])
                self.dma('sp', alog[:, :], W['alog_bc'][:, :], [], [('alog',)])
                self.act(aneg[:, :], alog[:, :], AF.Exp, [('alog',)], [('aneg',)])
                self.ts('dve', aneg[:, :], aneg[:, :], -1.0, None, ALU.mult, None, [('aneg',)], [('aneg',)])
                for tt in range(16):
                    ps, pr = self.next_ps()
                    self.mm(ps[:, 0:32], [(H[:, kc, tt * 128:(tt + 1) * 128], wsm[:, kc, :]) for kc in range(8)],
                            [('wsm', 0), ('wsm', 1)] + [('H', kc, tt) for kc in range(8)], [pr])
                    self.tt('dve', smraw[:, tt, :], ps[:, 0:32], bias_bc[:, :], ALU.add, [pr, ('bias_bc',)], [('smraw',)])
                f2 = lambda t: t[:, :, :].rearrange("p a b -> p (a b)")
                self.act(f2(e1), f2(smraw), AF.Exp, [('smraw',)], [('sm_e',)])
                self.act(f2(spl), f2(e1), AF.Ln, [('sm_e',)], [('spl',)], bias=cc[:, 2:3])
                self.ts('dve', tmp[:, :, 16:24], e1[:, :, 16:24], 1.0, None, ALU.add, None, [('sm_e',)], [('sm_tmp',)])
                self.s.op('dve', lambda e: e.reciprocal(tmp[:, :, 16:24], tmp[:, :, 16:24]), [('sm_tmp',)], [('sm_tmp',)])
                self.tt('dve', bet[:, :, :], e1[:, :, 16:24], tmp[:, :, 16:24], ALU.mult, [('sm_e',), ('sm_tmp',)], [('bet',)])
                self.ts('dve', nbet[:, :, :], bet[:, :, :], -1.0, None, ALU.mult, None, [('bet',)], [('nbet',)])
                self.tt('dve', ag[:, :, :], spl[:, :, :], aneg[:, :].unsqueeze(1).broadcast_to([128, 16, 32]), ALU.mult,
                        [('spl',), ('aneg',)], [('ag',)])
                ps, pr = self.next_ps()
                self.mm(ps[:, :], [(self.trile_f[:, :], f2(ag))], [('ag',), ('trile',)], [pr])
                self.copy('dve', f2(acum), ps[:, :], [pr], [('acum',)])
                self.act(f2(ea), ps[:, :], AF.Exp, [pr], [('ea',)])
                ps2, pr2 = self.next_ps()
                self.mm(ps2[:, :], [(self.ones_f[:, :], f2(ag))], [('ag',), ('ones',)], [pr2])
                self.act(f2(cdec), ps2[:, :], AF.Exp, [pr2], [('cdec',)])
                self.tt('dve', f2(tot), ps2[:, :], f2(acum), ALU.subtract, [pr2, ('acum',)], [('tot',)])
                self.act(f2(dte), f2(tot), AF.Exp, [('tot',)], [('dte',)])
                self.tt('dve', bg[:, :, :], bet[:, :, :], ea[:, :, 24:32], ALU.mult, [('bet',), ('ea',)], [('bg',)])
                self.s.flush()
            P = dict(convw_s=convw_s, convb_s=convb_s, convw_g=convw_g, dsk=dsk, gnw=gnw, spl=spl, ag=ag,
                     ea=ea, dte=dte, cdec=cdec, bet=bet, nbet=nbet, bg=bg)
            if os.environ.get("MIX_STOP") == "prelude":
                return
            MIXSEL = os.environ.get("MIX_SEL", "ssd,gdn").split(",")
            if 'ssd' in MIXSEL:
                for g in range(2):
                    self.ssd_group(g, P)
            if 'gdn' in MIXSEL:
                for hg in range(2):
                    self.gdn_group(hg, P)

    def ssd_group(self, g, P):
        X, H = self.X, self.H
        Win = self.W['a_w_in']
        Wout = self.W['a_w_out']
        cc = self.cc
        with ExitStack() as es0:
            xs_tok = self.sb("xs_tok", [128, 16, 512], BF16, es0)
            siluz = self.sb("siluz", [128, 16, 512], BF16, es0)
            BT = self.sb("BT", [128, L], BF16, es0)
            CT = self.sb("CT", [128, L], BF16, es0)
            B_tok = self.sb("B_tok", [128, 16, 128], BF16, es0)
            with ExitStack() as es:
                R = dict(w=self.ring("cw", 3, [128, 8, 128], BF16, es),
                         raw=self.ring("craw", 2, [128, L + 3], F32, es),
                         acc=self.ring("cacc", 1, [128, L], F32, es))
                xsb = self.ring("xsb", 2, [128, L], BF16, es)
                wz = self.sb("wz", [128, 8, 512], BF16, es)
                self.dma('pool', wz[:, :, :], Win[:, g * 512:(g + 1) * 512].rearrange("(c p) n -> p c n", p=128), [], [('wz',)])
                self._zero_halo(R, es)
                tiles = []
                for j in range(4):
                    ti = g * 4 + j
                    xb, xr = xsb()
                    tiles.append(dict(col=1024 + ti * 128, cw=P['convw_s'][:, ti, :], cb=P['convb_s'][:, ti:ti + 1],
                                      silu_out=xb[:, :], silu_reg=xr,
                                      post=(lambda o, r, xb=xb, xr=xr, j=j: self.to_tok(xb, xr, xs_tok, 'xs_tok', j * 128))))
                ti = 8 + g
                tiles.append(dict(col=1024 + ti * 128, cw=P['convw_s'][:, ti, :], cb=P['convb_s'][:, ti:ti + 1],
                                  silu_out=BT[:, :], silu_reg=('BT',),
                                  post=(lambda o, r: self.to_tok(BT, ('BT',), B_tok, 'B_tok', 0))))
                ti = 10 + g
                tiles.append(dict(col=1024 + ti * 128, cw=P['convw_s'][:, ti, :], cb=P['convb_s'][:, ti:ti + 1],
                                  silu_out=CT[:, :], silu_reg=('CT',)))
                self.conv_pipeline(tiles, R)
                for tt in range(16):
                    ps, pr = self.next_ps()
                    self.mm(ps[:, :], [(H[:, kc, tt * 128:(tt + 1) * 128], wz[:, kc, :]) for kc in range(8)],
                            [('wz',)] + [('H', kc, tt) for kc in range(8)], [pr])
                    self.act(siluz[:, tt, :], ps[:, :], AF.Silu, [pr], [('siluz', tt)])
                self.s.flush()
            if os.environ.get("MIX_STOP") == "ssd1":
                return
            yT = self.sb("yT", [128, 4, L], BF16, es0)
            with ExitStack() as es:
                ring = lambda n, k, sh, dt=F32: self.ring(n, k, sh, dt, es)
                cbm_r = ring("cbm", 2, [128, 128])
                lseg_r = ring("lseg", 2, [128, 4, 128])
                LT_r = ring("LT", 2, [128, 4, 128])
                MT_r = ring("MT", 2, [128, 8, 128], BF16)
                xdt_r = ring("xdt", 2, [128, 512], BF16)
                xdd_r = ring("xdd", 2, [128, 512], BF16)
                t1_r = ring("t1", 1, [128, 512])
                t3_r = ring("t3", 1, [128, 512])
                yg_r = ring("yg", 2, [128, 512])
                junk_r = ring("junk", 1, [128, 512], BF16)
                yn_r = ring("ynb", 2, [128, 512], BF16)
                sm_r = ring("ssm", 2, [128, 4])
                S = self.sb("ssdS", [128, 512], F32, es)
                Sb = self.sb("ssdSb", [128, 512], BF16, es)
                snw = self.sb("snw", [128, 512], F32, es)
                self.dma('sp', snw[:, :], self.W['snw'][:, g * 512:(g + 1) * 512], [], [('snw',)])
                hs = slice(g * 8, (g + 1) * 8)
                v8 = lambda ap: ap.rearrange("p (h d) -> p h d", h=8)
                bc8 = lambda ap: ap.unsqueeze(2).broadcast_to([128, 8, 64])
                sA = {}
                sB = {}

                def ssd_a(tt):
                    ts_ = slice(tt * 128, (tt + 1) * 128)
                    ps_cb, pcr = self.next_ps()
                    self.mm(ps_cb[:, 0:128], [(BT[:, ts_], CT[:, ts_])], [('BT',), ('CT',)], [pcr])
                    cbm, cbr = cbm_r()
                    self.tt('dve', cbm[:, :], ps_cb[:, 0:128], self.trile_f[:, :], ALU.mult, [pcr, ('trile',)], [cbr])
                    MT, mr = MT_r()
                    halves = []
                    for half in range(2):
                        lseg, lr = lseg_r()
                        ps_s, psr = self.next_ps()
                        for i in range(4):
                            hh = g * 8 + half * 4 + i
                            self.ts('dve', r32(lseg[:, i, :]), self.gt_f[:, :], P['ag'][:, tt, hh:hh + 1], None,
                                    ALU.mult, None, [('gt',), ('ag',)], [(lr, i)])
                            self.mm(ps_s[:, i * 128:(i + 1) * 128], [(r32(lseg[:, i, :]), r32(self.trile_r[:, :]))],
                                    [(lr, i), ('trile_r',)], [psr])
                        halves.append((ps_s, psr))
                    for half, (ps_s, psr) in enumerate(halves):
                        LT, ltr = LT_r()
                        self.act(LT[:, :, :].rearrange("p a b -> p (a b)"), ps_s[:, :], AF.Exp, [psr], [ltr])
                        self.tt('dve', MT[:, half * 4:(half + 1) * 4, :], LT[:, :, :],
                                cbm[:, :].unsqueeze(1).broadcast_to([128, 4, 128]), ALU.mult, [ltr, cbr], [(mr, half)])
                    xdt, xdr = xdt_r()
                    self.tt('dve', v8(xdt[:, :]), v8(xs_tok[:, tt, :]), bc8(P['spl'][:, tt, hs]), ALU.mult,
                            [('xs_tok', tt), ('spl',)], [xdr])
                    xdd, xddr = xdd_r()
                    self.tt('pool', v8(xdd[:, :]), v8(xdt[:, :]), bc8(P['dte'][:, tt, hs]), ALU.mult,
                            [xdr, ('dte',)], [xddr])
                    sA[tt] = dict(MT=MT, mr=mr, xdt=xdt, xdr=xdr, xdd=xdd, xddr=xddr, ts_=ts_)

                def ssd_b(tt):
                    d = sA.pop(tt)
                    MT, mr, xdt, xdr, xdd, xddr, ts_ = d['MT'], d['mr'], d['xdt'], d['xdr'], d['xdd'], d['xddr'], d['ts_']
                    ps_y, pyr = self.next_ps()
                    for i in range(8):
                        self.mm(ps_y[:, i * 64:(i + 1) * 64], [(MT[:, i, :], xdt[:, i * 64:(i + 1) * 64])],
                                [(mr, i // 4), xdr], [pyr])
                    t1, t1r = t1_r()
                    if tt > 0:
                        ps_o, por = self.next_ps()
                        self.mm(ps_o[:, :], [(CT[:, ts_], Sb[:, :])], [('CT',), ('ssdSb',)], [por])
                        self.tt('dve', v8(t1[:, :]), v8(ps_o[:, :]), bc8(P['ea'][:, tt, hs]), ALU.mult, [por, ('ea',)], [t1r])
                        self.tt('dve', t1[:, :], t1[:, :], ps_y[:, :], ALU.add, [t1r, pyr], [t1r])
                    else:
                        self.copy('dve', t1[:, :], ps_y[:, :], [pyr], [t1r])
                    t3, t3r = t3_r()
                    self.tt('pool', v8(t3[:, :]), v8(xs_tok[:, tt, :]), bc8(P['dsk'][:, hs]), ALU.mult,
                            [('xs_tok', tt), ('dsk',)], [t3r])
                    self.tt('dve', t3[:, :], t3[:, :], t1[:, :], ALU.add, [t3r, t1r], [t3r])
                    yg, ygr = yg_r()
                    self.tt('pool', yg[:, :], t3[:, :], siluz[:, tt, :], ALU.mult, [t3r, ('siluz', tt)], [ygr])
                    sm, smr = sm_r()
                    junk, jr = junk_r()
                    self.act(junk[:, :], yg[:, :], AF.Square, [ygr], [jr, (smr, 0)], accum_out=sm[:, 0:1])
                    if tt < 15:
                        ps_st, pstr = self.next_ps()
                        self.mm(ps_st[:, :], [(B_tok[:, tt, :], xdd[:, :])], [('B_tok', tt), xddr], [pstr])
                        if tt == 0:
                            self.copy('dve', S[:, :], ps_st[:, :], [pstr], [('ssdS',)])
                        else:
                            self.tt('dve', v8(S[:, :]), v8(S[:, :]), bc8(P['cdec'][:, tt, hs]), ALU.mult,
                                    [('ssdS',), ('cdec',)], [('ssdS',)])
                            self.tt('dve', S[:, :], S[:, :], ps_st[:, :], ALU.add, [('ssdS',), pstr], [('ssdS',)])
                        self.copy('act', Sb[:, :], S[:, :], [('ssdS',)], [('ssdSb',)])
                    sB[tt] = dict(yg=yg, ygr=ygr, sm=sm, smr=smr, ts_=ts_)

                def ssd_c(tt):
                    d = sB.pop(tt)
                    yg, ygr, sm, smr, ts_ = d['yg'], d['ygr'], d['sm'], d['smr'], d['ts_']
                    self.act(sm[:, 1:2], sm[:, 0:1], AF.Ln, [(smr, 0)], [(smr, 1)], bias=cc[:, 0:1], scale=1.0 / 512)
                    self.act(sm[:, 2:3], sm[:, 1:2], AF.Exp, [(smr, 1)], [(smr, 2)], scale=-0.5)
                    yn, ynr = yn_r()
                    self.stt(yn[:, :], yg[:, :], sm[:, 2:3], snw[:, :], ALU.mult, ALU.mult,
                             [ygr, (smr, 2), ('snw',)], [ynr])
                    ps_t, ptr_ = self.next_ps()
                    psb = ps_t[:, :].bitcast(BF16)
                    for j in range(4):
                        self.tr(psb[:, j * 128:(j + 1) * 128], yn[:, j * 128:(j + 1) * 128], self.ident_b[:, :],
                                [ynr, ('ident_b',)], [ptr_])
                    self.copy('act', yT[:, 0:4, ts_], psb[:, 0:512].rearrange("p (c t) -> p c t", c=4), [ptr_],
                              [('yT', j, tt) for j in range(4)])

                for t in range(18):
                    if t < 16:
                        ssd_a(t)
                    if 1 <= t <= 16:
                        ssd_b(t - 1)
                    if t >= 2:
                        ssd_c(t - 2)
                self.s.flush()
            with ExitStack() as es:
                wo = self.sb("wo_s", [128, 4, D], BF16, es)
                self.dma('pool', wo[:, :, :], Wout[g * 512:(g + 1) * 512, :].rearrange("(c p) n -> p c n", p=128), [], [('wo_s',)])
                for tb in range(4):
                    t0, t1_ = tb * 512, (tb + 1) * 512
                    for dt in range(8):
                        ps, pr = self.next_ps()
                        self.mm(ps[:, :], [(wo[:, j, dt * 128:(dt + 1) * 128], yT[:, j, t0:t1_]) for j in range(4)],
                                [('wo_s',)] + [('yT', j, tt) for j in range(4) for tt in range(tb * 4, tb * 4 + 4)], [pr])
                        self.tt('dve', X[:, dt, t0:t1_], ps[:, :], X[:, dt, t0:t1_], ALU.add,
                                [pr] + XR(dt, t0, t1_), XR(dt, t0, t1_))
                self.s.flush()

    def _zero_halo(self, R, es):
        for i, t in enumerate(R['raw'].tiles):
            self.s.op('pool', (lambda e, t=t: e.memset(t[:, 0:3], 0.0)), [], [("craw", i)])

    def gdn_group(self, hg, P):
        X, H = self.X, self.H
        Win = self.W['a_w_in']
        Wout = self.W['a_w_out']
        cc = self.cc
        QOFF = 2576
        with ExitStack() as es0:
            siluzg = self.sb("siluzg", [128, 16, 512], BF16, es0)
            ogT = self.sb("ogT", [128, 4, L], BF16, es0)
            with ExitStack() as es:
                wzg = self.sb("wzg", [128, 8, 512], BF16, es)
                c0 = 5648 + hg * 512
                self.dma('pool', wzg[:, :, :], Win[:, c0:c0 + 512].rearrange("(c p) n -> p c n", p=128), [], [('wzg',)])
                for tt in range(16):
                    ps, pr = self.next_ps()
                    self.mm(ps[:, :], [(H[:, kc, tt * 128:(tt + 1) * 128], wzg[:, kc, :]) for kc in range(8)],
                            [('wzg',)] + [('H', kc, tt) for kc in range(8)], [pr])
                    self.act(siluzg[:, tt, :], ps[:, :], AF.Silu, [pr], [('siluzg', tt)])
                self.s.flush()
            for i in range(4):
                h = hg * 4 + i
                with ExitStack() as esh:
                    qTn = self.sb("qTn", [128, L], BF16, esh)
                    kTn = self.sb("kTn", [128, L], BF16, esh)
                    k_tok = self.sb("k_tok", [128, 16, 128], BF16, esh)
                    v_tok = self.sb("v_tok", [128, 16, 128], BF16, esh)
                    with ExitStack() as es:
                        R = dict(w=self.ring("cw", 2, [128, 8, 128], BF16, es),
                                 raw=self.ring("craw", 2, [128, L + 3], F32, es),
                                 acc=self.ring("cacc", 2, [128, L], F32, es))
                        kvb = ogT[:, i, :]
                        sqq4 = [self.sb(f"gsqq{j}", [128, 512], BF16, es) for j in range(4)]
                        self._zero_halo(R, es)
                        def l2post(which, dstT, dn):
                            def post(acc, ar):
                                pss = []
                                for tb in range(4):
                                    t0, t1 = tb * 512, (tb + 1) * 512
                                    self.act(sqq4[tb][:, :], acc[:, t0:t1], AF.Square, [ar], [('gsqq', tb)])
                                for tb in range(4):
                                    ps, pr = self.next_ps()
                                    self.mm(ps[:, :], [(self.ones_b[:, :], sqq4[tb][:, :])], [('gsqq', tb), ('ones_b',)], [pr])
                                    pss.append((ps, pr))
                                for tb in range(4):
                                    ps, pr = pss[tb]
                                    self.act(ps[:, :], ps[:, :], AF.Ln, [pr], [pr], bias=cc[:, 0:1])
                                for tb in range(4):
                                    ps, pr = pss[tb]
                                    self.act(ps[:, :], ps[:, :], AF.Exp, [pr], [pr],
                                             bias=cc[:, 4:5] if which == 0 else cc[:, 3:4], scale=-0.5)
                                for tb in range(4):
                                    ps, pr = pss[tb]
                                    t0, t1 = tb * 512, (tb + 1) * 512
                                    self.tt('dve', dstT[:, t0:t1], acc[:, t0:t1], ps[:, :], ALU.mult, [ar, pr], [(dn, tb)])
                                if which == 1:
                                    self.to_tok(kTn, None, k_tok, 'k_tok', 0, src_regs=[('kTn', tb) for tb in range(4)])
                            return post
                        tiles = []
                        for which, dstT, dn in ((0, qTn, 'qTn'), (1, kTn, 'kTn')):
                            ti = which * 8 + h
                            tiles.append(dict(col=QOFF + ti * 128, cw=P['convw_g'][:, ti, :], cb=None, post=l2post(which, dstT, dn)))
                        ti = 16 + h
                        tiles.append(dict(col=QOFF + ti * 128, cw=P['convw_g'][:, ti, :], cb=None, silu_out=kvb, silu_reg=('kvb',),
                                          post=(lambda o, r: self.to_tok(kvb, ('kvb',), v_tok, 'v_tok', 0))))
                        self.conv_pipeline(tiles, R)
                        self.s.flush()
                    with ExitStack() as es:
                        ub = self.sb("g_ub", [128, 16, 128], F32, es)
                        wT = self.sb("g_wT", [128, 16, 128], BF16, es)
                        qkT = self.sb("g_qkT", [128, 16, 128], BF16, es)
                        NCTX = int(os.environ.get("GDN_NCTX", "4"))
                        ctxs = []
                        for ci in range(NCTX):
                            t = lambda n, sh=[128, 128], dt=F32: self.sb(f"g_{n}{ci}", sh, dt, es)
                            ctxs.append(dict(gmask=t("gmask"), DLT=t("DLT", [128, 256]), PPa=t("PPa", [128, 384]),
                                             PPb=t("PPb", [128, 384]), vb=t("vb"), kb2=t("kb2"), ci=ci))
                        gcol = lambda tt: P['ag'][:, tt, 24 + h:25 + h]
                        GPH = int(os.environ.get("GDN_PH", "3"))

                        def p2_group(grp):
                            tts = tuple(range(grp * NCTX, (grp + 1) * NCTX))
                            rgs = [(lambda n, ci=c['ci']: ('g_' + n, ci)) for c in ctxs]
                            ps1s, ps3s, ps4s = [], [], []
                            for c, tt, rg in zip(ctxs, tts, rgs):
                                self.ts('dve', r32(c['gmask'][:, :]), self.gt_f[:, :], gcol(tt), None, ALU.mult, None,
                                        [('gt',), ('ag',)], [rg('gmask')])
                                self.act(r32(c['vb'][:, :]), v_tok[:, tt, :], AF.Identity, [('v_tok', tt), ('bet',)], [rg('vb')],
                                         scale=P['bet'][:, tt, h:h + 1])
                                self.act(r32(c['kb2'][:, :]), k_tok[:, tt, :], AF.Identity, [('k_tok', tt), ('bg',)], [rg('kb2')],
                                         scale=P['bg'][:, tt, h:h + 1])
                            for c, tt, rg in zip(ctxs, tts, rgs):
                                ts_ = slice(tt * 128, (tt + 1) * 128)
                                ps1, p1r = self.next_ps()
                                self.mm(ps1[:, 0:128], [(r32(self.trile_r[:, :]), r32(c['gmask'][:, :]))], [rg('gmask'), ('trile_r',)], [p1r])
                                self.mm(ps1[:, 128:256], [(r32(c['gmask'][:, :]), r32(self.trile_r[:, :]))], [rg('gmask'), ('trile_r',)], [p1r])
                                self.mm(ps1[:, 256:384], [(kTn[:, ts_], kTn[:, ts_])], [('kTn', tt // 4)], [p1r])
                                self.mm(ps1[:, 384:512], [(kTn[:, ts_], qTn[:, ts_])], [('kTn', tt // 4), ('qTn', tt // 4)], [p1r])
                                ps1s.append((ps1, p1r))
                            yield
                            for c, tt, rg, (ps1, p1r) in zip(ctxs, tts, rgs, ps1s):
                                self.act(c['DLT'][:, :], ps1[:, 0:256], AF.Exp, [p1r], [rg('DLT')])
                                self.tt('dve', c['DLT'][:, :], c['DLT'][:, :], self.gtle_f[:, :], ALU.mult,
                                        [rg('DLT'), ('gtle',)], [rg('DLT')])
                            for c, tt, rg, (ps1, p1r) in zip(ctxs, tts, rgs, ps1s):
                                self.stt(r32(c['PPa'][:, 0:128]), ps1[:, 256:384], P['nbet'][:, tt, h:h + 1], c['DLT'][:, 0:128],
                                         ALU.mult, ALU.mult, [p1r, ('nbet',), rg('DLT')], [rg('PPa')])
                                self.tt('dve', qkT[:, tt, :], ps1[:, 384:512], c['DLT'][:, 128:256], ALU.mult,
                                        [p1r, rg('DLT')], [('g_qkT', tt)])
                            yield
                            for c, tt, rg in zip(ctxs, tts, rgs):
                                ps4, p4r = self.next_ps()
                                self.tr(ps4[:, 0:128], c['PPa'][:, 0:128], self.ident_f[:, :], [rg('PPa'), ('ident',)], [p4r])
                                ps4s.append((ps4, p4r))
                            for c, tt, rg, (ps4, p4r) in zip(ctxs, tts, rgs, ps4s):
                                self.copy('act', r32(c['PPa'][:, 128:256]), ps4[:, 0:128], [p4r], [rg('PPa')])
                                self.tt('dve', r32(c['PPa'][:, 256:384]), c['PPa'][:, 128:256], self.ident_f[:, :], ALU.add,
                                        [rg('PPa'), ('ident',)], [rg('PPa')])
                            cur, nxt = 'PPa', 'PPb'
                            for k in range(6):
                                psas = []
                                for c, rg in zip(ctxs, rgs):
                                    psa, par = self.next_ps()
                                    pp = c[cur]
                                    self.mm(psa[:, 0:128], [(r32(pp[:, 128:256]), r32(pp[:, 0:128]))], [rg(cur)], [par])
                                    if k == 0:
                                        self.mm(psa[:, 128:256], [(r32(pp[:, 0:128]), r32(pp[:, 128:256]))], [rg(cur)], [par])
                                    elif k <= 4:
                                        self.mm(psa[:, 128:384], [(r32(pp[:, 0:128]), r32(pp[:, 128:384]))], [rg(cur)], [par])
                                    else:
                                        self.mm(psa[:, 256:384], [(r32(pp[:, 0:128]), r32(pp[:, 256:384]))], [rg(cur)], [par])
                                    psas.append((psa, par))
                                yield
                                for c, rg, (psa, par) in zip(ctxs, rgs, psas):
                                    pp, pn = c[cur], c[nxt]
                                    w = 256 if k < 5 else 128
                                    self.copy('act', r32(pn[:, 0:w]), psa[:, 0:w], [par], [rg(nxt)])
                                    if k == 0:
                                        self.copy('dve', r32(pn[:, 256:384]), pp[:, 256:384], [rg(cur)], [rg(nxt)])
                                    else:
                                        self.tt('dve', r32(pn[:, 256:384]), pp[:, 256:384], psa[:, 256:384], ALU.add,
                                                [rg(cur), par], [rg(nxt)])
                                cur, nxt = nxt, cur
                                yield
                            psbs = []
                            for c, rg in zip(ctxs, rgs):
                                psb_, pbr = self.next_ps()
                                pp = c[cur]
                                self.mm(psb_[:, 0:128], [(r32(pp[:, 0:128]), r32(pp[:, 256:384]))], [rg(cur)], [pbr])
                                psbs.append((psb_, pbr))
                            yield
                            for c, rg, (psb_, pbr) in zip(ctxs, rgs, psbs):
                                pp = c[cur]
                                self.tt('dve', r32(pp[:, 256:384]), pp[:, 256:384], psb_[:, 0:128], ALU.add, [rg(cur), pbr], [rg(cur)])
                            yield
                            TT = cur
                            psus = []
                            for c, tt, rg in zip(ctxs, tts, rgs):
                                psu, pur = self.next_ps()
                                self.mm(psu[:, 0:128], [(r32(c[TT][:, 256:384]), r32(c['vb'][:, :]))], [rg(TT), rg('vb')], [pur])
                                self.mm(psu[:, 128:256], [(r32(c['kb2'][:, :]), r32(c[TT][:, 256:384]))], [rg(TT), rg('kb2')], [pur])
                                psus.append((psu, pur))
                            for c, tt, rg, (psu, pur) in zip(ctxs, tts, rgs, psus):
                                self.copy('act', ub[:, tt, :], psu[:, 0:128], [pur], [('g_ub', tt)])
                                self.copy('dve', wT[:, tt, :], psu[:, 128:256], [pur], [('g_wT', tt)])
                        S = self.sb("g_S", [128, 128], F32, es)
                        Sb = self.sb("g_Sb", [128, 128], BF16, es)
                        u_r = self.ring("g_u", 2, [128, 128], BF16, es)
                        kd_r = self.ring("g_kd", 2, [128, 128], BF16, es)
                        t_r = self.ring("g_t", 2, [128, 128], F32, es)
                        o_r = self.ring("g_o", 2, [128, 128], F32, es)
                        on_r = self.ring("g_on", 2, [128, 128], F32, es)
                        og_r = self.ring("g_og", 2, [128, 128], BF16, es)
                        jk_r = self.ring("g_jk", 1, [128, 128], BF16, es)
                        sm_r = self.ring("g_sm", 2, [128, 4], F32, es)
                        sst = {}

                        def scan_a(tt):
                            ts_ = slice(tt * 128, (tt + 1) * 128)
                            u, ur = u_r()
                            d = sst[tt] = dict(u=u, ur=ur)
                            if tt > 0:
                                ps_ws, pwr = self.next_ps()
                                self.mm(ps_ws[:, 0:128], [(wT[:, tt, :], Sb[:, :])], [('g_wT', tt), ('g_Sb',)], [pwr])
                                self.tt('dve', u[:, :], ub[:, tt, :], ps_ws[:, 0:128], ALU.subtract, [('g_ub', tt), pwr], [ur])
                            else:
                                self.copy('dve', u[:, :], ub[:, tt, :], [('g_ub', tt)], [ur])

                        def scan_b(tt):
                            ts_ = slice(tt * 128, (tt + 1) * 128)
                            d = sst.pop(tt)
                            u, ur = d['u'], d['ur']
                            if tt > 0:
                                ps_o1, po1r = self.next_ps()
                                self.mm(ps_o1[:, 0:128], [(qTn[:, ts_], Sb[:, :])], [('qTn', tt // 4), ('g_Sb',)], [po1r])
                            ps_o2, po2r = self.next_ps()
                            self.mm(ps_o2[:, 0:128], [(qkT[:, tt, :], u[:, :])], [('g_qkT', tt), ur], [po2r])
                            o, orr = o_r()
                            if tt > 0:
                                t, tr_ = t_r()
                                self.act(t[:, :], ps_o1[:, 0:128], AF.Identity, [po1r, ('ea',)], [tr_],
                                         scale=P['ea'][:, tt, 24 + h:25 + h])
                                self.tt('dve', o[:, :], t[:, :], ps_o2[:, 0:128], ALU.add, [tr_, po2r], [orr])
                            else:
                                self.copy('dve', o[:, :], ps_o2[:, 0:128], [po2r], [orr])
                            if tt < 15:
                                kd, kdr = kd_r()
                                self.act(kd[:, :], k_tok[:, tt, :], AF.Identity, [('k_tok', tt), ('dte',)], [kdr],
                                         scale=P['dte'][:, tt, 24 + h:25 + h])
                                ps_sk, pskr = self.next_ps()
                                self.mm(ps_sk[:, 0:128], [(kd[:, :], u[:, :])], [kdr, ur], [pskr])
                                if tt == 0:
                                    self.copy('dve', S[:, :], ps_sk[:, 0:128], [pskr], [('g_S',)])
                                else:
                                    self.stt(S[:, :], S[:, :], P['cdec'][:, tt, 24 + h:25 + h], ps_sk[:, 0:128], ALU.mult, ALU.add,
                                             [('g_S',), ('cdec',), pskr], [('g_S',)])
                                self.copy('act', Sb[:, :], S[:, :], [('g_S',)], [('g_Sb',)])
                            sm, smr = sm_r()
                            jk, jr = jk_r()
                            self.act(jk[:, :], o[:, :], AF.Square, [orr], [jr, (smr, 0)], accum_out=sm[:, 0:1])
                            self.act(sm[:, 1:2], sm[:, 0:1], AF.Ln, [(smr, 0)], [(smr, 1)], bias=cc[:, 0:1], scale=1.0 / 128)
                            self.act(sm[:, 2:3], sm[:, 1:2], AF.Exp, [(smr, 1)], [(smr, 2)], scale=-0.5)
                            on, onr = on_r()
                            self.stt(on[:, :], o[:, :], sm[:, 2:3], P['gnw'][:, :], ALU.mult, ALU.mult, [orr, (smr, 2), ('gnw',)], [onr])
                            og, ogr = og_r()
                            self.tt('dve', og[:, :], on[:, :], siluzg[:, tt, i * 128:(i + 1) * 128], ALU.mult,
                                    [onr, ('siluzg', tt)], [ogr])
                            ps_t, ptr_ = self.next_ps()
                            psb = ps_t[:, :].bitcast(BF16)
                            self.tr(psb[:, 0:128], og[:, :], self.ident_b[:, :], [ogr, ('ident_b',)], [ptr_])
                            self.copy('act', ogT[:, i, ts_], psb[:, 0:128], [ptr_], [('ogT', i, tt)])

                        queue = []
                        for grp in range(16 // NCTX if GPH >= 2 else 0):
                            for _ in p2_group(grp):
                                if queue:
                                    queue.pop(0)()
                            if GPH >= 3:
                                for tt in range(grp * NCTX, (grp + 1) * NCTX):
                                    queue.append(lambda tt=tt: scan_a(tt))
                                    queue.append(lambda tt=tt: scan_b(tt))
                        while queue:
                            queue.pop(0)()
                        self.s.flush()
            with ExitStack() as es:
                wo = self.sb("wo_g", [128, 4, D], BF16, es)
                r0 = 1024 + hg * 512
                self.dma('pool', wo[:, :, :], Wout[r0:r0 + 512, :].rearrange("(c p) n -> p c n", p=128), [], [('wo_g',)])
                for tb in range(4):
                    t0, t1_ = tb * 512, (tb + 1) * 512
                    for dt in range(8):
                        ps, pr = self.next_ps()
                        self.mm(ps[:, :], [(wo[:, j, dt * 128:(dt + 1) * 128], ogT[:, j, t0:t1_]) for j in range(4)],
                                [('wo_g',)] + [('ogT', j, tt) for j in range(4) for tt in range(tb * 4, tb * 4 + 4)], [pr])
                        self.tt('dve', X[:, dt, t0:t1_], ps[:, :], X[:, dt, t0:t1_], ALU.add,
                                [pr] + XR(dt, t0, t1_), XR(dt, t0, t1_))
                self.s.flush()


_CACHE = {}


def consts():
    i = np.arange(128)
    c = {}
    c['c_ident'] = np.eye(128, dtype=np.float32)
    c['c_ones'] = np.ones((128, 128), np.float32)
    c['c_uincl'] = (i[:, None] >= i[None, :]).astype(np.float32)
    c['c_trile'] = (i[:, None] <= i[None, :]).astype(np.float32)
    c['c_gt'] = (i[:, None] > i[None, :]).astype(np.float32)
    blk = np.zeros((128, 128), np.float32)
    blk[:64, :64] = 1
    blk[64:, 64:] = 1
    c['c_blk'] = blk
    col = np.arange(512)
    c['c_mask4'] = np.stack([(col[None, :] > (i[:, None] + 128 * k)) for k in range(4)], axis=1).astype(np.float32)
    return c


def fm_cols(v):
    return np.ascontiguousarray(np.asarray(v, np.float32).reshape(8, 128).T)


def make_inputs(inp, nseq, ncores):
    f = lambda a: np.ascontiguousarray(np.asarray(a, dtype=np.float32))
    shared = consts()
    normw = np.stack([fm_cols(inp['a_norm_w'][0]), fm_cols(inp['mlp_norm_w'][0]),
                      fm_cols(inp['c_norm_w'][0]), fm_cols(inp['mlp_norm_w'][1])], axis=1)
    shared['normw'] = np.ascontiguousarray(normw)
    shared['mlp_w1'] = f(inp['mlp_w1'])
    shared['mlp_w2'] = f(inp['mlp_w2'])
    shared['c_w_qkv'] = f(inp['c_w_qkv'][0])
    shared['c_w_o'] = f(inp['c_w_o'][0])
    shared['qkw'] = np.ascontiguousarray(np.stack([np.tile(f(inp['c_q_norm_w'][0]), 2),
                                                   np.tile(f(inp['c_k_norm_w'][0]), 2)], axis=1))
    shared['a_w_in'] = f(inp['a_w_in'][0])
    shared['a_w_out'] = f(inp['a_w_out'][0])
    bc = lambda v: np.ascontiguousarray(np.broadcast_to(np.asarray(v, np.float32)[None, :], (128, len(v))))
    cws = f(inp['ssd_conv_w'][0])
    shared['convw_s'] = np.ascontiguousarray(cws.reshape(4, 12, 128).transpose(2, 1, 0))
    shared['convb_s'] = np.ascontiguousarray(f(inp['ssd_conv_b'][0]).reshape(12, 128).T)
    cwg = f(inp['gdn_conv_w'][0])
    shared['convw_g'] = np.ascontiguousarray(cwg.reshape(4, 24, 128).transpose(2, 1, 0))
    shared['dsk'] = bc(f(inp['ssd_d_skip'][0]))
    shared['snw'] = bc(f(inp['ssd_norm_w'][0]))
    shared['gnw'] = bc(f(inp['gdn_norm_w'][0]))
    z8 = np.zeros(8, np.float32)
    shared['bias_bc'] = bc(np.concatenate([f(inp['ssd_dt_bias'][0]), z8, f(inp['gdn_dt_bias'][0])]))
    shared['alog_bc'] = bc(np.concatenate([f(inp['ssd_a_log'][0]), z8, f(inp['gdn_a_log'][0])]))
    x = f(inp['x'])
    maps = []
    for c in range(ncores):
        m = dict(shared)
        m['x'] = np.ascontiguousarray(x[c * nseq:(c + 1) * nseq])
        maps.append(m)
    return maps


ALL_STAGES = ('mix0', 'mlp0', 'attn', 'mlp1')


def run(inp, nseq=2, ncores=8, stages=ALL_STAGES, trace=False):
    key = (nseq, tuple(stages))
    if key not in _CACHE:
        _CACHE[key] = Builder(nseq, stages).build()
    nc = _CACHE[key]
    maps = make_inputs(inp, nseq, ncores)
    res = run_bass_kernel_spmd(nc, maps, core_ids=list(range(ncores)), trace=trace)
    outs = [r["out"] for r in res.results]
    return np.concatenate(outs, axis=0), res


def kernel(**inputs):
    out, _ = run(inputs)
    return out.astype(np.float32)
```

```python
import os
import numpy as np
from contextlib import ExitStack
import concourse.bass as bass
import concourse.mybir as mybir
from concourse.bass_utils import run_bass_kernel_spmd

F32 = mybir.dt.float32
BF16 = mybir.dt.bfloat16
F32R = mybir.dt.float32r


def r32(ap):
    return ap.bitcast(F32R)
AF = mybir.ActivationFunctionType
ALU = mybir.AluOpType

L = 2048
D = 1024
EPS = 1e-6
EPOCH = 30000
NDSEM = 8


class Sched:
    CE = ('pe', 'act', 'dve', 'pool')

    def __init__(self, nc, esem, dsem):
        self.nc = nc
        self.esem = esem
        self.dsem = dsem
        self.cnt = {e: 0 for e in self.CE}
        self.ndma = {'sp': 0, 'pool': 0}
        self.seen = {e: {} for e in ('pe', 'act', 'dve', 'pool', 'sp')}
        self.reset()

    def reset(self):
        self.ops = {e: [] for e in ('pe', 'act', 'dve', 'pool', 'sp')}
        self.last_w = {}
        self.readers = {}

    def op(self, eng, fn, reads=(), writes=(), dma=False):
        if eng == 'pool' and not dma and os.environ.get("POOL2DVE"):
            eng = 'dve'
        self.nop_total = getattr(self, 'nop_total', 0) + 1
        cut = os.environ.get("OPCUT")
        if cut and self.nop_total > int(cut):
            return None
        if os.environ.get("OPTRACE"):
            import traceback
            fr = traceback.extract_stack(limit=4)
            print("OP", self.nop_total, eng, "dma" if dma else "", [f"{f.name}:{f.lineno}" for f in fr[:-1]])
        writes = list(writes) + [r for r in reads if r[0] in ('ps', 'psacc') and r not in writes]
        idx = len(self.ops[eng])
        tok = (eng, idx)
        deps = {}
        for r in reads:
            w = self.last_w.get(r)
            if w is not None:
                deps[w] = True
        for r in writes:
            w = self.last_w.get(r)
            if w is not None and w not in deps:
                deps[w] = False
            for t in self.readers.get(r, ()):
                if t not in deps:
                    deps[t] = False
        for r in reads:
            self.readers.setdefault(r, []).append(tok)
        for r in writes:
            self.last_w[r] = tok
            self.readers[r] = []
        self.ops[eng].append(dict(fn=fn, deps=deps, dma=dma))
        return tok

    def flush(self):
        nc = self.nc
        ops = self.ops
        need = set()
        for e, lst in ops.items():
            for i, o in enumerate(lst):
                keep = []
                for (e2, i2), raw in o['deps'].items():
                    o2 = ops[e2][i2]
                    if o2['dma']:
                        keep.append((e2, i2))
                    elif e2 == e:
                        if e == 'pe':
                            continue
                        keep.append((e2, i2))
                    else:
                        keep.append((e2, i2))
                o['keep'] = keep
                for k in keep:
                    if not ops[k[0]][k[1]]['dma']:
                        need.add(k)
        for e in self.CE:
            for i, o in enumerate(ops[e]):
                if o['dma']:
                    continue
                if (e, i) in need:
                    self.cnt[e] += 1
                    c = self.cnt[e]
                    o['sig'] = (self.esem[e][(c - 1) // EPOCH], (c - 1) % EPOCH + 1)
                else:
                    o['sig'] = None
        pending = {'sp': [], 'pool': []}
        for q in ('sp', 'pool'):
            for o in ops[q]:
                if o['dma']:
                    n = self.ndma[q]
                    self.ndma[q] += 1
                    o['sig'] = (self.dsem[q][n % NDSEM], 16 * (n // NDSEM + 1))
                    o['prewait'] = (self.dsem[q][n % NDSEM], 16 * (n // NDSEM)) if n >= NDSEM else None
                    pending[q].append(o['sig'])

        def emit(e, eng):
            seen = self.seen[e]

            def wait(s, v):
                if seen.get(s[0], 0) >= v:
                    return
                seen[s[0]] = v
                eng.wait_ge(s[1], v)

            for o in ops[e]:
                if o.get('prewait') is not None:
                    wait(*o['prewait'])
                for k in o['keep']:
                    sg = ops[k[0]][k[1]]['sig']
                    wait(*sg)
                ins = o['fn'](eng)
                if o['sig'] is not None:
                    ins.then_inc(o['sig'][0][1], 16 if o['dma'] else 1)
            if e in pending:
                for sg in pending[e][-NDSEM:]:
                    wait(*sg)

        with nc.Block() as block:
            if ops['sp']:
                @block.sync
                def _(eng):
                    emit('sp', eng)
            if ops['pe']:
                @block.tensor
                def _(eng):
                    emit('pe', eng)
            if ops['act']:
                @block.scalar
                def _(eng):
                    emit('act', eng)
            if ops['dve']:
                @block.vector
                def _(eng):
                    emit('dve', eng)
            if ops['pool']:
                @block.gpsimd
                def _(eng):
                    emit('pool', eng)
        self.reset()


def XR(c, t0, t1):
    return [('X', c, tt) for tt in range(t0 // 128, (t1 + 127) // 128)]


def HR(c, t0, t1):
    return [('H', c, tt) for tt in range(t0 // 128, (t1 + 127) // 128)]


class Builder:
    def __init__(self, nseq, stages):
        self.nseq = nseq
        self.stages = stages
        nc = bass.Bass("TRN2", target_bir_lowering=False)
        self.nc = nc
        self.es = ExitStack()
        self.dram = {}
        self._uid = 0

    def din(self, name, shape, dtype=F32):
        t = self.nc.dram_tensor(name, list(shape), dtype, kind="ExternalInput").ap()
        self.dram[name] = t
        return t

    def sb(self, name, shape, dtype, es=None):
        es = es or self.es
        return es.enter_context(self.nc.sbuf_tensor(f"{name}_u{self.uid()}", list(shape), dtype))

    def psum(self, name, shape, dtype):
        return self.es.enter_context(self.nc.psum_tensor(name, list(shape), dtype))

    def uid(self):
        self._uid += 1
        return self._uid

    def mm(self, out, pairs, reads, writes):
        pairs = list(pairs)

        def fn(pe):
            n = len(pairs)
            ins = None
            for i, (l, r) in enumerate(pairs):
                ins = pe.matmul(out, l, r, start=(i == 0), stop=(i == n - 1))
            return ins
        self.s.op('pe', fn, reads, writes)

    def mm1(self, out, l, r, start, stop, reads, writes):
        self.s.op('pe', lambda pe: pe.matmul(out, l, r, start=start, stop=stop), reads, writes)

    def asel(self, out, in_, base, cm, n, reads, writes):
        self.s.op('pool', lambda e: e.affine_select(out, in_, [[1, n]], ALU.is_gt, 0.0, base=base,
                                                    channel_multiplier=cm), reads, writes)

    def tr(self, out, in_, ident, reads, writes):
        self.s.op('pe', lambda pe: pe.transpose(out, in_, ident), reads, writes)

    def act(self, out, in_, func, reads, writes, bias=None, scale=None, accum_out=None, eng='act'):
        kw = {}
        if bias is not None:
            kw['bias'] = bias
        if scale is not None:
            kw['scale'] = scale
        if accum_out is not None:
            kw['accum_out'] = accum_out
        self.s.op('act', lambda e: e.activation(out, in_, func, **kw), reads, writes)

    def tt(self, eng, out, a, b, op, reads, writes):
        self.s.op(eng, lambda e: e.tensor_tensor(out, a, b, op), reads, writes)

    def ts(self, eng, out, a, s1, s2, op0, op1, reads, writes):
        if op1 is None:
            self.s.op(eng, lambda e: e.tensor_scalar(out, a, s1, None, op0), reads, writes)
        else:
            self.s.op(eng, lambda e: e.tensor_scalar(out, a, s1, s2, op0, op1), reads, writes)

    def stt(self, out, a, sc, b, op0, op1, reads, writes):
        self.s.op('dve', lambda e: e.scalar_tensor_tensor(out, a, sc, b, op0, op1), reads, writes)

    def copy(self, eng, out, in_, reads, writes):
        if eng == 'act':
            self.s.op('act', lambda e: e.copy(out, in_), reads, writes)
        else:
            self.s.op(eng, lambda e: e.tensor_copy(out, in_), reads, writes)

    def dma(self, q, out, in_, reads, writes):
        self.s.op(q, lambda e: e.dma_start(out=out, in_=in_), reads, writes, dma=True)

    def next_ps(self):
        i = self._psi
        self._psi = (i + 1) % len(self.psr)
        return self.psr[i], ('ps', i)

    def build(self):
        nc = self.nc
        ns = self.nseq
        x = self.din("x", [ns, L, D])
        out = nc.dram_tensor("out", [ns, L, D], F32, kind="ExternalOutput").ap()
        self.x_d, self.out_d = x, out
        d = self.din
        W = {}
        W['ident'] = d("c_ident", [128, 128])
        W['ones'] = d("c_ones", [128, 128])
        W['uincl'] = d("c_uincl", [128, 128])
        W['trile'] = d("c_trile", [128, 128])
        W['gt'] = d("c_gt", [128, 128])
        W['blk'] = d("c_blk", [128, 128])
        W['mask4'] = d("c_mask4", [128, 4, 512])
        W['normw'] = d("normw", [128, 4, 8])
        W['mlp_w1'] = d("mlp_w1", [2, D, 4096])
        W['mlp_w2'] = d("mlp_w2", [2, 4096, D])
        W['c_w_qkv'] = d("c_w_qkv", [D, 3072])
        W['c_w_o'] = d("c_w_o", [D, D])
        W['qkw'] = d("qkw", [128, 2])
        W['a_w_in'] = d("a_w_in", [D, 6688])
        W['a_w_out'] = d("a_w_out", [2048, D])
        W['convw_s'] = d("convw_s", [128, 12, 4])
        W['convb_s'] = d("convb_s", [128, 12])
        W['convw_g'] = d("convw_g", [128, 24, 4])
        W['dsk'] = d("dsk", [128, 16])
        W['snw'] = d("snw", [128, 1024])
        W['gnw'] = d("gnw", [128, 128])
        W['bias_bc'] = d("bias_bc", [128, 32])
        W['alog_bc'] = d("alog_bc", [128, 32])
        self.W = W

        es = self.es
        sems = {}
        si = [0]

        def newsem(name):
            h = es.enter_context(nc.semaphore(name))
            si[0] += 1
            return (si[0], h)
        esem = {e: [newsem(f"s_{e}{i}") for i in range(3)] for e in Sched.CE}
        dsem = {q: [newsem(f"d_{q}{i}") for i in range(NDSEM)] for q in ('sp', 'pool')}
        self.s = Sched(nc, esem, dsem)

        self.X = self.sb("X", [128, 8, L], F32)
        self.H = self.sb("H", [128, 8, L], BF16)
        self.ident_f = self.sb("ident_f", [128, 128], F32)
        self.ones_f = self.sb("ones_f", [128, 128], F32)
        self.uinclneg_f = self.sb("uinclneg_f", [128, 128], F32)
        self.uincl_f = self.sb("uincl_f", [128, 128], F32)
        self.trile_f = self.sb("trile_f", [128, 128], F32)
        self.gt_f = self.sb("gt_f", [128, 128], F32)
        self.gtle_f = self.sb("gtle_f", [128, 256], F32)
        self.ones_r = self.sb("ones_r", [128, 128], F32)
        self.trile_r = self.sb("trile_r", [128, 128], F32)
        self.ones_b = self.sb("ones_b", [128, 128], BF16)
        self.ident_b = self.sb("ident_b", [128, 128], BF16)
        self.blk_b = self.sb("blk_b", [128, 128], BF16)
        self.normw = self.sb("normw_sb", [128, 4, 8], F32)
        self.qkw = self.sb("qkw_sb", [128, 2], F32)
        self.cc = self.sb("constcols", [128, 8], F32)
        self.psr = [self.psum(f"ps{i}", [128, 512], F32) for i in range(6)]
        self.psacc = [self.psum(f"psacc{i}", [128, 512], F32) for i in range(2)]
        self._psi = 0

        self.stage_consts()
        for sq in range(ns):
            self.stage_load(sq)
            if 'mix0' in self.stages:
                self.stage_mix0()
            if 'mlp0' in self.stages:
                self.stage_mlp(0)
            if 'attn' in self.stages:
                self.stage_attn()
            if 'mlp1' in self.stages:
                self.stage_mlp(1)
            self.stage_store(sq)
        self.es.close()
        return nc

    def stage_consts(self):
        W = self.W
        q = 'sp'
        for name, t in (('ident', self.ident_f), ('ones', self.ones_f), ('uincl', self.uincl_f),
                        ('trile', self.trile_f), ('gt', self.gt_f)):
            self.dma(q, t[:, :], W[name][:, :], [], [(name,)])
        self.dma(q, self.gtle_f[:, 0:128], W['gt'][:, :], [], [('gtle',)])
        self.dma(q, self.gtle_f[:, 128:256], W['trile'][:, :], [], [('gtle',)])
        self.dma(q, self.normw[:, :, :], W['normw'][:, :, :], [], [('normw',)])
        self.dma(q, self.qkw[:, :], W['qkw'][:, :], [], [('qkw',)])
        for i, v in enumerate((EPS, float(np.log(0.125)), 1.0, 0.0, float(-0.5 * np.log(128.0)))):
            self.s.op('pool', (lambda e, i=i, v=v: e.memset(self.cc[:, i:i + 1], v)), [], [('cc', i)])
        with ExitStack() as es:
            tmp = self.sb("ctmp", [128, 128], F32, es)
            self.dma(q, tmp[:, :], W['blk'][:, :], [], [('ctmp',)])
            self.copy('dve', self.blk_b[:, :], tmp[:, :], [('ctmp',)], [('blk_b',)])
            self.copy('dve', self.ones_b[:, :], self.ones_f[:, :], [('ones',)], [('ones_b',)])
            self.copy('dve', self.ident_b[:, :], self.ident_f[:, :], [('ident',)], [('ident_b',)])
            self.ts('dve', r32(self.uinclneg_f[:, :]), self.uincl_f[:, :], -1.0, None, ALU.mult, None,
                    [('uincl',)], [('uinclneg',)])
            self.copy('dve', r32(self.ones_r[:, :]), self.ones_f[:, :], [('ones',)], [('ones_r',)])
            self.copy('dve', r32(self.trile_r[:, :]), self.trile_f[:, :], [('trile',)], [('trile_r',)])
            self.s.flush()

    def stage_load(self, sq):
        X = self.X
        with ExitStack() as es:
            stg = [self.sb(f"ldstg{i}", [128, D], F32, es) for i in range(2)]
            for tt in range(16):
                st = stg[tt % 2]
                sr = ('ldstg', tt % 2)
                self.dma('sp', st[:, :], self.x_d[sq, tt * 128:(tt + 1) * 128, :], [], [sr])
                for half in range(2):
                    ps, pr = self.next_ps()
                    for j in range(4):
                        c = half * 4 + j
                        self.tr(ps[:, j * 128:(j + 1) * 128], st[:, c * 128:(c + 1) * 128], self.ident_f[:, :],
                                [sr, ('ident',)], [pr])
                    dst = X[:, half * 4:half * 4 + 4, tt * 128:(tt + 1) * 128]
                    src = ps[:, :].rearrange("p (c t) -> p c t", c=4)
                    wr = [('X', half * 4 + j, tt) for j in range(4)]
                    self.copy('act' if half == 0 else 'dve', dst, src, [pr], wr)
            self.s.flush()

    def stage_store(self, sq):
        X = self.X
        with ExitStack() as es:
            stg = [self.sb(f"ststg{i}", [128, D], F32, es) for i in range(2)]
            for tt in range(16):
                st = stg[tt % 2]
                sr = ('ststg', tt % 2)
                for half in range(2):
                    ps, pr = self.next_ps()
                    for j in range(4):
                        c = half * 4 + j
                        self.tr(ps[:, j * 128:(j + 1) * 128], X[:, c, tt * 128:(tt + 1) * 128], self.ident_f[:, :],
                                [('X', c, tt), ('ident',)], [pr])
                    self.copy('act' if half == 0 else 'dve', st[:, half * 512:(half + 1) * 512], ps[:, :], [pr], [sr])
                self.dma('sp', self.out_d[sq, tt * 128:(tt + 1) * 128, :], st[:, :], [sr], [('out', sq, tt)])
            self.s.flush()

    def rmsnorm(self, widx):
        with ExitStack() as es:
            self._rmsnorm(widx, es)
            self.s.flush()

    def _rmsnorm(self, widx, es):
        X, H = self.X, self.H
        sq = [self.sb(f"rn_sq{i}", [128, 8, 512], BF16, es) for i in range(2)]
        lnv = [self.sb(f"rn_ln{i}", [128, 512], F32, es) for i in range(2)]
        rstd = [self.sb(f"rn_rs{i}", [128, 512], F32, es) for i in range(2)]
        for tb in range(4):
            b = tb % 2
            t0, t1 = tb * 512, (tb + 1) * 512
            for c in range(8):
                if c % 2 == 0:
                    self.act(sq[b][:, c, :], X[:, c, t0:t1], AF.Square, XR(c, t0, t1), [('rn_sq', b, c)])
                else:
                    self.tt('pool', sq[b][:, c, :], X[:, c, t0:t1], X[:, c, t0:t1], ALU.mult,
                            XR(c, t0, t1), [('rn_sq', b, c)])
            ps, pr = self.next_ps()
            self.mm(ps[:, :], [(self.ones_b[:, :], sq[b][:, c, :]) for c in range(8)],
                    [('rn_sq', b, c) for c in range(8)] + [('ones_b',)], [pr])
            self.act(lnv[b][:, :], ps[:, :], AF.Ln, [pr], [('rn_ln', b)], bias=self.cc[:, 0:1], scale=1.0 / D)
            self.act(rstd[b][:, :], lnv[b][:, :], AF.Exp, [('rn_ln', b)], [('rn_rs', b)], scale=-0.5)
            for c in range(8):
                self.stt(H[:, c, t0:t1], X[:, c, t0:t1], self.normw[:, widx, c:c + 1], rstd[b][:, :],
                         ALU.mult, ALU.mult, XR(c, t0, t1) + [('rn_rs', b), ('normw',)], HR(c, t0, t1))

    def stage_mlp(self, layer):
        X, H = self.X, self.H
        W1 = self.W['mlp_w1']
        W2 = self.W['mlp_w2']
        with ExitStack() as es:
            self._rmsnorm(1 + 2 * layer, es)
            w1 = [self.sb(f"w1_{i}", [128, 8, 512], BF16, es) for i in range(2)]
            w2 = [self.sb(f"w2_{i}", [128, 4, D], BF16, es) for i in range(2)]
            A = [self.sb(f"mlpA{i}", [128, 4, L], BF16, es) for i in range(2)]
            R = [self.sb(f"mlpR{i}", [128, 512], F32, es) for i in range(3)]
            ri = 0
            for e in range(8):
                b = e % 2
                self.dma('pool', w1[b][:, :, :],
                         W1[layer, :, e * 512:(e + 1) * 512].rearrange("(c p) n -> p c n", p=128),
                         [], [('w1', b)])
                self.dma('pool', w2[b][:, :, :],
                         W2[layer, e * 512:(e + 1) * 512, :].rearrange("(c p) n -> p c n", p=128),
                         [], [('w2', b)])
                for tb in range(4):
                    t0, t1 = tb * 512, (tb + 1) * 512
                    for j in range(4):
                        ps, pr = self.next_ps()
                        self.mm(ps[:, :], [(w1[b][:, kc, j * 128:(j + 1) * 128], H[:, kc, t0:t1]) for kc in range(8)],
                                [('w1', b)] + [r for kc in range(8) for r in HR(kc, t0, t1)], [pr])
                        r = R[ri % 3]
                        rr = ('mlpR', ri % 3)
                        ri += 1
                        self.act(r[:, :], ps[:, :], AF.Relu, [pr], [rr])
                        self.tt('pool', A[b][:, j, t0:t1], r[:, :], r[:, :], ALU.mult, [rr], [('mlpA', b, j, tb)])
                for tb in range(4):
                    t0, t1 = tb * 512, (tb + 1) * 512
                    for dt in range(8):
                        ps, pr = self.next_ps()
                        self.mm(ps[:, :], [(w2[b][:, j, dt * 128:(dt + 1) * 128], A[b][:, j, t0:t1]) for j in range(4)],
                                [('w2', b)] + [('mlpA', b, j, tb) for j in range(4)], [pr])
                        self.tt('dve', X[:, dt, t0:t1], ps[:, :], X[:, dt, t0:t1], ALU.add,
                                [pr] + XR(dt, t0, t1), XR(dt, t0, t1))
            self.s.flush()

    def stage_attn(self):
        X, H = self.X, self.H
        Wqkv = self.W['c_w_qkv']
        Wo = self.W['c_w_o']
        cc = self.cc
        self.rmsnorm(2)
        with ExitStack() as es0:
            qT = self.sb("qT", [128, 8, L], BF16, es0)
            kT = self.sb("kT", [128, 8, L], BF16, es0)
            with ExitStack() as es:
                wq = [self.sb(f"wq{i}", [128, 8, 128], BF16, es) for i in range(3)]
                wv = [self.sb(f"wv{i}", [128, 8, 512], BF16, es) for i in range(2)]
                qraw = [self.sb(f"qraw{i}", [128, 512], F32, es) for i in range(2)]
                sqq = [self.sb(f"sqq{i}", [128, 512], BF16, es) for i in range(2)]
                lnv = [self.sb(f"qln{i}", [128, 512], F32, es) for i in range(2)]
                rstd = [self.sb(f"qrs{i}", [128, 512], F32, es) for i in range(2)]
                for half in range(2):
                    self.dma('pool', wv[half][:, :, :],
                             Wqkv[:, 2048 + half * 512:2048 + (half + 1) * 512].rearrange("(c p) n -> p c n", p=128),
                             [], [('wv', half)])
                wi = 0
                PA = int(os.environ.get("ATT_PA", "3"))
                items = []
                for c in range(8 if (PA & 1) else 0):
                    for which in range(2):
                        col0 = which * 1024 + c * 128
                        w = wq[wi % 3]
                        wr = ('wq', wi % 3)
                        wi += 1
                        for tb in range(4):
                            items.append(dict(c=c, which=which, col0=col0, w=w, wr=wr, tb=tb, i=len(items)))

                def qk_a(it):
                    if it['tb'] == 0:
                        self.dma('pool', it['w'][:, :, :],
                                 Wqkv[:, it['col0']:it['col0'] + 128].rearrange("(c p) n -> p c n", p=128), [], [it['wr']])
                    t0, t1 = it['tb'] * 512, (it['tb'] + 1) * 512
                    b = it['i'] % 2
                    ps, pr = self.next_ps()
                    self.mm(ps[:, :], [(it['w'][:, kc, :], H[:, kc, t0:t1]) for kc in range(8)],
                            [it['wr']] + [r for kc in range(8) for r in HR(kc, t0, t1)], [pr])
                    self.act(sqq[b][:, :], ps[:, :], AF.Square, [pr], [('sqq', b)])
                    self.copy('dve', qraw[b][:, :], ps[:, :], [pr], [('qraw', b)])

                def qk_b(it):
                    which, c, tb = it['which'], it['c'], it['tb']
                    t0, t1 = tb * 512, (tb + 1) * 512
                    b = it['i'] % 2
                    dst = qT if which == 0 else kT
                    dn = 'qT' if which == 0 else 'kT'
                    ps2, pr2 = self.next_ps()
                    self.mm(ps2[:, :], [(self.blk_b[:, :], sqq[b][:, :])], [('sqq', b), ('blk_b',)], [pr2])
                    self.act(lnv[b][:, :], ps2[:, :], AF.Ln, [pr2], [('qln', b)], bias=cc[:, 0:1], scale=1.0 / 64)
                    self.act(rstd[b][:, :], lnv[b][:, :], AF.Exp, [('qln', b)], [('qrs', b)],
                             bias=cc[:, 1:2] if which == 0 else cc[:, 3:4], scale=-0.5)
                    self.stt(dst[:, c, t0:t1], qraw[b][:, :], self.qkw[:, which:which + 1], rstd[b][:, :],
                             ALU.mult, ALU.mult, [('qraw', b), ('qrs', b), ('qkw',)], [(dn, c, tb)])

                for t in range(len(items) + 1):
                    if t < len(items):
                        qk_a(items[t])
                    if t >= 1:
                        qk_b(items[t - 1])
                for tt in range(16 if (PA & 2) else 0):
                    t0, t1 = tt * 128, (tt + 1) * 128
                    pss = []
                    for half in range(2):
                        ps, pr = self.next_ps()
                        self.mm(ps[:, :], [(H[:, kc, t0:t1], wv[half][:, kc, :]) for kc in range(8)],
                                [('wv', half)] + [('H', kc, tt) for kc in range(8)], [pr])
                        pss.append((ps, pr))
                    for half in range(2):
                        ps, pr = pss[half]
                        self.copy('act' if half == 0 else 'dve', H[:, half * 4:half * 4 + 4, t0:t1],
                                  ps[:, :].rearrange("p (c t) -> p c t", c=4), [pr],
                                  [('H', half * 4 + j, tt) for j in range(4)])
                self.s.flush()
            with ExitStack() as es:
                eb = [self.sb(f"at_e{i}", [128, 512], F32, es) for i in range(3)]
                spb = [self.sb(f"at_sp{i}", [128, 512], F32, es) for i in range(3)]
                tmpb = [self.sb(f"at_tmp{i}", [128, 512], F32, es) for i in range(2)]
                Rs = [self.sb(f"at_rs{i}", [128, 512], F32, es) for i in range(2)]
                ATb = [self.sb(f"at_A{i}", [128, 512], BF16, es) for i in range(3)]
                qpad = [[self.sb(f"qpad{par}{j}", [128, 512], BF16, es) for j in range(2)] for par in range(2)]
                for par in range(2):
                    for j in range(2):
                        self.s.op('pool', (lambda e, t=qpad[par][j]: e.memset(t[:, :], 0.0)), [], [('qpad', par, j)])
                self.mask4 = self.sb("mask4", [128, 4, 512], F32, es)
                self.dma('sp', self.mask4[:, :, :], self.W['mask4'][:, :, :], [], [('mask4',)])
                it_g = 0
                ai = 0
                pairs = []
                gi = 0
                for h in range(16):
                    for G in range(4):
                        kmax = 4 * G + 3
                        for kb in range(kmax, -1, -1):
                            pairs.append(dict(h=h, G=G, kb=kb, first=(kb == kmax), last=(kb == 0), diag=(kb >= 4 * G),
                                              gi=gi, i=len(pairs)))
                        gi += 1
                rstate = {'cur': 0}

                def opnd(p):
                    h, G, kb = p['h'], p['G'], p['kb']
                    c = h // 2
                    b0 = (h % 2) * 64
                    par, j = h % 2, p['gi'] % 2
                    q_s = qpad[par][j][:, :]
                    k_s = kT[:, c, kb * 128:(kb + 1) * 128]
                    return c, b0, q_s, k_s, ('qpad', par, j), ('kT', c, kb // 4)

                def stA(p):
                    c, b0, q_s, k_s, qr, kr = opnd(p)
                    b = p['i'] % 3
                    if p['first']:
                        G = p['G']
                        self.copy('dve', q_s[b0:b0 + 64, :], qT[b0:b0 + 64, c, G * 512:(G + 1) * 512], [('qT', c, G)], [qr])
                    ps_z, pzr = self.next_ps()
                    self.mm(ps_z[:, :], [(k_s, q_s)], [kr, qr], [pzr])
                    self.act(eb[b][:, :], ps_z[:, :], AF.Exp, [pzr], [('at_e', b)])
                    self.act(r32(spb[b][:, :]), eb[b][:, :], AF.Ln, [('at_e', b)], [('at_sp', b)], bias=cc[:, 2:3])
                    if p['diag']:
                        self.tt('pool', r32(spb[b][:, :]), spb[b][:, :], self.mask4[:, p['kb'] - 4 * p['G'], :], ALU.mult,
                                [('at_sp', b), ('mask4',)], [('at_sp', b)])

                def stB(p):
                    c, b0, q_s, k_s, qr, kr = opnd(p)
                    b = p['i'] % 3
                    b2 = p['i'] % 2
                    c0 = 128 * (p['kb'] - 4 * p['G']) if p['diag'] else 0
                    if p['first']:
                        self.s.op('pool', lambda e: e.memset(Rs[0][:, :], 0.0), [], [('at_rs', 0)])
                    ps_n, pnr = self.next_ps()
                    self.mm1(ps_n[:, :], k_s, q_s, True, False, [kr, qr], [pnr])
                    self.mm1(ps_n[:, c0:512], r32(self.uinclneg_f[:, :]), r32(spb[b][:, c0:512]), False, True,
                             [('at_sp', b), ('uinclneg',)], [pnr])
                    if p['first']:
                        self.act(ATb[b][:, :], ps_n[:, :], AF.Exp, [pnr], [('at_A', b)])
                    else:
                        self.tt('dve', tmpb[b2][:, :], ps_n[:, :], Rs[0][:, :], ALU.subtract,
                                [pnr, ('at_rs', 0)], [('at_tmp', b2)])
                        self.act(ATb[b][:, :], tmpb[b2][:, :], AF.Exp, [('at_tmp', b2)], [('at_A', b)])
                    if p['diag']:
                        self.tt('pool', ATb[b][:, :], ATb[b][:, :], self.mask4[:, p['kb'] - 4 * p['G'], :], ALU.mult,
                                [('at_A', b), ('mask4',)], [('at_A', b)])
                    if not p['last']:
                        ps_r, prr = self.next_ps()
                        self.mm(ps_r[:, c0:512], [(r32(self.ones_r[:, :]), r32(spb[b][:, c0:512]))], [('at_sp', b), ('ones_r',)], [prr])
                        self.tt('dve', Rs[0][:, c0:512], ps_r[:, c0:512], Rs[0][:, c0:512], ALU.add,
                                [prr, ('at_rs', 0)], [('at_rs', 0)])

                def stC(p):
                    c, b0, q_s, k_s, qr, kr = opnd(p)
                    h, G, kb = p['h'], p['G'], p['kb']
                    b = p['i'] % 3
                    acc = self.psacc[p['gi'] % 2]
                    accr = ('psacc', p['gi'] % 2)
                    v_s = H[:, c, kb * 128:(kb + 1) * 128]
                    self.mm1(acc[:, :], v_s, ATb[b][:, :], p['first'], p['last'],
                             [('H', c, kb), ('at_A', b)], [accr])
                    if p['last']:
                        self.copy('act', qT[b0:b0 + 64, c, G * 512:(G + 1) * 512], acc[b0:b0 + 64, :], [accr], [('qT', c, G)])

                n = len(pairs)
                for t in range(n + 2):
                    if t < n:
                        stA(pairs[t])
                    if 0 <= t - 1 < n:
                        stB(pairs[t - 1])
                    if 0 <= t - 2 < n:
                        stC(pairs[t - 2])
                self.s.flush()
            with ExitStack() as es:
                wo = [self.sb(f"wo{i}", [128, 8, 128], BF16, es) for i in range(2)]
                for dt in range(8):
                    w = wo[dt % 2]
                    wr = ('wo', dt % 2)
                    self.dma('pool', w[:, :, :], Wo[:, dt * 128:(dt + 1) * 128].rearrange("(c p) n -> p c n", p=128),
                             [], [wr])
                    for tb in range(4):
                        t0, t1 = tb * 512, (tb + 1) * 512
                        ps, pr = self.next_ps()
                        self.mm(ps[:, :], [(w[:, c, :], qT[:, c, t0:t1]) for c in range(8)],
                                [wr] + [('qT', c, tb) for c in range(8)], [pr])
                        self.tt('dve', X[:, dt, t0:t1], ps[:, :], X[:, dt, t0:t1], ALU.add,
                                [pr] + XR(dt, t0, t1), XR(dt, t0, t1))
                self.s.flush()

    def ring(self, name, n, shape, dtype, es):
        tiles = [self.sb(f"{name}{i}", shape, dtype, es) for i in range(n)]
        state = {'i': 0}

        def nxt():
            i = state['i'] % n
            state['i'] += 1
            return tiles[i], (name, i)
        nxt.tiles = tiles
        return nxt

    def conv_proj(self, col0, R):
        H = self.H
        Win = self.W['a_w_in']
        w, wr = R['w']()
        self.dma('pool', w[:, :, :], Win[:, col0:col0 + 128].rearrange("(c p) n -> p c n", p=128), [], [wr])
        raw, rr = R['raw']()
        for tb in range(4):
            t0, t1 = tb * 512, (tb + 1) * 512
            ps, pr = self.next_ps()
            self.mm(ps[:, :], [(w[:, kc, :], H[:, kc, t0:t1]) for kc in range(8)],
                    [wr] + [r for kc in range(8) for r in HR(kc, t0, t1)], [pr])
            self.copy('act', raw[:, 3 + t0:3 + t1], ps[:, :], [pr], [rr])
        return raw, rr

    def conv_act(self, raw, rr, cw, cb, R, silu_out=None, silu_reg=None):
        acc, ar = R['acc']()
        if cb is not None:
            self.ts('dve', acc[:, :], raw[:, 3:3 + L], cw[:, 3:4], cb, ALU.mult, ALU.add, [rr, ('convw',)], [ar])
        else:
            self.ts('dve', acc[:, :], raw[:, 3:3 + L], cw[:, 3:4], None, ALU.mult, None, [rr, ('convw',)], [ar])
        for k in (2, 1, 0):
            self.stt(acc[:, :], raw[:, k:k + L], cw[:, k:k + 1], acc[:, :], ALU.mult, ALU.add, [rr, ar, ('convw',)], [ar])
        if silu_out is not None:
            self.act(silu_out, acc[:, :], AF.Silu, [ar], [silu_reg])
            return silu_out, silu_reg
        self.act(acc[:, :], acc[:, :], AF.Silu, [ar], [ar])
        return acc, ar

    def conv_pipeline(self, tiles, R):
        nxt = self.conv_proj(tiles[0]['col'], R)
        for j, t in enumerate(tiles):
            cur = nxt
            if j + 1 < len(tiles):
                nxt = self.conv_proj(tiles[j + 1]['col'], R)
            out, outr = self.conv_act(cur[0], cur[1], t['cw'], t['cb'], R, t.get('silu_out'), t.get('silu_reg'))
            if t.get('post'):
                t['post'](out, outr)

    def to_tok(self, src, sr, dst, dr_name, width_off, src_regs=None):
        for q in range(4):
            ps, pr = self.next_ps()
            psb = ps[:, :].bitcast(BF16)
            for j in range(4):
                tt = q * 4 + j
                self.tr(psb[:, j * 128:(j + 1) * 128], src[:, tt * 128:(tt + 1) * 128], self.ident_b[:, :],
                        (src_regs if src_regs is not None else [sr]) + [('ident_b',)], [pr])
            self.copy('act' if q % 2 == 0 else 'dve', dst[:, q * 4:q * 4 + 4, width_off:width_off + 128],
                      psb[:, 0:512].rearrange("p (c t) -> p c t", c=4), [pr],
                      [(dr_name, q * 4 + j) for j in range(4)])

    def stage_mix0(self):
        X, H = self.X, self.H
        W = self.W
        cc = self.cc
        Win = W['a_w_in']
        Wout = W['a_w_out']
        self.rmsnorm(0)
        with ExitStack() as es0:
            sb = lambda n, sh, dt=F32: self.sb(n, sh, dt, es0)
            convw_s = sb("convw_s", [128, 12, 4])
            convb_s = sb("convb_s", [128, 12])
            convw_g = sb("convw_g", [128, 24, 4])
            dsk = sb("dsk", [128, 16])
            gnw = sb("gnw", [128, 128])
            spl = sb("spl", [128, 16, 32])
            ag = sb("ag", [128, 16, 32])
            ea = sb("ea", [128, 16, 32])
            dte = sb("dte", [128, 16, 32])
            cdec = sb("cdec", [128, 16, 32])
            bet = sb("bet", [128, 16, 8])
            nbet = sb("nbet", [128, 16, 8])
            bg = sb("bg", [128, 16, 8])
            self.dma('sp', convw_s[:, :, :], W['convw_s'][:, :, :], [], [('convw',)])
            self.dma('sp', convb_s[:, :], W['convb_s'][:, :], [], [('convw',)])
            self.dma('sp', convw_g[:, :, :], W['convw_g'][:, :, :], [], [('convw',)])
            self.dma('sp', dsk[:, :], W['dsk'][:, :], [], [('dsk',)])
            self.dma('sp', gnw[:, :], W['gnw'][:, :], [], [('gnw',)])
            with ExitStack() as es:
                wsm = self.sb("wsm", [128, 8, 32], BF16, es)
                bias_bc = self.sb("bias_bc", [128, 32], F32, es)
                alog = self.sb("alog", [128, 32], F32, es)
                aneg = self.sb("aneg", [128, 32], F32, es)
                smraw = self.sb("smraw", [128, 16, 32], F32, es)
                e1 = self.sb("sm_e", [128, 16, 32], F32, es)
                acum = self.sb("acum", [128, 16, 32], F32, es)
                tot = self.sb("tot", [128, 16, 32], F32, es)
                tmp = self.sb("sm_tmp", [128, 16, 32], F32, es)
                self.dma('pool', wsm[:, :, 0:16], Win[:, 2560:2576].rearrange("(c p) n -> p c n", p=128), [], [('wsm', 0)])
                self.dma('pool', wsm[:, :, 16:32], Win[:, 6672:6688].rearrange("(c p) n -> p c n", p=128), [], [('wsm', 1)])
                self.dma('sp', bias_bc[:, :], W['bias_bc'][:, :], [], [('bias_bc',)])
                self.dma('sp', alog[:, :], W['alog_bc'][:, :], [], [('alog',)])
                self.act(aneg[:, :], alog[:, :], AF.Exp, [('alog',)], [('aneg',)])
                self.ts('dve', aneg[:, :], aneg[:, :], -1.0, None, ALU.mult, None, [('aneg',)], [('aneg',)])
                for tt in range(16):
                    ps, pr = self.next_ps()
                    self.mm(ps[:, 0:32], [(H[:, kc, tt * 128:(tt + 1) * 128], wsm[:, kc, :]) for kc in range(8)],
                            [('wsm', 0), ('wsm', 1)] + [('H', kc, tt) for kc in range(8)], [pr])
                    self.tt('dve', smraw[:, tt, :], ps[:, 0:32], bias_bc[:, :], ALU.add, [pr, ('bias_bc',)], [('smraw',)])
                f2 = lambda t: t[:, :, :].rearrange("p a b -> p (a b)")
                self.act(f2(e1), f2(smraw), AF.Exp, [('smraw',)], [('sm_e',)])
                self.act(f2(spl), f2(e1), AF.Ln, [('sm_e',)], [('spl',)], bias=cc[:, 2:3])
                self.ts('dve', tmp[:, :, 16:24], e1[:, :, 16:24], 1.0, None, ALU.add, None, [('sm_e',)], [('sm_tmp',)])
                self.s.op('dve', lambda e: e.reciprocal(tmp[:, :, 16:24], tmp[:, :, 16:24]), [('sm_tmp',)], [('sm_tmp',)])
                self.tt('dve', bet[:, :, :], e1[:, :, 16:24], tmp[:, :, 16:24], ALU.mult, [('sm_e',), ('sm_tmp',)], [('bet',)])
                self.ts('dve', nbet[:, :, :], bet[:, :, :], -1.0, None, ALU.mult, None, [('bet',)], [('nbet',)])
                self.tt('dve', ag[:, :, :], spl[:, :, :], aneg[:, :].unsqueeze(1).broadcast_to([128, 16, 32]), ALU.mult,
                        [('spl',), ('aneg',)], [('ag',)])
                ps, pr = self.next_ps()
                self.mm(ps[:, :], [(self.trile_f[:, :], f2(ag))], [('ag',), ('trile',)], [pr])
                self.copy('dve', f2(acum), ps[:, :], [pr], [('acum',)])
                self.act(f2(ea), ps[:, :], AF.Exp, [pr], [('ea',)])
                ps2, pr2 = self.next_ps()
                self.mm(ps2[:, :], [(self.ones_f[:, :], f2(ag))], [('ag',), ('ones',)], [pr2])
                self.act(f2(cdec), ps2[:, :], AF.Exp, [pr2], [('cdec',)])
                self.tt('dve', f2(tot), ps2[:, :], f2(acum), ALU.subtract, [pr2, ('acum',)], [('tot',)])
                self.act(f2(dte), f2(tot), AF.Exp, [('tot',)], [('dte',)])
                self.tt('dve', bg[:, :, :], bet[:, :, :], ea[:, :, 24:32], ALU.mult, [('bet',), ('ea',)], [('bg',)])
                self.s.flush()
            P = dict(convw_s=convw_s, convb_s=convb_s, convw_g=convw_g, dsk=dsk, gnw=gnw, spl=spl, ag=ag,
                     ea=ea, dte=dte, cdec=cdec, bet=bet, nbet=nbet, bg=bg)
            if os.environ.get("MIX_STOP") == "prelude":
                return
            MIXSEL = os.environ.get("MIX_SEL", "ssd,gdn").split(",")
            if 'ssd' in MIXSEL:
                for g in range(2):
                    self.ssd_group(g, P)
            if 'gdn' in MIXSEL:
                for hg in range(2):
                    self.gdn_group(hg, P)

    def ssd_group(self, g, P):
        X, H = self.X, self.H
        Win = self.W['a_w_in']
        Wout = self.W['a_w_out']
        cc = self.cc
        with ExitStack() as es0:
            xs_tok = self.sb("xs_tok", [128, 16, 512], BF16, es0)
            siluz = self.sb("siluz", [128, 16, 512], BF16, es0)
            BT = self.sb("BT", [128, L], BF16, es0)
            CT = self.sb("CT", [128, L], BF16, es0)
            B_tok = self.sb("B_tok", [128, 16, 128], BF16, es0)
            with ExitStack() as es:
                R = dict(w=self.ring("cw", 3, [128, 8, 128], BF16, es),
                         raw=self.ring("craw", 2, [128, L + 3], F32, es),
                         acc=self.ring("cacc", 1, [128, L], F32, es))
                xsb = self.ring("xsb", 2, [128, L], BF16, es)
                wz = self.sb("wz", [128, 8, 512], BF16, es)
                self.dma('pool', wz[:, :, :], Win[:, g * 512:(g + 1) * 512].rearrange("(c p) n -> p c n", p=128), [], [('wz',)])
                self._zero_halo(R, es)
                tiles = []
                for j in range(4):
                    ti = g * 4 + j
                    xb, xr = xsb()
                    tiles.append(dict(col=1024 + ti * 128, cw=P['convw_s'][:, ti, :], cb=P['convb_s'][:, ti:ti + 1],
                                      silu_out=xb[:, :], silu_reg=xr,
                                      post=(lambda o, r, xb=xb, xr=xr, j=j: self.to_tok(xb, xr, xs_tok, 'xs_tok', j * 128))))
                ti = 8 + g
                tiles.append(dict(col=1024 + ti * 128, cw=P['convw_s'][:, ti, :], cb=P['convb_s'][:, ti:ti + 1],
                                  silu_out=BT[:, :], silu_reg=('BT',),
                                  post=(lambda o, r: self.to_tok(BT, ('BT',), B_tok, 'B_tok', 0))))
                ti = 10 + g
                tiles.append(dict(col=1024 + ti * 128, cw=P['convw_s'][:, ti, :], cb=P['convb_s'][:, ti:ti + 1],
                                  silu_out=CT[:, :], silu_reg=('CT',)))
                self.conv_pipeline(tiles, R)
                for tt in range(16):
                    ps, pr = self.next_ps()
                    self.mm(ps[:, :], [(H[:, kc, tt * 128:(tt + 1) * 128], wz[:, kc, :]) for kc in range(8)],
                            [('wz',)] + [('H', kc, tt) for kc in range(8)], [pr])
                    self.act(siluz[:, tt, :], ps[:, :], AF.Silu, [pr], [('siluz', tt)])
                self.s.flush()
            if os.environ.get("MIX_STOP") == "ssd1":
                return
            yT = self.sb("yT", [128, 4, L], BF16, es0)
            with ExitStack() as es:
                ring = lambda n, k, sh, dt=F32: self.ring(n, k, sh, dt, es)
                cbm_r = ring("cbm", 2, [128, 128])
                lseg_r = ring("lseg", 2, [128, 4, 128])
                LT_r = ring("LT", 2, [128, 4, 128])
                MT_r = ring("MT", 2, [128, 8, 128], BF16)
                xdt_r = ring("xdt", 2, [128, 512], BF16)
                xdd_r = ring("xdd", 2, [128, 512], BF16)
                t1_r = ring("t1", 1, [128, 512])
                t3_r = ring("t3", 1, [128, 512])
                yg_r = ring("yg", 2, [128, 512])
                junk_r = ring("junk", 1, [128, 512], BF16)
                yn_r = ring("ynb", 2, [128, 512], BF16)
                sm_r = ring("ssm", 2, [128, 4])
                S = self.sb("ssdS", [128, 512], F32, es)
                Sb = self.sb("ssdSb", [128, 512], BF16, es)
                snw = self.sb("snw", [128, 512], F32, es)
                self.dma('sp', snw[:, :], self.W['snw'][:, g * 512:(g + 1) * 512], [], [('snw',)])
                hs = slice(g * 8, (g + 1) * 8)
                v8 = lambda ap: ap.rearrange("p (h d) -> p h d", h=8)
                bc8 = lambda ap: ap.unsqueeze(2).broadcast_to([128, 8, 64])
                sA = {}
                sB = {}

                def ssd_a(tt):
                    ts_ = slice(tt * 128, (tt + 1) * 128)
                    ps_cb, pcr = self.next_ps()
                    self.mm(ps_cb[:, 0:128], [(BT[:, ts_], CT[:, ts_])], [('BT',), ('CT',)], [pcr])
                    cbm, cbr = cbm_r()
                    self.tt('dve', cbm[:, :], ps_cb[:, 0:128], self.trile_f[:, :], ALU.mult, [pcr, ('trile',)], [cbr])
                    MT, mr = MT_r()
                    halves = []
                    for half in range(2):
                        lseg, lr = lseg_r()
                        ps_s, psr = self.next_ps()
                        for i in range(4):
                            hh = g * 8 + half * 4 + i
                            self.ts('dve', r32(lseg[:, i, :]), self.gt_f[:, :], P['ag'][:, tt, hh:hh + 1], None,
                                    ALU.mult, None, [('gt',), ('ag',)], [(lr, i)])
                            self.mm(ps_s[:, i * 128:(i + 1) * 128], [(r32(lseg[:, i, :]), r32(self.trile_r[:, :]))],
                                    [(lr, i), ('trile_r',)], [psr])
                        halves.append((ps_s, psr))
                    for half, (ps_s, psr) in enumerate(halves):
                        LT, ltr = LT_r()
                        self.act(LT[:, :, :].rearrange("p a b -> p (a b)"), ps_s[:, :], AF.Exp, [psr], [ltr])
                        self.tt('dve', MT[:, half * 4:(half + 1) * 4, :], LT[:, :, :],
                                cbm[:, :].unsqueeze(1).broadcast_to([128, 4, 128]), ALU.mult, [ltr, cbr], [(mr, half)])
                    xdt, xdr = xdt_r()
                    self.tt('dve', v8(xdt[:, :]), v8(xs_tok[:, tt, :]), bc8(P['spl'][:, tt, hs]), ALU.mult,
                            [('xs_tok', tt), ('spl',)], [xdr])
                    xdd, xddr = xdd_r()
                    self.tt('pool', v8(xdd[:, :]), v8(xdt[:, :]), bc8(P['dte'][:, tt, hs]), ALU.mult,
                            [xdr, ('dte',)], [xddr])
                    sA[tt] = dict(MT=MT, mr=mr, xdt=xdt, xdr=xdr, xdd=xdd, xddr=xddr, ts_=ts_)

                def ssd_b(tt):
                    d = sA.pop(tt)
                    MT, mr, xdt, xdr, xdd, xddr, ts_ = d['MT'], d['mr'], d['xdt'], d['xdr'], d['xdd'], d['xddr'], d['ts_']
                    ps_y, pyr = self.next_ps()
                    for i in range(8):
                        self.mm(ps_y[:, i * 64:(i + 1) * 64], [(MT[:, i, :], xdt[:, i * 64:(i + 1) * 64])],
                                [(mr, i // 4), xdr], [pyr])
                    t1, t1r = t1_r()
                    if tt > 0:
                        ps_o, por = self.next_ps()
                        self.mm(ps_o[:, :], [(CT[:, ts_], Sb[:, :])], [('CT',), ('ssdSb',)], [por])
                        self.tt('dve', v8(t1[:, :]), v8(ps_o[:, :]), bc8(P['ea'][:, tt, hs]), ALU.mult, [por, ('ea',)], [t1r])
                        self.tt('dve', t1[:, :], t1[:, :], ps_y[:, :], ALU.add, [t1r, pyr], [t1r])
                    else:
                        self.copy('dve', t1[:, :], ps_y[:, :], [pyr], [t1r])
                    t3, t3r = t3_r()
                    self.tt('pool', v8(t3[:, :]), v8(xs_tok[:, tt, :]), bc8(P['dsk'][:, hs]), ALU.mult,
                            [('xs_tok', tt), ('dsk',)], [t3r])
                    self.tt('dve', t3[:, :], t3[:, :], t1[:, :], ALU.add, [t3r, t1r], [t3r])
                    yg, ygr = yg_r()
                    self.tt('pool', yg[:, :], t3[:, :], siluz[:, tt, :], ALU.mult, [t3r, ('siluz', tt)], [ygr])
                    sm, smr = sm_r()
                    junk, jr = junk_r()
                    self.act(junk[:, :], yg[:, :], AF.Square, [ygr], [jr, (smr, 0)], accum_out=sm[:, 0:1])
                    if tt < 15:
                        ps_st, pstr = self.next_ps()
                        self.mm(ps_st[:, :], [(B_tok[:, tt, :], xdd[:, :])], [('B_tok', tt), xddr], [pstr])
                        if tt == 0:
                            self.copy('dve', S[:, :], ps_st[:, :], [pstr], [('ssdS',)])
                        else:
                            self.tt('dve', v8(S[:, :]), v8(S[:, :]), bc8(P['cdec'][:, tt, hs]), ALU.mult,
                                    [('ssdS',), ('cdec',)], [('ssdS',)])
                            self.tt('dve', S[:, :], S[:, :], ps_st[:, :], ALU.add, [('ssdS',), pstr], [('ssdS',)])
                        self.copy('act', Sb[:, :], S[:, :], [('ssdS',)], [('ssdSb',)])
                    sB[tt] = dict(yg=yg, ygr=ygr, sm=sm, smr=smr, ts_=ts_)

                def ssd_c(tt):
                    d = sB.pop(tt)
                    yg, ygr, sm, smr, ts_ = d['yg'], d['ygr'], d['sm'], d['smr'], d['ts_']
                    self.act(sm[:, 1:2], sm[:, 0:1], AF.Ln, [(smr, 0)], [(smr, 1)], bias=cc[:, 0:1], scale=1.0 / 512)
                    self.act(sm[:, 2:3], sm[:, 1:2], AF.Exp, [(smr, 1)], [(smr, 2)], scale=-0.5)
                    yn, ynr = yn_r()
                    self.stt(yn[:, :], yg[:, :], sm[:, 2:3], snw[:, :], ALU.mult, ALU.mult,
                             [ygr, (smr, 2), ('snw',)], [ynr])
                    ps_t, ptr_ = self.next_ps()
                    psb = ps_t[:, :].bitcast(BF16)
                    for j in range(4):
                        self.tr(psb[:, j * 128:(j + 1) * 128], yn[:, j * 128:(j + 1) * 128], self.ident_b[:, :],
                                [ynr, ('ident_b',)], [ptr_])
                    self.copy('act', yT[:, 0:4, ts_], psb[:, 0:512].rearrange("p (c t) -> p c t", c=4), [ptr_],
                              [('yT', j, tt) for j in range(4)])

                for t in range(18):
                    if t < 16:
                        ssd_a(t)
                    if 1 <= t <= 16:
                        ssd_b(t - 1)
                    if t >= 2:
                        ssd_c(t - 2)
                self.s.flush()
            with ExitStack() as es:
                wo = self.sb("wo_s", [128, 4, D], BF16, es)
                self.dma('pool', wo[:, :, :], Wout[g * 512:(g + 1) * 512, :].rearrange("(c p) n -> p c n", p=128), [], [('wo_s',)])
                for tb in range(4):
                    t0, t1_ = tb * 512, (tb + 1) * 512
                    for dt in range(8):
                        ps, pr = self.next_ps()
                        self.mm(ps[:, :], [(wo[:, j, dt * 128:(dt + 1) * 128], yT[:, j, t0:t1_]) for j in range(4)],
                                [('wo_s',)] + [('yT', j, tt) for j in range(4) for tt in range(tb * 4, tb * 4 + 4)], [pr])
                        self.tt('dve', X[:, dt, t0:t1_], ps[:, :], X[:, dt, t0:t1_], ALU.add,
                                [pr] + XR(dt, t0, t1_), XR(dt, t0, t1_))
                self.s.flush()

    def _zero_halo(self, R, es):
        for i, t in enumerate(R['raw'].tiles):
            self.s.op('pool', (lambda e, t=t: e.memset(t[:, 0:3], 0.0)), [], [("craw", i)])

    def gdn_group(self, hg, P):
        X, H = self.X, self.H
        Win = self.W['a_w_in']
        Wout = self.W['a_w_out']
        cc = self.cc
        QOFF = 2576
        with ExitStack() as es0:
            siluzg = self.sb("siluzg", [128, 16, 512], BF16, es0)
            ogT = self.sb("ogT", [128, 4, L], BF16, es0)
            with ExitStack() as es:
                wzg = self.sb("wzg", [128, 8, 512], BF16, es)
                c0 = 5648 + hg * 512
                self.dma('pool', wzg[:, :, :], Win[:, c0:c0 + 512].rearrange("(c p) n -> p c n", p=128), [], [('wzg',)])
                for tt in range(16):
                    ps, pr = self.next_ps()
                    self.mm(ps[:, :], [(H[:, kc, tt * 128:(tt + 1) * 128], wzg[:, kc, :]) for kc in range(8)],
                            [('wzg',)] + [('H', kc, tt) for kc in range(8)], [pr])
                    self.act(siluzg[:, tt, :], ps[:, :], AF.Silu, [pr], [('siluzg', tt)])
                self.s.flush()
            for i in range(4):
                h = hg * 4 + i
                with ExitStack() as esh:
                    qTn = self.sb("qTn", [128, L], BF16, esh)
                    kTn = self.sb("kTn", [128, L], BF16, esh)
                    k_tok = self.sb("k_tok", [128, 16, 128], BF16, esh)
                    v_tok = self.sb("v_tok", [128, 16, 128], BF16, esh)
                    with ExitStack() as es:
                        R = dict(w=self.ring("cw", 2, [128, 8, 128], BF16, es),
                                 raw=self.ring("craw", 2, [128, L + 3], F32, es),
                                 acc=self.ring("cacc", 2, [128, L], F32, es))
                        kvb = ogT[:, i, :]
                        sqq4 = [self.sb(f"gsqq{j}", [128, 512], BF16, es) for j in range(4)]
                        self._zero_halo(R, es)
                        def l2post(which, dstT, dn):
                            def post(acc, ar):
                                pss = []
                                for tb in range(4):
                                    t0, t1 = tb * 512, (tb + 1) * 512
                                    self.act(sqq4[tb][:, :], acc[:, t0:t1], AF.Square, [ar], [('gsqq', tb)])
                                for tb in range(4):
                                    ps, pr = self.next_ps()
                                    self.mm(ps[:, :], [(self.ones_b[:, :], sqq4[tb][:, :])], [('gsqq', tb), ('ones_b',)], [pr])
                                    pss.append((ps, pr))
                                for tb in range(4):
                                    ps, pr = pss[tb]
                                    self.act(ps[:, :], ps[:, :], AF.Ln, [pr], [pr], bias=cc[:, 0:1])
                                for tb in range(4):
                                    ps, pr = pss[tb]
                                    self.act(ps[:, :], ps[:, :], AF.Exp, [pr], [pr],
                                             bias=cc[:, 4:5] if which == 0 else cc[:, 3:4], scale=-0.5)
                                for tb in range(4):
                                    ps, pr = pss[tb]
                                    t0, t1 = tb * 512, (tb + 1) * 512
                                    self.tt('dve', dstT[:, t0:t1], acc[:, t0:t1], ps[:, :], ALU.mult, [ar, pr], [(dn, tb)])
                                if which == 1:
                                    self.to_tok(kTn, None, k_tok, 'k_tok', 0, src_regs=[('kTn', tb) for tb in range(4)])
                            return post
                        tiles = []
                        for which, dstT, dn in ((0, qTn, 'qTn'), (1, kTn, 'kTn')):
                            ti = which * 8 + h
                            tiles.append(dict(col=QOFF + ti * 128, cw=P['convw_g'][:, ti, :], cb=None, post=l2post(which, dstT, dn)))
                        ti = 16 + h
                        tiles.append(dict(col=QOFF + ti * 128, cw=P['convw_g'][:, ti, :], cb=None, silu_out=kvb, silu_reg=('kvb',),
                                          post=(lambda o, r: self.to_tok(kvb, ('kvb',), v_tok, 'v_tok', 0))))
                        self.conv_pipeline(tiles, R)
                        self.s.flush()
                    with ExitStack() as es:
                        ub = self.sb("g_ub", [128, 16, 128], F32, es)
                        wT = self.sb("g_wT", [128, 16, 128], BF16, es)
                        qkT = self.sb("g_qkT", [128, 16, 128], BF16, es)
                        NCTX = int(os.environ.get("GDN_NCTX", "4"))
                        ctxs = []
                        for ci in range(NCTX):
                            t = lambda n, sh=[128, 128], dt=F32: self.sb(f"g_{n}{ci}", sh, dt, es)
                            ctxs.append(dict(gmask=t("gmask"), DLT=t("DLT", [128, 256]), PPa=t("PPa", [128, 384]),
                                             PPb=t("PPb", [128, 384]), vb=t("vb"), kb2=t("kb2"), ci=ci))
                        gcol = lambda tt: P['ag'][:, tt, 24 + h:25 + h]
                        GPH = int(os.environ.get("GDN_PH", "3"))

                        def p2_group(grp):
                            tts = tuple(range(grp * NCTX, (grp + 1) * NCTX))
                            rgs = [(lambda n, ci=c['ci']: ('g_' + n, ci)) for c in ctxs]
                            ps1s, ps3s, ps4s = [], [], []
                            for c, tt, rg in zip(ctxs, tts, rgs):
                                self.ts('dve', r32(c['gmask'][:, :]), self.gt_f[:, :], gcol(tt), None, ALU.mult, None,
                                        [('gt',), ('ag',)], [rg('gmask')])
                                self.act(r32(c['vb'][:, :]), v_tok[:, tt, :], AF.Identity, [('v_tok', tt), ('bet',)], [rg('vb')],
                                         scale=P['bet'][:, tt, h:h + 1])
                                self.act(r32(c['kb2'][:, :]), k_tok[:, tt, :], AF.Identity, [('k_tok', tt), ('bg',)], [rg('kb2')],
                                         scale=P['bg'][:, tt, h:h + 1])
                            for c, tt, rg in zip(ctxs, tts, rgs):
                                ts_ = slice(tt * 128, (tt + 1) * 128)
                                ps1, p1r = self.next_ps()
                                self.mm(ps1[:, 0:128], [(r32(self.trile_r[:, :]), r32(c['gmask'][:, :]))], [rg('gmask'), ('trile_r',)], [p1r])
                                self.mm(ps1[:, 128:256], [(r32(c['gmask'][:, :]), r32(self.trile_r[:, :]))], [rg('gmask'), ('trile_r',)], [p1r])
                                self.mm(ps1[:, 256:384], [(kTn[:, ts_], kTn[:, ts_])], [('kTn', tt // 4)], [p1r])
                                self.mm(ps1[:, 384:512], [(kTn[:, ts_], qTn[:, ts_])], [('kTn', tt // 4), ('qTn', tt // 4)], [p1r])
                                ps1s.append((ps1, p1r))
                            yield
                            for c, tt, rg, (ps1, p1r) in zip(ctxs, tts, rgs, ps1s):
                                self.act(c['DLT'][:, :], ps1[:, 0:256], AF.Exp, [p1r], [rg('DLT')])
                                self.tt('dve', c['DLT'][:, :], c['DLT'][:, :], self.gtle_f[:, :], ALU.mult,
                                        [rg('DLT'), ('gtle',)], [rg('DLT')])
                            for c, tt, rg, (ps1, p1r) in zip(ctxs, tts, rgs, ps1s):
                                self.stt(r32(c['PPa'][:, 0:128]), ps1[:, 256:384], P['nbet'][:, tt, h:h + 1], c['DLT'][:, 0:128],
                                         ALU.mult, ALU.mult, [p1r, ('nbet',), rg('DLT')], [rg('PPa')])
                                self.tt('dve', qkT[:, tt, :], ps1[:, 384:512], c['DLT'][:, 128:256], ALU.mult,
                                        [p1r, rg('DLT')], [('g_qkT', tt)])
                            yield
                            for c, tt, rg in zip(ctxs, tts, rgs):
                                ps4, p4r = self.next_ps()
                                self.tr(ps4[:, 0:128], c['PPa'][:, 0:128], self.ident_f[:, :], [rg('PPa'), ('ident',)], [p4r])
                                ps4s.append((ps4, p4r))
                            for c, tt, rg, (ps4, p4r) in zip(ctxs, tts, rgs, ps4s):
                                self.copy('act', r32(c['PPa'][:, 128:256]), ps4[:, 0:128], [p4r], [rg('PPa')])
                                self.tt('dve', r32(c['PPa'][:, 256:384]), c['PPa'][:, 128:256], self.ident_f[:, :], ALU.add,
                                        [rg('PPa'), ('ident',)], [rg('PPa')])
                            cur, nxt = 'PPa', 'PPb'
                            for k in range(6):
                                psas = []
                                for c, rg in zip(ctxs, rgs):
                                    psa, par = self.next_ps()
                                    pp = c[cur]
                                    self.mm(psa[:, 0:128], [(r32(pp[:, 128:256]), r32(pp[:, 0:128]))], [rg(cur)], [par])
                                    if k == 0:
                                        self.mm(psa[:, 128:256], [(r32(pp[:, 0:128]), r32(pp[:, 128:256]))], [rg(cur)], [par])
                                    elif k <= 4:
                                        self.mm(psa[:, 128:384], [(r32(pp[:, 0:128]), r32(pp[:, 128:384]))], [rg(cur)], [par])
                                    else:
                                        self.mm(psa[:, 256:384], [(r32(pp[:, 0:128]), r32(pp[:, 256:384]))], [rg(cur)], [par])
                                    psas.append((psa, par))
                                yield
                                for c, rg, (psa, par) in zip(ctxs, rgs, psas):
                                    pp, pn = c[cur], c[nxt]
                                    w = 256 if k < 5 else 128
                                    self.copy('act', r32(pn[:, 0:w]), psa[:, 0:w], [par], [rg(nxt)])
                                    if k == 0:
                                        self.copy('dve', r32(pn[:, 256:384]), pp[:, 256:384], [rg(cur)], [rg(nxt)])
                                    else:
                                        self.tt('dve', r32(pn[:, 256:384]), pp[:, 256:384], psa[:, 256:384], ALU.add,
                                                [rg(cur), par], [rg(nxt)])
                                cur, nxt = nxt, cur
                                yield
                            psbs = []
                            for c, rg in zip(ctxs, rgs):
                                psb_, pbr = self.next_ps()
                                pp = c[cur]
                                self.mm(psb_[:, 0:128], [(r32(pp[:, 0:128]), r32(pp[:, 256:384]))], [rg(cur)], [pbr])
                                psbs.append((psb_, pbr))
                            yield
                            for c, rg, (psb_, pbr) in zip(ctxs, rgs, psbs):
                                pp = c[cur]
                                self.tt('dve', r32(pp[:, 256:384]), pp[:, 256:384], psb_[:, 0:128], ALU.add, [rg(cur), pbr], [rg(cur)])
                            yield
                            TT = cur
                            psus = []
                            for c, tt, rg in zip(ctxs, tts, rgs):
                                psu, pur = self.next_ps()
                                self.mm(psu[:, 0:128], [(r32(c[TT][:, 256:384]), r32(c['vb'][:, :]))], [rg(TT), rg('vb')], [pur])
                                self.mm(psu[:, 128:256], [(r32(c['kb2'][:, :]), r32(c[TT][:, 256:384]))], [rg(TT), rg('kb2')], [pur])
                                psus.append((psu, pur))
                            for c, tt, rg, (psu, pur) in zip(ctxs, tts, rgs, psus):
                                self.copy('act', ub[:, tt, :], psu[:, 0:128], [pur], [('g_ub', tt)])
                                self.copy('dve', wT[:, tt, :], psu[:, 128:256], [pur], [('g_wT', tt)])
                        S = self.sb("g_S", [128, 128], F32, es)
                        Sb = self.sb("g_Sb", [128, 128], BF16, es)
                        u_r = self.ring("g_u", 2, [128, 128], BF16, es)
                        kd_r = self.ring("g_kd", 2, [128, 128], BF16, es)
                        t_r = self.ring("g_t", 2, [128, 128], F32, es)
                        o_r = self.ring("g_o", 2, [128, 128], F32, es)
                        on_r = self.ring("g_on", 2, [128, 128], F32, es)
                        og_r = self.ring("g_og", 2, [128, 128], BF16, es)
                        jk_r = self.ring("g_jk", 1, [128, 128], BF16, es)
                        sm_r = self.ring("g_sm", 2, [128, 4], F32, es)
                        sst = {}

                        def scan_a(tt):
                            ts_ = slice(tt * 128, (tt + 1) * 128)
                            u, ur = u_r()
                            d = sst[tt] = dict(u=u, ur=ur)
                            if tt > 0:
                                ps_ws, pwr = self.next_ps()
                                self.mm(ps_ws[:, 0:128], [(wT[:, tt, :], Sb[:, :])], [('g_wT', tt), ('g_Sb',)], [pwr])
                                self.tt('dve', u[:, :], ub[:, tt, :], ps_ws[:, 0:128], ALU.subtract, [('g_ub', tt), pwr], [ur])
                            else:
                                self.copy('dve', u[:, :], ub[:, tt, :], [('g_ub', tt)], [ur])

                        def scan_b(tt):
                            ts_ = slice(tt * 128, (tt + 1) * 128)
                            d = sst.pop(tt)
                            u, ur = d['u'], d['ur']
                            if tt > 0:
                                ps_o1, po1r = self.next_ps()
                                self.mm(ps_o1[:, 0:128], [(qTn[:, ts_], Sb[:, :])], [('qTn', tt // 4), ('g_Sb',)], [po1r])
                            ps_o2, po2r = self.next_ps()
                            self.mm(ps_o2[:, 0:128], [(qkT[:, tt, :], u[:, :])], [('g_qkT', tt), ur], [po2r])
                            o, orr = o_r()
                            if tt > 0:
                                t, tr_ = t_r()
                                self.act(t[:, :], ps_o1[:, 0:128], AF.Identity, [po1r, ('ea',)], [tr_],
                                         scale=P['ea'][:, tt, 24 + h:25 + h])
                                self.tt('dve', o[:, :], t[:, :], ps_o2[:, 0:128], ALU.add, [tr_, po2r], [orr])
                            else:
                                self.copy('dve', o[:, :], ps_o2[:, 0:128], [po2r], [orr])
                            if tt < 15:
                                kd, kdr = kd_r()
                                self.act(kd[:, :], k_tok[:, tt, :], AF.Identity, [('k_tok', tt), ('dte',)], [kdr],
                                         scale=P['dte'][:, tt, 24 + h:25 + h])
                                ps_sk, pskr = self.next_ps()
                                self.mm(ps_sk[:, 0:128], [(kd[:, :], u[:, :])], [kdr, ur], [pskr])
                                if tt == 0:
                                    self.copy('dve', Sb[:, :], ps_sk[:, 0:128], [pskr], [('g_Sb',)])
                                    self.copy('dve', S[:, :], ps_sk[:, 0:128], [pskr], [('g_S',)])
                                else:
                                    self.stt(Sb[:, :], S[:, :], P['cdec'][:, tt, 24 + h:25 + h], ps_sk[:, 0:128], ALU.mult, ALU.add,
                                             [('g_S',), ('cdec',), pskr], [('g_Sb',)])
                                    self.stt(S[:, :], S[:, :], P['cdec'][:, tt, 24 + h:25 + h], ps_sk[:, 0:128], ALU.mult, ALU.add,
                                             [('g_S',), ('cdec',), pskr], [('g_S',)])
                            sm, smr = sm_r()
                            jk, jr = jk_r()
                            self.act(jk[:, :], o[:, :], AF.Square, [orr], [jr, (smr, 0)], accum_out=sm[:, 0:1])
                            self.act(sm[:, 1:2], sm[:, 0:1], AF.Ln, [(smr, 0)], [(smr, 1)], bias=cc[:, 0:1], scale=1.0 / 128)
                            self.act(sm[:, 2:3], sm[:, 1:2], AF.Exp, [(smr, 1)], [(smr, 2)], scale=-0.5)
                            on, onr = on_r()
                            self.stt(on[:, :], o[:, :], sm[:, 2:3], P['gnw'][:, :], ALU.mult, ALU.mult, [orr, (smr, 2), ('gnw',)], [onr])
                            og, ogr = og_r()
                            self.tt('dve', og[:, :], on[:, :], siluzg[:, tt, i * 128:(i + 1) * 128], ALU.mult,
                                    [onr, ('siluzg', tt)], [ogr])
                            ps_t, ptr_ = self.next_ps()
                            psb = ps_t[:, :].bitcast(BF16)
                            self.tr(psb[:, 0:128], og[:, :], self.ident_b[:, :], [ogr, ('ident_b',)], [ptr_])
                            self.copy('act', ogT[:, i, ts_], psb[:, 0:128], [ptr_], [('ogT', i, tt)])

                        queue = []
                        for grp in range(16 // NCTX if GPH >= 2 else 0):
                            for _ in p2_group(grp):
                                if queue:
                                    queue.pop(0)()
                            if GPH >= 3:
                                for tt in range(grp * NCTX, (grp + 1) * NCTX):
                                    queue.append(lambda tt=tt: scan_a(tt))
                                    queue.append(lambda tt=tt: scan_b(tt))
                        while queue:
                            queue.pop(0)()
                        self.s.flush()
            with ExitStack() as es:
                wo = self.sb("wo_g", [128, 4, D], BF16, es)
                r0 = 1024 + hg * 512
                self.dma('pool', wo[:, :, :], Wout[r0:r0 + 512, :].rearrange("(c p) n -> p c n", p=128), [], [('wo_g',)])
                for tb in range(4):
                    t0, t1_ = tb * 512, (tb + 1) * 512
                    for dt in range(8):
                        ps, pr = self.next_ps()
                        self.mm(ps[:, :], [(wo[:, j, dt * 128:(dt + 1) * 128], ogT[:, j, t0:t1_]) for j in range(4)],
                                [('wo_g',)] + [('ogT', j, tt) for j in range(4) for tt in range(tb * 4, tb * 4 + 4)], [pr])
                        self.tt('dve', X[:, dt, t0:t1_], ps[:, :], X[:, dt, t0:t1_], ALU.add,
                                [pr] + XR(dt, t0, t1_), XR(dt, t0, t1_))
                self.s.flush()


_CACHE = {}


def consts():
    i = np.arange(128)
    c = {}
    c['c_ident'] = np.eye(128, dtype=np.float32)
    c['c_ones'] = np.ones((128, 128), np.float32)
    c['c_uincl'] = (i[:, None] >= i[None, :]).astype(np.float32)
    c['c_trile'] = (i[:, None] <= i[None, :]).astype(np.float32)
    c['c_gt'] = (i[:, None] > i[None, :]).astype(np.float32)
    blk = np.zeros((128, 128), np.float32)
    blk[:64, :64] = 1
    blk[64:, 64:] = 1
    c['c_blk'] = blk
    col = np.arange(512)
    c['c_mask4'] = np.stack([(col[None, :] > (i[:, None] + 128 * k)) for k in range(4)], axis=1).astype(np.float32)
    return c


def fm_cols(v):
    return np.ascontiguousarray(np.asarray(v, np.float32).reshape(8, 128).T)


def make_inputs(inp, nseq, ncores):
    f = lambda a: np.ascontiguousarray(np.asarray(a, dtype=np.float32))
    shared = consts()
    normw = np.stack([fm_cols(inp['a_norm_w'][0]), fm_cols(inp['mlp_norm_w'][0]),
                      fm_cols(inp['c_norm_w'][0]), fm_cols(inp['mlp_norm_w'][1])], axis=1)
    shared['normw'] = np.ascontiguousarray(normw)
    shared['mlp_w1'] = f(inp['mlp_w1'])
    shared['mlp_w2'] = f(inp['mlp_w2'])
    shared['c_w_qkv'] = f(inp['c_w_qkv'][0])
    shared['c_w_o'] = f(inp['c_w_o'][0])
    shared['qkw'] = np.ascontiguousarray(np.stack([np.tile(f(inp['c_q_norm_w'][0]), 2),
                                                   np.tile(f(inp['c_k_norm_w'][0]), 2)], axis=1))
    shared['a_w_in'] = f(inp['a_w_in'][0])
    shared['a_w_out'] = f(inp['a_w_out'][0])
    bc = lambda v: np.ascontiguousarray(np.broadcast_to(np.asarray(v, np.float32)[None, :], (128, len(v))))
    cws = f(inp['ssd_conv_w'][0])
    shared['convw_s'] = np.ascontiguousarray(cws.reshape(4, 12, 128).transpose(2, 1, 0))
    shared['convb_s'] = np.ascontiguousarray(f(inp['ssd_conv_b'][0]).reshape(12, 128).T)
    cwg = f(inp['gdn_conv_w'][0])
    shared['convw_g'] = np.ascontiguousarray(cwg.reshape(4, 24, 128).transpose(2, 1, 0))
    shared['dsk'] = bc(f(inp['ssd_d_skip'][0]))
    shared['snw'] = bc(f(inp['ssd_norm_w'][0]))
    shared['gnw'] = bc(f(inp['gdn_norm_w'][0]))
    z8 = np.zeros(8, np.float32)
    shared['bias_bc'] = bc(np.concatenate([f(inp['ssd_dt_bias'][0]), z8, f(inp['gdn_dt_bias'][0])]))
    shared['alog_bc'] = bc(np.concatenate([f(inp['ssd_a_log'][0]), z8, f(inp['gdn_a_log'][0])]))
    x = f(inp['x'])
    maps = []
    for c in range(ncores):
        m = dict(shared)
        m['x'] = np.ascontiguousarray(x[c * nseq:(c + 1) * nseq])
        maps.append(m)
    return maps


ALL_STAGES = ('mix0', 'mlp0', 'attn', 'mlp1')


def run(inp, nseq=2, ncores=8, stages=ALL_STAGES, trace=False):
    key = (nseq, tuple(stages))
    if key not in _CACHE:
        _CACHE[key] = Builder(nseq, stages).build()
    nc = _CACHE[key]
    maps = make_inputs(inp, nseq, ncores)
    res = run_bass_kernel_spmd(nc, maps, core_ids=list(range(ncores)), trace=trace)
    outs = [r["out"] for r in res.results]
    return np.concatenate(outs, axis=0), res


def kernel(**inputs):
    out, _ = run(inputs)
    return out.astype(np.float32)
```
